# Optimizing a Trainium2 kernel written in Bass

```python
import math
import jax, jax.numpy as jnp
from jax import lax
import numpy as np

D_MODEL = 1024
BATCH = 8
SEQ = 2048
DEPTH = 2
DEC_BATCH = 128
DEC_SEQ = 1
PAST_LEN = 16384
PAGE_SIZE = 128

R_HEADS = 4
R_DK = 128
R_DV = 256
R_QK = R_HEADS * R_DK
R_VAL = R_HEADS * R_DV
ROPE_BASE = 10000.0
M_HEADS = 16
M_HEADDIM = 64
M_INNER = M_HEADS * M_HEADDIM
M_GROUPS = 2
M_STATE = 128
M_CONV_DIM = M_INNER + 2 * M_GROUPS * M_STATE
G_HEADS = 8
G_DK = 128
G_DV = 128
G_KEY = G_HEADS * G_DK
G_VAL = G_HEADS * G_DV
G_QKV = 2 * G_KEY + G_VAL
CONV_W = 4
CHUNK = 64
FFN_DIM = 2048
PLE_DIM = 256
N_BRANCH = 3
DN_ALPHA = (2 * DEPTH) ** 0.25
DN_BETA = (8 * DEPTH) ** -0.25
LN_EPS = 1e-5
NORM_EPS = 1e-6

IN_SPLITS = (R_QK, R_QK, R_VAL, R_VAL,
             M_INNER, M_CONV_DIM, M_HEADS,
             G_QKV, G_VAL, G_HEADS, G_HEADS,
             D_MODEL, D_MODEL, D_MODEL)
IN_DIM = sum(IN_SPLITS)

kernel_name = "hybrid_ret_ssd_gdn_decode_step"


def split_cols(h, sizes):
    idx = np.cumsum(sizes)[:-1].tolist()
    return jnp.split(h, idx, axis=-1)


def layer_norm(x, g, b):
    xf = x.astype(jnp.float32)
    mu = jnp.mean(xf, -1, keepdims=True)
    var = jnp.mean(jnp.square(xf - mu), -1, keepdims=True)
    return ((xf - mu) * lax.rsqrt(var + LN_EPS) * g + b).astype(x.dtype)


def rms(x):
    return x * lax.rsqrt(jnp.mean(jnp.square(x), -1, keepdims=True) + NORM_EPS)


def l2norm(x):
    return x * lax.rsqrt(jnp.sum(jnp.square(x), -1, keepdims=True) + NORM_EPS)


def swiglu(x, wg, wu, wd):
    return (jax.nn.silu(x @ wg) * (x @ wu)) @ wd


def rope(x, pos0):
    L = x.shape[1]
    half = x.shape[-1] // 2
    inv = ROPE_BASE ** (-jnp.arange(half, dtype=jnp.float32) / half)
    ang = (jnp.arange(L, dtype=jnp.float32) + pos0)[:, None] * inv[None, :]
    cos = jnp.cos(ang)[None, :, None, :]
    sin = jnp.sin(ang)[None, :, None, :]
    x1, x2 = x[..., :half], x[..., half:]
    return jnp.concatenate([x1 * cos - x2 * sin, x1 * sin + x2 * cos], axis=-1)


def causal_conv(x, buf, w, b=None):
    L = x.shape[1]
    xp = jnp.concatenate([buf.astype(x.dtype), x], axis=1)
    out = xp[:, 0:L] * w[0]
    for j in range(1, CONV_W):
        out = out + xp[:, j:j + L] * w[j]
    if b is not None:
        out = out + b
    return out, xp[:, L:]


def chunk_len(L):
    return CHUNK if L % CHUNK == 0 else L


def to_chunks(t, C):
    B, L = t.shape[:2]
    return jnp.moveaxis(t.reshape((B, L // C, C) + t.shape[2:]), 1, 0)


def from_chunks(t):
    n, B, C = t.shape[:3]
    return jnp.moveaxis(t, 0, 1).reshape((B, n * C) + t.shape[3:])


def seg_decay(cum):
    C = cum.shape[1]
    mask = jnp.tril(jnp.ones((C, C), bool))
    diff = jnp.moveaxis(cum[:, :, None, :] - cum[:, None, :, :], -1, 1)
    return jnp.exp(jnp.where(mask, diff, -jnp.inf))


def decay_linear_attn(q, k, v, log_a, s0):
    C = chunk_len(q.shape[1])

    def step(s, inp):
        qc, kc, vc, la = inp
        cum = jnp.cumsum(la, axis=1)
        scores = jnp.einsum("bihk,bjhk->bhij", qc, kc) * seg_decay(cum)
        o = (jnp.einsum("bhij,bjhv->bihv", scores, vc)
             + jnp.einsum("bihk,bhkv->bihv", qc * jnp.exp(cum)[..., None], s))
        last = cum[:, -1]
        s = (s * jnp.exp(last)[:, :, None, None]
             + jnp.einsum("bjhk,bjhv->bhkv", kc * jnp.exp(last[:, None] - cum)[..., None], vc))
        return s, o

    s, o = lax.scan(step, s0, tuple(to_chunks(t, C) for t in (q, k, v, log_a)))
    return from_chunks(o), s


def gated_delta(q, k, v, g, beta, s0):
    C = chunk_len(q.shape[1])
    eye = jnp.eye(C, dtype=jnp.float32)
    strict = jnp.tril(jnp.ones((C, C), bool), -1)

    def step(s, inp):
        qc, kc, vc, gc, bc = inp
        cum = jnp.cumsum(gc, axis=1)
        dec = seg_decay(cum)
        kb = kc * bc[..., None]
        l0 = jnp.where(strict, jnp.einsum("bihk,bjhk->bhij", kb, kc) * dec, 0.0)
        tinv = lax.linalg.triangular_solve(eye + l0, jnp.broadcast_to(eye, l0.shape),
                                           left_side=True, lower=True, unit_diagonal=True)
        rhs = vc * bc[..., None] - jnp.einsum("bjhk,bhkv->bjhv", kb * jnp.exp(cum)[..., None], s)
        u = jnp.einsum("bhij,bjhv->bihv", tinv, rhs)
        attn = jnp.einsum("bihk,bjhk->bhij", qc, kc) * dec
        o = (jnp.einsum("bihk,bhkv->bihv", qc * jnp.exp(cum)[..., None], s)
             + jnp.einsum("bhij,bjhv->bihv", attn, u))
        last = cum[:, -1]
        s = (s * jnp.exp(last)[:, :, None, None]
             + jnp.einsum("bjhk,bjhv->bhkv", kc * jnp.exp(last[:, None] - cum)[..., None], u))
        return s, o

    s, o = lax.scan(step, s0, tuple(to_chunks(t, C) for t in (q, k, v, g, beta)))
    return from_chunks(o), s


def token_mix(h, st, prm, i, pos0):
    st_ret, st_ssm, st_ssm_conv, st_gdn, st_gdn_conv = st
    B, L, _ = h.shape
    f32 = jnp.float32
    (rq, rk, rv, rg, mz, mxbc, mdt, gqkv, gz, ga, gb, m1, m2, m3) = split_cols(h @ prm["w_in"][i], IN_SPLITS)

    q = rope(rq.reshape(B, L, R_HEADS, R_DK).astype(f32), pos0)
    k = rope(rk.reshape(B, L, R_HEADS, R_DK).astype(f32), pos0) * (R_DK ** -0.5)
    v = rv.reshape(B, L, R_HEADS, R_DV).astype(f32)
    log_gamma = jnp.log1p(-jnp.exp2(-5.0 - jnp.arange(R_HEADS, dtype=f32)))
    o, s_ret = decay_linear_attn(q, k, v, jnp.broadcast_to(log_gamma, (B, L, R_HEADS)), st_ret.astype(f32))
    mu = jnp.mean(o, -1, keepdims=True)
    var = jnp.mean(jnp.square(o - mu), -1, keepdims=True)
    o = (o - mu) * lax.rsqrt(var + LN_EPS)
    y_ret = (o.reshape(B, L, R_VAL) * jax.nn.silu(rg.astype(f32))).astype(h.dtype) @ prm["w_ret_out"][i]

    xbc, conv_ssm = causal_conv(mxbc, st_ssm_conv, prm["ssm_conv_w"][i], prm["ssm_conv_b"][i])
    xbc = jax.nn.silu(xbc).astype(f32)
    xs, bm, cm = split_cols(xbc, (M_INNER, M_GROUPS * M_STATE, M_GROUPS * M_STATE))
    dt = jax.nn.softplus(mdt.astype(f32) + prm["ssm_dt_bias"][i])
    la = -jnp.exp(prm["ssm_a_log"][i]) * dt
    xh = xs.reshape(B, L, M_HEADS, M_HEADDIM)
    rep = M_HEADS // M_GROUPS
    bk = jnp.repeat(bm.reshape(B, L, M_GROUPS, M_STATE), rep, axis=2)
    cq = jnp.repeat(cm.reshape(B, L, M_GROUPS, M_STATE), rep, axis=2)
    o, s_ssm = decay_linear_attn(cq, bk, xh * dt[..., None], la, st_ssm.astype(f32))
    o = o + prm["ssm_d"][i][:, None] * xh
    o = (o.reshape(B, L, M_INNER) * jax.nn.silu(mz.astype(f32))).reshape(B, L, M_GROUPS, M_INNER // M_GROUPS)
    y_ssm = (rms(o).reshape(B, L, M_INNER) * prm["ssm_norm_w"][i]).astype(h.dtype) @ prm["w_ssm_out"][i]

    qkv, conv_gdn = causal_conv(gqkv, st_gdn_conv, prm["gdn_conv_w"][i])
    qkv = jax.nn.silu(qkv).astype(f32)
    gq, gk, gv = split_cols(qkv, (G_KEY, G_KEY, G_VAL))
    gq = l2norm(gq.reshape(B, L, G_HEADS, G_DK)) * (G_DK ** -0.5)
    gk = l2norm(gk.reshape(B, L, G_HEADS, G_DK))
    gv = gv.reshape(B, L, G_HEADS, G_DV)
    beta = jax.nn.sigmoid(gb.astype(f32))
    g = -jnp.exp(prm["gdn_a_log"][i]) * jax.nn.softplus(ga.astype(f32) + prm["gdn_dt_bias"][i])
    o, s_gdn = gated_delta(gq, gk, gv, g, beta, st_gdn.astype(f32))
    o = rms(o) * prm["gdn_norm_w"][i]
    y_gdn = (o.reshape(B, L, G_VAL) * jax.nn.silu(gz.astype(f32))).astype(h.dtype) @ prm["w_gdn_out"][i]

    mixed = jax.nn.sigmoid(m1) * y_ret + jax.nn.sigmoid(m2) * y_ssm + jax.nn.sigmoid(m3) * y_gdn
    return mixed @ prm["w_o"][i], (s_ret, s_ssm, conv_ssm, s_gdn, conv_gdn)


def layer(x, p_i, st, prm, i, pos0):
    g, b = prm["ln_g"][i], prm["ln_b"][i]
    wg, wu, wd = prm["ffn_wg"][i], prm["ffn_wu"][i], prm["ffn_wd"][i]
    x = layer_norm(DN_ALPHA * x + 0.5 * swiglu(x, wg[0], wu[0], wd[0]), g[0], b[0])
    mix, new_st = token_mix(x, st, prm, i, pos0)
    x = layer_norm(DN_ALPHA * x + mix, g[1], b[1])
    x = layer_norm(DN_ALPHA * x + 0.5 * swiglu(x, wg[1], wu[1], wd[1]), g[2], b[2])
    pe = jax.nn.sigmoid(x @ prm["pe_gate"][i]) * (p_i.astype(x.dtype) @ prm["pe_proj"][i])
    x = layer_norm(DN_ALPHA * x + pe, g[3], b[3])
    return x, new_st


def trunk(x, p, states, prm, pos0):
    new = []
    for i in range(DEPTH):
        x, st_i = layer(x, p[i], tuple(s[i] for s in states), prm, i, pos0)
        new.append(st_i)
    stacked = tuple(jnp.stack([n[j] for n in new]) for j in range(5))
    return x, stacked


def setup_inputs(seed: int = 0) -> dict:
    key = jax.random.key(seed)
    ks = iter(jax.random.split(key, 40))
    nrm = lambda shape, s: jax.random.normal(next(ks), shape, jnp.float32) * s

    def inv_softplus_dt(shape):
        u = jax.random.uniform(next(ks), shape, jnp.float32)
        dt = jnp.exp(u * (math.log(0.1) - math.log(0.001)) + math.log(0.001))
        return dt + jnp.log(-jnp.expm1(-dt))

    def a_log(shape):
        return jnp.log(jax.random.uniform(next(ks), shape, jnp.float32, 1.0, 16.0))

    return {
        "x_prompt": nrm((BATCH, SEQ, D_MODEL), 1.0),
        "x_sample": nrm((DEC_BATCH, DEC_SEQ, D_MODEL), 1.0),
        "state_ret": nrm((DEPTH, DEC_BATCH, R_HEADS, R_DK, R_DV), 1.0),
        "state_ssm": nrm((DEPTH, DEC_BATCH, M_HEADS, M_STATE, M_HEADDIM), 0.3),
        "state_ssm_conv": nrm((DEPTH, DEC_BATCH, CONV_W - 1, M_CONV_DIM), 1.0),
        "state_gdn": nrm((DEPTH, DEC_BATCH, G_HEADS, G_DK, G_DV), 0.3),
        "state_gdn_conv": nrm((DEPTH, DEC_BATCH, CONV_W - 1, G_QKV), 1.0),
        "p_prompt": nrm((DEPTH, BATCH, SEQ, PLE_DIM), 1.0),
        "p_sample": nrm((DEPTH, DEC_BATCH, DEC_SEQ, PLE_DIM), 1.0),
        "ln_g": 1.0 + nrm((DEPTH, 4, D_MODEL), 0.02),
        "ln_b": nrm((DEPTH, 4, D_MODEL), 0.02),
        "ffn_wg": nrm((DEPTH, 2, D_MODEL, FFN_DIM), D_MODEL ** -0.5),
        "ffn_wu": nrm((DEPTH, 2, D_MODEL, FFN_DIM), D_MODEL ** -0.5),
        "ffn_wd": nrm((DEPTH, 2, FFN_DIM, D_MODEL), FFN_DIM ** -0.5 * DN_BETA),
        "w_in": nrm((DEPTH, D_MODEL, IN_DIM), D_MODEL ** -0.5),
        "ssm_conv_w": nrm((DEPTH, CONV_W, M_CONV_DIM), CONV_W ** -0.5),
        "ssm_conv_b": nrm((DEPTH, M_CONV_DIM), 0.01),
        "ssm_dt_bias": inv_softplus_dt((DEPTH, M_HEADS)),
        "ssm_a_log": a_log((DEPTH, M_HEADS)),
        "ssm_d": 1.0 + nrm((DEPTH, M_HEADS), 0.1),
        "ssm_norm_w": 1.0 + nrm((DEPTH, M_INNER), 0.02),
        "gdn_conv_w": nrm((DEPTH, CONV_W, G_QKV), CONV_W ** -0.5),
        "gdn_dt_bias": inv_softplus_dt((DEPTH, G_HEADS)),
        "gdn_a_log": a_log((DEPTH, G_HEADS)),
        "gdn_norm_w": 1.0 + nrm((DEPTH, G_DV), 0.02),
        "w_ret_out": nrm((DEPTH, R_VAL, D_MODEL), R_VAL ** -0.5 * DN_BETA),
        "w_ssm_out": nrm((DEPTH, M_INNER, D_MODEL), M_INNER ** -0.5 * DN_BETA),
        "w_gdn_out": nrm((DEPTH, G_VAL, D_MODEL), G_VAL ** -0.5 * DN_BETA),
        "w_o": nrm((DEPTH, D_MODEL, D_MODEL), D_MODEL ** -0.5 * DN_BETA),
        "pe_proj": nrm((DEPTH, PLE_DIM, D_MODEL), PLE_DIM ** -0.5 * DN_BETA),
        "pe_gate": nrm((DEPTH, D_MODEL, D_MODEL), D_MODEL ** -0.5),
    }


def reference(x_prompt, x_sample, state_ret, state_ssm, state_ssm_conv, state_gdn, state_gdn_conv,
              p_prompt, p_sample, ln_g, ln_b, ffn_wg, ffn_wu, ffn_wd, w_in,
              ssm_conv_w, ssm_conv_b, ssm_dt_bias, ssm_a_log, ssm_d, ssm_norm_w,
              gdn_conv_w, gdn_dt_bias, gdn_a_log, gdn_norm_w,
              w_ret_out, w_ssm_out, w_gdn_out, w_o, pe_proj, pe_gate):
    prm = dict(ln_g=ln_g, ln_b=ln_b, ffn_wg=ffn_wg, ffn_wu=ffn_wu, ffn_wd=ffn_wd, w_in=w_in,
               ssm_conv_w=ssm_conv_w, ssm_conv_b=ssm_conv_b, ssm_dt_bias=ssm_dt_bias,
               ssm_a_log=ssm_a_log, ssm_d=ssm_d, ssm_norm_w=ssm_norm_w,
               gdn_conv_w=gdn_conv_w, gdn_dt_bias=gdn_dt_bias, gdn_a_log=gdn_a_log,
               gdn_norm_w=gdn_norm_w, w_ret_out=w_ret_out, w_ssm_out=w_ssm_out,
               w_gdn_out=w_gdn_out, w_o=w_o, pe_proj=pe_proj, pe_gate=pe_gate)
    f32 = jnp.float32
    zero_states = (jnp.zeros((DEPTH, BATCH, R_HEADS, R_DK, R_DV), f32),
                   jnp.zeros((DEPTH, BATCH, M_HEADS, M_STATE, M_HEADDIM), f32),
                   jnp.zeros((DEPTH, BATCH, CONV_W - 1, M_CONV_DIM), x_prompt.dtype),
                   jnp.zeros((DEPTH, BATCH, CONV_W - 1, G_QKV), x_prompt.dtype)[:, :, :, :0].sum() * 0 + jnp.zeros((DEPTH, BATCH, G_HEADS, G_DK, G_DV), f32) if False else jnp.zeros((DEPTH, BATCH, G_HEADS, G_DK, G_DV), f32),
                   jnp.zeros((DEPTH, BATCH, CONV_W - 1, G_QKV), x_prompt.dtype))
    y_prompt, (rp, sp, scp, gp, gcp) = trunk(x_prompt, p_prompt, zero_states, prm, 0)
    y_sample, (rs, ss, scs, gs, gcs) = trunk(
        x_sample, p_sample, (state_ret, state_ssm, state_ssm_conv, state_gdn, state_gdn_conv), prm, PAST_LEN)
    return (y_prompt, y_sample, rp, sp, scp, gp, gcp, rs, ss, scs, gs, gcs)
```

```python
import numpy as np
from contextlib import ExitStack
import concourse.bass as bass
import concourse.mybir as mybir
from concourse.bass_utils import run_bass_kernel_spmd

F32, BF16 = mybir.dt.float32, mybir.dt.bfloat16
AF = mybir.ActivationFunctionType
ALU = mybir.AluOpType

D = 1024
DEPTH = 2
FFN = 2048
PLE = 256
IN_DIM = 12832
DN_ALPHA = (2 * DEPTH) ** 0.25
LN_EPS = 1e-5
NORM_EPS = 1e-6
PAST_LEN = 16384
NS = 16

C_RQ, C_RK, C_RV, C_RG = 0, 512, 1024, 2048
C_MZ, C_MXBC, C_MDT = 3072, 4096, 5632
C_GQKV, C_GZ, C_GA, C_GB = 5648, 8720, 9744, 9752
C_M1, C_M2, C_M3 = 9760, 10784, 11808


class Sem:
    def __init__(self, name):
        self.name = name
        self.count = 0
        self.h = None


class Reg:
    __slots__ = ("writers", "readers")

    def __init__(self):
        self.writers = {}
        self.readers = {}


class Buf:
    def __init__(self, prog, name, t, nreg=1):
        self.prog, self.name, self.t = prog, name, t
        self.regs = [[Reg()] for _ in range(nreg)]
        self.dsem = None

    def sem(self):
        if self.dsem is None:
            self.dsem = self.prog.named_sem("d_" + getattr(self, "sem_name", self.name))
        return self.dsem

    def __getitem__(self, k):
        return self.t[k]


class Op:
    __slots__ = ("eng", "fn", "deps", "needed", "is_dma", "sem", "val", "waits", "dmawaits")

    def __init__(self, eng, fn, is_dma):
        self.eng, self.fn, self.is_dma = eng, fn, is_dma
        self.deps = []
        self.dmawaits = []
        self.needed = False
        self.sem = None
        self.val = 0


def _regs(spec):
    out = []
    for s in spec:
        if isinstance(s, Buf):
            for g in s.regs:
                out.extend(g)
        else:
            b, idx = s
            if isinstance(idx, int):
                out.extend(b.regs[idx])
            else:
                for i in idx:
                    out.extend(b.regs[i])
    return out


class Arena:
    GRAN = 256

    def __init__(self, prog, name, nbytes):
        self.prog = prog
        self.nbytes = nbytes
        self.base = prog.sbuf(name, [128, nbytes // 2], BF16)
        self.gr = [Reg() for _ in range(nbytes // self.GRAN)]
        self.off = 0
        self.n = 0

    def reset(self, off=0):
        self.off = off

    def alloc(self, name, free_shape, dt, nreg=1):
        esz = 2 if dt == BF16 else 4
        nel = int(np.prod(free_shape))
        nb = nel * esz
        nb_al = (nb + self.GRAN - 1) // self.GRAN * self.GRAN
        assert self.off + nb_al <= self.nbytes, f"arena overflow allocating {name}: {self.off}+{nb_al}>{self.nbytes}"
        o2 = self.off // 2
        v = self.base.t[:, o2:o2 + nb // 2]
        if dt != BF16:
            v = v.bitcast(dt)
        if len(free_shape) > 1:
            names = " ".join(f"d{i}" for i in range(len(free_shape)))
            kw = {f"d{i}": int(free_shape[i]) for i in range(len(free_shape))}
            v = v.rearrange(f"p ({names}) -> p {names}", **kw)
        self.n += 1
        b = Buf(self.prog, f"{name}_{self.n}", v, 1)
        b.sem_name = f"a_{name}_{self.off}"
        g0 = self.off // self.GRAN
        ng = nb_al // self.GRAN
        grs = self.gr[g0:g0 + ng]
        if nreg == 1:
            b.regs = [grs]
        else:
            assert ng % nreg == 0, (name, ng, nreg)
            k = ng // nreg
            b.regs = [grs[i * k:(i + 1) * k] for i in range(nreg)]
        self.off += nb_al
        return b


class Prog:
    ENGS = ("pe", "act", "dve", "pool", "sp")
    ATTR = {"pe": "tensor", "act": "scalar", "dve": "vector", "pool": "gpsimd", "sp": "sync"}

    def __init__(self, nc):
        self.nc = nc
        self.es = ExitStack()
        self.ops = {e: [] for e in self.ENGS}
        self.sems = []
        self.esem = {e: self.new_sem("e_" + e) for e in ("pe", "act", "dve", "pool")}
        self.nbuf = 0
        self.out_sems = set()

    def new_sem(self, name):
        s = Sem(name)
        self.sems.append(s)
        return s

    def named_sem(self, name):
        d = self.__dict__.setdefault("_named", {})
        if name not in d:
            d[name] = self.new_sem(name)
        return d[name]

    def sbuf(self, name, shape, dt, nreg=1):
        nb = int(np.prod(shape[1:])) * (2 if dt == BF16 else 4)
        self.sb_bytes = getattr(self, "sb_bytes", 0) + nb
        self.sb_log = getattr(self, "sb_log", []) + [(name, nb)]
        t = self.es.enter_context(self.nc.sbuf_tensor("s_" + name, list(shape), dt))
        return Buf(self, name, t, nreg)

    def psum(self, name, shape, dt, nreg=1):
        t = self.es.enter_context(self.nc.psum_tensor("p_" + name, list(shape), dt))
        return Buf(self, name, t, nreg)

    def dram(self, name, shape, dt, kind, nreg=1):
        t = self.nc.dram_tensor(name, list(shape), dt, kind=kind)
        return Buf(self, name, t.ap(), nreg)

    def _dep(self, c, p):
        if p is None or p is c:
            return
        if p.is_dma:
            c.dmawaits.append((p.sem, p.sem.count))
            return
        p.needed = True
        c.deps.append(p)

    def op(self, eng, fn, reads=(), writes=(), dma_sem=None):
        is_dma = dma_sem is not None
        o = Op(eng, fn, is_dma)
        st = self.__dict__.setdefault("tagstat", {})
        key = (getattr(self, "tag", "-"), eng)
        st[key] = st.get(key, 0) + 1
        rr, ww = _regs(reads), _regs(writes)
        for r in rr:
            for e, p in r.writers.items():
                if (not is_dma) and (not p.is_dma) and e == eng and eng == "pe":
                    continue
                self._dep(o, p)
        for r in ww:
            for e, p in r.readers.items():
                if (not is_dma) and (not p.is_dma) and e == eng and eng == "pe":
                    continue
                self._dep(o, p)
            for e, p in r.writers.items():
                if (not is_dma) and (not p.is_dma) and e == eng and eng == "pe":
                    continue
                if is_dma and p.is_dma and p.sem is dma_sem:
                    continue
                self._dep(o, p)
        key = ("dma", id(o)) if is_dma else eng
        if is_dma:
            dma_sem.count += 16
            o.sem, o.val = dma_sem, dma_sem.count
        for r in rr:
            r.readers[key] = o
        for r in ww:
            r.writers = {key: o}
            r.readers = {}
        self.ops[eng].append(o)
        return o

    def pe(self, fn, reads, writes):
        return self.op("pe", fn, reads, writes)

    def act(self, fn, reads, writes):
        return self.op("act", fn, reads, writes)

    def dve(self, fn, reads, writes):
        return self.op("dve", fn, reads, writes)

    def pool(self, fn, reads, writes):
        return self.op("pool", fn, reads, writes)

    def dma(self, q, out, in_, reads, writes, sem, is_output=False, **kw):
        if is_output:
            self.out_sems.add(sem)
        return self.op(q, lambda e: e.dma_start(out=out, in_=in_, **kw), reads, writes, dma_sem=sem)

    def finish(self):
        nc = self.nc
        fin = Op("sp", None, False)
        for s in self.sems:
            if s.name.startswith("d_") and s.count > 0:
                fin.dmawaits.append((s, s.count))
        self.ops["sp"].append(fin)
        for e in ("pe", "act", "dve", "pool"):
            n = 0
            for o in self.ops[e]:
                if o.is_dma:
                    continue
                if o.needed:
                    n += 1
                    o.sem, o.val = self.esem[e], n
            self.esem[e].count = n
        for s in self.sems:
            if s.count > 0:
                s.h = self.es.enter_context(nc.semaphore(s.name))
        block = self.es.enter_context(nc.Block())
        stats = {}
        for e in self.ENGS:
            ops = self.ops[e]

            def body(eng, ops=ops, e=e):
                seen = {}
                nw = 0
                for o in ops:
                    ws = {}
                    for p in o.deps:
                        ws[p.sem] = max(ws.get(p.sem, 0), p.val)
                    for s, v in o.dmawaits:
                        ws[s] = max(ws.get(s, 0), v)
                    for s, v in ws.items():
                        if seen.get(s, 0) >= v:
                            continue
                        seen[s] = v
                        eng.wait_ge(s.h, v)
                        nw += 1
                    if o.fn is None:
                        continue
                    ins = o.fn(eng)
                    if o.is_dma:
                        ins.then_inc(o.sem.h, 16)
                    elif o.needed:
                        ins.then_inc(o.sem.h, 1)
                stats[e] = (len(ops), nw)

            getattr(block, self.ATTR[e])(body)
        self.es.close()
        return stats


class Cfg:
    def __init__(self, NH=2, NTH=8, sample=True, layers=2, mix=("ret", "ssd", "gdn"), pegate=True, ffn=True):
        self.NH, self.NTH, self.sample, self.layers = NH, NTH, sample, layers
        self.mix, self.pegate, self.ffn = mix, pegate, ffn
        self.dbg = {}
        self.T = NH * NTH * 128


class MK:
    def __init__(self, cfg):
        self.cfg = cfg
        nc = bass.Bass("TRN2", target_bir_lowering=False)
        self.nc = nc
        self.P = Prog(nc)
        self.declare_io()
        self.alloc()

    def din(self, name, shape):
        return self.nc.dram_tensor(name, list(shape), F32, kind="ExternalInput").ap()

    def dout(self, name, shape):
        return self.nc.dram_tensor(name, list(shape), F32, kind="ExternalOutput").ap()

    def declare_io(self):
        c = self.cfg
        T = c.T
        L = DEPTH
        self.i = {}
        I = self.i
        I["xp"] = self.din("xp", [T, D])
        I["pp"] = self.din("pp", [L, T, PLE])
        if c.sample:
            I["xs"] = self.din("xs", [NS, D])
            I["ps"] = self.din("ps", [L, NS, PLE])
            I["st_ret"] = self.din("st_ret", [L, NS, 4, 128, 256])
            I["st_ssm"] = self.din("st_ssm", [L, NS, 16, 128, 64])
            I["st_ssm_conv"] = self.din("st_ssm_conv", [L, NS, 3, 1536])
            I["st_gdn"] = self.din("st_gdn", [L, NS, 8, 128, 128])
            I["st_gdn_conv"] = self.din("st_gdn_conv", [L, NS, 3, 3072])
        for nm, shp in [("ln_g", [L, 4, D]), ("ln_b", [L, 4, D]), ("ffn_wg", [L, 2, D, FFN]),
                        ("ffn_wu", [L, 2, D, FFN]), ("ffn_wd", [L, 2, FFN, D]), ("w_in", [L, D, IN_DIM]),
                        ("ssm_conv_w", [L, 4, 1536]), ("ssm_conv_b", [L, 1536]), ("ssm_dt_bias", [L, 16]),
                        ("ssm_a_log", [L, 16]), ("ssm_d", [L, 16]), ("ssm_norm_w", [L, 1024]),
                        ("gdn_conv_w", [L, 4, 3072]), ("gdn_dt_bias", [L, 8]), ("gdn_a_log", [L, 8]),
                        ("gdn_norm_w", [L, 128]), ("w_ret_out", [L, D, D]), ("w_ssm_out", [L, D, D]),
                        ("w_gdn_out", [L, D, D]), ("w_o", [L, D, D]), ("pe_proj", [L, PLE, D]),
                        ("pe_gate", [L, D, D])]:
            I[nm] = self.din(nm, shp)
        I["c_ident"] = self.din("c_ident", [128, 128])
        I["c_rope"] = self.din("c_rope", [T + NS, 4, 64])
        I["c_retmask"] = self.din("c_retmask", [4, 128, 128])
        I["c_retrow"] = self.din("c_retrow", [4, 3, 128])
        I["c_masks"] = self.din("c_masks", [8, 128, 128])
        self.o = {}
        O = self.o
        O["y_p"] = self.dout("y_p", [T, D])
        O["ret_p"] = self.dout("ret_p", [L, 4, 128, 256])
        O["ssm_p"] = self.dout("ssm_p", [L, 16, 128, 64])
        O["ssm_conv_p"] = self.dout("ssm_conv_p", [L, 3, 1536])
        O["gdn_p"] = self.dout("gdn_p", [L, 8, 128, 128])
        O["gdn_conv_p"] = self.dout("gdn_conv_p", [L, 3, 3072])
        if c.sample:
            O["y_s"] = self.dout("y_s", [NS, D])
            O["ret_s"] = self.dout("ret_s", [L, NS, 4, 128, 256])
            O["ssm_s"] = self.dout("ssm_s", [L, NS, 16, 128, 64])
            O["ssm_conv_s"] = self.dout("ssm_conv_s", [L, NS, 3, 1536])
            O["gdn_s"] = self.dout("gdn_s", [L, NS, 8, 128, 128])
            O["gdn_conv_s"] = self.dout("gdn_conv_s", [L, NS, 3, 3072])

    def alloc(self):
        c, P = self.cfg, self.P
        self.NTT = c.NTH + (1 if c.sample else 0)
        self.TS = c.NTH * 128 + (NS if c.sample else 0)
        NTT, TS = self.NTT, self.TS
        self.xa = P.sbuf("xa", [128, NTT, D], F32, nreg=NTT * 2)
        self.xT = P.sbuf("xT", [128, 8, TS], BF16, nreg=NTT)
        self.hT = P.sbuf("hT", [128, 16, TS], BF16, nreg=16)
        self.NSLOT = 6
        self.wslots = [P.sbuf(f"w{i}", [128, 8, 512], BF16) for i in range(self.NSLOT)]
        self.wi = 0
        self.lnp = P.sbuf("lnp", [128, 2, D], F32)
        self.identb = P.sbuf("identb", [128, 128], BF16)
        self.identf = P.sbuf("identf", [128, 128], F32)
        self.tmpA = [P.sbuf(f"tmpA{i}", [128, D], F32) for i in range(2)]
        self.tmpB = [P.sbuf(f"tmpB{i}", [128, D], F32) for i in range(2)]
        self.xb = [P.sbuf(f"xb{i}", [128, D], BF16) for i in range(1)]
        self.lnst = [P.sbuf(f"lnst{i}", [128, 2, 6], F32) for i in range(2)]
        self.lnmv = [P.sbuf(f"lnmv{i}", [128, 4], F32) for i in range(2)]
        self.ps = P.psum("ps", [128, 8, 512], F32, nreg=8)
        self.psi = 0
        self.rr = {}
        P.dma("pool", self.identb.t[:], self.i["c_ident"], [], [self.identb], self.identb.sem())
        P.dma("sp", self.identf.t[:], self.i["c_ident"], [], [self.identf], self.identf.sem())

    def rot(self, key, lst):
        i = self.rr.get(key, 0)
        self.rr[key] = i + 1
        return lst[i % len(lst)]

    def bank(self):
        b = 2 + self.psi
        self.psi = (self.psi + 1) % 6
        return b

    def bank2(self):
        if self.psi % 2:
            self.psi = (self.psi + 1) % 6
        b = 2 + self.psi
        self.psi = (self.psi + 2) % 6
        return b

    def wslot(self):
        s = self.wslots[self.wi % self.NSLOT]
        self.wi += 1
        return s

    def wload(self, slot, c0, src2d, K=1024):
        ncols = src2d.shape[1]
        kc = K // 128
        self.P.dma("pool", slot.t[:, 0:kc, c0:c0 + ncols], src2d.rearrange("(k p) c -> p k c", p=128),
                   [], [slot], slot.sem())

    def tiles(self):
        c = self.cfg
        out = [(m, 128, m * 128) for m in range(c.NTH)]
        if self.has_sample:
            out.append((c.NTH, NS, c.NTH * 128))
        return out

    def nblocks(self):
        c = self.cfg
        TP = c.NTH * 128
        out = [(n0, min(512, TP - n0)) for n0 in range(0, TP, 512)]
        if self.has_sample:
            out.append((TP, NS))
        return out

    def xT_regs(self, n0, nsz):
        return (self.xT, list(range(n0 // 128, (n0 + nsz - 1) // 128 + 1)))

    @staticmethod
    def run_pipelined(gens, depth):
        it = iter(gens)
        active = []
        done = False
        while True:
            if not done and len(active) < depth:
                try:
                    active.append(next(it))
                except StopIteration:
                    done = True
            if not active:
                if done:
                    break
                continue
            for g in list(active):
                try:
                    next(g)
                except StopIteration:
                    active.remove(g)

    def mm(self, out, lhsT, rhs, start, stop, reads, writes):
        n = int(np.prod(rhs.shape[1:]))
        cyc = max(64, n) * (4 if rhs.dtype == F32 else 1)
        pc = self.P.__dict__.setdefault("pecost", {})
        t = getattr(self.P, "tag", "-")
        pc[t] = pc.get(t, 0) + cyc
        self.P.pe(lambda e: e.matmul(out, lhsT, rhs, start=start, stop=stop), reads, writes)

    def emit_xT(self, m, rows, col0, src, final_out=None):
        P = self.P
        xa, xT, ps = self.xa, self.xT, self.ps
        xb = self.rot("xb", self.xb)
        P.act(lambda e: e.activation(xa.t[:rows, m, :], src.t[:rows, :], AF.Copy, scale=float(DN_ALPHA)),
              [src], [(xa, [2 * m, 2 * m + 1])])
        P.dve(lambda e: e.tensor_copy(xb.t[:rows, :], src.t[:rows, :]), [src], [xb])
        b = self.bank()
        pst = ps.t[:, b, :].bitcast(BF16).rearrange("p (k n) -> p k n", k=8)
        for k in range(8):
            P.pe(lambda e, k=k: e.transpose(pst[:, k, :rows], xb.t[:rows, k * 128:(k + 1) * 128],
                                            self.identb.t[:rows, :rows]),
                 [xb, self.identb], [(ps, b)])
        P.act(lambda e: e.activation(xT.t[:, :, col0:col0 + rows], pst[:, :, :rows], AF.Copy),
              [(ps, b)], [(xT, m)])

    def layer_norm(self, l, idx, final=False):
        self.P.tag = "ln"
        P = self.P
        I = self.i
        lnp = self.lnp
        P.dma("sp", lnp.t[:, 0, :], I["ln_g"][l, idx, :].partition_broadcast(128), [], [lnp], lnp.sem())
        P.dma("sp", lnp.t[:, 1, :], I["ln_b"][l, idx, :].partition_broadcast(128), [], [lnp], lnp.sem())
        xa = self.xa

        def tile_gen(i, m, rows, col0):
            par = i % 2
            st, mv, tA, tB = self.lnst[par], self.lnmv[par], self.tmpA[par], self.tmpB[par]
            xr = (xa, [2 * m, 2 * m + 1])
            P.dve(lambda e: e.bn_stats(st.t[:rows, 0, :], xa.t[:rows, m, 0:512]), [xr], [st])
            P.dve(lambda e: e.bn_stats(st.t[:rows, 1, :], xa.t[:rows, m, 512:1024]), [xr], [st])
            P.dve(lambda e: e.bn_aggr(mv.t[:rows, 0:2], st.t[:rows, :, :]), [st], [mv])
            yield
            P.act(lambda e: e.activation(mv.t[:rows, 2:3], mv.t[:rows, 1:2], AF.Ln, bias=float(LN_EPS)), [mv], [mv])
            P.act(lambda e: e.activation(mv.t[:rows, 3:4], mv.t[:rows, 2:3], AF.Exp, scale=-0.5), [mv], [mv])
            yield
            P.dve(lambda e: e.tensor_scalar(tA.t[:rows, :], xa.t[:rows, m, :], mv.t[:rows, 0:1], mv.t[:rows, 3:4],
                                            ALU.subtract, ALU.mult), [xr, mv], [tA])
            P.dve(lambda e: e.tensor_tensor(tA.t[:rows, :], tA.t[:rows, :], lnp.t[:rows, 0, :], ALU.mult), [tA, lnp], [tA])
            P.dve(lambda e: e.tensor_tensor(tB.t[:rows, :], tA.t[:rows, :], lnp.t[:rows, 1, :], ALU.add), [tA, lnp], [tB])
            yield
            if final:
                self.store_y(m, rows, tB)
            else:
                self.emit_xT(m, rows, col0, tB)

        self.run_pipelined((tile_gen(i, m, rows, col0) for i, (m, rows, col0) in enumerate(self.tiles())), 2)

    def store_y(self, m, rows, src):
        c = self.cfg
        if rows == 128:
            t0 = (self.half * c.NTH + m) * 128
            dst = self.o["y_p"][t0:t0 + 128, :]
        else:
            dst = self.o["y_s"][:, :]
        self.P.dma("sp", dst, src.t[:rows, :], [src], [], src.sem(), is_output=True)

    def load_x(self):
        c = self.cfg
        for (m, rows, col0) in self.tiles():
            tB = self.rot("tmpB", self.tmpB)
            if rows == 128:
                t0 = (self.half * c.NTH + m) * 128
                src = self.i["xp"][t0:t0 + 128, :]
            else:
                src = self.i["xs"][:, :]
            self.P.dma("sp", tB.t[:rows, :], src, [], [tB], tB.sem())
            self.emit_xT(m, rows, col0, tB)

    def ffn(self, l, idx):
        self.P.tag = "ffn"
        P, I = self.P, self.i
        ps, xT, hT, xa = self.ps, self.xT, self.hT, self.xa
        wg, wu, wd = I["ffn_wg"][l, idx], I["ffn_wu"][l, idx], I["ffn_wd"][l, idx]
        slots = []

        def loadA(hb):
            s = self.wslot()
            self.wload(s, 0, wg[:, hb * 256:(hb + 1) * 256])
            self.wload(s, 256, wu[:, hb * 256:(hb + 1) * 256])
            return s

        nxt = loadA(0)
        for hb in range(8):
            cur = nxt
            if hb + 1 < 8:
                nxt = loadA(hb + 1)
            for jj in range(2):
                j = hb * 2 + jj
                for (n0, nsz) in self.nblocks():
                    bg, bu = self.bank(), self.bank()
                    xr = self.xT_regs(n0, nsz)
                    for k in range(8):
                        self.mm(ps.t[:, bg, :nsz], cur.t[:, k, jj * 128:(jj + 1) * 128], xT.t[:, k, n0:n0 + nsz],
                                k == 0, k == 7, [cur, xr], [(ps, bg)])
                    for k in range(8):
                        self.mm(ps.t[:, bu, :nsz], cur.t[:, k, 256 + jj * 128:256 + (jj + 1) * 128],
                                xT.t[:, k, n0:n0 + nsz], k == 0, k == 7, [cur, xr], [(ps, bu)])
                    tA = self.rot("tmpA", self.tmpA)
                    P.act(lambda e, bg=bg, nsz=nsz, tA=tA: e.activation(tA.t[:, :nsz], ps.t[:, bg, :nsz], AF.Silu),
                          [(ps, bg)], [tA])
                    P.dve(lambda e, bu=bu, nsz=nsz, n0=n0, j=j, tA=tA: e.tensor_tensor(
                        hT.t[:, j, n0:n0 + nsz], ps.t[:, bu, :nsz], tA.t[:, :nsz], ALU.mult),
                        [(ps, bu), tA], [(hT, j)])
        def loadB(o):
            s0, s1 = self.wslot(), self.wslot()
            self.P.dma("pool", s0.t[:, :, :], wd[0:1024, o * 512:(o + 1) * 512].rearrange("(k p) c -> p k c", p=128),
                       [], [s0], s0.sem())
            self.P.dma("pool", s1.t[:, :, :], wd[1024:2048, o * 512:(o + 1) * 512].rearrange("(k p) c -> p k c", p=128),
                       [], [s1], s1.sem())
            return (s0, s1)

        nxt = loadB(0)
        for o in range(2):
            cur = nxt
            if o == 0:
                nxt = loadB(1)
            for (m, rows, col0) in self.tiles():
                b = self.bank()
                for j in range(16):
                    s = cur[j // 8]
                    self.mm(ps.t[:rows, b, :], hT.t[:, j, col0:col0 + rows], s.t[:, j % 8, :], j == 0, j == 15,
                            [(hT, j), s], [(ps, b)])
                P.dve(lambda e, m=m, rows=rows, b=b, o=o: e.scalar_tensor_tensor(
                    xa.t[:rows, m, o * 512:(o + 1) * 512], ps.t[:rows, b, :], 0.5,
                    xa.t[:rows, m, o * 512:(o + 1) * 512], ALU.mult, ALU.add),
                    [(ps, b), (xa, 2 * m + o)], [(xa, 2 * m + o)])

    def pe_gate(self, l):
        self.P.tag = "pegate"
        P, I = self.P, self.i
        c = self.cfg
        ps, xT, xa = self.ps, self.xT, self.xa
        sg0, sg1, sp = self.wslot(), self.wslot(), self.wslot()
        self.wload(sg0, 0, I["pe_gate"][l][:, 0:512])
        self.wload(sg1, 0, I["pe_gate"][l][:, 512:1024])
        for o in range(2):
            P.dma("pool", sp.t[:, 2 * o:2 * o + 2, :],
                  I["pe_proj"][l][:, o * 512:(o + 1) * 512].rearrange("(k p) c -> p k c", p=128), [], [sp], sp.sem())
        sg = (sg0, sg1)

        def tile_body(m, rows, col0):
            pb = self.rot("xb", self.xb)
            if rows == 128:
                t0 = (self.half * c.NTH + m) * 128
                src = I["pp"][l, t0:t0 + 128, :]
            else:
                src = I["ps"][l, :, :]
            P.dma("pool", pb.t[:rows, 0:PLE], src, [], [pb], pb.sem())
            b = self.bank()
            pst = ps.t[:, b, :].bitcast(BF16).rearrange("p (k n) -> p k n", k=8)
            for k in range(2):
                P.pe(lambda e, k=k, rows=rows, pb=pb: e.transpose(pst[:, k, :rows], pb.t[:rows, k * 128:(k + 1) * 128],
                                                                self.identb.t[:rows, :rows]),
                     [pb, self.identb], [(ps, b)])
            pT = self.rot("pT", self.pT)
            P.dve(lambda e, rows=rows, pT=pT: e.tensor_copy(pT.t[:, :, :rows], pst[:, 0:2, :rows]), [(ps, b)], [pT])
            tA = self.rot("tmpA", self.tmpA)
            for o in range(2):
                bg, bp = self.bank(), self.bank()
                for k in range(8):
                    self.mm(ps.t[:rows, bg, :], xT.t[:, k, col0:col0 + rows], sg[o].t[:, k, :], k == 0, k == 7,
                            [(xT, m), sg[o]], [(ps, bg)])
                for k in range(2):
                    self.mm(ps.t[:rows, bp, :], pT.t[:, k, :rows], sp.t[:, 2 * o + k, :], k == 0, k == 1,
                            [pT, sp], [(ps, bp)])
                P.act(lambda e, rows=rows, bg=bg, o=o, tA=tA: e.activation(
                    tA.t[:rows, o * 512:(o + 1) * 512], ps.t[:rows, bg, :], AF.Sigmoid), [(ps, bg)], [tA])
                P.dve(lambda e, rows=rows, bp=bp, o=o, tA=tA: e.tensor_tensor(
                    tA.t[:rows, o * 512:(o + 1) * 512], ps.t[:rows, bp, :], tA.t[:rows, o * 512:(o + 1) * 512], ALU.mult),
                    [(ps, bp), tA], [tA])
                P.dve(lambda e, m=m, rows=rows, o=o, tA=tA: e.tensor_tensor(
                    xa.t[:rows, m, o * 512:(o + 1) * 512], xa.t[:rows, m, o * 512:(o + 1) * 512],
                    tA.t[:rows, o * 512:(o + 1) * 512], ALU.add),
                    [tA, (xa, 2 * m + o)], [(xa, 2 * m + o)])

        for (m, rows, col0) in self.tiles():
            tile_body(m, rows, col0)


    def alloc_mix(self):
        P, I = self.P, self.i
        c = self.cfg
        self.S = P.sbuf("S", [128, 1024], F32, nreg=16)
        self.Sb = P.sbuf("Sb", [128, 1024], BF16, nreg=16)
        self.cst = P.sbuf("cst", [128, 8], F32)
        P.dve(lambda e: e.memset(self.cst.t[:, 0:1], float(LN_EPS)), [], [self.cst])
        P.dve(lambda e: e.memset(self.cst.t[:, 1:2], float(NORM_EPS)), [], [self.cst])
        P.dve(lambda e: e.memset(self.cst.t[:, 2:3], -0.5), [], [self.cst])
        P.dve(lambda e: e.memset(self.cst.t[:, 3:4], 1.0), [], [self.cst])
        P.dve(lambda e: e.memset(self.cst.t[:, 4:5], 1.0 / 512.0), [], [self.cst])
        P.dve(lambda e: e.memset(self.cst.t[:, 5:6], 1.0 / 128.0), [], [self.cst])
        self.eyeb = P.sbuf("eyeb", [128, 16, 16], BF16)
        self.eyep = P.sbuf("eyep", [16, 16], F32)
        self.eyef = P.sbuf("eyef", [128, 16, 16], F32)
        P.dma("sp", self.eyef.t[:], I["c_ident"][0:16, 0:16].unsqueeze(0).broadcast_to([128, 16, 16]), [], [self.eyef], self.eyef.sem())
        P.dma("pool", self.eyeb.t[:], I["c_ident"][0:16, 0:16].unsqueeze(0).broadcast_to([128, 16, 16]),
              [], [self.eyeb], self.eyeb.sem())
        P.dma("sp", self.eyep.t[:], I["c_ident"][0:16, 0:16], [], [self.eyep], self.eyep.sem())
        self.masks = P.sbuf("masks", [128, 8, 128], F32)
        P.dma("sp", self.masks.t[:], I["c_masks"].rearrange("m j i -> j m i"), [], [self.masks], self.masks.sem())
        self.A = Arena(P, "arena", 74 * 1024)
        self.tail_ssm = [P.sbuf(f"tail_ssm{l}", [128, 12, 3], F32, nreg=12) for l in range(DEPTH)]
        self.tail_gdn = [P.sbuf(f"tail_gdn{l}", [128, 24, 3], F32, nreg=24) for l in range(DEPTH)]
        L = DEPTH
        self.d_state = {}
        for nm in ("ret_p", "ssm_p", "gdn_p"):
            self.d_state[nm] = Buf(P, "dst_" + nm, self.o[nm], nreg=L)
        self.state_view = {
            "ret_p": lambda ap: ap.rearrange("h k v -> k h v"),
            "ssm_p": lambda ap: ap.rearrange("h n d -> n h d"),
            "gdn_p": lambda ap: ap.rearrange("h k v -> k h v"),
        }

    def state_load(self, nm, l):
        P, S, Sb = self.P, self.S, self.Sb
        db = self.d_state[nm]
        hd = {"ret_p": 4, "ssm_p": 16, "gdn_p": 8}[nm]
        if self.half == 0:
            P.dve(lambda e: e.memset(S.t[:], 0.0), [], [S])
        else:
            P.dma("sp", S.t[:].rearrange("p (h v) -> p h v", h=hd), self.state_view[nm](db.t[l]), [(db, l)], [S], S.sem())
        P.act(lambda e: e.activation(Sb.t[:], S.t[:], AF.Copy), [S], [Sb])

    def state_store(self, nm, l):
        P, S = self.P, self.S
        db = self.d_state[nm]
        hd = {"ret_p": 4, "ssm_p": 16, "gdn_p": 8}[nm]
        P.dma("sp", self.state_view[nm](db.t[l]), S.t[:].rearrange("p (h v) -> p h v", h=hd), [S], [(db, l)], S.sem(),
              is_output=True)

    def sigmoid_chain(self, tmp_ap, src_ap, reads, tmp_buf, scale_in=-1.0, bias_in=0.0):
        P = self.P
        if isinstance(bias_in, float) and bias_in == 0.0:
            P.act(lambda e: e.activation(tmp_ap, src_ap, AF.Exp, scale=scale_in), reads, [tmp_buf])
        else:
            P.act(lambda e: e.activation(tmp_ap, src_ap, AF.Exp, scale=scale_in, bias=bias_in), reads, [tmp_buf])
        P.act(lambda e: e.activation(tmp_ap, tmp_ap, AF.Ln, bias=1.0), [tmp_buf], [tmp_buf])
        P.act(lambda e: e.activation(tmp_ap, tmp_ap, AF.Exp, scale=-1.0), [tmp_buf], [tmp_buf])

    def rstd_pool(self, nst, rows, eps_col, scale_col=None):
        P, cst = self.P, self.cst
        src = nst.t[:rows, 1:2]
        if scale_col is not None:
            P.pool(lambda e: e.tensor_tensor(nst.t[:rows, 2:3], src, cst.t[:rows, scale_col:scale_col + 1], ALU.mult),
                   [nst, cst], [nst])
            src = nst.t[:rows, 2:3]
        P.pool(lambda e: e.tensor_tensor(nst.t[:rows, 2:3], src, cst.t[:rows, eps_col:eps_col + 1], ALU.add),
               [nst, cst], [nst])
        P.pool(lambda e: e.tensor_tensor(nst.t[:rows, 3:4], nst.t[:rows, 2:3], cst.t[:rows, 2:3], ALU.pow),
               [nst, cst], [nst])

    def transposes_to(self, dst_ap_fn, src_fn, n, rows, src_buf, dst_writes, bt=None):
        P, ps = self.P, self.ps
        bt = self.bank() if bt is None else bt
        pst = ps.t[:, bt, :].bitcast(BF16).rearrange("p (k n) -> p k n", k=8)
        for k in range(n):
            P.pe(lambda e, k=k: e.transpose(pst[:, k, :rows], src_fn(k), self.identb.t[:rows, :rows]),
                 [src_buf, self.identb], [(ps, bt)])
        dst_ap_fn(pst, bt)

    def retention(self, l):
        self.P.tag = "ret"
        P, I, c = self.P, self.i, self.cfg
        ps, xT, hT = self.ps, self.xT, self.hT
        w_in = I["w_in"][l]
        S, Sb, A = self.S, self.Sb, self.A
        A.reset()
        retmask = A.alloc("retmask", [4, 128], F32)
        retrow = A.alloc("retrow", [4, 128], F32)
        retcol = A.alloc("retcol", [4], F32)
        P.dma("sp", retmask.t[:], I["c_retmask"].rearrange("h j i -> j h i"), [], [retmask], retmask.sem())
        for h in range(4):
            P.dma("sp", retrow.t[:, h, :], I["c_retrow"][h, 0, :].partition_broadcast(128), [], [retrow], retrow.sem())
        P.dma("sp", retcol.t[:], I["c_retrow"][:, 1, :].rearrange("h j -> j h"), [], [retcol], retcol.sem(),
              allow_slow_non_contiguous=True)
        rope_b = [A.alloc("rope", [4, 64], F32) for _ in range(2)]
        qkr_b = [A.alloc("qkr", [2, 2, 128], BF16) for _ in range(2)]
        rt_b = [A.alloc("rt", [4, 256], F32, nreg=4) for _ in range(2)]
        qkT_b = [A.alloc("qkT", [4, 128], BF16) for _ in range(2)]
        qdT_b = [A.alloc("qdT", [128], BF16) for _ in range(4)]
        kdec_b = [A.alloc("kdec", [128], BF16) for _ in range(4)]
        vbf_b = [A.alloc("vbf", [256], BF16) for _ in range(4)]
        sgt_b = [A.alloc("sgt", [256], F32) for _ in range(4)]
        scm_b = [A.alloc("scm", [128], BF16) for _ in range(4)]
        ogt_b = [A.alloc("ogt", [256], F32) for _ in range(4)]
        ogb_b = [A.alloc("ogb", [256], BF16) for _ in range(2)]
        nst_b = [A.alloc("nst", [8], F32) for _ in range(4)]
        st6_b = [A.alloc("st6", [6], F32) for _ in range(4)]
        if self.has_sample:
            qTm = A.alloc("qTm", [16, 16], BF16)
            ktm = A.alloc("ktm", [16, 128], BF16)
            Ss_b = [A.alloc("Ss", [2, 256], F32) for _ in range(2)]
            Ssb_b = [A.alloc("Ssb", [256], BF16) for _ in range(2)]
        self.state_load("ret_p", l)
        lg = [float(np.float64(np.log1p(-np.float32(2.0) ** np.float32(-5.0 - h)).astype(np.float32))) for h in range(4)]

        def load_pair(pr):
            sA, sB, sC = self.wslot(), self.wslot(), self.wslot()
            h0, h1 = 2 * pr, 2 * pr + 1
            for i, h in enumerate((h0, h1)):
                self.wload(sA, i * 256, w_in[:, C_RQ + h * 128:C_RQ + (h + 1) * 128])
                self.wload(sA, i * 256 + 128, w_in[:, C_RK + h * 128:C_RK + (h + 1) * 128])
            for s_, h in ((sB, h0), (sC, h1)):
                self.wload(s_, 0, w_in[:, C_RV + h * 256:C_RV + (h + 1) * 256])
                self.wload(s_, 256, w_in[:, C_RG + h * 256:C_RG + (h + 1) * 256])
            return (sA, sB, sC)

        def tile_gen(i, pr, W, m, rows, col0):
            par = i % 2
            X, Y, Z = 2 + 3 * par, 3 + 3 * par, 4 + 3 * par
            sA = W[0]
            is_s = rows != 128
            t0 = (self.half * c.NTH * 128 + col0) if not is_s else c.T
            rope, rt, qkr, qkT = rope_b[par], rt_b[par], qkr_b[par], qkT_b[par]
            P.dma("sp", rope.t[:rows], I["c_rope"][t0:t0 + rows], [], [rope], rope.sem())
            for k in range(8):
                self.mm(ps.t[:rows, X, :], xT.t[:, k, col0:col0 + rows], sA.t[:, k, :], k == 0, k == 7,
                        [(xT, m), sA], [(ps, X)])
            qk5 = ps.t[:rows, X, :].rearrange("p (h a b f) -> p h a b f", h=2, a=2, b=2)
            x1, x2 = qk5[:, :, :, 0, :], qk5[:, :, :, 1, :]
            rp = rope.t[:rows].rearrange("p (a b) f -> p a b f", a=2)
            cos = rp[:, :, 0, :].unsqueeze(1).broadcast_to([rows, 2, 2, 64])
            sin = rp[:, :, 1, :].unsqueeze(1).broadcast_to([rows, 2, 2, 64])
            tv = [rt.t[:rows, j, :].rearrange("p (h a f) -> p h a f", h=2, a=2) for j in range(4)]
            P.dve(lambda e: e.tensor_tensor(tv[0], x1, cos, ALU.mult), [(ps, X), rope], [(rt, 0)])
            P.dve(lambda e: e.tensor_tensor(tv[1], x2, sin, ALU.mult), [(ps, X), rope], [(rt, 1)])
            P.dve(lambda e: e.tensor_tensor(tv[2], x1, sin, ALU.mult), [(ps, X), rope], [(rt, 2)])
            P.dve(lambda e: e.tensor_tensor(tv[3], x2, cos, ALU.mult), [(ps, X), rope], [(rt, 3)])
            P.dve(lambda e: e.tensor_tensor(qkr.t[:rows, :, :, 0:64], tv[0], tv[1], ALU.subtract), [(rt, 0), (rt, 1)], [qkr])
            P.dve(lambda e: e.tensor_tensor(qkr.t[:rows, :, :, 64:128], tv[2], tv[3], ALU.add), [(rt, 2), (rt, 3)], [qkr])
            yield

            def evac(pst, bt):
                P.act(lambda e: e.activation(qkT.t[:, :, :rows], pst[:, 0:4, :rows], AF.Copy), [(ps, bt)], [qkT])

            self.transposes_to(evac, lambda k: qkr.t[:rows, k // 2, k % 2, :], 4, rows, qkr, None, bt=Y)
            yield
            for hh in range(2):
                yield from head_gen(par, (X, Y, Z), pr, W, m, rows, col0, hh, qkr, qkT)

        def head_gen(par, banks, pr, W, m, rows, col0, hh, qkr, qkT):
            X, Y, Z = banks
            h = 2 * pr + hh
            sV = W[1 + hh]
            is_s = rows != 128
            gam = float(np.exp(lg[h]))
            bi = 2 * par + hh
            vbf, sgt, qdT, kdec, scm, ogt, nst, st6 = (vbf_b[bi], sgt_b[bi], qdT_b[bi], kdec_b[bi], scm_b[bi], ogt_b[bi],
                                                      nst_b[bi], st6_b[bi])
            ogb = ogb_b[par]
            for k in range(8):
                self.mm(ps.t[:rows, X, :], xT.t[:, k, col0:col0 + rows], sV.t[:, k, :], k == 0, k == 7, [(xT, m), sV], [(ps, X)])
            P.act(lambda e: e.activation(vbf.t[:rows, :], ps.t[:rows, X, 0:256], AF.Copy), [(ps, X)], [vbf])
            self.sigmoid_chain(sgt.t[:rows, :], ps.t[:rows, X, 256:512], [(ps, X)], sgt)
            yield
            P.dve(lambda e: e.tensor_tensor(sgt.t[:rows, :], ps.t[:rows, X, 256:512], sgt.t[:rows, :], ALU.mult), [(ps, X), sgt], [sgt])
            bo = Z if not is_s else 0
            Sr = (S, list(range(4 * h, 4 * h + 4)))
            Sbr = (Sb, list(range(4 * h, 4 * h + 4)))
            if not is_s:
                P.dve(lambda e: e.tensor_tensor(qdT.t[:, :], qkT.t[:, 2 * hh, :], retrow.t[:, h, :], ALU.mult), [qkT, retrow], [qdT])
                P.dve(lambda e: e.tensor_scalar(kdec.t[:, :], qkr.t[:, hh, 1, :], retcol.t[:, h:h + 1], None, ALU.mult), [qkr, retcol], [kdec])
                self.mm(ps.t[:, Y, 0:128], qkT.t[:, 2 * hh + 1, :], qkT.t[:, 2 * hh, :], True, True, [qkT], [(ps, Y)])
                yield
                P.dve(lambda e: e.tensor_tensor(scm.t[:, :], ps.t[:, Y, 0:128], retmask.t[:, h, :], ALU.mult), [(ps, Y), retmask], [scm])
                yield
                self.mm(ps.t[:, bo, 0:256], scm.t[:, :], vbf.t[:, :], True, False, [scm, vbf], [(ps, bo)])
                self.mm(ps.t[:, bo, 0:256], qdT.t[:, :], Sb.t[:, h * 256:(h + 1) * 256], False, True, [qdT, Sbr], [(ps, bo)])
                self.mm(ps.t[:, Y, 0:256], kdec.t[:, :], vbf.t[:, :], True, True, [kdec, vbf], [(ps, Y)])
                yield
                P.dve(lambda e: e.scalar_tensor_tensor(S.t[:, h * 256:(h + 1) * 256], S.t[:, h * 256:(h + 1) * 256],
                                                       float(np.exp(lg[h] * 128)), ps.t[:, Y, 0:256], ALU.mult, ALU.add), [Sr, (ps, Y)], [Sr])
                P.act(lambda e: e.activation(Sb.t[:, h * 256:(h + 1) * 256], S.t[:, h * 256:(h + 1) * 256], AF.Copy), [Sr], [Sbr])
            else:
                P.dve(lambda e: e.tensor_tensor(qTm.t[:], qkT.t[:, 2 * hh, 0:16].unsqueeze(1).broadcast_to([128, 16, 16]),
                                                self.eyeb.t[:], ALU.mult), [qkT, self.eyeb], [qTm])
                P.dve(lambda e: e.tensor_tensor(ktm.t[0:16], qkr.t[0:16, hh, 1, :].unsqueeze(1).broadcast_to([16, 16, 128]),
                                                self.eyep.t[:, :].unsqueeze(2).broadcast_to([16, 16, 128]), ALU.mult), [qkr, self.eyep], [ktm])
                for s_ in range(NS):
                    self.ret_sample(l, h, s_, gam, vbf, bo, Ss_b[s_ % 2], Ssb_b[s_ % 2], qTm, ktm, Y)
            P.dve(lambda e: e.bn_stats(st6.t[:rows, :], ps.t[:rows, bo, 0:256]), [(ps, bo)], [st6])
            P.dve(lambda e: e.bn_aggr(nst.t[:rows, 0:2], st6.t[:rows, :]), [st6], [nst])
            yield
            self.rstd_pool(nst, rows, 0)
            yield
            P.dve(lambda e: e.tensor_scalar(ogt.t[:rows, :], ps.t[:rows, bo, 0:256], nst.t[:rows, 0:1], nst.t[:rows, 3:4],
                                            ALU.subtract, ALU.mult), [(ps, bo), nst], [ogt])
            P.dve(lambda e: e.tensor_tensor(ogb.t[:rows, :], ogt.t[:rows, :], sgt.t[:rows, :], ALU.mult), [ogt, sgt], [ogb])
            yield

            def evac(pst, bt):
                P.act(lambda e: e.activation(hT.t[:, 2 * h:2 * h + 2, col0:col0 + rows], pst[:, 0:2, :rows], AF.Copy),
                      [(ps, bt)], [(hT, [2 * h, 2 * h + 1])])

            self.transposes_to(evac, lambda k: ogb.t[:rows, k * 128:(k + 1) * 128], 2, rows, ogb, None, bt=X)
            yield

        nxt = load_pair(0)
        for pr in range(2):
            W = nxt
            if pr == 0:
                nxt = load_pair(1)
            self.run_pipelined((tile_gen(i, pr, W, m, rows, col0) for i, (m, rows, col0) in enumerate(self.tiles())), 2)
        self.state_store("ret_p", l)

    def ret_sample(self, l, h, s_, gam, vbf, bo, Ss, Ssb, qTm, ktm, bd):
        P, I, ps = self.P, self.i, self.ps
        P.dma("sp", Ss.t[:, 0, :], I["st_ret"][l, s_, h], [], [Ss], Ss.sem())
        self.mm(ps.t[:, bd, 0:256], ktm.t[0:16, s_, :], vbf.t[0:16, :], True, True, [ktm, vbf], [(ps, bd)])
        P.dve(lambda e: e.scalar_tensor_tensor(Ss.t[:, 1, :], Ss.t[:, 0, :], gam, ps.t[:, bd, 0:256], ALU.mult, ALU.add),
              [Ss, (ps, bd)], [Ss])
        P.dma("sp", self.o["ret_s"][l, s_, h], Ss.t[:, 1, :], [Ss], [], Ss.sem(), is_output=True)
        P.act(lambda e: e.activation(Ssb.t[:, :], Ss.t[:, 1, :], AF.Copy), [Ss], [Ssb])
        self.mm(ps.t[0:16, bo, 0:256], qTm.t[:, s_, :], Ssb.t[:, :], s_ == 0, s_ == NS - 1, [qTm, Ssb], [(ps, bo)])

    def finale(self, l, w_out, mcol):
        self.P.tag = "finale"
        P, I = self.P, self.i
        ps, xT, hT, xa = self.ps, self.xT, self.hT, self.xa
        w_in, w_o = I["w_in"][l], I["w_o"][l]
        A = self.A
        A.reset()
        gT_b = [A.alloc("gT", [8, 128], BF16) for _ in range(2)]
        gtok_b = [A.alloc("gtok", [D], BF16) for _ in range(2)]
        Wout = (self.wslot(), self.wslot())
        Wm = (self.wslot(), self.wslot())
        Wo = (self.wslot(), self.wslot())
        for o in range(2):
            self.wload(Wout[o], 0, w_out[:, o * 512:(o + 1) * 512])
            self.wload(Wm[o], 0, w_in[:, mcol + o * 512:mcol + (o + 1) * 512])
            self.wload(Wo[o], 0, w_o[:, o * 512:(o + 1) * 512])

        def tile_gen(i, m, rows, col0):
            par = i % 2
            by, bm = 4 * par, 4 * par + 2
            tA, gtok, gT = self.tmpA[par], gtok_b[par], gT_b[par]
            for o in range(2):
                for k in range(8):
                    self.mm(ps.t[:rows, bm + o, :], xT.t[:, k, col0:col0 + rows], Wm[o].t[:, k, :], k == 0, k == 7,
                            [(xT, m), Wm[o]], [(ps, bm + o)])
            for o in range(2):
                for k in range(8):
                    self.mm(ps.t[:rows, by + o, :], hT.t[:, k, col0:col0 + rows], Wout[o].t[:, k, :], k == 0, k == 7,
                            [(hT, k), Wout[o]], [(ps, by + o)])
            pm = ps.t[:rows, bm:bm + 2, :].rearrange("p b n -> p (b n)")
            py = ps.t[:rows, by:by + 2, :].rearrange("p b n -> p (b n)")
            self.sigmoid_chain(tA.t[:rows, :], pm, [(ps, bm), (ps, bm + 1)], tA)
            yield
            P.dve(lambda e: e.tensor_tensor(gtok.t[:rows, :], py, tA.t[:rows, :], ALU.mult), [(ps, by), (ps, by + 1), tA], [gtok])
            yield

            def evac(pst, bt):
                P.act(lambda e: e.activation(gT.t[:, :, :rows], pst[:, :, :rows], AF.Copy), [(ps, bt)], [gT])

            self.transposes_to(evac, lambda k: gtok.t[:rows, k * 128:(k + 1) * 128], 8, rows, gtok, None, bt=bm)
            yield
            for o in range(2):
                bo = by + o
                for k in range(8):
                    self.mm(ps.t[:rows, bo, :], gT.t[:, k, :rows], Wo[o].t[:, k, :], k == 0, k == 7, [gT, Wo[o]], [(ps, bo)])
            yield
            for o in range(2):
                bo = by + o
                P.dve(lambda e, o=o, bo=bo: e.tensor_tensor(xa.t[:rows, m, o * 512:(o + 1) * 512],
                                                            xa.t[:rows, m, o * 512:(o + 1) * 512], ps.t[:rows, bo, :], ALU.add),
                      [(ps, bo), (xa, 2 * m + o)], [(xa, 2 * m + o)])

        self.run_pipelined((tile_gen(i, m, rows, col0) for i, (m, rows, col0) in enumerate(self.tiles())), 2)

    def colvecs(self, dst_ap, tk, r, c, dst_buf):
        P, ps = self.P, self.ps
        b = self.bank()
        for cc in range(c):
            P.pe(lambda e, cc=cc: e.transpose(ps.t[:, b, cc * r:(cc + 1) * r], tk.t[:r, cc * 128:(cc + 1) * 128],
                                              self.identf.t[:r, :r]), [tk, self.identf], [(ps, b)])
        P.act(lambda e: e.activation(dst_ap, ps.t[:, b, 0:c * r], AF.Copy), [(ps, b)], [dst_buf])

    def ssd(self, l):
        self.P.tag = "ssd"
        P, I, c = self.P, self.i, self.cfg
        ps, xT, hT, S, Sb, A = self.ps, self.xT, self.hT, self.S, self.Sb, self.A
        w_in = I["w_in"][l]
        TP = c.NTH * 128
        assert TP <= 512
        hs = self.has_sample
        masks = self.masks
        U, ONES, MB = masks.t[:, 0, :], masks.t[:, 1, :], masks.t[:, 2, :]
        A.reset()
        cwb = A.alloc("cwb", [12, 5], F32)
        negb = A.alloc("negb", [12], F32)
        nwT = A.alloc("nwT", [8], F32)
        dtb = A.alloc("dtb", [16], F32)
        Ab = A.alloc("Ab", [16], F32)
        Db = A.alloc("Db", [16], F32)
        wdt = A.alloc("wdt", [8, 16], BF16)
        off0 = A.off
        tk = A.alloc("tk", [1536], F32)
        tk2 = A.alloc("tk2", [1024], F32)
        P.dma("sp", tk.t[0:4, :], I["ssm_conv_w"][l], [], [tk], tk.sem())
        P.dma("sp", tk.t[4:5, :], I["ssm_conv_b"][l].unsqueeze(0), [], [tk], tk.sem())
        self.colvecs(cwb.t[:].rearrange("p c j -> p (c j)"), tk, 5, 12, cwb)
        P.act(lambda e: e.activation(negb.t[:, :], cwb.t[:, :, 4], AF.Copy, scale=-1.0), [cwb], [negb])
        P.dma("sp", tk2.t[0:1, :], I["ssm_norm_w"][l].unsqueeze(0), [], [tk2], tk2.sem())
        self.colvecs(nwT.t[:, :], tk2, 1, 8, nwT)
        P.dma("sp", dtb.t[:], I["ssm_dt_bias"][l].partition_broadcast(128), [], [dtb], dtb.sem())
        P.dma("sp", Ab.t[:], I["ssm_a_log"][l].partition_broadcast(128), [], [Ab], Ab.sem())
        P.dma("sp", Db.t[:], I["ssm_d"][l].partition_broadcast(128), [], [Db], Db.sem())
        P.act(lambda e: e.activation(Ab.t[:], Ab.t[:], AF.Exp), [Ab], [Ab])
        P.act(lambda e: e.activation(Ab.t[:], Ab.t[:], AF.Copy, scale=-1.0), [Ab], [Ab])
        P.dma("pool", wdt.t[:], w_in[:, C_MDT:C_MDT + 16].rearrange("(k p) c -> p k c", p=128), [], [wdt], wdt.sem())
        A.reset(off0)
        xbc = A.alloc("xbc", [12, TP], BF16, nreg=12)
        xraw_b = [A.alloc("xraw", [3 + TP], F32) for _ in range(2)]
        acc_b = [A.alloc("acc", [TP], F32) for _ in range(2)]
        sgm_b = [A.alloc("sgm", [TP], F32) for _ in range(2)]
        f1 = A.alloc("f1", [1024], F32)
        f2 = A.alloc("f2", [1024], F32)
        f3 = A.alloc("f3", [1024], F32)
        xs_sb = A.alloc("xs_sb", [1024], BF16)
        v = A.alloc("v", [1024], BF16)
        vdec = A.alloc("vdec", [1024], BF16)
        Mh_b = [A.alloc("Mh", [8, 128], BF16) for _ in range(2)]
        Bd = A.alloc("Bd", [8, 128], F32)
        Btok = A.alloc("Btok", [2, 128], BF16)
        ogb = A.alloc("ogb", [1024], BF16)
        sm = A.alloc("sm", [8, 16], F32)
        cumT = A.alloc("cumT", [128], F32)
        nst = A.alloc("nst", [8], F32)
        stg = A.alloc("stg", [512], F32)
        if hs:
            xbcS = A.alloc("xbcS", [12, 16], BF16)
            stc_b = [A.alloc("stc", [3, 128], F32) for _ in range(2)]
            xrs_b = [A.alloc("xrs", [4, 16], F32) for _ in range(2)]
            accS_b = [A.alloc("accS", [16], F32) for _ in range(2)]
            sgS_b = [A.alloc("sgS", [16], F32) for _ in range(2)]
            Eall = A.alloc("Eall", [16, 16], F32)
            Bde = A.alloc("Bde", [16, 16], F32)
            Bm = A.alloc("Bm", [256], BF16)
            Cm = A.alloc("Cm", [2, 16], BF16)
        tail = self.tail_ssm[l]
        if self.half == 0:
            P.dve(lambda e: e.memset(tail.t[:], 0.0), [], [tail])
        self.state_load("ssm_p", l)
        Wx = [self.wslot() for _ in range(3)]
        for i in range(3):
            self.wload(Wx[i], 0, w_in[:, C_MXBC + i * 512:C_MXBC + (i + 1) * 512])
        Wz = (self.wslot(), self.wslot())
        for o in range(2):
            self.wload(Wz[o], 0, w_in[:, C_MZ + o * 512:C_MZ + (o + 1) * 512])

        def conv_chunk(cc):
            slot = Wx[cc // 4]
            cs = (cc % 4) * 128
            par = cc % 2
            acc, sgm = acc_b[par], sgm_b[par]
            b = 2 + par
            for k in range(8):
                self.mm(ps.t[:, b, :TP], slot.t[:, k, cs:cs + 128], xT.t[:, k, 0:TP], k == 0, k == 7,
                        [slot, (xT, list(range(c.NTH)))], [(ps, b)])
            xraw = xraw_b[par]
            P.act(lambda e: e.activation(xraw.t[:, 0:3], tail.t[:, cc, :], AF.Copy), [(tail, cc)], [xraw])
            P.act(lambda e: e.activation(xraw.t[:, 3:3 + TP], ps.t[:, b, :TP], AF.Copy), [(ps, b)], [xraw])
            P.act(lambda e: e.activation(tail.t[:, cc, :], xraw.t[:, TP:TP + 3], AF.Copy), [xraw], [(tail, cc)])
            yield
            P.dve(lambda e: e.tensor_scalar(acc.t[:, :], xraw.t[:, 0:TP], cwb.t[:, cc, 0:1], None, ALU.mult), [xraw, cwb], [acc])
            for j in range(1, 4):
                P.dve(lambda e, j=j: e.scalar_tensor_tensor(acc.t[:, :], xraw.t[:, j:j + TP], cwb.t[:, cc, j:j + 1], acc.t[:, :],
                                                            ALU.mult, ALU.add), [xraw, cwb, acc], [acc])
            yield
            self.sigmoid_chain(sgm.t[:, :], acc.t[:, :], [acc, negb], sgm, scale_in=-1.0, bias_in=negb.t[:, cc:cc + 1])
            yield
            P.dve(lambda e: e.scalar_tensor_tensor(xbc.t[:, cc, :], acc.t[:, :], cwb.t[:, cc, 4:5], sgm.t[:, :], ALU.add, ALU.mult),
                  [acc, cwb, sgm], [(xbc, cc)])
            if hs:
                yield
                b2 = 4 + par
                for k in range(8):
                    self.mm(ps.t[:, b2, 0:16], slot.t[:, k, cs:cs + 128], xT.t[:, k, TP:TP + 16], k == 0, k == 7,
                            [slot, (xT, c.NTH)], [(ps, b2)])
                stc = stc_b[par]
                P.dma("sp", stc.t[0:16, :, :], I["st_ssm_conv"][l, :, :, cc * 128:(cc + 1) * 128], [], [stc], stc.sem())
                for j in range(3):
                    P.pe(lambda e, j=j: e.transpose(ps.t[:, b2, 16 + 16 * j:32 + 16 * j], stc.t[0:16, j, :], self.identf.t[0:16, 0:16]),
                         [stc, self.identf], [(ps, b2)])
                xrs = xrs_b[par]
                P.act(lambda e: e.activation(xrs.t[:, 0:3, :], ps.t[:, b2, 16:64].rearrange("p (j s) -> p j s", j=3), AF.Copy),
                      [(ps, b2)], [xrs])
                P.act(lambda e: e.activation(xrs.t[:, 3, :], ps.t[:, b2, 0:16], AF.Copy), [(ps, b2)], [xrs])
                accS, sgS = accS_b[par], sgS_b[par]
                P.dve(lambda e: e.tensor_scalar(accS.t[:, :], xrs.t[:, 0, :], cwb.t[:, cc, 0:1], None, ALU.mult), [xrs, cwb], [accS])
                for j in range(1, 4):
                    P.dve(lambda e, j=j: e.scalar_tensor_tensor(accS.t[:, :], xrs.t[:, j, :], cwb.t[:, cc, j:j + 1], accS.t[:, :],
                                                                ALU.mult, ALU.add), [xrs, cwb, accS], [accS])
                yield
                self.sigmoid_chain(sgS.t[:, :], accS.t[:, :], [accS, negb], sgS, scale_in=-1.0, bias_in=negb.t[:, cc:cc + 1])
                yield
                P.dve(lambda e: e.scalar_tensor_tensor(xbcS.t[:, cc, :], accS.t[:, :], cwb.t[:, cc, 4:5], sgS.t[:, :], ALU.add, ALU.mult),
                      [accS, cwb, sgS], [xbcS])

        self.run_pipelined((conv_chunk(cc) for cc in range(12)), 2)
        def conv_rows(c0, n, dst_fn):
            for i in range(3):
                b = self.bank()
                for k in range(8):
                    self.mm(ps.t[:n, b, :], xT.t[:, k, c0:c0 + n], Wx[i].t[:, k, :], k == 0, k == 7,
                            [Wx[i], (xT, list(range(self.NTT)))], [(ps, b)])
                P.act(lambda e, b=b: e.activation(stg.t[:n, :], ps.t[:n, b, :], AF.Copy), [(ps, b)], [stg])
                P.dma("sp", dst_fn(i), stg.t[:n, :], [stg], [], stg.sem(), is_output=True)

        if self.half == c.NH - 1:
            conv_rows(TP - 3, 3, lambda i: self.o["ssm_conv_p"][l, :, i * 512:(i + 1) * 512])
        if hs:
            conv_rows(TP, 16, lambda i: self.o["ssm_conv_s"][l, :, 2, i * 512:(i + 1) * 512])
            P.dma("sp", self.o["ssm_conv_s"][l, :, 0:2, :], I["st_ssm_conv"][l, :, 1:3, :], [], [], stg.sem(), is_output=True)

        def tile_body(m, rows, col0):
            is_s = rows != 128
            src = xbcS if is_s else xbc
            sc0 = 0 if is_s else col0
            DT, LA, CUM, ETOK, ELAST, DECL, T16 = [sm.t[:rows, i, :] for i in range(7)]
            bdt = 6
            for k in range(8):
                self.mm(ps.t[:rows, bdt, 0:16], xT.t[:, k, col0:col0 + rows], wdt.t[:, k, :], k == 0, k == 7, [(xT, m), wdt], [(ps, bdt)])
            P.dve(lambda e: e.tensor_tensor(T16, ps.t[:rows, bdt, 0:16], dtb.t[:rows, :], ALU.add), [(ps, bdt), dtb], [sm])
            P.act(lambda e: e.activation(T16, T16, AF.Exp), [sm], [sm])
            P.act(lambda e: e.activation(DT, T16, AF.Ln, bias=1.0), [sm], [sm])
            P.dve(lambda e: e.tensor_tensor(LA, DT, Ab.t[:rows, :], ALU.mult), [sm, Ab], [sm])
            def evac_xs(pst, bt):
                P.act(lambda e: e.activation(xs_sb.t[:rows, :], pst[:rows, :, :].rearrange("p k n -> p (k n)"), AF.Copy), [(ps, bt)], [xs_sb])
            self.transposes_T(evac_xs, lambda k: src.t[:, k, sc0:sc0 + rows], 8, rows, src, bt=7)
            if is_s and l == 0 and self.cfg.dbg.get("ssd_dump"):
                dx = self.nc.dram_tensor("dbg_xs", [16, 1024], F32, kind="ExternalOutput").ap()
                dd = self.nc.dram_tensor("dbg_dt", [16, 16], F32, kind="ExternalOutput").ap()
                P.dma("pool", dx, xs_sb.t[0:16, :], [xs_sb], [], xs_sb.sem(), is_output=True)
                P.dma("sp", dd, sm.t[0:16, 0, :], [sm], [], sm.sem(), is_output=True)
            dt_b = DT.unsqueeze(2).broadcast_to([rows, 16, 64])
            P.dve(lambda e: e.tensor_tensor(v.t[:rows, :].rearrange("p (h d) -> p h d", h=16),
                                            xs_sb.t[:rows, :].rearrange("p (h d) -> p h d", h=16), dt_b, ALU.mult), [xs_sb, sm], [v])
            def evac_b(pst, bt):
                P.act(lambda e: e.activation(Btok.t[:rows, :, :], pst[:rows, 0:2, :], AF.Copy), [(ps, bt)], [Btok])
            self.transposes_T(evac_b, lambda k: src.t[:, 8 + k, sc0:sc0 + rows], 2, rows, src, bt=7)
            if not is_s:
                bc = 6
                self.mm(ps.t[:, bc, 32:48], U, LA, True, True, [masks, sm], [(ps, bc)])
                self.mm(ps.t[:, bc, 48:64], ONES, LA, True, True, [masks, sm], [(ps, bc)])
                self.mm(ps.t[0:16, bc, 64:192], LA, U, True, True, [masks, sm], [(ps, bc)])
                P.act(lambda e: e.activation(CUM, ps.t[:, bc, 32:48], AF.Copy), [(ps, bc)], [sm])
                P.act(lambda e: e.activation(ETOK, ps.t[:, bc, 32:48], AF.Exp), [(ps, bc)], [sm])
                P.act(lambda e: e.activation(ELAST, ps.t[:, bc, 48:64], AF.Exp), [(ps, bc)], [sm])
                P.dve(lambda e: e.tensor_tensor(DECL, ps.t[:, bc, 48:64], CUM, ALU.subtract), [(ps, bc), sm], [sm])
                P.act(lambda e: e.activation(DECL, DECL, AF.Exp), [sm], [sm])
                P.act(lambda e: e.activation(cumT.t[0:16, :], ps.t[0:16, bc, 64:192], AF.Copy), [(ps, bc)], [cumT])
                P.dve(lambda e: e.tensor_tensor(vdec.t[:, :].rearrange("p (h d) -> p h d", h=16),
                                                v.t[:, :].rearrange("p (h d) -> p h d", h=16),
                                                DECL.unsqueeze(2).broadcast_to([128, 16, 64]), ALU.mult), [v, sm], [vdec])
                bsc = 6
                for g in range(2):
                    self.mm(ps.t[:, bsc, 256 + g * 128:256 + (g + 1) * 128], xbc.t[:, 8 + g, col0:col0 + 128], xbc.t[:, 10 + g, col0:col0 + 128],
                            True, True, [xbc], [(ps, bsc)])
                po, pcs, pds = 0, 4, 2
                for g in range(2):
                    self.mm(ps.t[:, pcs + g, :], xbc.t[:, 10 + g, col0:col0 + 128], Sb.t[:, g * 512:(g + 1) * 512], True, True,
                            [xbc, (Sb, list(range(8 * g, 8 * g + 8)))], [(ps, pcs + g)])
                P.dve(lambda e: e.tensor_tensor(f2.t[:, :].rearrange("p (h d) -> p h d", h=16),
                                                ps.t[:, pcs:pcs + 2, :].rearrange("p b (h d) -> p (b h) d", h=8),
                                                ETOK.unsqueeze(2).broadcast_to([128, 16, 64]), ALU.mult), [(ps, pcs), (ps, pcs + 1), sm], [f2])
                for g in range(2):
                    P.dve(lambda e, g=g: e.tensor_tensor(Bd.t[0:16, :, :], cumT.t[0:16, :].unsqueeze(1).broadcast_to([16, 8, 128]),
                                                         self.eyep.t[:, 8 * g:8 * g + 8].unsqueeze(2).broadcast_to([16, 8, 128]), ALU.mult),
                          [cumT, self.eyep], [Bd])
                    pa = 2
                    for hf in range(2):
                        self.mm(ps.t[:, pa + hf, :], ONES[0:16, :], Bd.t[0:16, 4 * hf:4 * hf + 4, :].rearrange("p h i -> p (h i)"),
                                True, False, [masks, Bd], [(ps, pa + hf)])
                        for hq in range(4):
                            self.mm(ps.t[:, pa + hf, hq * 128:(hq + 1) * 128], self.identf.t[:, :], MB, False, hq == 3,
                                    [masks, self.identf], [(ps, pa + hf)])
                    pav = ps.t[:, pa:pa + 2, :].rearrange("p b (h i) -> p (b h) i", h=4)
                    P.dve(lambda e, g=g, pav=pav: e.tensor_tensor(f1.t[:, :].rearrange("p (h i) -> p h i", h=8), pav,
                                                                   CUM[:, 8 * g:8 * g + 8].unsqueeze(2).broadcast_to([128, 8, 128]),
                                                                   ALU.subtract), [(ps, pa), (ps, pa + 1), sm], [f1])
                    P.act(lambda e: e.activation(f1.t[:, :], f1.t[:, :], AF.Exp), [f1], [f1])
                    Mh = self.rot("Mh", Mh_b)
                    P.dve(lambda e, g=g, Mh=Mh: e.tensor_tensor(Mh.t[:, :, :], f1.t[:, :].rearrange("p (h i) -> p h i", h=8),
                                                                ps.t[:, bsc, 256 + g * 128:256 + (g + 1) * 128].unsqueeze(1).broadcast_to([128, 8, 128]),
                                                                ALU.mult), [f1, (ps, bsc)], [Mh])
                    for h in range(8):
                        hg = 8 * g + h
                        self.mm(ps.t[:, po + g, h * 64:(h + 1) * 64], Mh.t[:, h, :], v.t[:, hg * 64:(hg + 1) * 64], True, True,
                                [Mh, v], [(ps, po + g)])
                P.dve(lambda e: e.tensor_tensor(f2.t[:, :], f2.t[:, :], ps.t[:, po:po + 2, :].rearrange("p b n -> p (b n)"), ALU.add),
                      [f2, (ps, po), (ps, po + 1)], [f2])
                for g in range(2):
                    self.mm(ps.t[:, pds + g, :], Btok.t[:, g, :], vdec.t[:, g * 512:(g + 1) * 512], True, True, [Btok, vdec], [(ps, pds + g)])
                P.dve(lambda e: e.tensor_tensor(f1.t[:, :].rearrange("p (h d) -> p h d", h=16), S.t[:, :].rearrange("p (h d) -> p h d", h=16),
                                                ELAST.unsqueeze(2).broadcast_to([128, 16, 64]), ALU.mult), [S, sm], [f1])
                P.dve(lambda e: e.tensor_tensor(S.t[:, :], f1.t[:, :], ps.t[:, pds:pds + 2, :].rearrange("p b n -> p (b n)"), ALU.add),
                      [f1, (ps, pds), (ps, pds + 1)], [S])
                P.act(lambda e: e.activation(Sb.t[:, :], S.t[:, :], AF.Copy), [S], [Sb])
            else:
                ELA = ELAST
                P.act(lambda e: e.activation(ELA, LA, AF.Exp), [sm], [sm])
                P.dve(lambda e: e.tensor_tensor(Bde.t[0:16, :, :], ELA.unsqueeze(1).broadcast_to([16, 16, 16]),
                                                self.eyep.t[:, :].unsqueeze(2).broadcast_to([16, 16, 16]), ALU.mult), [sm, self.eyep], [Bde])
                be = 6
                self.mm(ps.t[:, be, 0:256], ONES[0:16, :], Bde.t[0:16, :, :].rearrange("p s h -> p (s h)"), True, True, [masks, Bde], [(ps, be)])
                P.act(lambda e: e.activation(Eall.t[:, :, :].rearrange("p s h -> p (s h)"), ps.t[:, be, 0:256], AF.Copy), [(ps, be)], [Eall])
                Ss_b = [f2, f3]
                for s_ in range(NS):
                    Ss = Ss_b[s_ % 2]
                    P.dma("sp", Ss.t[:, :].rearrange("p (h d) -> p h d", h=16), I["st_ssm"][l, s_].rearrange("h n d -> n h d"), [], [Ss], Ss.sem())
                    P.dve(lambda e, s_=s_: e.tensor_scalar(Bm.t[0:16, :], Btok.t[0:16, :, :].rearrange("p g n -> p (g n)"),
                                                           self.eyep.t[:, s_:s_ + 1], None, ALU.mult), [Btok, self.eyep], [Bm])
                    P.dve(lambda e, s_=s_: e.tensor_tensor(Cm.t[:, :, :], xbcS.t[:, 10:12, :],
                                                           self.eyeb.t[:, s_, :].unsqueeze(1).broadcast_to([128, 2, 16]), ALU.mult),
                          [xbcS, self.eyeb], [Cm])
                    pds = 2
                    for g in range(2):
                        self.mm(ps.t[:, pds + g, :], Bm.t[0:16, g * 128:(g + 1) * 128], v.t[0:16, g * 512:(g + 1) * 512], True, True,
                                [Bm, v], [(ps, pds + g)])
                    P.dve(lambda e, s_=s_, Ss=Ss: e.tensor_tensor(f1.t[:, :].rearrange("p (h d) -> p h d", h=16),
                                                                  Ss.t[:, :].rearrange("p (h d) -> p h d", h=16),
                                                                  Eall.t[:, s_, :].unsqueeze(2).broadcast_to([128, 16, 64]), ALU.mult),
                          [Ss, Eall], [f1])
                    P.dve(lambda e, Ss=Ss, pds=pds: e.tensor_tensor(Ss.t[:, :], f1.t[:, :], ps.t[:, pds:pds + 2, :].rearrange("p b n -> p (b n)"),
                                                                    ALU.add), [f1, (ps, pds), (ps, pds + 1)], [Ss])
                    P.dma("sp", self.o["ssm_s"][l, s_].rearrange("h n d -> n h d"), Ss.t[:, :].rearrange("p (h d) -> p h d", h=16),
                          [Ss], [], Ss.sem(), is_output=True)
                    P.act(lambda e, Ss=Ss: e.activation(vdec.t[:, :], Ss.t[:, :], AF.Copy), [Ss], [vdec])
                    for g in range(2):
                        self.mm(ps.t[0:16, g, :], Cm.t[:, g, :], vdec.t[:, g * 512:(g + 1) * 512], s_ == 0, s_ == NS - 1,
                                [Cm, vdec], [(ps, g)])
                P.act(lambda e: e.activation(f2.t[0:16, :], ps.t[0:16, 0:2, :].rearrange("p b n -> p (b n)"), AF.Copy), [(ps, 0), (ps, 1)], [f2])
            P.dve(lambda e: e.tensor_tensor(f3.t[:rows, :].rearrange("p (h d) -> p h d", h=16),
                                            xs_sb.t[:rows, :].rearrange("p (h d) -> p h d", h=16),
                                            Db.t[:rows, :].unsqueeze(2).broadcast_to([rows, 16, 64]), ALU.mult), [xs_sb, Db], [f3])
            P.dve(lambda e: e.tensor_tensor(f2.t[:rows, :], f2.t[:rows, :], f3.t[:rows, :], ALU.add), [f2, f3], [f2])
            pz = 4
            for o in range(2):
                for k in range(8):
                    self.mm(ps.t[:rows, pz + o, :], xT.t[:, k, col0:col0 + rows], Wz[o].t[:, k, :], k == 0, k == 7, [(xT, m), Wz[o]], [(ps, pz + o)])
            pzv = ps.t[:rows, pz:pz + 2, :].rearrange("p b n -> p (b n)")
            self.sigmoid_chain(f3.t[:rows, :], pzv, [(ps, pz), (ps, pz + 1)], f3)
            P.dve(lambda e: e.tensor_tensor(f3.t[:rows, :], pzv, f3.t[:rows, :], ALU.mult), [(ps, pz), (ps, pz + 1), f3], [f3])
            P.dve(lambda e: e.tensor_tensor(f2.t[:rows, :], f2.t[:rows, :], f3.t[:rows, :], ALU.mult), [f2, f3], [f2])
            for g in range(2):
                P.act(lambda e, g=g: e.activation(f3.t[:rows, g * 512:(g + 1) * 512], f2.t[:rows, g * 512:(g + 1) * 512], AF.Square,
                                                  accum_out=nst.t[:rows, 4 + g:5 + g]), [f2], [f3, nst])
            P.pool(lambda e: e.tensor_tensor(nst.t[:rows, 0:2], nst.t[:rows, 4:6], self.cst.t[:rows, 4:5].broadcast_to([rows, 2]), ALU.mult),
                   [nst, self.cst], [nst])
            P.pool(lambda e: e.tensor_tensor(nst.t[:rows, 0:2], nst.t[:rows, 0:2], self.cst.t[:rows, 1:2].broadcast_to([rows, 2]), ALU.add),
                   [nst, self.cst], [nst])
            P.pool(lambda e: e.tensor_tensor(nst.t[:rows, 2:4], nst.t[:rows, 0:2], self.cst.t[:rows, 2:3].broadcast_to([rows, 2]), ALU.pow),
                   [nst, self.cst], [nst])
            for g in range(2):
                P.dve(lambda e, g=g: e.tensor_scalar(ogb.t[:rows, g * 512:(g + 1) * 512], f2.t[:rows, g * 512:(g + 1) * 512],
                                                     nst.t[:rows, 2 + g:3 + g], None, ALU.mult), [f2, nst], [ogb])

            def evac(pst, bt):
                P.dve(lambda e: e.tensor_tensor(hT.t[:, 0:8, col0:col0 + rows], pst[:, :, :rows],
                                                nwT.t[:, :].unsqueeze(2).broadcast_to([128, 8, rows]), ALU.mult),
                      [(ps, bt), nwT], [(hT, list(range(8)))])

            self.transposes_to(evac, lambda k: ogb.t[:rows, k * 128:(k + 1) * 128], 8, rows, ogb, None, bt=7)

        for (m, rows, col0) in self.tiles():
            tile_body(m, rows, col0)
        self.state_store("ssm_p", l)

    def gdn(self, l):
        self.P.tag = "gdn"
        P, I, c = self.P, self.i, self.cfg
        ps, xT, hT, S, Sb, A = self.ps, self.xT, self.hT, self.S, self.Sb, self.A
        w_in = I["w_in"][l]
        TP = c.NTH * 128
        hs = self.has_sample
        masks = self.masks
        ONES, MBI, MBS = masks.t[:, 1, :], masks.t[:, 4, :], masks.t[:, 5, :]
        U64 = masks.t[:, 3, :]
        A.reset()
        cw = A.alloc("cw", [24, 4], F32)
        dtb = A.alloc("dtb", [8], F32)
        Ab = A.alloc("Ab", [8], F32)
        nwb = A.alloc("nwb", [128], F32)
        wab = A.alloc("wab", [8, 16], BF16)
        off0 = A.off
        tk = A.alloc("tk", [3072], F32)
        P.dma("sp", tk.t[0:4, :], I["gdn_conv_w"][l], [], [tk], tk.sem())
        self.colvecs(cw.t[:].rearrange("p c j -> p (c j)"), tk, 4, 24, cw)
        P.dma("sp", dtb.t[:], I["gdn_dt_bias"][l].partition_broadcast(128), [], [dtb], dtb.sem())
        P.dma("sp", Ab.t[:], I["gdn_a_log"][l].partition_broadcast(128), [], [Ab], Ab.sem())
        P.dma("sp", nwb.t[:], I["gdn_norm_w"][l].partition_broadcast(128), [], [nwb], nwb.sem())
        P.act(lambda e: e.activation(Ab.t[:], Ab.t[:], AF.Exp), [Ab], [Ab])
        P.act(lambda e: e.activation(Ab.t[:], Ab.t[:], AF.Copy, scale=-1.0), [Ab], [Ab])
        P.dma("pool", wab.t[:], w_in[:, C_GA:C_GA + 16].rearrange("(k p) c -> p k c", p=128), [], [wab], wab.sem())
        A.reset(off0)
        qkv = A.alloc("qkv", [24, TP], BF16, nreg=24)
        if hs:
            qkvS = A.alloc("qkvS", [24, 16], F32)
        off1 = A.off
        xraw_b = [A.alloc("xraw", [3 + TP], F32) for _ in range(2)]
        NPAR = 2 if hs else 3
        xraw_b = xraw_b + [A.alloc("xraw", [3 + TP], F32) for _ in range(NPAR - 2)]
        acc_b = [A.alloc("acc", [TP], F32) for _ in range(NPAR)]
        sgm_b = [A.alloc("sgm", [TP], F32) for _ in range(NPAR)]
        sq_b = [A.alloc("sq", [TP], F32) for _ in range(NPAR)]
        stg = A.alloc("stg", [512], F32)
        if hs:
            stc_b = [A.alloc("stc", [3, 128], F32) for _ in range(2)]
            xrs_b = [A.alloc("xrs", [4, 16], F32) for _ in range(2)]
            accS_b = [A.alloc("accS", [16], F32) for _ in range(2)]
            sgS_b = [A.alloc("sgS", [16], F32) for _ in range(2)]
            sqS_b = [A.alloc("sqS", [16], F32) for _ in range(2)]
        tail = self.tail_gdn[l]
        if self.half == 0:
            P.dve(lambda e: e.memset(tail.t[:], 0.0), [], [tail])
        self.state_load("gdn_p", l)
        lnq = float(np.log(128.0 ** -0.5))

        def l2n_g(dst_ap, xin_ap, sq_ap, n, is_q, bufs_r, bufs_w, b):
            P.act(lambda e: e.activation(sq_ap, xin_ap, AF.Square), bufs_r, [bufs_w[0]])
            self.mm(ps.t[:, b, :n], self.masks.t[:, 1, :], sq_ap, True, True, [masks, bufs_w[0]], [(ps, b)])
            yield
            P.act(lambda e: e.activation(sq_ap, ps.t[:, b, :n], AF.Ln, bias=float(NORM_EPS)), [(ps, b)], [bufs_w[0]])
            if is_q:
                P.act(lambda e: e.activation(sq_ap, sq_ap, AF.Exp, scale=-0.5, bias=lnq), [bufs_w[0]], [bufs_w[0]])
            else:
                P.act(lambda e: e.activation(sq_ap, sq_ap, AF.Exp, scale=-0.5), [bufs_w[0]], [bufs_w[0]])
            yield
            P.dve(lambda e: e.tensor_tensor(dst_ap, xin_ap, sq_ap, ALU.mult), list(bufs_r) + [bufs_w[0]], [bufs_w[1]])

        def conv_chunk(cc, slot):
            cs = (cc % 4) * 128
            par = cc % NPAR
            acc, sgm, sq = acc_b[par], sgm_b[par], sq_b[par]
            b = 2 + 2 * par
            bl = 3 + 2 * par
            for k in range(8):
                self.mm(ps.t[:, b, :TP], slot.t[:, k, cs:cs + 128], xT.t[:, k, 0:TP], k == 0, k == 7,
                        [slot, (xT, list(range(c.NTH)))], [(ps, b)])
            xraw = xraw_b[par]
            P.act(lambda e: e.activation(xraw.t[:, 0:3], tail.t[:, cc, :], AF.Copy), [(tail, cc)], [xraw])
            P.act(lambda e: e.activation(xraw.t[:, 3:3 + TP], ps.t[:, b, :TP], AF.Copy), [(ps, b)], [xraw])
            P.act(lambda e: e.activation(tail.t[:, cc, :], xraw.t[:, TP:TP + 3], AF.Copy), [xraw], [(tail, cc)])
            yield
            P.dve(lambda e: e.tensor_scalar(acc.t[:, :], xraw.t[:, 0:TP], cw.t[:, cc, 0:1], None, ALU.mult), [xraw, cw], [acc])
            for j in range(1, 4):
                P.dve(lambda e, j=j: e.scalar_tensor_tensor(acc.t[:, :], xraw.t[:, j:j + TP], cw.t[:, cc, j:j + 1], acc.t[:, :],
                                                            ALU.mult, ALU.add), [xraw, cw, acc], [acc])
            yield
            self.sigmoid_chain(sgm.t[:, :], acc.t[:, :], [acc], sgm)
            yield
            if cc < 16:
                P.dve(lambda e: e.tensor_tensor(acc.t[:, :], acc.t[:, :], sgm.t[:, :], ALU.mult), [acc, sgm], [acc])
                yield from l2n_g(qkv.t[:, cc, :], acc.t[:, :], sq.t[:, :], TP, cc < 8, [acc], [sq, (qkv, cc)], bl)
            else:
                P.dve(lambda e: e.tensor_tensor(qkv.t[:, cc, :], acc.t[:, :], sgm.t[:, :], ALU.mult), [acc, sgm], [(qkv, cc)])
            if hs:
                yield
                b2 = 6 + par
                for k in range(8):
                    self.mm(ps.t[:, b2, 0:16], slot.t[:, k, cs:cs + 128], xT.t[:, k, TP:TP + 16], k == 0, k == 7,
                            [slot, (xT, c.NTH)], [(ps, b2)])
                stc = stc_b[par]
                P.dma("sp", stc.t[0:16, :, :], I["st_gdn_conv"][l, :, :, cc * 128:(cc + 1) * 128], [], [stc], stc.sem())
                for j in range(3):
                    P.pe(lambda e, j=j: e.transpose(ps.t[:, b2, 16 + 16 * j:32 + 16 * j], stc.t[0:16, j, :], self.identf.t[0:16, 0:16]),
                         [stc, self.identf], [(ps, b2)])
                xrs = xrs_b[par]
                P.act(lambda e: e.activation(xrs.t[:, 0:3, :], ps.t[:, b2, 16:64].rearrange("p (j s) -> p j s", j=3), AF.Copy),
                      [(ps, b2)], [xrs])
                P.act(lambda e: e.activation(xrs.t[:, 3, :], ps.t[:, b2, 0:16], AF.Copy), [(ps, b2)], [xrs])
                accS, sgS, sqS = accS_b[par], sgS_b[par], sqS_b[par]
                P.dve(lambda e: e.tensor_scalar(accS.t[:, :], xrs.t[:, 0, :], cw.t[:, cc, 0:1], None, ALU.mult), [xrs, cw], [accS])
                for j in range(1, 4):
                    P.dve(lambda e, j=j: e.scalar_tensor_tensor(accS.t[:, :], xrs.t[:, j, :], cw.t[:, cc, j:j + 1], accS.t[:, :],
                                                                ALU.mult, ALU.add), [xrs, cw, accS], [accS])
                yield
                self.sigmoid_chain(sgS.t[:, :], accS.t[:, :], [accS], sgS)
                yield
                if cc < 16:
                    P.dve(lambda e: e.tensor_tensor(accS.t[:, :], accS.t[:, :], sgS.t[:, :], ALU.mult), [accS, sgS], [accS])
                    yield from l2n_g(qkvS.t[:, cc, :], accS.t[:, :], sqS.t[:, :], 16, cc < 8, [accS], [sqS, qkvS], b2)
                else:
                    P.dve(lambda e: e.tensor_tensor(qkvS.t[:, cc, :], accS.t[:, :], sgS.t[:, :], ALU.mult), [accS, sgS], [qkvS])

        def conv_rows(slot, i, c0, n, dst):
            b = self.bank()
            for k in range(8):
                self.mm(ps.t[:n, b, :], xT.t[:, k, c0:c0 + n], slot.t[:, k, :], k == 0, k == 7,
                        [slot, (xT, list(range(self.NTT)))], [(ps, b)])
            P.act(lambda e: e.activation(stg.t[:n, :], ps.t[:n, b, :], AF.Copy), [(ps, b)], [stg])
            P.dma("sp", dst, stg.t[:n, :], [stg], [], stg.sem(), is_output=True)

        def load_slot(i):
            sl = self.wslot()
            self.wload(sl, 0, w_in[:, C_GQKV + i * 512:C_GQKV + (i + 1) * 512])
            return sl

        nxt = load_slot(0)
        for i in range(6):
            slot = nxt
            if i + 1 < 6:
                nxt = load_slot(i + 1)
            self.run_pipelined((conv_chunk(cc, slot) for cc in range(4 * i, 4 * i + 4)), NPAR)
            if self.half == c.NH - 1:
                conv_rows(slot, i, TP - 3, 3, self.o["gdn_conv_p"][l, :, i * 512:(i + 1) * 512])
            if hs:
                conv_rows(slot, i, TP, 16, self.o["gdn_conv_s"][l, :, 2, i * 512:(i + 1) * 512])
        if hs:
            P.dma("sp", self.o["gdn_conv_s"][l, :, 0:2, :], I["st_gdn_conv"][l, :, 1:3, :], [], [], stg.sem(), is_output=True)
        if self.cfg.dbg.get("gdn_stop", 9) <= 1:
            return
        Wz = (self.wslot(), self.wslot())
        for o in range(2):
            self.wload(Wz[o], 0, w_in[:, C_GZ + o * 512:C_GZ + (o + 1) * 512])

        A.reset(off1)
        sm = A.alloc("sm", [14, 8], F32)
        osb = A.alloc("osb", [1024], F32)
        f3 = A.alloc("f3", [1024], F32)
        ogb = A.alloc("ogb", [1024], BF16)
        nst = A.alloc("nst", [3, 8], F32)
        off2 = A.off
        cumT = A.alloc("cumT", [2, 128], F32)
        Bd1 = A.alloc("Bd1", [4, 128], F32)
        Bd2 = Bd1
        dtmp = A.alloc("dtmp", [4, 128], F32)
        ecb = dtmp
        f1 = A.alloc("f1", [512], F32)
        attnT = A.alloc("attnT", [4, 128], F32)
        qdT = A.alloc("qdT", [4, 128], F32)
        kTf = A.alloc("kTf", [4, 128], F32)
        Ya, Yb = A.alloc("Ya", [4, 128], F32), A.alloc("Yb", [4, 128], F32)
        YTa, YTb = A.alloc("YTa", [4, 128], F32), A.alloc("YTb", [4, 128], F32)
        PT = A.alloc("PT", [4, 128], F32)
        ktok = A.alloc("ktok", [4, 128], BF16)
        kdec = A.alloc("kdec", [2, 4, 128], F32)
        vb = A.alloc("vb", [4, 128], F32)
        rhs = A.alloc("rhs", [4, 128], F32)
        ub = A.alloc("ub", [4, 128], F32)
        if hs:
            A.reset(off2)
            Eall = A.alloc("Eall", [16, 8], F32)
            Bde = A.alloc("Bde", [16, 8], F32)
            kTm = A.alloc("kTm", [8, 16], F32)
            qTm = A.alloc("qTm", [8, 16], F32)
            ktS = A.alloc("ktS", [1024], F32)
            vbS = A.alloc("vbS", [1024], F32)
            um = A.alloc("um", [1024], F32)
            tS = A.alloc("tS", [1024], F32)
            SsB = A.alloc("SsB", [1024], F32)
            oacc = A.alloc("oacc", [1024], F32)

        def gates(m, rows, col0):
            R = lambda i: sm.t[:rows, i, :]
            bab = 6
            for k in range(8):
                self.mm(ps.t[:rows, bab, 0:16], xT.t[:, k, col0:col0 + rows], wab.t[:, k, :], k == 0, k == 7, [(xT, m), wab], [(ps, bab)])
            P.act(lambda e: e.activation(R(7), ps.t[:rows, bab, 8:16], AF.Exp, scale=-1.0), [(ps, bab)], [sm])
            P.act(lambda e: e.activation(R(6), R(7), AF.Ln, bias=1.0), [sm], [sm])
            P.act(lambda e: e.activation(R(6), R(6), AF.Copy, scale=-1.0), [sm], [sm])
            P.act(lambda e: e.activation(R(0), R(6), AF.Exp), [sm], [sm])
            P.dve(lambda e: e.tensor_tensor(R(7), ps.t[:rows, bab, 0:8], dtb.t[:rows, :], ALU.add), [(ps, bab), dtb], [sm])
            P.act(lambda e: e.activation(R(7), R(7), AF.Exp), [sm], [sm])
            P.act(lambda e: e.activation(R(7), R(7), AF.Ln, bias=1.0), [sm], [sm])
            P.dve(lambda e: e.tensor_tensor(R(1), R(7), Ab.t[:rows, :], ALU.mult), [sm, Ab], [sm])

        def tile_body(m, rows, col0):
            R = lambda i: sm.t[:, i, :]
            gates(m, 128, col0)
            bc = 6
            self.mm(ps.t[:, bc, 32:40], U64, R(1), True, True, [masks, sm], [(ps, bc)])
            self.mm(ps.t[:, bc, 40:48], masks.t[:, 6, :], R(1), True, True, [masks, sm], [(ps, bc)])
            self.mm(ps.t[:, bc, 48:56], masks.t[:, 7, :], R(1), True, True, [masks, sm], [(ps, bc)])
            self.mm(ps.t[:, bc, 56:64], ONES, R(1), True, True, [masks, sm], [(ps, bc)])
            P.act(lambda e: e.activation(R(2), ps.t[:, bc, 32:40], AF.Copy), [(ps, bc)], [sm])
            P.act(lambda e: e.activation(R(3), ps.t[:, bc, 32:40], AF.Exp), [(ps, bc)], [sm])
            P.dve(lambda e: e.tensor_tensor(R(5), ps.t[:, bc, 40:48], R(2), ALU.subtract), [(ps, bc), sm], [sm])
            P.act(lambda e: e.activation(R(5), R(5), AF.Exp), [sm], [sm])
            P.dve(lambda e: e.tensor_copy(sm.t[:, 12:14, :], sm.t[:, 5:6, :].broadcast_to([128, 2, 8])), [sm], [sm])
            P.dve(lambda e: e.memset(sm.t[64:128, 12, :], 0.0), [sm], [sm])
            P.dve(lambda e: e.memset(sm.t[0:64, 13, :], 0.0), [sm], [sm])
            P.act(lambda e: e.activation(R(8), ps.t[:, bc, 48:56], AF.Exp), [(ps, bc)], [sm])
            P.act(lambda e: e.activation(R(7), ps.t[:, bc, 48:56], AF.Copy), [(ps, bc)], [sm])
            P.dve(lambda e: e.tensor_tensor(R(9), ps.t[:, bc, 56:64], R(7), ALU.subtract), [(ps, bc), sm], [sm])
            P.act(lambda e: e.activation(R(9), R(9), AF.Exp), [sm], [sm])
            P.dve(lambda e: e.scalar_tensor_tensor(R(4), R(0), -1.0, R(3), ALU.mult, ALU.mult), [sm], [sm])
            P.dve(lambda e: e.tensor_tensor(R(10), R(2), R(6), ALU.add), [sm], [sm])
            self.mm(ps.t[0:8, bc, 64:192], R(2), self.identf.t[:, :], True, True, [sm, self.identf], [(ps, bc)])
            self.mm(ps.t[0:8, bc, 192:320], R(10), self.identf.t[:, :], True, True, [sm, self.identf], [(ps, bc)])
            P.act(lambda e: e.activation(cumT.t[0:8, :, :].rearrange("p a i -> p (a i)"), ps.t[0:8, bc, 64:320], AF.Copy), [(ps, bc)], [cumT])
            if self.cfg.dbg.get("gdn_stop", 9) <= 2:
                return
            for g in range(2):
                group_body(m, col0, g)
            if self.cfg.dbg.get("gdn_stop", 9) <= 5:
                return
            post(m, 128, col0, osb)

        def group_body(m, col0, g):
            R = lambda i: sm.t[:, i, :]
            hsl = slice(4 * g, 4 * g + 4)
            eye_g = self.eyep.t[0:8, 4 * g:4 * g + 4].unsqueeze(2).broadcast_to([8, 4, 128])
            P.dve(lambda e: e.tensor_tensor(Bd1.t[0:8, :, :], cumT.t[0:8, 0, :].unsqueeze(1).broadcast_to([8, 4, 128]), eye_g, ALU.mult),
                  [cumT, self.eyep], [Bd1])
            P.dve(lambda e: e.tensor_tensor(Bd2.t[0:8, :, :], cumT.t[0:8, 1, :].unsqueeze(1).broadcast_to([8, 4, 128]), eye_g, ALU.mult),
                  [cumT, self.eyep], [Bd2])
            bd1f = Bd1.t[0:8, :, :].rearrange("p h i -> p (h i)")
            bd2f = Bd2.t[0:8, :, :].rearrange("p h i -> p (h i)")
            cum_b = R(2)[:, hsl].unsqueeze(2).broadcast_to([128, 4, 128])
            v4 = lambda ap: ap.rearrange("p (h i) -> p h i", h=4)
            self.mm(ps.t[:, 2, :], ONES[0:8, :], bd1f, True, True, [masks, Bd1], [(ps, 2)])
            P.act(lambda e: e.activation(ecb.t[:, :, :].rearrange("p h i -> p (h i)"), ps.t[:, 2, :], AF.Exp), [(ps, 2)], [ecb])
            P.dve(lambda e: e.tensor_tensor(qdT.t[:, :, :], qkv.t[:, 4 * g:4 * g + 4, col0:col0 + 128], ecb.t[:, :, :], ALU.mult), [qkv, ecb], [qdT])
            P.act(lambda e: e.activation(kTf.t[:, :, :], qkv.t[:, 8 + 4 * g:12 + 4 * g, col0:col0 + 128], AF.Copy), [qkv], [kTf])
            self.mm(ps.t[:, 3, :], ONES[0:8, :], bd1f, True, False, [masks, Bd1], [(ps, 3)])
            for hq in range(4):
                self.mm(ps.t[:, 3, hq * 128:(hq + 1) * 128], self.identf.t[:, :], MBI, False, hq == 3, [masks, self.identf], [(ps, 3)])
            P.dve(lambda e: e.tensor_tensor(dtmp.t[:, :, :], v4(ps.t[:, 3, :]), cum_b, ALU.subtract), [(ps, 3), sm], [dtmp])
            P.act(lambda e: e.activation(dtmp.t[:, :, :], dtmp.t[:, :, :], AF.Exp), [dtmp], [dtmp])
            for h in range(4):
                hh = 4 * g + h
                self.mm(ps.t[:, 4, h * 128:(h + 1) * 128], qkv.t[:, 8 + hh, col0:col0 + 128], qkv.t[:, hh, col0:col0 + 128], True, True,
                        [qkv], [(ps, 4)])
            P.dve(lambda e: e.tensor_tensor(attnT.t[:, :, :], v4(ps.t[:, 4, :]), dtmp.t[:, :, :], ALU.mult), [(ps, 4), dtmp], [attnT])
            self.mm(ps.t[:, 3, :], ONES[0:8, :], bd2f, True, False, [masks, Bd2], [(ps, 3)])
            for hq in range(4):
                self.mm(ps.t[:, 3, hq * 128:(hq + 1) * 128], self.identf.t[:, :], MBS, False, hq == 3, [masks, self.identf], [(ps, 3)])
            P.dve(lambda e: e.tensor_tensor(dtmp.t[:, :, :], v4(ps.t[:, 3, :]), cum_b, ALU.subtract), [(ps, 3), sm], [dtmp])
            P.act(lambda e: e.activation(dtmp.t[:, :, :], dtmp.t[:, :, :], AF.Exp), [dtmp], [dtmp])
            for h in range(4):
                hh = 4 * g + h
                self.mm(ps.t[:, 4, h * 128:(h + 1) * 128], qkv.t[:, 8 + hh, col0:col0 + 128], qkv.t[:, 8 + hh, col0:col0 + 128], True, True,
                        [qkv], [(ps, 4)])
            P.dve(lambda e: e.scalar_tensor_tensor(YTa.t[:, :, :], v4(ps.t[:, 4, :]), -1.0, dtmp.t[:, :, :], ALU.mult, ALU.mult),
                  [(ps, 4), dtmp], [YTa])
            for h in range(4):
                P.pe(lambda e, h=h: e.transpose(ps.t[:, 5, h * 128:(h + 1) * 128], YTa.t[:, h, :], self.identf.t[:, :]),
                     [YTa, self.identf], [(ps, 5)])
            P.act(lambda e: e.activation(Ya.t[:, :, :], v4(ps.t[:, 5, :]), AF.Copy), [(ps, 5)], [Ya])
            P.dve(lambda e: e.tensor_tensor(PT.t[:, :, :], YTa.t[:, :, :], self.identf.t[:, :].unsqueeze(1).broadcast_to([128, 4, 128]), ALU.add),
                  [YTa, self.identf], [PT])
            if self.cfg.dbg.get("gdn_stop", 9) <= 3:
                return
            Y, YT, Yn, YTn = Ya, YTa, Yb, YTb
            for lev in range(1, 6):
                for h in range(4):
                    self.mm(ps.t[:, 2, h * 128:(h + 1) * 128], YT.t[:, h, :], Y.t[:, h, :], True, True, [YT, Y], [(ps, 2)])
                P.act(lambda e, Yn=Yn: e.activation(Yn.t[:, :, :], v4(ps.t[:, 2, :]), AF.Copy), [(ps, 2)], [Yn])
                if lev < 5:
                    for h in range(4):
                        self.mm(ps.t[:, 3, h * 128:(h + 1) * 128], Y.t[:, h, :], YT.t[:, h, :], True, True, [YT, Y], [(ps, 3)])
                    P.act(lambda e, YTn=YTn: e.activation(YTn.t[:, :, :], v4(ps.t[:, 3, :]), AF.Copy), [(ps, 3)], [YTn])
                for h in range(4):
                    self.mm(ps.t[:, 4, h * 128:(h + 1) * 128], Yn.t[:, h, :], PT.t[:, h, :], True, True, [Yn, PT], [(ps, 4)])
                P.dve(lambda e: e.tensor_tensor(PT.t[:, :, :], PT.t[:, :, :], v4(ps.t[:, 4, :]), ALU.add), [PT, (ps, 4)], [PT])
                Y, YT, Yn, YTn = Yn, YTn, Y, YT
            if self.cfg.dbg.get("gdn_stop", 9) <= 4:
                return
            def ev_k(pst, bt):
                P.act(lambda e: e.activation(ktok.t[:, :, :], pst[:, 0:4, :], AF.Copy), [(ps, bt)], [ktok])
            self.transposes_T(ev_k, lambda k: qkv.t[:, 8 + 4 * g + k, col0:col0 + 128], 4, 128, qkv, bt=7)
            for cq in range(2):
                P.dve(lambda e, cq=cq: e.tensor_tensor(kdec.t[:, cq, :, :], ktok.t[:, :, :],
                                                       R(12 + cq)[:, hsl].unsqueeze(2).broadcast_to([128, 4, 128]), ALU.mult),
                      [ktok, sm], [kdec])
            def ev_v(pst, bt):
                P.dve(lambda e: e.tensor_tensor(vb.t[:, :, :], pst[:, 0:4, :], R(0)[:, hsl].unsqueeze(2).broadcast_to([128, 4, 128]), ALU.mult),
                      [(ps, bt), sm], [vb])
            self.transposes_T(ev_v, lambda k: qkv.t[:, 16 + 4 * g + k, col0:col0 + 128], 4, 128, qkv, bt=7)
            Sg = (S, list(range(8 * g, 8 * g + 8)))
            Sbg = (Sb, list(range(8 * g, 8 * g + 8)))
            def chunk_body(ch):
                rs = slice(64 * ch, 64 * ch + 64)
                for h in range(4):
                    hh = 4 * g + h
                    self.mm(ps.t[:, 2, h * 128:(h + 1) * 128], kTf.t[:, h, :], S.t[:, hh * 128:(hh + 1) * 128], True, True,
                            [kTf, Sg], [(ps, 2)])
                rw = slice(0, 128) if ch == 0 else rs
                nrw = 128 if ch == 0 else 64
                nbe_b = R(4)[rw, hsl].unsqueeze(2).broadcast_to([nrw, 4, 128])
                P.dve(lambda e: e.tensor_tensor(rhs.t[rw, :, :], v4(ps.t[rw, 2, :]), nbe_b, ALU.mult), [(ps, 2), sm], [rhs])
                P.dve(lambda e: e.tensor_tensor(rhs.t[rw, :, :], rhs.t[rw, :, :], vb.t[rw, :, :], ALU.add), [rhs, vb], [rhs])
                for h in range(4):
                    self.mm(ps.t[:, 3, h * 128:(h + 1) * 128], PT.t[:, h, :], rhs.t[:, h, :], True, True, [PT, rhs], [(ps, 3)])
                P.act(lambda e: e.activation(ub.t[rw, :, :], v4(ps.t[rw, 3, :]), AF.Copy), [(ps, 3)], [ub])
                for h in range(4):
                    hh = 4 * g + h
                    self.mm(ps.t[:, 4, h * 128:(h + 1) * 128], qdT.t[:, h, :], S.t[:, hh * 128:(hh + 1) * 128], True, False, [qdT, Sg], [(ps, 4)])
                    self.mm(ps.t[:, 4, h * 128:(h + 1) * 128], attnT.t[:, h, :], ub.t[:, h, :], False, True, [attnT, ub], [(ps, 4)])
                P.act(lambda e: e.activation(osb.t[rs, g * 512:(g + 1) * 512], ps.t[rs, 4, :], AF.Copy), [(ps, 4)], [osb])
                for h in range(4):
                    self.mm(ps.t[:, 5, h * 128:(h + 1) * 128], kdec.t[:, ch, h, :], ub.t[:, h, :], True, True, [kdec, ub], [(ps, 5)])
                el_b = sm.t[:, 8 + ch, hsl].unsqueeze(2).broadcast_to([128, 4, 128])
                P.dve(lambda e: e.tensor_tensor(v4(f1.t[:, :]), v4(S.t[:, g * 512:(g + 1) * 512]), el_b, ALU.mult), [Sg, sm], [f1])
                P.dve(lambda e: e.tensor_tensor(S.t[:, g * 512:(g + 1) * 512], f1.t[:, :], ps.t[:, 5, :], ALU.add), [f1, (ps, 5)], [Sg])

            for ch in range(2):
                chunk_body(ch)

        def post(m, rows, col0, o_buf):
            v8 = lambda ap: ap.rearrange("p (h d) -> p h d", h=8)
            P.dve(lambda e: e.tensor_tensor(f3.t[:rows, :], o_buf.t[:rows, :], o_buf.t[:rows, :], ALU.mult), [o_buf], [f3])
            P.dve(lambda e: e.tensor_reduce(nst.t[:rows, 0, :], v8(f3.t[:rows, :]), mybir.AxisListType.X, ALU.add), [f3], [nst])
            P.pool(lambda e: e.tensor_tensor(nst.t[:rows, 1, :], nst.t[:rows, 0, :], self.cst.t[:rows, 5:6].broadcast_to([rows, 8]), ALU.mult),
                   [nst, self.cst], [nst])
            P.pool(lambda e: e.tensor_tensor(nst.t[:rows, 1, :], nst.t[:rows, 1, :], self.cst.t[:rows, 1:2].broadcast_to([rows, 8]), ALU.add),
                   [nst, self.cst], [nst])
            P.pool(lambda e: e.tensor_tensor(nst.t[:rows, 2, :], nst.t[:rows, 1, :], self.cst.t[:rows, 2:3].broadcast_to([rows, 8]), ALU.pow),
                   [nst, self.cst], [nst])
            P.dve(lambda e: e.tensor_tensor(v8(o_buf.t[:rows, :]), v8(o_buf.t[:rows, :]), nst.t[:rows, 2, :].unsqueeze(2).broadcast_to([rows, 8, 128]),
                                            ALU.mult), [o_buf, nst], [o_buf])
            P.dve(lambda e: e.tensor_tensor(v8(o_buf.t[:rows, :]), v8(o_buf.t[:rows, :]), nwb.t[:rows, :].unsqueeze(1).broadcast_to([rows, 8, 128]),
                                            ALU.mult), [o_buf, nwb], [o_buf])
            pz = 4
            for o in range(2):
                for k in range(8):
                    self.mm(ps.t[:rows, pz + o, :], xT.t[:, k, col0:col0 + rows], Wz[o].t[:, k, :], k == 0, k == 7, [(xT, m), Wz[o]], [(ps, pz + o)])
            pzv = ps.t[:rows, pz:pz + 2, :].rearrange("p b n -> p (b n)")
            self.sigmoid_chain(f3.t[:rows, :], pzv, [(ps, pz), (ps, pz + 1)], f3)
            P.dve(lambda e: e.tensor_tensor(f3.t[:rows, :], pzv, f3.t[:rows, :], ALU.mult), [(ps, pz), (ps, pz + 1), f3], [f3])
            P.dve(lambda e: e.tensor_tensor(ogb.t[:rows, :], o_buf.t[:rows, :], f3.t[:rows, :], ALU.mult), [o_buf, f3], [ogb])

            def evac(pst, bt):
                P.act(lambda e: e.activation(hT.t[:, 0:8, col0:col0 + rows], pst[:, :, :rows], AF.Copy), [(ps, bt)], [(hT, list(range(8)))])

            self.transposes_to(evac, lambda k: ogb.t[:rows, k * 128:(k + 1) * 128], 8, rows, ogb, None, bt=7)

        def sample_body(m, col0):
            R = lambda i: sm.t[0:16, i, :]
            gates(m, 16, col0)
            P.act(lambda e: e.activation(R(3), R(1), AF.Exp), [sm], [sm])
            P.dve(lambda e: e.scalar_tensor_tensor(R(4), R(0), -1.0, R(3), ALU.mult, ALU.mult), [sm], [sm])
            P.dve(lambda e: e.tensor_tensor(Bde.t[0:16, :, :], R(3).unsqueeze(1).broadcast_to([16, 16, 8]),
                                            self.eyep.t[:, :].unsqueeze(2).broadcast_to([16, 16, 8]), ALU.mult), [sm, self.eyep], [Bde])
            self.mm(ps.t[:, 6, 0:128], ONES[0:16, :], Bde.t[0:16, :, :].rearrange("p s h -> p (s h)"), True, True, [masks, Bde], [(ps, 6)])
            P.act(lambda e: e.activation(Eall.t[:, :, :].rearrange("p s h -> p (s h)"), ps.t[:, 6, 0:128], AF.Copy), [(ps, 6)], [Eall])
            for h in range(8):
                P.pe(lambda e, h=h: e.transpose(ps.t[0:16, 2 + h // 4, (h % 4) * 128:(h % 4 + 1) * 128], qkvS.t[:, 8 + h, :], self.identf.t[:, :]),
                     [qkvS, self.identf], [(ps, 2 + h // 4)])
            P.act(lambda e: e.activation(ktS.t[0:16, :], ps.t[0:16, 2:4, :].rearrange("p b n -> p (b n)"), AF.Copy), [(ps, 2), (ps, 3)], [ktS])
            for h in range(8):
                P.pe(lambda e, h=h: e.transpose(ps.t[0:16, 4 + h // 4, (h % 4) * 128:(h % 4 + 1) * 128], qkvS.t[:, 16 + h, :], self.identf.t[:, :]),
                     [qkvS, self.identf], [(ps, 4 + h // 4)])
            P.dve(lambda e: e.tensor_tensor(vbS.t[0:16, :].rearrange("p (h d) -> p h d", h=8),
                                            ps.t[0:16, 4:6, :].rearrange("p b (h d) -> p (b h) d", h=4),
                                            R(0).unsqueeze(2).broadcast_to([16, 8, 128]), ALU.mult), [(ps, 4), (ps, 5), sm], [vbS])
            Ss_b = [osb, SsB]
            P.dve(lambda e: e.memset(oacc.t[0:16, :], 0.0), [], [oacc])
            for s_ in range(NS):
                Ss = Ss_b[s_ % 2]
                P.dma("sp", Ss.t[:, :].rearrange("p (h v) -> p h v", h=8), I["st_gdn"][l, s_].rearrange("h k v -> k h v"), [], [Ss], Ss.sem())
                P.dve(lambda e, s_=s_: e.tensor_tensor(kTm.t[:, :, :], qkvS.t[:, 8:16, :],
                                                       self.eyef.t[:, s_, :].unsqueeze(1).broadcast_to([128, 8, 16]), ALU.mult),
                      [qkvS, self.eyef], [kTm])
                P.dve(lambda e, s_=s_: e.tensor_tensor(qTm.t[:, :, :], qkvS.t[:, 0:8, :],
                                                       self.eyef.t[:, s_, :].unsqueeze(1).broadcast_to([128, 8, 16]), ALU.mult),
                      [qkvS, self.eyef], [qTm])
                for h in range(8):
                    self.mm(ps.t[0:16, 2 + h // 4, (h % 4) * 128:(h % 4 + 1) * 128], kTm.t[:, h, :], Ss.t[:, h * 128:(h + 1) * 128], True, True,
                            [kTm, Ss], [(ps, 2 + h // 4)])
                P.dve(lambda e: e.tensor_tensor(tS.t[0:16, :].rearrange("p (h d) -> p h d", h=8),
                                                ps.t[0:16, 2:4, :].rearrange("p b (h d) -> p (b h) d", h=4),
                                                R(4).unsqueeze(2).broadcast_to([16, 8, 128]), ALU.mult), [(ps, 2), (ps, 3), sm], [tS])
                P.dve(lambda e, s_=s_: e.scalar_tensor_tensor(um.t[0:16, :], vbS.t[0:16, :], self.eyep.t[0:16, s_:s_ + 1], tS.t[0:16, :],
                                                              ALU.mult, ALU.add), [vbS, self.eyep, tS], [um])
                for h in range(8):
                    self.mm(ps.t[:, 4 + h // 4, (h % 4) * 128:(h % 4 + 1) * 128], ktS.t[0:16, h * 128:(h + 1) * 128], um.t[0:16, h * 128:(h + 1) * 128],
                            True, True, [ktS, um], [(ps, 4 + h // 4)])
                P.dve(lambda e, s_=s_, Ss=Ss: e.tensor_tensor(tS.t[:, :].rearrange("p (h d) -> p h d", h=8), Ss.t[:, :].rearrange("p (h d) -> p h d", h=8),
                                                              Eall.t[:, s_, :].unsqueeze(2).broadcast_to([128, 8, 128]), ALU.mult), [Ss, Eall], [tS])
                P.dve(lambda e, Ss=Ss: e.tensor_tensor(Ss.t[:, :], tS.t[:, :], ps.t[:, 4:6, :].rearrange("p b n -> p (b n)"), ALU.add),
                      [tS, (ps, 4), (ps, 5)], [Ss])
                P.dma("sp", self.o["gdn_s"][l, s_].rearrange("h k v -> k h v"), Ss.t[:, :].rearrange("p (h v) -> p h v", h=8), [Ss], [], Ss.sem(),
                      is_output=True)
                for h in range(8):
                    self.mm(ps.t[0:16, h // 4, (h % 4) * 128:(h % 4 + 1) * 128], qTm.t[:, h, :], Ss.t[:, h * 128:(h + 1) * 128],
                            True, True, [qTm, Ss], [(ps, h // 4)])
                P.dve(lambda e: e.tensor_tensor(oacc.t[0:16, :], oacc.t[0:16, :], ps.t[0:16, 0:2, :].rearrange("p b n -> p (b n)"), ALU.add),
                      [oacc, (ps, 0), (ps, 1)], [oacc])
            post(m, 16, col0, oacc)

        for (m, rows, col0) in self.tiles():
            if rows == 128:
                tile_body(m, rows, col0)
            elif self.cfg.dbg.get("gdn_stop", 9) >= 7:
                sample_body(m, col0)
        self.state_store("gdn_p", l)

    def transposes_T(self, evac, src_fn, n, rows, src_buf, bt=None):
        P, ps = self.P, self.ps
        bt = self.bank() if bt is None else bt
        pst = ps.t[:, bt, :].bitcast(BF16).rearrange("p (k n) -> p k n", k=8)
        for k in range(n):
            P.pe(lambda e, k=k: e.transpose(pst[:rows, k, :], src_fn(k), self.identb.t[:, :]), [src_buf, self.identb], [(ps, bt)])
        evac(pst, bt)

    def token_mix(self, l):
        I = self.i
        if "ret" in self.cfg.mix:
            self.retention(l)
            self.finale(l, I["w_ret_out"][l], C_M1)
        if "ssd" in self.cfg.mix:
            self.ssd(l)
            self.finale(l, I["w_ssm_out"][l], C_M2)
        if "gdn" in self.cfg.mix:
            self.gdn(l)
            self.finale(l, I["w_gdn_out"][l], C_M3)

    def build(self):
        c = self.cfg
        self.pT = [self.P.sbuf(f"pT{i}", [128, 2, 128], BF16) for i in range(2)]
        self.alloc_mix()
        for half in range(c.NH):
            self.half = half
            self.has_sample = c.sample and half == c.NH - 1
            self.load_x()
            for l in range(c.layers):
                last = (l == c.layers - 1)
                if c.ffn:
                    self.ffn(l, 0)
                self.layer_norm(l, 0)
                self.token_mix(l)
                self.layer_norm(l, 1)
                if c.ffn:
                    self.ffn(l, 1)
                self.layer_norm(l, 2)
                if c.pegate:
                    self.pe_gate(l)
                self.layer_norm(l, 3, final=last)
        return self.P.finish()


def make_consts(T):
    c = {}
    c["c_ident"] = np.eye(128, dtype=np.float32)
    half = 64
    inv = (np.float32(10000.0) ** (-np.arange(half, dtype=np.float32) / np.float32(half))).astype(np.float32)
    pos = np.concatenate([np.arange(T, dtype=np.float32), np.full(NS, PAST_LEN, np.float32)])
    ang = (pos[:, None] * inv[None, :]).astype(np.float32).astype(np.float64)
    rope = np.zeros((T + NS, 4, 64), np.float32)
    rope[:, 0] = np.cos(ang)
    rope[:, 1] = np.sin(ang)
    rope[:, 2] = np.cos(ang) * 128 ** -0.5
    rope[:, 3] = np.sin(ang) * 128 ** -0.5
    c["c_rope"] = rope
    gam = 1.0 - 2.0 ** (-5.0 - np.arange(4))
    lg = np.log1p(-(2.0 ** (-5.0 - np.arange(4)))).astype(np.float32).astype(np.float64)
    i = np.arange(128)
    dm = i[None, :] - i[:, None]
    mask = np.zeros((4, 128, 128), np.float32)
    row = np.zeros((4, 3, 128), np.float32)
    for h in range(4):
        mask[h] = np.where(dm >= 0, np.exp(lg[h] * np.maximum(dm, 0)), 0.0)
        row[h, 0] = np.exp(lg[h] * (i + 1))
        row[h, 1] = np.exp(lg[h] * (127 - i))
        row[h, 2, 0] = np.exp(lg[h] * 128)
        row[h, 2, 1] = np.exp(lg[h])
    c["c_retmask"] = mask
    c["c_retrow"] = row
    mk = np.zeros((8, 128, 128), np.float32)
    jj, ii = np.meshgrid(np.arange(128), np.arange(128), indexing="ij")
    blk = (jj // 64) == (ii // 64)
    mk[0] = (jj <= ii)
    mk[1] = 1.0
    mk[2] = np.where(jj <= ii, 0.0, -30000.0)
    mk[3] = (jj <= ii) & blk
    mk[4] = np.where((jj <= ii) & blk, 0.0, -30000.0)
    mk[5] = np.where((jj < ii) & blk, 0.0, -30000.0)
    mk[6] = blk
    mk[7] = (jj < 64)
    c["c_masks"] = mk
    return c


_CACHE = {}


def kernel(**inputs):
    NC = 8
    cfg = Cfg(NH=4, NTH=4, sample=True, layers=2)
    mk = MK(cfg)
    mk.build()
    T = cfg.T
    consts = make_consts(T)
    f = lambda a: np.ascontiguousarray(np.asarray(a, dtype=np.float32))
    wnames = ["ln_g", "ln_b", "ffn_wg", "ffn_wu", "ffn_wd", "w_in", "ssm_conv_w", "ssm_conv_b", "ssm_dt_bias",
              "ssm_a_log", "ssm_d", "ssm_norm_w", "gdn_conv_w", "gdn_dt_bias", "gdn_a_log", "gdn_norm_w",
              "w_ret_out", "w_ssm_out", "w_gdn_out", "w_o", "pe_proj", "pe_gate"]
    W = {k: f(inputs[k]) for k in wnames}
    xp, xs = np.asarray(inputs["x_prompt"]), np.asarray(inputs["x_sample"])
    pp, ps_ = np.asarray(inputs["p_prompt"]), np.asarray(inputs["p_sample"])
    in_maps = []
    for c in range(NC):
        sl = slice(NS * c, NS * (c + 1))
        m = dict(W)
        m.update(consts)
        m["xp"] = f(xp[c])
        m["pp"] = f(pp[:, c])
        m["xs"] = f(xs[sl, 0])
        m["ps"] = f(ps_[:, sl, 0])
        m["st_ret"] = f(np.asarray(inputs["state_ret"])[:, sl])
        m["st_ssm"] = f(np.asarray(inputs["state_ssm"])[:, sl])
        m["st_ssm_conv"] = f(np.asarray(inputs["state_ssm_conv"])[:, sl])
        m["st_gdn"] = f(np.asarray(inputs["state_gdn"])[:, sl])
        m["st_gdn_conv"] = f(np.asarray(inputs["state_gdn_conv"])[:, sl])
        in_maps.append({k: v for k, v in m.items() if k in mk.i})
    res = run_bass_kernel_spmd(mk.nc, in_maps, core_ids=list(range(NC)))
    R = res.results
    cat0 = lambda k: np.stack([R[c][k] for c in range(NC)], axis=0)
    y_p = cat0("y_p")
    y_s = np.concatenate([R[c]["y_s"] for c in range(NC)], 0)[:, None, :]
    outs = [y_p, y_s]
    for k in ("ret_p", "ssm_p", "ssm_conv_p", "gdn_p", "gdn_conv_p"):
        outs.append(np.stack([R[c][k] for c in range(NC)], axis=1))
    for k in ("ret_s", "ssm_s", "ssm_conv_s", "gdn_s", "gdn_conv_s"):
        outs.append(np.concatenate([R[c][k] for c in range(NC)], axis=1))
    return tuple(np.ascontiguousarray(o, dtype=np.float32) for o in outs)
```

```python
import numpy as np
from contextlib import ExitStack
import concourse.bass as bass
import concourse.mybir as mybir
from concourse.bass_utils import run_bass_kernel_spmd

F32, BF16 = mybir.dt.float32, mybir.dt.bfloat16
AF = mybir.ActivationFunctionType
ALU = mybir.AluOpType

D = 1024
DEPTH = 2
FFN = 2048
PLE = 256
IN_DIM = 12832
DN_ALPHA = (2 * DEPTH) ** 0.25
LN_EPS = 1e-5
NORM_EPS = 1e-6
PAST_LEN = 16384
NS = 16

C_RQ, C_RK, C_RV, C_RG = 0, 512, 1024, 2048
C_MZ, C_MXBC, C_MDT = 3072, 4096, 5632
C_GQKV, C_GZ, C_GA, C_GB = 5648, 8720, 9744, 9752
C_M1, C_M2, C_M3 = 9760, 10784, 11808


class Sem:
    def __init__(self, name):
        self.name = name
        self.count = 0
        self.h = None


class Reg:
    __slots__ = ("writers", "readers")

    def __init__(self):
        self.writers = {}
        self.readers = {}


class Buf:
    def __init__(self, prog, name, t, nreg=1):
        self.prog, self.name, self.t = prog, name, t
        self.regs = [[Reg()] for _ in range(nreg)]
        self.dsem = None

    def sem(self):
        if self.dsem is None:
            self.dsem = self.prog.named_sem("d_" + getattr(self, "sem_name", self.name))
        return self.dsem

    def __getitem__(self, k):
        return self.t[k]


class Op:
    __slots__ = ("eng", "fn", "deps", "needed", "is_dma", "sem", "val", "waits", "dmawaits")

    def __init__(self, eng, fn, is_dma):
        self.eng, self.fn, self.is_dma = eng, fn, is_dma
        self.deps = []
        self.dmawaits = []
        self.needed = False
        self.sem = None
        self.val = 0


def _regs(spec):
    out = []
    for s in spec:
        if isinstance(s, Buf):
            for g in s.regs:
                out.extend(g)
        else:
            b, idx = s
            if isinstance(idx, int):
                out.extend(b.regs[idx])
            else:
                for i in idx:
                    out.extend(b.regs[i])
    return out


class Arena:
    GRAN = 256

    def __init__(self, prog, name, nbytes):
        self.prog = prog
        self.nbytes = nbytes
        self.base = prog.sbuf(name, [128, nbytes // 2], BF16)
        self.gr = [Reg() for _ in range(nbytes // self.GRAN)]
        self.off = 0
        self.n = 0

    def reset(self, off=0):
        self.off = off

    def alloc(self, name, free_shape, dt, nreg=1):
        esz = 2 if dt == BF16 else 4
        nel = int(np.prod(free_shape))
        nb = nel * esz
        nb_al = (nb + self.GRAN - 1) // self.GRAN * self.GRAN
        assert self.off + nb_al <= self.nbytes, f"arena overflow allocating {name}: {self.off}+{nb_al}>{self.nbytes}"
        o2 = self.off // 2
        v = self.base.t[:, o2:o2 + nb // 2]
        if dt != BF16:
            v = v.bitcast(dt)
        if len(free_shape) > 1:
            names = " ".join(f"d{i}" for i in range(len(free_shape)))
            kw = {f"d{i}": int(free_shape[i]) for i in range(len(free_shape))}
            v = v.rearrange(f"p ({names}) -> p {names}", **kw)
        self.n += 1
        b = Buf(self.prog, f"{name}_{self.n}", v, 1)
        b.sem_name = f"a_{name}_{self.off}"
        g0 = self.off // self.GRAN
        ng = nb_al // self.GRAN
        grs = self.gr[g0:g0 + ng]
        if nreg == 1:
            b.regs = [grs]
        else:
            assert ng % nreg == 0, (name, ng, nreg)
            k = ng // nreg
            b.regs = [grs[i * k:(i + 1) * k] for i in range(nreg)]
        self.off += nb_al
        return b


class Prog:
    ENGS = ("pe", "act", "dve", "pool", "sp")
    ATTR = {"pe": "tensor", "act": "scalar", "dve": "vector", "pool": "gpsimd", "sp": "sync"}

    def __init__(self, nc):
        self.nc = nc
        self.es = ExitStack()
        self.ops = {e: [] for e in self.ENGS}
        self.sems = []
        self.esem = {e: self.new_sem("e_" + e) for e in ("pe", "act", "dve", "pool")}
        self.nbuf = 0
        self.out_sems = set()

    def new_sem(self, name):
        s = Sem(name)
        self.sems.append(s)
        return s

    def named_sem(self, name):
        d = self.__dict__.setdefault("_named", {})
        if name not in d:
            d[name] = self.new_sem(name)
        return d[name]

    def sbuf(self, name, shape, dt, nreg=1):
        nb = int(np.prod(shape[1:])) * (2 if dt == BF16 else 4)
        self.sb_bytes = getattr(self, "sb_bytes", 0) + nb
        self.sb_log = getattr(self, "sb_log", []) + [(name, nb)]
        t = self.es.enter_context(self.nc.sbuf_tensor("s_" + name, list(shape), dt))
        return Buf(self, name, t, nreg)

    def psum(self, name, shape, dt, nreg=1):
        t = self.es.enter_context(self.nc.psum_tensor("p_" + name, list(shape), dt))
        return Buf(self, name, t, nreg)

    def dram(self, name, shape, dt, kind, nreg=1):
        t = self.nc.dram_tensor(name, list(shape), dt, kind=kind)
        return Buf(self, name, t.ap(), nreg)

    def _dep(self, c, p):
        if p is None or p is c:
            return
        if p.is_dma:
            c.dmawaits.append((p.sem, p.sem.count))
            return
        p.needed = True
        c.deps.append(p)

    def op(self, eng, fn, reads=(), writes=(), dma_sem=None):
        is_dma = dma_sem is not None
        o = Op(eng, fn, is_dma)
        st = self.__dict__.setdefault("tagstat", {})
        key = (getattr(self, "tag", "-"), eng)
        st[key] = st.get(key, 0) + 1
        rr, ww = _regs(reads), _regs(writes)
        for r in rr:
            for e, p in r.writers.items():
                if (not is_dma) and (not p.is_dma) and e == eng and eng == "pe":
                    continue
                self._dep(o, p)
        for r in ww:
            for e, p in r.readers.items():
                if (not is_dma) and (not p.is_dma) and e == eng and eng == "pe":
                    continue
                self._dep(o, p)
            for e, p in r.writers.items():
                if (not is_dma) and (not p.is_dma) and e == eng and eng == "pe":
                    continue
                if is_dma and p.is_dma and p.sem is dma_sem:
                    continue
                self._dep(o, p)
        key = ("dma", id(o)) if is_dma else eng
        if is_dma:
            dma_sem.count += 16
            o.sem, o.val = dma_sem, dma_sem.count
        for r in rr:
            r.readers[key] = o
        for r in ww:
            r.writers = {key: o}
            r.readers = {}
        self.ops[eng].append(o)
        return o

    def pe(self, fn, reads, writes):
        return self.op("pe", fn, reads, writes)

    def act(self, fn, reads, writes):
        return self.op("act", fn, reads, writes)

    def dve(self, fn, reads, writes):
        return self.op("dve", fn, reads, writes)

    def pool(self, fn, reads, writes):
        return self.op("pool", fn, reads, writes)

    def dma(self, q, out, in_, reads, writes, sem, is_output=False, **kw):
        if is_output:
            self.out_sems.add(sem)
        return self.op(q, lambda e: e.dma_start(out=out, in_=in_, **kw), reads, writes, dma_sem=sem)

    def finish(self):
        nc = self.nc
        fin = Op("sp", None, False)
        for s in self.sems:
            if s.name.startswith("d_") and s.count > 0:
                fin.dmawaits.append((s, s.count))
        self.ops["sp"].append(fin)
        for e in ("pe", "act", "dve", "pool"):
            n = 0
            for o in self.ops[e]:
                if o.is_dma:
                    continue
                if o.needed:
                    n += 1
                    o.sem, o.val = self.esem[e], n
            self.esem[e].count = n
        for s in self.sems:
            if s.count > 0:
                s.h = self.es.enter_context(nc.semaphore(s.name))
        block = self.es.enter_context(nc.Block())
        stats = {}
        for e in self.ENGS:
            ops = self.ops[e]

            def body(eng, ops=ops, e=e):
                seen = {}
                nw = 0
                for o in ops:
                    ws = {}
                    for p in o.deps:
                        ws[p.sem] = max(ws.get(p.sem, 0), p.val)
                    for s, v in o.dmawaits:
                        ws[s] = max(ws.get(s, 0), v)
                    for s, v in ws.items():
                        if seen.get(s, 0) >= v:
                            continue
                        seen[s] = v
                        eng.wait_ge(s.h, v)
                        nw += 1
                    if o.fn is None:
                        continue
                    ins = o.fn(eng)
                    if o.is_dma:
                        ins.then_inc(o.sem.h, 16)
                    elif o.needed:
                        ins.then_inc(o.sem.h, 1)
                stats[e] = (len(ops), nw)

            getattr(block, self.ATTR[e])(body)
        self.es.close()
        return stats


class Cfg:
    def __init__(self, NH=2, NTH=8, sample=True, layers=2, mix=("ret", "ssd", "gdn"), pegate=True, ffn=True):
        self.NH, self.NTH, self.sample, self.layers = NH, NTH, sample, layers
        self.mix, self.pegate, self.ffn = mix, pegate, ffn
        self.dbg = {}
        self.T = NH * NTH * 128


class MK:
    def __init__(self, cfg):
        self.cfg = cfg
        nc = bass.Bass("TRN2", target_bir_lowering=False)
        self.nc = nc
        self.P = Prog(nc)
        self.declare_io()
        self.alloc()

    def din(self, name, shape):
        return self.nc.dram_tensor(name, list(shape), F32, kind="ExternalInput").ap()

    def dout(self, name, shape):
        return self.nc.dram_tensor(name, list(shape), F32, kind="ExternalOutput").ap()

    def declare_io(self):
        c = self.cfg
        T = c.T
        L = DEPTH
        self.i = {}
        I = self.i
        I["xp"] = self.din("xp", [T, D])
        I["pp"] = self.din("pp", [L, T, PLE])
        if c.sample:
            I["xs"] = self.din("xs", [NS, D])
            I["ps"] = self.din("ps", [L, NS, PLE])
            I["st_ret"] = self.din("st_ret", [L, NS, 4, 128, 256])
            I["st_ssm"] = self.din("st_ssm", [L, NS, 16, 128, 64])
            I["st_ssm_conv"] = self.din("st_ssm_conv", [L, NS, 3, 1536])
            I["st_gdn"] = self.din("st_gdn", [L, NS, 8, 128, 128])
            I["st_gdn_conv"] = self.din("st_gdn_conv", [L, NS, 3, 3072])
        for nm, shp in [("ln_g", [L, 4, D]), ("ln_b", [L, 4, D]), ("ffn_wg", [L, 2, D, FFN]),
                        ("ffn_wu", [L, 2, D, FFN]), ("ffn_wd", [L, 2, FFN, D]), ("w_in", [L, D, IN_DIM]),
                        ("ssm_conv_w", [L, 4, 1536]), ("ssm_conv_b", [L, 1536]), ("ssm_dt_bias", [L, 16]),
                        ("ssm_a_log", [L, 16]), ("ssm_d", [L, 16]), ("ssm_norm_w", [L, 1024]),
                        ("gdn_conv_w", [L, 4, 3072]), ("gdn_dt_bias", [L, 8]), ("gdn_a_log", [L, 8]),
                        ("gdn_norm_w", [L, 128]), ("w_ret_out", [L, D, D]), ("w_ssm_out", [L, D, D]),
                        ("w_gdn_out", [L, D, D]), ("w_o", [L, D, D]), ("pe_proj", [L, PLE, D]),
                        ("pe_gate", [L, D, D])]:
            I[nm] = self.din(nm, shp)
        I["c_ident"] = self.din("c_ident", [128, 128])
        I["c_rope"] = self.din("c_rope", [T + NS, 4, 64])
        I["c_retmask"] = self.din("c_retmask", [4, 128, 128])
        I["c_retrow"] = self.din("c_retrow", [4, 3, 128])
        I["c_masks"] = self.din("c_masks", [8, 128, 128])
        self.o = {}
        O = self.o
        O["y_p"] = self.dout("y_p", [T, D])
        O["ret_p"] = self.dout("ret_p", [L, 4, 128, 256])
        O["ssm_p"] = self.dout("ssm_p", [L, 16, 128, 64])
        O["ssm_conv_p"] = self.dout("ssm_conv_p", [L, 3, 1536])
        O["gdn_p"] = self.dout("gdn_p", [L, 8, 128, 128])
        O["gdn_conv_p"] = self.dout("gdn_conv_p", [L, 3, 3072])
        if c.sample:
            O["y_s"] = self.dout("y_s", [NS, D])
            O["ret_s"] = self.dout("ret_s", [L, NS, 4, 128, 256])
            O["ssm_s"] = self.dout("ssm_s", [L, NS, 16, 128, 64])
            O["ssm_conv_s"] = self.dout("ssm_conv_s", [L, NS, 3, 1536])
            O["gdn_s"] = self.dout("gdn_s", [L, NS, 8, 128, 128])
            O["gdn_conv_s"] = self.dout("gdn_conv_s", [L, NS, 3, 3072])

    def alloc(self):
        c, P = self.cfg, self.P
        self.NTT = c.NTH + (1 if c.sample else 0)
        self.TS = c.NTH * 128 + (NS if c.sample else 0)
        NTT, TS = self.NTT, self.TS
        self.xa = P.sbuf("xa", [128, NTT, D], F32, nreg=NTT * 2)
        self.xT = P.sbuf("xT", [128, 8, TS], BF16, nreg=NTT)
        self.hT = P.sbuf("hT", [128, 16, TS], BF16, nreg=16)
        self.NSLOT = 6
        self.wslots = [P.sbuf(f"w{i}", [128, 8, 512], BF16) for i in range(self.NSLOT)]
        self.wi = 0
        self.lnp = P.sbuf("lnp", [128, 2, D], F32)
        self.identb = P.sbuf("identb", [128, 128], BF16)
        self.identf = P.sbuf("identf", [128, 128], F32)
        self.tmpA = [P.sbuf(f"tmpA{i}", [128, D], F32) for i in range(2)]
        self.tmpB = [P.sbuf(f"tmpB{i}", [128, D], F32) for i in range(2)]
        self.xb = [P.sbuf(f"xb{i}", [128, D], BF16) for i in range(1)]
        self.lnst = [P.sbuf(f"lnst{i}", [128, 2, 6], F32) for i in range(2)]
        self.lnmv = [P.sbuf(f"lnmv{i}", [128, 4], F32) for i in range(2)]
        self.ps = P.psum("ps", [128, 8, 512], F32, nreg=8)
        self.psi = 0
        self.rr = {}
        P.dma("pool", self.identb.t[:], self.i["c_ident"], [], [self.identb], self.identb.sem())
        P.dma("sp", self.identf.t[:], self.i["c_ident"], [], [self.identf], self.identf.sem())

    def rot(self, key, lst):
        i = self.rr.get(key, 0)
        self.rr[key] = i + 1
        return lst[i % len(lst)]

    def bank(self):
        b = 2 + self.psi
        self.psi = (self.psi + 1) % 6
        return b

    def bank2(self):
        if self.psi % 2:
            self.psi = (self.psi + 1) % 6
        b = 2 + self.psi
        self.psi = (self.psi + 2) % 6
        return b

    def wslot(self):
        s = self.wslots[self.wi % self.NSLOT]
        self.wi += 1
        return s

    def wload(self, slot, c0, src2d, K=1024):
        ncols = src2d.shape[1]
        kc = K // 128
        self.P.dma("pool", slot.t[:, 0:kc, c0:c0 + ncols], src2d.rearrange("(k p) c -> p k c", p=128),
                   [], [slot], slot.sem())

    def tiles(self):
        c = self.cfg
        out = [(m, 128, m * 128) for m in range(c.NTH)]
        if self.has_sample:
            out.append((c.NTH, NS, c.NTH * 128))
        return out

    def nblocks(self):
        c = self.cfg
        TP = c.NTH * 128
        out = [(n0, min(512, TP - n0)) for n0 in range(0, TP, 512)]
        if self.has_sample:
            out.append((TP, NS))
        return out

    def xT_regs(self, n0, nsz):
        return (self.xT, list(range(n0 // 128, (n0 + nsz - 1) // 128 + 1)))

    @staticmethod
    def run_pipelined(gens, depth):
        it = iter(gens)
        active = []
        done = False
        while True:
            if not done and len(active) < depth:
                try:
                    active.append(next(it))
                except StopIteration:
                    done = True
            if not active:
                if done:
                    break
                continue
            for g in list(active):
                try:
                    next(g)
                except StopIteration:
                    active.remove(g)

    def mm(self, out, lhsT, rhs, start, stop, reads, writes):
        n = int(np.prod(rhs.shape[1:]))
        cyc = max(64, n) * (4 if rhs.dtype == F32 else 1)
        pc = self.P.__dict__.setdefault("pecost", {})
        t = getattr(self.P, "tag", "-")
        pc[t] = pc.get(t, 0) + cyc
        self.P.pe(lambda e: e.matmul(out, lhsT, rhs, start=start, stop=stop), reads, writes)

    def emit_xT(self, m, rows, col0, src, final_out=None):
        P = self.P
        xa, xT, ps = self.xa, self.xT, self.ps
        xb = self.rot("xb", self.xb)
        P.act(lambda e: e.activation(xa.t[:rows, m, :], src.t[:rows, :], AF.Copy, scale=float(DN_ALPHA)),
              [src], [(xa, [2 * m, 2 * m + 1])])
        P.dve(lambda e: e.tensor_copy(xb.t[:rows, :], src.t[:rows, :]), [src], [xb])
        b = self.bank()
        pst = ps.t[:, b, :].bitcast(BF16).rearrange("p (k n) -> p k n", k=8)
        for k in range(8):
            P.pe(lambda e, k=k: e.transpose(pst[:, k, :rows], xb.t[:rows, k * 128:(k + 1) * 128],
                                            self.identb.t[:rows, :rows]),
                 [xb, self.identb], [(ps, b)])
        P.act(lambda e: e.activation(xT.t[:, :, col0:col0 + rows], pst[:, :, :rows], AF.Copy),
              [(ps, b)], [(xT, m)])

    def layer_norm(self, l, idx, final=False):
        self.P.tag = "ln"
        P = self.P
        I = self.i
        lnp = self.lnp
        P.dma("sp", lnp.t[:, 0, :], I["ln_g"][l, idx, :].partition_broadcast(128), [], [lnp], lnp.sem())
        P.dma("sp", lnp.t[:, 1, :], I["ln_b"][l, idx, :].partition_broadcast(128), [], [lnp], lnp.sem())
        xa = self.xa

        def tile_gen(i, m, rows, col0):
            par = i % 2
            st, mv, tA, tB = self.lnst[par], self.lnmv[par], self.tmpA[par], self.tmpB[par]
            xr = (xa, [2 * m, 2 * m + 1])
            P.dve(lambda e: e.bn_stats(st.t[:rows, 0, :], xa.t[:rows, m, 0:512]), [xr], [st])
            P.dve(lambda e: e.bn_stats(st.t[:rows, 1, :], xa.t[:rows, m, 512:1024]), [xr], [st])
            P.dve(lambda e: e.bn_aggr(mv.t[:rows, 0:2], st.t[:rows, :, :]), [st], [mv])
            yield
            P.act(lambda e: e.activation(mv.t[:rows, 2:3], mv.t[:rows, 1:2], AF.Ln, bias=float(LN_EPS)), [mv], [mv])
            P.act(lambda e: e.activation(mv.t[:rows, 3:4], mv.t[:rows, 2:3], AF.Exp, scale=-0.5), [mv], [mv])
            yield
            P.dve(lambda e: e.tensor_scalar(tA.t[:rows, :], xa.t[:rows, m, :], mv.t[:rows, 0:1], mv.t[:rows, 3:4],
                                            ALU.subtract, ALU.mult), [xr, mv], [tA])
            P.dve(lambda e: e.tensor_tensor(tA.t[:rows, :], tA.t[:rows, :], lnp.t[:rows, 0, :], ALU.mult), [tA, lnp], [tA])
            P.dve(lambda e: e.tensor_tensor(tB.t[:rows, :], tA.t[:rows, :], lnp.t[:rows, 1, :], ALU.add), [tA, lnp], [tB])
            yield
            if final:
                self.store_y(m, rows, tB)
            else:
                self.emit_xT(m, rows, col0, tB)

        self.run_pipelined((tile_gen(i, m, rows, col0) for i, (m, rows, col0) in enumerate(self.tiles())), 2)

    def store_y(self, m, rows, src):
        c = self.cfg
        if rows == 128:
            t0 = (self.half * c.NTH + m) * 128
            dst = self.o["y_p"][t0:t0 + 128, :]
        else:
            dst = self.o["y_s"][:, :]
        self.P.dma("sp", dst, src.t[:rows, :], [src], [], src.sem(), is_output=True)

    def load_x(self):
        c = self.cfg
        for (m, rows, col0) in self.tiles():
            tB = self.rot("tmpB", self.tmpB)
            if rows == 128:
                t0 = (self.half * c.NTH + m) * 128
                src = self.i["xp"][t0:t0 + 128, :]
            else:
                src = self.i["xs"][:, :]
            self.P.dma("sp", tB.t[:rows, :], src, [], [tB], tB.sem())
            self.emit_xT(m, rows, col0, tB)

    def ffn(self, l, idx):
        self.P.tag = "ffn"
        P, I = self.P, self.i
        ps, xT, hT, xa = self.ps, self.xT, self.hT, self.xa
        wg, wu, wd = I["ffn_wg"][l, idx], I["ffn_wu"][l, idx], I["ffn_wd"][l, idx]
        slots = []

        def loadA(hb):
            s = self.wslot()
            self.wload(s, 0, wg[:, hb * 256:(hb + 1) * 256])
            self.wload(s, 256, wu[:, hb * 256:(hb + 1) * 256])
            return s

        nxt = loadA(0)
        for hb in range(8):
            cur = nxt
            if hb + 1 < 8:
                nxt = loadA(hb + 1)
            for jj in range(2):
                j = hb * 2 + jj
                for (n0, nsz) in self.nblocks():
                    bg, bu = self.bank(), self.bank()
                    xr = self.xT_regs(n0, nsz)
                    for k in range(8):
                        self.mm(ps.t[:, bg, :nsz], cur.t[:, k, jj * 128:(jj + 1) * 128], xT.t[:, k, n0:n0 + nsz],
                                k == 0, k == 7, [cur, xr], [(ps, bg)])
                    for k in range(8):
                        self.mm(ps.t[:, bu, :nsz], cur.t[:, k, 256 + jj * 128:256 + (jj + 1) * 128],
                                xT.t[:, k, n0:n0 + nsz], k == 0, k == 7, [cur, xr], [(ps, bu)])
                    tA = self.rot("tmpA", self.tmpA)
                    P.act(lambda e, bg=bg, nsz=nsz, tA=tA: e.activation(tA.t[:, :nsz], ps.t[:, bg, :nsz], AF.Silu),
                          [(ps, bg)], [tA])
                    P.dve(lambda e, bu=bu, nsz=nsz, n0=n0, j=j, tA=tA: e.tensor_tensor(
                        hT.t[:, j, n0:n0 + nsz], ps.t[:, bu, :nsz], tA.t[:, :nsz], ALU.mult),
                        [(ps, bu), tA], [(hT, j)])
        def loadB(o):
            s0, s1 = self.wslot(), self.wslot()
            self.P.dma("pool", s0.t[:, :, :], wd[0:1024, o * 512:(o + 1) * 512].rearrange("(k p) c -> p k c", p=128),
                       [], [s0], s0.sem())
            self.P.dma("pool", s1.t[:, :, :], wd[1024:2048, o * 512:(o + 1) * 512].rearrange("(k p) c -> p k c", p=128),
                       [], [s1], s1.sem())
            return (s0, s1)

        nxt = loadB(0)
        for o in range(2):
            cur = nxt
            if o == 0:
                nxt = loadB(1)
            for (m, rows, col0) in self.tiles():
                b = self.bank()
                for j in range(16):
                    s = cur[j // 8]
                    self.mm(ps.t[:rows, b, :], hT.t[:, j, col0:col0 + rows], s.t[:, j % 8, :], j == 0, j == 15,
                            [(hT, j), s], [(ps, b)])
                P.dve(lambda e, m=m, rows=rows, b=b, o=o: e.scalar_tensor_tensor(
                    xa.t[:rows, m, o * 512:(o + 1) * 512], ps.t[:rows, b, :], 0.5,
                    xa.t[:rows, m, o * 512:(o + 1) * 512], ALU.mult, ALU.add),
                    [(ps, b), (xa, 2 * m + o)], [(xa, 2 * m + o)])

    def pe_gate(self, l):
        self.P.tag = "pegate"
        P, I = self.P, self.i
        c = self.cfg
        ps, xT, xa = self.ps, self.xT, self.xa
        sg0, sg1, sp = self.wslot(), self.wslot(), self.wslot()
        self.wload(sg0, 0, I["pe_gate"][l][:, 0:512])
        self.wload(sg1, 0, I["pe_gate"][l][:, 512:1024])
        for o in range(2):
            P.dma("pool", sp.t[:, 2 * o:2 * o + 2, :],
                  I["pe_proj"][l][:, o * 512:(o + 1) * 512].rearrange("(k p) c -> p k c", p=128), [], [sp], sp.sem())
        sg = (sg0, sg1)

        def tile_body(m, rows, col0):
            pb = self.rot("xb", self.xb)
            if rows == 128:
                t0 = (self.half * c.NTH + m) * 128
                src = I["pp"][l, t0:t0 + 128, :]
            else:
                src = I["ps"][l, :, :]
            P.dma("pool", pb.t[:rows, 0:PLE], src, [], [pb], pb.sem())
            b = self.bank()
            pst = ps.t[:, b, :].bitcast(BF16).rearrange("p (k n) -> p k n", k=8)
            for k in range(2):
                P.pe(lambda e, k=k, rows=rows, pb=pb: e.transpose(pst[:, k, :rows], pb.t[:rows, k * 128:(k + 1) * 128],
                                                                self.identb.t[:rows, :rows]),
                     [pb, self.identb], [(ps, b)])
            pT = self.rot("pT", self.pT)
            P.dve(lambda e, rows=rows, pT=pT: e.tensor_copy(pT.t[:, :, :rows], pst[:, 0:2, :rows]), [(ps, b)], [pT])
            tA = self.rot("tmpA", self.tmpA)
            for o in range(2):
                bg, bp = self.bank(), self.bank()
                for k in range(8):
                    self.mm(ps.t[:rows, bg, :], xT.t[:, k, col0:col0 + rows], sg[o].t[:, k, :], k == 0, k == 7,
                            [(xT, m), sg[o]], [(ps, bg)])
                for k in range(2):
                    self.mm(ps.t[:rows, bp, :], pT.t[:, k, :rows], sp.t[:, 2 * o + k, :], k == 0, k == 1,
                            [pT, sp], [(ps, bp)])
                P.act(lambda e, rows=rows, bg=bg, o=o, tA=tA: e.activation(
                    tA.t[:rows, o * 512:(o + 1) * 512], ps.t[:rows, bg, :], AF.Sigmoid), [(ps, bg)], [tA])
                P.dve(lambda e, rows=rows, bp=bp, o=o, tA=tA: e.tensor_tensor(
                    tA.t[:rows, o * 512:(o + 1) * 512], ps.t[:rows, bp, :], tA.t[:rows, o * 512:(o + 1) * 512], ALU.mult),
                    [(ps, bp), tA], [tA])
                P.dve(lambda e, m=m, rows=rows, o=o, tA=tA: e.tensor_tensor(
                    xa.t[:rows, m, o * 512:(o + 1) * 512], xa.t[:rows, m, o * 512:(o + 1) * 512],
                    tA.t[:rows, o * 512:(o + 1) * 512], ALU.add),
                    [tA, (xa, 2 * m + o)], [(xa, 2 * m + o)])

        for (m, rows, col0) in self.tiles():
            tile_body(m, rows, col0)


    def alloc_mix(self):
        P, I = self.P, self.i
        c = self.cfg
        self.S = P.sbuf("S", [128, 1024], F32, nreg=16)
        self.Sb = P.sbuf("Sb", [128, 1024], BF16, nreg=16)
        self.cst = P.sbuf("cst", [128, 8], F32)
        P.dve(lambda e: e.memset(self.cst.t[:, 0:1], float(LN_EPS)), [], [self.cst])
        P.dve(lambda e: e.memset(self.cst.t[:, 1:2], float(NORM_EPS)), [], [self.cst])
        P.dve(lambda e: e.memset(self.cst.t[:, 2:3], -0.5), [], [self.cst])
        P.dve(lambda e: e.memset(self.cst.t[:, 3:4], 1.0), [], [self.cst])
        P.dve(lambda e: e.memset(self.cst.t[:, 4:5], 1.0 / 512.0), [], [self.cst])
        P.dve(lambda e: e.memset(self.cst.t[:, 5:6], 1.0 / 128.0), [], [self.cst])
        self.eyeb = P.sbuf("eyeb", [128, 16, 16], BF16)
        self.eyep = P.sbuf("eyep", [16, 16], F32)
        self.eyef = P.sbuf("eyef", [128, 16, 16], F32)
        P.dma("sp", self.eyef.t[:], I["c_ident"][0:16, 0:16].unsqueeze(0).broadcast_to([128, 16, 16]), [], [self.eyef], self.eyef.sem())
        P.dma("pool", self.eyeb.t[:], I["c_ident"][0:16, 0:16].unsqueeze(0).broadcast_to([128, 16, 16]),
              [], [self.eyeb], self.eyeb.sem())
        P.dma("sp", self.eyep.t[:], I["c_ident"][0:16, 0:16], [], [self.eyep], self.eyep.sem())
        self.masks = P.sbuf("masks", [128, 8, 128], F32)
        P.dma("sp", self.masks.t[:], I["c_masks"].rearrange("m j i -> j m i"), [], [self.masks], self.masks.sem())
        self.A = Arena(P, "arena", 74 * 1024)
        self.tail_ssm = [P.sbuf(f"tail_ssm{l}", [128, 12, 3], F32, nreg=12) for l in range(DEPTH)]
        self.tail_gdn = [P.sbuf(f"tail_gdn{l}", [128, 24, 3], F32, nreg=24) for l in range(DEPTH)]
        L = DEPTH
        self.d_state = {}
        for nm in ("ret_p", "ssm_p", "gdn_p"):
            self.d_state[nm] = Buf(P, "dst_" + nm, self.o[nm], nreg=L)
        self.state_view = {
            "ret_p": lambda ap: ap.rearrange("h k v -> k h v"),
            "ssm_p": lambda ap: ap.rearrange("h n d -> n h d"),
            "gdn_p": lambda ap: ap.rearrange("h k v -> k h v"),
        }

    def state_load(self, nm, l):
        P, S, Sb = self.P, self.S, self.Sb
        db = self.d_state[nm]
        hd = {"ret_p": 4, "ssm_p": 16, "gdn_p": 8}[nm]
        if self.half == 0:
            P.dve(lambda e: e.memset(S.t[:], 0.0), [], [S])
        else:
            P.dma("sp", S.t[:].rearrange("p (h v) -> p h v", h=hd), self.state_view[nm](db.t[l]), [(db, l)], [S], S.sem())
        P.act(lambda e: e.activation(Sb.t[:], S.t[:], AF.Copy), [S], [Sb])

    def state_store(self, nm, l):
        P, S = self.P, self.S
        db = self.d_state[nm]
        hd = {"ret_p": 4, "ssm_p": 16, "gdn_p": 8}[nm]
        P.dma("sp", self.state_view[nm](db.t[l]), S.t[:].rearrange("p (h v) -> p h v", h=hd), [S], [(db, l)], S.sem(),
              is_output=True)

    def sigmoid_chain(self, tmp_ap, src_ap, reads, tmp_buf, scale_in=-1.0, bias_in=0.0):
        P = self.P
        if isinstance(bias_in, float) and bias_in == 0.0:
            P.act(lambda e: e.activation(tmp_ap, src_ap, AF.Exp, scale=scale_in), reads, [tmp_buf])
        else:
            P.act(lambda e: e.activation(tmp_ap, src_ap, AF.Exp, scale=scale_in, bias=bias_in), reads, [tmp_buf])
        P.act(lambda e: e.activation(tmp_ap, tmp_ap, AF.Ln, bias=1.0), [tmp_buf], [tmp_buf])
        P.act(lambda e: e.activation(tmp_ap, tmp_ap, AF.Exp, scale=-1.0), [tmp_buf], [tmp_buf])

    def rstd_pool(self, nst, rows, eps_col, scale_col=None):
        P, cst = self.P, self.cst
        src = nst.t[:rows, 1:2]
        if scale_col is not None:
            P.pool(lambda e: e.tensor_tensor(nst.t[:rows, 2:3], src, cst.t[:rows, scale_col:scale_col + 1], ALU.mult),
                   [nst, cst], [nst])
            src = nst.t[:rows, 2:3]
        P.pool(lambda e: e.tensor_tensor(nst.t[:rows, 2:3], src, cst.t[:rows, eps_col:eps_col + 1], ALU.add),
               [nst, cst], [nst])
        P.pool(lambda e: e.tensor_tensor(nst.t[:rows, 3:4], nst.t[:rows, 2:3], cst.t[:rows, 2:3], ALU.pow),
               [nst, cst], [nst])

    def transposes_to(self, dst_ap_fn, src_fn, n, rows, src_buf, dst_writes, bt=None):
        P, ps = self.P, self.ps
        bt = self.bank() if bt is None else bt
        pst = ps.t[:, bt, :].bitcast(BF16).rearrange("p (k n) -> p k n", k=8)
        for k in range(n):
            P.pe(lambda e, k=k: e.transpose(pst[:, k, :rows], src_fn(k), self.identb.t[:rows, :rows]),
                 [src_buf, self.identb], [(ps, bt)])
        dst_ap_fn(pst, bt)

    def retention(self, l):
        self.P.tag = "ret"
        P, I, c = self.P, self.i, self.cfg
        ps, xT, hT = self.ps, self.xT, self.hT
        w_in = I["w_in"][l]
        S, Sb, A = self.S, self.Sb, self.A
        A.reset()
        retmask = A.alloc("retmask", [4, 128], F32)
        retrow = A.alloc("retrow", [4, 128], F32)
        retcol = A.alloc("retcol", [4], F32)
        P.dma("sp", retmask.t[:], I["c_retmask"].rearrange("h j i -> j h i"), [], [retmask], retmask.sem())
        for h in range(4):
            P.dma("sp", retrow.t[:, h, :], I["c_retrow"][h, 0, :].partition_broadcast(128), [], [retrow], retrow.sem())
        P.dma("sp", retcol.t[:], I["c_retrow"][:, 1, :].rearrange("h j -> j h"), [], [retcol], retcol.sem(),
              allow_slow_non_contiguous=True)
        rope_b = [A.alloc("rope", [4, 64], F32) for _ in range(2)]
        qkr_b = [A.alloc("qkr", [2, 2, 128], BF16) for _ in range(2)]
        rt_b = [A.alloc("rt", [4, 256], F32, nreg=4) for _ in range(2)]
        qkT_b = [A.alloc("qkT", [4, 128], BF16) for _ in range(2)]
        qdT_b = [A.alloc("qdT", [128], BF16) for _ in range(4)]
        kdec_b = [A.alloc("kdec", [128], BF16) for _ in range(4)]
        vbf_b = [A.alloc("vbf", [256], BF16) for _ in range(4)]
        sgt_b = [A.alloc("sgt", [256], F32) for _ in range(4)]
        scm_b = [A.alloc("scm", [128], BF16) for _ in range(4)]
        ogt_b = [A.alloc("ogt", [256], F32) for _ in range(4)]
        ogb_b = [A.alloc("ogb", [256], BF16) for _ in range(2)]
        nst_b = [A.alloc("nst", [8], F32) for _ in range(4)]
        st6_b = [A.alloc("st6", [6], F32) for _ in range(4)]
        if self.has_sample:
            qTm = A.alloc("qTm", [16, 16], BF16)
            ktm = A.alloc("ktm", [16, 128], BF16)
            Ss_b = [A.alloc("Ss", [2, 256], F32) for _ in range(2)]
            Ssb_b = [A.alloc("Ssb", [256], BF16) for _ in range(2)]
        self.state_load("ret_p", l)
        lg = [float(np.float64(np.log1p(-np.float32(2.0) ** np.float32(-5.0 - h)).astype(np.float32))) for h in range(4)]

        def load_pair(pr):
            sA, sB, sC = self.wslot(), self.wslot(), self.wslot()
            h0, h1 = 2 * pr, 2 * pr + 1
            for i, h in enumerate((h0, h1)):
                self.wload(sA, i * 256, w_in[:, C_RQ + h * 128:C_RQ + (h + 1) * 128])
                self.wload(sA, i * 256 + 128, w_in[:, C_RK + h * 128:C_RK + (h + 1) * 128])
            for s_, h in ((sB, h0), (sC, h1)):
                self.wload(s_, 0, w_in[:, C_RV + h * 256:C_RV + (h + 1) * 256])
                self.wload(s_, 256, w_in[:, C_RG + h * 256:C_RG + (h + 1) * 256])
            return (sA, sB, sC)

        def tile_gen(i, pr, W, m, rows, col0):
            par = i % 2
            X, Y, Z = 2 + 3 * par, 3 + 3 * par, 4 + 3 * par
            sA = W[0]
            is_s = rows != 128
            t0 = (self.half * c.NTH * 128 + col0) if not is_s else c.T
            rope, rt, qkr, qkT = rope_b[par], rt_b[par], qkr_b[par], qkT_b[par]
            P.dma("sp", rope.t[:rows], I["c_rope"][t0:t0 + rows], [], [rope], rope.sem())
            for k in range(8):
                self.mm(ps.t[:rows, X, :], xT.t[:, k, col0:col0 + rows], sA.t[:, k, :], k == 0, k == 7,
                        [(xT, m), sA], [(ps, X)])
            qk5 = ps.t[:rows, X, :].rearrange("p (h a b f) -> p h a b f", h=2, a=2, b=2)
            x1, x2 = qk5[:, :, :, 0, :], qk5[:, :, :, 1, :]
            rp = rope.t[:rows].rearrange("p (a b) f -> p a b f", a=2)
            cos = rp[:, :, 0, :].unsqueeze(1).broadcast_to([rows, 2, 2, 64])
            sin = rp[:, :, 1, :].unsqueeze(1).broadcast_to([rows, 2, 2, 64])
            tv = [rt.t[:rows, j, :].rearrange("p (h a f) -> p h a f", h=2, a=2) for j in range(4)]
            P.dve(lambda e: e.tensor_tensor(tv[0], x1, cos, ALU.mult), [(ps, X), rope], [(rt, 0)])
            P.dve(lambda e: e.tensor_tensor(tv[1], x2, sin, ALU.mult), [(ps, X), rope], [(rt, 1)])
            P.dve(lambda e: e.tensor_tensor(tv[2], x1, sin, ALU.mult), [(ps, X), rope], [(rt, 2)])
            P.dve(lambda e: e.tensor_tensor(tv[3], x2, cos, ALU.mult), [(ps, X), rope], [(rt, 3)])
            P.dve(lambda e: e.tensor_tensor(qkr.t[:rows, :, :, 0:64], tv[0], tv[1], ALU.subtract), [(rt, 0), (rt, 1)], [qkr])
            P.dve(lambda e: e.tensor_tensor(qkr.t[:rows, :, :, 64:128], tv[2], tv[3], ALU.add), [(rt, 2), (rt, 3)], [qkr])
            yield

            def evac(pst, bt):
                P.act(lambda e: e.activation(qkT.t[:, :, :rows], pst[:, 0:4, :rows], AF.Copy), [(ps, bt)], [qkT])

            self.transposes_to(evac, lambda k: qkr.t[:rows, k // 2, k % 2, :], 4, rows, qkr, None, bt=Y)
            yield
            for hh in range(2):
                yield from head_gen(par, (X, Y, Z), pr, W, m, rows, col0, hh, qkr, qkT)

        def head_gen(par, banks, pr, W, m, rows, col0, hh, qkr, qkT):
            X, Y, Z = banks
            h = 2 * pr + hh
            sV = W[1 + hh]
            is_s = rows != 128
            gam = float(np.exp(lg[h]))
            bi = 2 * par + hh
            vbf, sgt, qdT, kdec, scm, ogt, nst, st6 = (vbf_b[bi], sgt_b[bi], qdT_b[bi], kdec_b[bi], scm_b[bi], ogt_b[bi],
                                                      nst_b[bi], st6_b[bi])
            ogb = ogb_b[par]
            for k in range(8):
                self.mm(ps.t[:rows, X, :], xT.t[:, k, col0:col0 + rows], sV.t[:, k, :], k == 0, k == 7, [(xT, m), sV], [(ps, X)])
            P.act(lambda e: e.activation(vbf.t[:rows, :], ps.t[:rows, X, 0:256], AF.Copy), [(ps, X)], [vbf])
            self.sigmoid_chain(sgt.t[:rows, :], ps.t[:rows, X, 256:512], [(ps, X)], sgt)
            yield
            P.dve(lambda e: e.tensor_tensor(sgt.t[:rows, :], ps.t[:rows, X, 256:512], sgt.t[:rows, :], ALU.mult), [(ps, X), sgt], [sgt])
            bo = Z if not is_s else 0
            Sr = (S, list(range(4 * h, 4 * h + 4)))
            Sbr = (Sb, list(range(4 * h, 4 * h + 4)))
            if not is_s:
                P.dve(lambda e: e.tensor_tensor(qdT.t[:, :], qkT.t[:, 2 * hh, :], retrow.t[:, h, :], ALU.mult), [qkT, retrow], [qdT])
                P.dve(lambda e: e.tensor_scalar(kdec.t[:, :], qkr.t[:, hh, 1, :], retcol.t[:, h:h + 1], None, ALU.mult), [qkr, retcol], [kdec])
                self.mm(ps.t[:, Y, 0:128], qkT.t[:, 2 * hh + 1, :], qkT.t[:, 2 * hh, :], True, True, [qkT], [(ps, Y)])
                yield
                P.dve(lambda e: e.tensor_tensor(scm.t[:, :], ps.t[:, Y, 0:128], retmask.t[:, h, :], ALU.mult), [(ps, Y), retmask], [scm])
                yield
                self.mm(ps.t[:, bo, 0:256], scm.t[:, :], vbf.t[:, :], True, False, [scm, vbf], [(ps, bo)])
                self.mm(ps.t[:, bo, 0:256], qdT.t[:, :], Sb.t[:, h * 256:(h + 1) * 256], False, True, [qdT, Sbr], [(ps, bo)])
                self.mm(ps.t[:, Y, 0:256], kdec.t[:, :], vbf.t[:, :], True, True, [kdec, vbf], [(ps, Y)])
                yield
                P.dve(lambda e: e.scalar_tensor_tensor(S.t[:, h * 256:(h + 1) * 256], S.t[:, h * 256:(h + 1) * 256],
                                                       float(np.exp(lg[h] * 128)), ps.t[:, Y, 0:256], ALU.mult, ALU.add), [Sr, (ps, Y)], [Sr])
                P.act(lambda e: e.activation(Sb.t[:, h * 256:(h + 1) * 256], S.t[:, h * 256:(h + 1) * 256], AF.Copy), [Sr], [Sbr])
            else:
                P.dve(lambda e: e.tensor_tensor(qTm.t[:], qkT.t[:, 2 * hh, 0:16].unsqueeze(1).broadcast_to([128, 16, 16]),
                                                self.eyeb.t[:], ALU.mult), [qkT, self.eyeb], [qTm])
                P.dve(lambda e: e.tensor_tensor(ktm.t[0:16], qkr.t[0:16, hh, 1, :].unsqueeze(1).broadcast_to([16, 16, 128]),
                                                self.eyep.t[:, :].unsqueeze(2).broadcast_to([16, 16, 128]), ALU.mult), [qkr, self.eyep], [ktm])
                for s_ in range(NS):
                    self.ret_sample(l, h, s_, gam, vbf, bo, Ss_b[s_ % 2], Ssb_b[s_ % 2], qTm, ktm, Y)
            P.dve(lambda e: e.bn_stats(st6.t[:rows, :], ps.t[:rows, bo, 0:256]), [(ps, bo)], [st6])
            P.dve(lambda e: e.bn_aggr(nst.t[:rows, 0:2], st6.t[:rows, :]), [st6], [nst])
            yield
            self.rstd_pool(nst, rows, 0)
            yield
            P.dve(lambda e: e.tensor_scalar(ogt.t[:rows, :], ps.t[:rows, bo, 0:256], nst.t[:rows, 0:1], nst.t[:rows, 3:4],
                                            ALU.subtract, ALU.mult), [(ps, bo), nst], [ogt])
            P.dve(lambda e: e.tensor_tensor(ogb.t[:rows, :], ogt.t[:rows, :], sgt.t[:rows, :], ALU.mult), [ogt, sgt], [ogb])
            yield

            def evac(pst, bt):
                P.act(lambda e: e.activation(hT.t[:, 2 * h:2 * h + 2, col0:col0 + rows], pst[:, 0:2, :rows], AF.Copy),
                      [(ps, bt)], [(hT, [2 * h, 2 * h + 1])])

            self.transposes_to(evac, lambda k: ogb.t[:rows, k * 128:(k + 1) * 128], 2, rows, ogb, None, bt=X)
            yield

        nxt = load_pair(0)
        for pr in range(2):
            W = nxt
            if pr == 0:
                nxt = load_pair(1)
            self.run_pipelined((tile_gen(i, pr, W, m, rows, col0) for i, (m, rows, col0) in enumerate(self.tiles())), 2)
        self.state_store("ret_p", l)

    def ret_sample(self, l, h, s_, gam, vbf, bo, Ss, Ssb, qTm, ktm, bd):
        P, I, ps = self.P, self.i, self.ps
        P.dma("sp", Ss.t[:, 0, :], I["st_ret"][l, s_, h], [], [Ss], Ss.sem())
        self.mm(ps.t[:, bd, 0:256], ktm.t[0:16, s_, :], vbf.t[0:16, :], True, True, [ktm, vbf], [(ps, bd)])
        P.dve(lambda e: e.scalar_tensor_tensor(Ss.t[:, 1, :], Ss.t[:, 0, :], gam, ps.t[:, bd, 0:256], ALU.mult, ALU.add),
              [Ss, (ps, bd)], [Ss])
        P.dma("sp", self.o["ret_s"][l, s_, h], Ss.t[:, 1, :], [Ss], [], Ss.sem(), is_output=True)
        P.act(lambda e: e.activation(Ssb.t[:, :], Ss.t[:, 1, :], AF.Copy), [Ss], [Ssb])
        self.mm(ps.t[0:16, bo, 0:256], qTm.t[:, s_, :], Ssb.t[:, :], s_ == 0, s_ == NS - 1, [qTm, Ssb], [(ps, bo)])

    def finale(self, l, w_out, mcol):
        self.P.tag = "finale"
        P, I = self.P, self.i
        ps, xT, hT, xa = self.ps, self.xT, self.hT, self.xa
        w_in, w_o = I["w_in"][l], I["w_o"][l]
        A = self.A
        A.reset()
        gT_b = [A.alloc("gT", [8, 128], BF16) for _ in range(2)]
        gtok_b = [A.alloc("gtok", [D], BF16) for _ in range(2)]
        Wout = (self.wslot(), self.wslot())
        Wm = (self.wslot(), self.wslot())
        Wo = (self.wslot(), self.wslot())
        for o in range(2):
            self.wload(Wout[o], 0, w_out[:, o * 512:(o + 1) * 512])
            self.wload(Wm[o], 0, w_in[:, mcol + o * 512:mcol + (o + 1) * 512])
            self.wload(Wo[o], 0, w_o[:, o * 512:(o + 1) * 512])

        def tile_gen(i, m, rows, col0):
            par = i % 2
            by, bm = 4 * par, 4 * par + 2
            tA, gtok, gT = self.tmpA[par], gtok_b[par], gT_b[par]
            for o in range(2):
                for k in range(8):
                    self.mm(ps.t[:rows, bm + o, :], xT.t[:, k, col0:col0 + rows], Wm[o].t[:, k, :], k == 0, k == 7,
                            [(xT, m), Wm[o]], [(ps, bm + o)])
            for o in range(2):
                for k in range(8):
                    self.mm(ps.t[:rows, by + o, :], hT.t[:, k, col0:col0 + rows], Wout[o].t[:, k, :], k == 0, k == 7,
                            [(hT, k), Wout[o]], [(ps, by + o)])
            pm = ps.t[:rows, bm:bm + 2, :].rearrange("p b n -> p (b n)")
            py = ps.t[:rows, by:by + 2, :].rearrange("p b n -> p (b n)")
            self.sigmoid_chain(tA.t[:rows, :], pm, [(ps, bm), (ps, bm + 1)], tA)
            yield
            P.dve(lambda e: e.tensor_tensor(gtok.t[:rows, :], py, tA.t[:rows, :], ALU.mult), [(ps, by), (ps, by + 1), tA], [gtok])
            yield

            def evac(pst, bt):
                P.act(lambda e: e.activation(gT.t[:, :, :rows], pst[:, :, :rows], AF.Copy), [(ps, bt)], [gT])

            self.transposes_to(evac, lambda k: gtok.t[:rows, k * 128:(k + 1) * 128], 8, rows, gtok, None, bt=bm)
            yield
            for o in range(2):
                bo = by + o
                for k in range(8):
                    self.mm(ps.t[:rows, bo, :], gT.t[:, k, :rows], Wo[o].t[:, k, :], k == 0, k == 7, [gT, Wo[o]], [(ps, bo)])
            yield
            for o in range(2):
                bo = by + o
                P.dve(lambda e, o=o, bo=bo: e.tensor_tensor(xa.t[:rows, m, o * 512:(o + 1) * 512],
                                                            xa.t[:rows, m, o * 512:(o + 1) * 512], ps.t[:rows, bo, :], ALU.add),
                      [(ps, bo), (xa, 2 * m + o)], [(xa, 2 * m + o)])

        self.run_pipelined((tile_gen(i, m, rows, col0) for i, (m, rows, col0) in enumerate(self.tiles())), 2)

    def colvecs(self, dst_ap, tk, r, c, dst_buf):
        P, ps = self.P, self.ps
        b = self.bank()
        for cc in range(c):
            P.pe(lambda e, cc=cc: e.transpose(ps.t[:, b, cc * r:(cc + 1) * r], tk.t[:r, cc * 128:(cc + 1) * 128],
                                              self.identf.t[:r, :r]), [tk, self.identf], [(ps, b)])
        P.act(lambda e: e.activation(dst_ap, ps.t[:, b, 0:c * r], AF.Copy), [(ps, b)], [dst_buf])

    def ssd(self, l):
        self.P.tag = "ssd"
        P, I, c = self.P, self.i, self.cfg
        ps, xT, hT, S, Sb, A = self.ps, self.xT, self.hT, self.S, self.Sb, self.A
        w_in = I["w_in"][l]
        TP = c.NTH * 128
        assert TP <= 512
        hs = self.has_sample
        masks = self.masks
        U, ONES, MB = masks.t[:, 0, :], masks.t[:, 1, :], masks.t[:, 2, :]
        A.reset()
        cwb = A.alloc("cwb", [12, 5], F32)
        negb = A.alloc("negb", [12], F32)
        nwT = A.alloc("nwT", [8], F32)
        dtb = A.alloc("dtb", [16], F32)
        Ab = A.alloc("Ab", [16], F32)
        Db = A.alloc("Db", [16], F32)
        wdt = A.alloc("wdt", [8, 16], BF16)
        off0 = A.off
        tk = A.alloc("tk", [1536], F32)
        tk2 = A.alloc("tk2", [1024], F32)
        P.dma("sp", tk.t[0:4, :], I["ssm_conv_w"][l], [], [tk], tk.sem())
        P.dma("sp", tk.t[4:5, :], I["ssm_conv_b"][l].unsqueeze(0), [], [tk], tk.sem())
        self.colvecs(cwb.t[:].rearrange("p c j -> p (c j)"), tk, 5, 12, cwb)
        P.act(lambda e: e.activation(negb.t[:, :], cwb.t[:, :, 4], AF.Copy, scale=-1.0), [cwb], [negb])
        P.dma("sp", tk2.t[0:1, :], I["ssm_norm_w"][l].unsqueeze(0), [], [tk2], tk2.sem())
        self.colvecs(nwT.t[:, :], tk2, 1, 8, nwT)
        P.dma("sp", dtb.t[:], I["ssm_dt_bias"][l].partition_broadcast(128), [], [dtb], dtb.sem())
        P.dma("sp", Ab.t[:], I["ssm_a_log"][l].partition_broadcast(128), [], [Ab], Ab.sem())
        P.dma("sp", Db.t[:], I["ssm_d"][l].partition_broadcast(128), [], [Db], Db.sem())
        P.act(lambda e: e.activation(Ab.t[:], Ab.t[:], AF.Exp), [Ab], [Ab])
        P.act(lambda e: e.activation(Ab.t[:], Ab.t[:], AF.Copy, scale=-1.0), [Ab], [Ab])
        P.dma("pool", wdt.t[:], w_in[:, C_MDT:C_MDT + 16].rearrange("(k p) c -> p k c", p=128), [], [wdt], wdt.sem())
        A.reset(off0)
        xbc = A.alloc("xbc", [12, TP], BF16, nreg=12)
        xraw_b = [A.alloc("xraw", [3 + TP], F32) for _ in range(2)]
        acc_b = [A.alloc("acc", [TP], F32) for _ in range(2)]
        sgm_b = [A.alloc("sgm", [TP], F32) for _ in range(2)]
        f1 = A.alloc("f1", [1024], F32)
        f2 = A.alloc("f2", [1024], F32)
        f3 = A.alloc("f3", [1024], F32)
        xs_sb = A.alloc("xs_sb", [1024], BF16)
        v = A.alloc("v", [1024], BF16)
        vdec = A.alloc("vdec", [1024], BF16)
        Mh_b = [A.alloc("Mh", [8, 128], BF16) for _ in range(2)]
        Bd = A.alloc("Bd", [8, 128], F32)
        Btok = A.alloc("Btok", [2, 128], BF16)
        ogb = A.alloc("ogb", [1024], BF16)
        sm = A.alloc("sm", [8, 16], F32)
        cumT = A.alloc("cumT", [128], F32)
        nst = A.alloc("nst", [8], F32)
        stg = A.alloc("stg", [512], F32)
        if hs:
            xbcS = A.alloc("xbcS", [12, 16], BF16)
            stc_b = [A.alloc("stc", [3, 128], F32) for _ in range(2)]
            xrs_b = [A.alloc("xrs", [4, 16], F32) for _ in range(2)]
            accS_b = [A.alloc("accS", [16], F32) for _ in range(2)]
            sgS_b = [A.alloc("sgS", [16], F32) for _ in range(2)]
            Eall = A.alloc("Eall", [16, 16], F32)
            Bde = A.alloc("Bde", [16, 16], F32)
            Bm = A.alloc("Bm", [256], BF16)
            Cm = A.alloc("Cm", [2, 16], BF16)
        tail = self.tail_ssm[l]
        if self.half == 0:
            P.dve(lambda e: e.memset(tail.t[:], 0.0), [], [tail])
        self.state_load("ssm_p", l)
        Wx = [self.wslot() for _ in range(3)]
        for i in range(3):
            self.wload(Wx[i], 0, w_in[:, C_MXBC + i * 512:C_MXBC + (i + 1) * 512])
        Wz = (self.wslot(), self.wslot())
        for o in range(2):
            self.wload(Wz[o], 0, w_in[:, C_MZ + o * 512:C_MZ + (o + 1) * 512])

        def conv_chunk(cc):
            slot = Wx[cc // 4]
            cs = (cc % 4) * 128
            par = cc % 2
            acc, sgm = acc_b[par], sgm_b[par]
            b = 2 + par
            for k in range(8):
                self.mm(ps.t[:, b, :TP], slot.t[:, k, cs:cs + 128], xT.t[:, k, 0:TP], k == 0, k == 7,
                        [slot, (xT, list(range(c.NTH)))], [(ps, b)])
            xraw = xraw_b[par]
            P.act(lambda e: e.activation(xraw.t[:, 0:3], tail.t[:, cc, :], AF.Copy), [(tail, cc)], [xraw])
            P.act(lambda e: e.activation(xraw.t[:, 3:3 + TP], ps.t[:, b, :TP], AF.Copy), [(ps, b)], [xraw])
            P.act(lambda e: e.activation(tail.t[:, cc, :], xraw.t[:, TP:TP + 3], AF.Copy), [xraw], [(tail, cc)])
            yield
            P.dve(lambda e: e.tensor_scalar(acc.t[:, :], xraw.t[:, 0:TP], cwb.t[:, cc, 0:1], None, ALU.mult), [xraw, cwb], [acc])
            for j in range(1, 4):
                P.dve(lambda e, j=j: e.scalar_tensor_tensor(acc.t[:, :], xraw.t[:, j:j + TP], cwb.t[:, cc, j:j + 1], acc.t[:, :],
                                                            ALU.mult, ALU.add), [xraw, cwb, acc], [acc])
            yield
            self.sigmoid_chain(sgm.t[:, :], acc.t[:, :], [acc, negb], sgm, scale_in=-1.0, bias_in=negb.t[:, cc:cc + 1])
            yield
            P.dve(lambda e: e.scalar_tensor_tensor(xbc.t[:, cc, :], acc.t[:, :], cwb.t[:, cc, 4:5], sgm.t[:, :], ALU.add, ALU.mult),
                  [acc, cwb, sgm], [(xbc, cc)])
            if hs:
                yield
                b2 = 4 + par
                for k in range(8):
                    self.mm(ps.t[:, b2, 0:16], slot.t[:, k, cs:cs + 128], xT.t[:, k, TP:TP + 16], k == 0, k == 7,
                            [slot, (xT, c.NTH)], [(ps, b2)])
                stc = stc_b[par]
                P.dma("sp", stc.t[0:16, :, :], I["st_ssm_conv"][l, :, :, cc * 128:(cc + 1) * 128], [], [stc], stc.sem())
                for j in range(3):
                    P.pe(lambda e, j=j: e.transpose(ps.t[:, b2, 16 + 16 * j:32 + 16 * j], stc.t[0:16, j, :], self.identf.t[0:16, 0:16]),
                         [stc, self.identf], [(ps, b2)])
                xrs = xrs_b[par]
                P.act(lambda e: e.activation(xrs.t[:, 0:3, :], ps.t[:, b2, 16:64].rearrange("p (j s) -> p j s", j=3), AF.Copy),
                      [(ps, b2)], [xrs])
                P.act(lambda e: e.activation(xrs.t[:, 3, :], ps.t[:, b2, 0:16], AF.Copy), [(ps, b2)], [xrs])
                accS, sgS = accS_b[par], sgS_b[par]
                P.dve(lambda e: e.tensor_scalar(accS.t[:, :], xrs.t[:, 0, :], cwb.t[:, cc, 0:1], None, ALU.mult), [xrs, cwb], [accS])
                for j in range(1, 4):
                    P.dve(lambda e, j=j: e.scalar_tensor_tensor(accS.t[:, :], xrs.t[:, j, :], cwb.t[:, cc, j:j + 1], accS.t[:, :],
                                                                ALU.mult, ALU.add), [xrs, cwb, accS], [accS])
                yield
                self.sigmoid_chain(sgS.t[:, :], accS.t[:, :], [accS, negb], sgS, scale_in=-1.0, bias_in=negb.t[:, cc:cc + 1])
                yield
                P.dve(lambda e: e.scalar_tensor_tensor(xbcS.t[:, cc, :], accS.t[:, :], cwb.t[:, cc, 4:5], sgS.t[:, :], ALU.add, ALU.mult),
                      [accS, cwb, sgS], [xbcS])

        self.run_pipelined((conv_chunk(cc) for cc in range(12)), 2)
        def conv_rows(c0, n, dst_fn):
            for i in range(3):
                b = self.bank()
                for k in range(8):
                    self.mm(ps.t[:n, b, :], xT.t[:, k, c0:c0 + n], Wx[i].t[:, k, :], k == 0, k == 7,
                            [Wx[i], (xT, list(range(self.NTT)))], [(ps, b)])
                P.act(lambda e, b=b: e.activation(stg.t[:n, :], ps.t[:n, b, :], AF.Copy), [(ps, b)], [stg])
                P.dma("sp", dst_fn(i), stg.t[:n, :], [stg], [], stg.sem(), is_output=True)

        if self.half == c.NH - 1:
            conv_rows(TP - 3, 3, lambda i: self.o["ssm_conv_p"][l, :, i * 512:(i + 1) * 512])
        if hs:
            conv_rows(TP, 16, lambda i: self.o["ssm_conv_s"][l, :, 2, i * 512:(i + 1) * 512])
            P.dma("sp", self.o["ssm_conv_s"][l, :, 0:2, :], I["st_ssm_conv"][l, :, 1:3, :], [], [], stg.sem(), is_output=True)

        def tile_body(m, rows, col0):
            is_s = rows != 128
            src = xbcS if is_s else xbc
            sc0 = 0 if is_s else col0
            DT, LA, CUM, ETOK, ELAST, DECL, T16 = [sm.t[:rows, i, :] for i in range(7)]
            bdt = 6
            for k in range(8):
                self.mm(ps.t[:rows, bdt, 0:16], xT.t[:, k, col0:col0 + rows], wdt.t[:, k, :], k == 0, k == 7, [(xT, m), wdt], [(ps, bdt)])
            P.dve(lambda e: e.tensor_tensor(T16, ps.t[:rows, bdt, 0:16], dtb.t[:rows, :], ALU.add), [(ps, bdt), dtb], [sm])
            P.act(lambda e: e.activation(T16, T16, AF.Exp), [sm], [sm])
            P.act(lambda e: e.activation(DT, T16, AF.Ln, bias=1.0), [sm], [sm])
            P.dve(lambda e: e.tensor_tensor(LA, DT, Ab.t[:rows, :], ALU.mult), [sm, Ab], [sm])
            def evac_xs(pst, bt):
                P.act(lambda e: e.activation(xs_sb.t[:rows, :], pst[:rows, :, :].rearrange("p k n -> p (k n)"), AF.Copy), [(ps, bt)], [xs_sb])
            self.transposes_T(evac_xs, lambda k: src.t[:, k, sc0:sc0 + rows], 8, rows, src, bt=7)
            if is_s and l == 0 and self.cfg.dbg.get("ssd_dump"):
                dx = self.nc.dram_tensor("dbg_xs", [16, 1024], F32, kind="ExternalOutput").ap()
                dd = self.nc.dram_tensor("dbg_dt", [16, 16], F32, kind="ExternalOutput").ap()
                P.dma("pool", dx, xs_sb.t[0:16, :], [xs_sb], [], xs_sb.sem(), is_output=True)
                P.dma("sp", dd, sm.t[0:16, 0, :], [sm], [], sm.sem(), is_output=True)
            dt_b = DT.unsqueeze(2).broadcast_to([rows, 16, 64])
            P.dve(lambda e: e.tensor_tensor(v.t[:rows, :].rearrange("p (h d) -> p h d", h=16),
                                            xs_sb.t[:rows, :].rearrange("p (h d) -> p h d", h=16), dt_b, ALU.mult), [xs_sb, sm], [v])
            def evac_b(pst, bt):
                P.act(lambda e: e.activation(Btok.t[:rows, :, :], pst[:rows, 0:2, :], AF.Copy), [(ps, bt)], [Btok])
            self.transposes_T(evac_b, lambda k: src.t[:, 8 + k, sc0:sc0 + rows], 2, rows, src, bt=7)
            if not is_s:
                bc = 6
                self.mm(ps.t[:, bc, 32:48], U, LA, True, True, [masks, sm], [(ps, bc)])
                self.mm(ps.t[:, bc, 48:64], ONES, LA, True, True, [masks, sm], [(ps, bc)])
                self.mm(ps.t[0:16, bc, 64:192], LA, U, True, True, [masks, sm], [(ps, bc)])
                P.act(lambda e: e.activation(CUM, ps.t[:, bc, 32:48], AF.Copy), [(ps, bc)], [sm])
                P.act(lambda e: e.activation(ETOK, ps.t[:, bc, 32:48], AF.Exp), [(ps, bc)], [sm])
                P.act(lambda e: e.activation(ELAST, ps.t[:, bc, 48:64], AF.Exp), [(ps, bc)], [sm])
                P.dve(lambda e: e.tensor_tensor(DECL, ps.t[:, bc, 48:64], CUM, ALU.subtract), [(ps, bc), sm], [sm])
                P.act(lambda e: e.activation(DECL, DECL, AF.Exp), [sm], [sm])
                P.act(lambda e: e.activation(cumT.t[0:16, :], ps.t[0:16, bc, 64:192], AF.Copy), [(ps, bc)], [cumT])
                P.dve(lambda e: e.tensor_tensor(vdec.t[:, :].rearrange("p (h d) -> p h d", h=16),
                                                v.t[:, :].rearrange("p (h d) -> p h d", h=16),
                                                DECL.unsqueeze(2).broadcast_to([128, 16, 64]), ALU.mult), [v, sm], [vdec])
                bsc = 6
                for g in range(2):
                    self.mm(ps.t[:, bsc, 256 + g * 128:256 + (g + 1) * 128], xbc.t[:, 8 + g, col0:col0 + 128], xbc.t[:, 10 + g, col0:col0 + 128],
                            True, True, [xbc], [(ps, bsc)])
                po, pcs, pds = 0, 4, 2
                for g in range(2):
                    self.mm(ps.t[:, pcs + g, :], xbc.t[:, 10 + g, col0:col0 + 128], Sb.t[:, g * 512:(g + 1) * 512], True, True,
                            [xbc, (Sb, list(range(8 * g, 8 * g + 8)))], [(ps, pcs + g)])
                P.dve(lambda e: e.tensor_tensor(f2.t[:, :].rearrange("p (h d) -> p h d", h=16),
                                                ps.t[:, pcs:pcs + 2, :].rearrange("p b (h d) -> p (b h) d", h=8),
                                                ETOK.unsqueeze(2).broadcast_to([128, 16, 64]), ALU.mult), [(ps, pcs), (ps, pcs + 1), sm], [f2])
                for g in range(2):
                    P.dve(lambda e, g=g: e.tensor_tensor(Bd.t[0:16, :, :], cumT.t[0:16, :].unsqueeze(1).broadcast_to([16, 8, 128]),
                                                         self.eyep.t[:, 8 * g:8 * g + 8].unsqueeze(2).broadcast_to([16, 8, 128]), ALU.mult),
                          [cumT, self.eyep], [Bd])
                    pa = 2
                    for hf in range(2):
                        self.mm(ps.t[:, pa + hf, :], ONES[0:16, :], Bd.t[0:16, 4 * hf:4 * hf + 4, :].rearrange("p h i -> p (h i)"),
                                True, False, [masks, Bd], [(ps, pa + hf)])
                        for hq in range(4):
                            self.mm(ps.t[:, pa + hf, hq * 128:(hq + 1) * 128], self.identf.t[:, :], MB, False, hq == 3,
                                    [masks, self.identf], [(ps, pa + hf)])
                    pav = ps.t[:, pa:pa + 2, :].rearrange("p b (h i) -> p (b h) i", h=4)
                    P.dve(lambda e, g=g, pav=pav: e.tensor_tensor(f1.t[:, :].rearrange("p (h i) -> p h i", h=8), pav,
                                                                   CUM[:, 8 * g:8 * g + 8].unsqueeze(2).broadcast_to([128, 8, 128]),
                                                                   ALU.subtract), [(ps, pa), (ps, pa + 1), sm], [f1])
                    P.act(lambda e: e.activation(f1.t[:, :], f1.t[:, :], AF.Exp), [f1], [f1])
                    Mh = self.rot("Mh", Mh_b)
                    P.dve(lambda e, g=g, Mh=Mh: e.tensor_tensor(Mh.t[:, :, :], f1.t[:, :].rearrange("p (h i) -> p h i", h=8),
                                                                ps.t[:, bsc, 256 + g * 128:256 + (g + 1) * 128].unsqueeze(1).broadcast_to([128, 8, 128]),
                                                                ALU.mult), [f1, (ps, bsc)], [Mh])
                    for h in range(8):
                        hg = 8 * g + h
                        self.mm(ps.t[:, po + g, h * 64:(h + 1) * 64], Mh.t[:, h, :], v.t[:, hg * 64:(hg + 1) * 64], True, True,
                                [Mh, v], [(ps, po + g)])
                P.dve(lambda e: e.tensor_tensor(f2.t[:, :], f2.t[:, :], ps.t[:, po:po + 2, :].rearrange("p b n -> p (b n)"), ALU.add),
                      [f2, (ps, po), (ps, po + 1)], [f2])
                for g in range(2):
                    self.mm(ps.t[:, pds + g, :], Btok.t[:, g, :], vdec.t[:, g * 512:(g + 1) * 512], True, True, [Btok, vdec], [(ps, pds + g)])
                P.dve(lambda e: e.tensor_tensor(f1.t[:, :].rearrange("p (h d) -> p h d", h=16), S.t[:, :].rearrange("p (h d) -> p h d", h=16),
                                                ELAST.unsqueeze(2).broadcast_to([128, 16, 64]), ALU.mult), [S, sm], [f1])
                P.dve(lambda e: e.tensor_tensor(S.t[:, :], f1.t[:, :], ps.t[:, pds:pds + 2, :].rearrange("p b n -> p (b n)"), ALU.add),
                      [f1, (ps, pds), (ps, pds + 1)], [S])
                P.act(lambda e: e.activation(Sb.t[:, :], S.t[:, :], AF.Copy), [S], [Sb])
            else:
                ELA = ELAST
                P.act(lambda e: e.activation(ELA, LA, AF.Exp), [sm], [sm])
                P.dve(lambda e: e.tensor_tensor(Bde.t[0:16, :, :], ELA.unsqueeze(1).broadcast_to([16, 16, 16]),
                                                self.eyep.t[:, :].unsqueeze(2).broadcast_to([16, 16, 16]), ALU.mult), [sm, self.eyep], [Bde])
                be = 6
                self.mm(ps.t[:, be, 0:256], ONES[0:16, :], Bde.t[0:16, :, :].rearrange("p s h -> p (s h)"), True, True, [masks, Bde], [(ps, be)])
                P.act(lambda e: e.activation(Eall.t[:, :, :].rearrange("p s h -> p (s h)"), ps.t[:, be, 0:256], AF.Copy), [(ps, be)], [Eall])
                Ss_b = [f2, f3]
                for s_ in range(NS):
                    Ss = Ss_b[s_ % 2]
                    P.dma("sp", Ss.t[:, :].rearrange("p (h d) -> p h d", h=16), I["st_ssm"][l, s_].rearrange("h n d -> n h d"), [], [Ss], Ss.sem())
                    P.dve(lambda e, s_=s_: e.tensor_scalar(Bm.t[0:16, :], Btok.t[0:16, :, :].rearrange("p g n -> p (g n)"),
                                                           self.eyep.t[:, s_:s_ + 1], None, ALU.mult), [Btok, self.eyep], [Bm])
                    P.dve(lambda e, s_=s_: e.tensor_tensor(Cm.t[:, :, :], xbcS.t[:, 10:12, :],
                                                           self.eyeb.t[:, s_, :].unsqueeze(1).broadcast_to([128, 2, 16]), ALU.mult),
                          [xbcS, self.eyeb], [Cm])
                    pds = 2
                    for g in range(2):
                        self.mm(ps.t[:, pds + g, :], Bm.t[0:16, g * 128:(g + 1) * 128], v.t[0:16, g * 512:(g + 1) * 512], True, True,
                                [Bm, v], [(ps, pds + g)])
                    P.dve(lambda e, s_=s_, Ss=Ss: e.tensor_tensor(f1.t[:, :].rearrange("p (h d) -> p h d", h=16),
                                                                  Ss.t[:, :].rearrange("p (h d) -> p h d", h=16),
                                                                  Eall.t[:, s_, :].unsqueeze(2).broadcast_to([128, 16, 64]), ALU.mult),
                          [Ss, Eall], [f1])
                    P.dve(lambda e, Ss=Ss, pds=pds: e.tensor_tensor(Ss.t[:, :], f1.t[:, :], ps.t[:, pds:pds + 2, :].rearrange("p b n -> p (b n)"),
                                                                    ALU.add), [f1, (ps, pds), (ps, pds + 1)], [Ss])
                    P.dma("sp", self.o["ssm_s"][l, s_].rearrange("h n d -> n h d"), Ss.t[:, :].rearrange("p (h d) -> p h d", h=16),
                          [Ss], [], Ss.sem(), is_output=True)
                    P.act(lambda e, Ss=Ss: e.activation(vdec.t[:, :], Ss.t[:, :], AF.Copy), [Ss], [vdec])
                    for g in range(2):
                        self.mm(ps.t[0:16, g, :], Cm.t[:, g, :], vdec.t[:, g * 512:(g + 1) * 512], s_ == 0, s_ == NS - 1,
                                [Cm, vdec], [(ps, g)])
                P.act(lambda e: e.activation(f2.t[0:16, :], ps.t[0:16, 0:2, :].rearrange("p b n -> p (b n)"), AF.Copy), [(ps, 0), (ps, 1)], [f2])
            P.dve(lambda e: e.tensor_tensor(f3.t[:rows, :].rearrange("p (h d) -> p h d", h=16),
                                            xs_sb.t[:rows, :].rearrange("p (h d) -> p h d", h=16),
                                            Db.t[:rows, :].unsqueeze(2).broadcast_to([rows, 16, 64]), ALU.mult), [xs_sb, Db], [f3])
            P.dve(lambda e: e.tensor_tensor(f2.t[:rows, :], f2.t[:rows, :], f3.t[:rows, :], ALU.add), [f2, f3], [f2])
            pz = 4
            for o in range(2):
                for k in range(8):
                    self.mm(ps.t[:rows, pz + o, :], xT.t[:, k, col0:col0 + rows], Wz[o].t[:, k, :], k == 0, k == 7, [(xT, m), Wz[o]], [(ps, pz + o)])
            pzv = ps.t[:rows, pz:pz + 2, :].rearrange("p b n -> p (b n)")
            self.sigmoid_chain(f3.t[:rows, :], pzv, [(ps, pz), (ps, pz + 1)], f3)
            P.dve(lambda e: e.tensor_tensor(f3.t[:rows, :], pzv, f3.t[:rows, :], ALU.mult), [(ps, pz), (ps, pz + 1), f3], [f3])
            P.dve(lambda e: e.tensor_tensor(f2.t[:rows, :], f2.t[:rows, :], f3.t[:rows, :], ALU.mult), [f2, f3], [f2])
            for g in range(2):
                P.act(lambda e, g=g: e.activation(f3.t[:rows, g * 512:(g + 1) * 512], f2.t[:rows, g * 512:(g + 1) * 512], AF.Square,
                                                  accum_out=nst.t[:rows, 4 + g:5 + g]), [f2], [f3, nst])
            P.pool(lambda e: e.tensor_tensor(nst.t[:rows, 0:2], nst.t[:rows, 4:6], self.cst.t[:rows, 4:5].broadcast_to([rows, 2]), ALU.mult),
                   [nst, self.cst], [nst])
            P.pool(lambda e: e.tensor_tensor(nst.t[:rows, 0:2], nst.t[:rows, 0:2], self.cst.t[:rows, 1:2].broadcast_to([rows, 2]), ALU.add),
                   [nst, self.cst], [nst])
            P.pool(lambda e: e.tensor_tensor(nst.t[:rows, 2:4], nst.t[:rows, 0:2], self.cst.t[:rows, 2:3].broadcast_to([rows, 2]), ALU.pow),
                   [nst, self.cst], [nst])
            for g in range(2):
                P.dve(lambda e, g=g: e.tensor_scalar(ogb.t[:rows, g * 512:(g + 1) * 512], f2.t[:rows, g * 512:(g + 1) * 512],
                                                     nst.t[:rows, 2 + g:3 + g], None, ALU.mult), [f2, nst], [ogb])

            def evac(pst, bt):
                P.dve(lambda e: e.tensor_tensor(hT.t[:, 0:8, col0:col0 + rows], pst[:, :, :rows],
                                                nwT.t[:, :].unsqueeze(2).broadcast_to([128, 8, rows]), ALU.mult),
                      [(ps, bt), nwT], [(hT, list(range(8)))])

            self.transposes_to(evac, lambda k: ogb.t[:rows, k * 128:(k + 1) * 128], 8, rows, ogb, None, bt=7)

        for (m, rows, col0) in self.tiles():
            tile_body(m, rows, col0)
        self.state_store("ssm_p", l)

    def gdn(self, l):
        self.P.tag = "gdn"
        P, I, c = self.P, self.i, self.cfg
        ps, xT, hT, S, Sb, A = self.ps, self.xT, self.hT, self.S, self.Sb, self.A
        w_in = I["w_in"][l]
        TP = c.NTH * 128
        hs = self.has_sample
        masks = self.masks
        ONES, MBI, MBS = masks.t[:, 1, :], masks.t[:, 4, :], masks.t[:, 5, :]
        U64 = masks.t[:, 3, :]
        A.reset()
        cw = A.alloc("cw", [24, 4], F32)
        dtb = A.alloc("dtb", [8], F32)
        Ab = A.alloc("Ab", [8], F32)
        nwb = A.alloc("nwb", [128], F32)
        wab = A.alloc("wab", [8, 16], BF16)
        off0 = A.off
        tk = A.alloc("tk", [3072], F32)
        P.dma("sp", tk.t[0:4, :], I["gdn_conv_w"][l], [], [tk], tk.sem())
        self.colvecs(cw.t[:].rearrange("p c j -> p (c j)"), tk, 4, 24, cw)
        P.dma("sp", dtb.t[:], I["gdn_dt_bias"][l].partition_broadcast(128), [], [dtb], dtb.sem())
        P.dma("sp", Ab.t[:], I["gdn_a_log"][l].partition_broadcast(128), [], [Ab], Ab.sem())
        P.dma("sp", nwb.t[:], I["gdn_norm_w"][l].partition_broadcast(128), [], [nwb], nwb.sem())
        P.act(lambda e: e.activation(Ab.t[:], Ab.t[:], AF.Exp), [Ab], [Ab])
        P.act(lambda e: e.activation(Ab.t[:], Ab.t[:], AF.Copy, scale=-1.0), [Ab], [Ab])
        P.dma("pool", wab.t[:], w_in[:, C_GA:C_GA + 16].rearrange("(k p) c -> p k c", p=128), [], [wab], wab.sem())
        A.reset(off0)
        qkv = A.alloc("qkv", [24, TP], BF16, nreg=24)
        if hs:
            qkvS = A.alloc("qkvS", [24, 16], F32)
        off1 = A.off
        xraw_b = [A.alloc("xraw", [3 + TP], F32) for _ in range(2)]
        NPAR = 2 if hs else 3
        xraw_b = xraw_b + [A.alloc("xraw", [3 + TP], F32) for _ in range(NPAR - 2)]
        acc_b = [A.alloc("acc", [TP], F32) for _ in range(NPAR)]
        sgm_b = [A.alloc("sgm", [TP], F32) for _ in range(NPAR)]
        sq_b = [A.alloc("sq", [TP], F32) for _ in range(NPAR)]
        stg = A.alloc("stg", [512], F32)
        if hs:
            stc_b = [A.alloc("stc", [3, 128], F32) for _ in range(2)]
            xrs_b = [A.alloc("xrs", [4, 16], F32) for _ in range(2)]
            accS_b = [A.alloc("accS", [16], F32) for _ in range(2)]
            sgS_b = [A.alloc("sgS", [16], F32) for _ in range(2)]
            sqS_b = [A.alloc("sqS", [16], F32) for _ in range(2)]
        tail = self.tail_gdn[l]
        if self.half == 0:
            P.dve(lambda e: e.memset(tail.t[:], 0.0), [], [tail])
        self.state_load("gdn_p", l)
        lnq = float(np.log(128.0 ** -0.5))

        def l2n_g(dst_ap, xin_ap, sq_ap, n, is_q, bufs_r, bufs_w, b):
            P.act(lambda e: e.activation(sq_ap, xin_ap, AF.Square), bufs_r, [bufs_w[0]])
            self.mm(ps.t[:, b, :n], self.masks.t[:, 1, :], sq_ap, True, True, [masks, bufs_w[0]], [(ps, b)])
            yield
            P.act(lambda e: e.activation(sq_ap, ps.t[:, b, :n], AF.Ln, bias=float(NORM_EPS)), [(ps, b)], [bufs_w[0]])
            if is_q:
                P.act(lambda e: e.activation(sq_ap, sq_ap, AF.Exp, scale=-0.5, bias=lnq), [bufs_w[0]], [bufs_w[0]])
            else:
                P.act(lambda e: e.activation(sq_ap, sq_ap, AF.Exp, scale=-0.5), [bufs_w[0]], [bufs_w[0]])
            yield
            P.dve(lambda e: e.tensor_tensor(dst_ap, xin_ap, sq_ap, ALU.mult), list(bufs_r) + [bufs_w[0]], [bufs_w[1]])

        def conv_chunk(cc, slot):
            cs = (cc % 4) * 128
            par = cc % NPAR
            acc, sgm, sq = acc_b[par], sgm_b[par], sq_b[par]
            b = 2 + 2 * par
            bl = 3 + 2 * par
            for k in range(8):
                self.mm(ps.t[:, b, :TP], slot.t[:, k, cs:cs + 128], xT.t[:, k, 0:TP], k == 0, k == 7,
                        [slot, (xT, list(range(c.NTH)))], [(ps, b)])
            xraw = xraw_b[par]
            P.act(lambda e: e.activation(xraw.t[:, 0:3], tail.t[:, cc, :], AF.Copy), [(tail, cc)], [xraw])
            P.act(lambda e: e.activation(xraw.t[:, 3:3 + TP], ps.t[:, b, :TP], AF.Copy), [(ps, b)], [xraw])
            P.act(lambda e: e.activation(tail.t[:, cc, :], xraw.t[:, TP:TP + 3], AF.Copy), [xraw], [(tail, cc)])
            yield
            P.dve(lambda e: e.tensor_scalar(acc.t[:, :], xraw.t[:, 0:TP], cw.t[:, cc, 0:1], None, ALU.mult), [xraw, cw], [acc])
            for j in range(1, 4):
                P.dve(lambda e, j=j: e.scalar_tensor_tensor(acc.t[:, :], xraw.t[:, j:j + TP], cw.t[:, cc, j:j + 1], acc.t[:, :],
                                                            ALU.mult, ALU.add), [xraw, cw, acc], [acc])
            yield
            self.sigmoid_chain(sgm.t[:, :], acc.t[:, :], [acc], sgm)
            yield
            if cc < 16:
                P.dve(lambda e: e.tensor_tensor(acc.t[:, :], acc.t[:, :], sgm.t[:, :], ALU.mult), [acc, sgm], [acc])
                yield from l2n_g(qkv.t[:, cc, :], acc.t[:, :], sq.t[:, :], TP, cc < 8, [acc], [sq, (qkv, cc)], bl)
            else:
                P.dve(lambda e: e.tensor_tensor(qkv.t[:, cc, :], acc.t[:, :], sgm.t[:, :], ALU.mult), [acc, sgm], [(qkv, cc)])
            if hs:
                yield
                b2 = 6 + par
                for k in range(8):
                    self.mm(ps.t[:, b2, 0:16], slot.t[:, k, cs:cs + 128], xT.t[:, k, TP:TP + 16], k == 0, k == 7,
                            [slot, (xT, c.NTH)], [(ps, b2)])
                stc = stc_b[par]
                P.dma("sp", stc.t[0:16, :, :], I["st_gdn_conv"][l, :, :, cc * 128:(cc + 1) * 128], [], [stc], stc.sem())
                for j in range(3):
                    P.pe(lambda e, j=j: e.transpose(ps.t[:, b2, 16 + 16 * j:32 + 16 * j], stc.t[0:16, j, :], self.identf.t[0:16, 0:16]),
                         [stc, self.identf], [(ps, b2)])
                xrs = xrs_b[par]
                P.act(lambda e: e.activation(xrs.t[:, 0:3, :], ps.t[:, b2, 16:64].rearrange("p (j s) -> p j s", j=3), AF.Copy),
                      [(ps, b2)], [xrs])
                P.act(lambda e: e.activation(xrs.t[:, 3, :], ps.t[:, b2, 0:16], AF.Copy), [(ps, b2)], [xrs])
                accS, sgS, sqS = accS_b[par], sgS_b[par], sqS_b[par]
                P.dve(lambda e: e.tensor_scalar(accS.t[:, :], xrs.t[:, 0, :], cw.t[:, cc, 0:1], None, ALU.mult), [xrs, cw], [accS])
                for j in range(1, 4):
                    P.dve(lambda e, j=j: e.scalar_tensor_tensor(accS.t[:, :], xrs.t[:, j, :], cw.t[:, cc, j:j + 1], accS.t[:, :],
                                                                ALU.mult, ALU.add), [xrs, cw, accS], [accS])
                yield
                self.sigmoid_chain(sgS.t[:, :], accS.t[:, :], [accS], sgS)
                yield
                if cc < 16:
                    P.dve(lambda e: e.tensor_tensor(accS.t[:, :], accS.t[:, :], sgS.t[:, :], ALU.mult), [accS, sgS], [accS])
                    yield from l2n_g(qkvS.t[:, cc, :], accS.t[:, :], sqS.t[:, :], 16, cc < 8, [accS], [sqS, qkvS], b2)
                else:
                    P.dve(lambda e: e.tensor_tensor(qkvS.t[:, cc, :], accS.t[:, :], sgS.t[:, :], ALU.mult), [accS, sgS], [qkvS])

        def conv_rows(slot, i, c0, n, dst):
            b = self.bank()
            for k in range(8):
                self.mm(ps.t[:n, b, :], xT.t[:, k, c0:c0 + n], slot.t[:, k, :], k == 0, k == 7,
                        [slot, (xT, list(range(self.NTT)))], [(ps, b)])
            P.act(lambda e: e.activation(stg.t[:n, :], ps.t[:n, b, :], AF.Copy), [(ps, b)], [stg])
            P.dma("sp", dst, stg.t[:n, :], [stg], [], stg.sem(), is_output=True)

        def load_slot(i):
            sl = self.wslot()
            self.wload(sl, 0, w_in[:, C_GQKV + i * 512:C_GQKV + (i + 1) * 512])
            return sl

        nxt = load_slot(0)
        for i in range(6):
            slot = nxt
            if i + 1 < 6:
                nxt = load_slot(i + 1)
            self.run_pipelined((conv_chunk(cc, slot) for cc in range(4 * i, 4 * i + 4)), NPAR)
            if self.half == c.NH - 1:
                conv_rows(slot, i, TP - 3, 3, self.o["gdn_conv_p"][l, :, i * 512:(i + 1) * 512])
            if hs:
                conv_rows(slot, i, TP, 16, self.o["gdn_conv_s"][l, :, 2, i * 512:(i + 1) * 512])
        if hs:
            P.dma("sp", self.o["gdn_conv_s"][l, :, 0:2, :], I["st_gdn_conv"][l, :, 1:3, :], [], [], stg.sem(), is_output=True)
        if self.cfg.dbg.get("gdn_stop", 9) <= 1:
            return
        Wz = (self.wslot(), self.wslot())
        for o in range(2):
            self.wload(Wz[o], 0, w_in[:, C_GZ + o * 512:C_GZ + (o + 1) * 512])

        A.reset(off1)
        sm = A.alloc("sm", [14, 8], F32)
        osb = A.alloc("osb", [1024], F32, nreg=2)
        f3 = A.alloc("f3", [1024], F32)
        ogb = A.alloc("ogb", [1024], BF16)
        nst = A.alloc("nst", [3, 8], F32)
        off2 = A.off
        sm_b = [sm, A.alloc("sm1", [14, 8], F32)]
        cumT_b = [A.alloc("cumT", [2, 128], F32) for _ in range(2)]
        Bd1 = A.alloc("Bd1", [4, 128], F32)
        o_d = A.off
        dtmp = A.alloc("dtmp", [4, 128], F32)
        A.reset(o_d)
        ktok = A.alloc("ktok", [4, 128], BF16)
        A.reset(o_d + 2048)
        f1 = A.alloc("f1", [512], F32)
        Ya, Yb = A.alloc("Ya", [4, 128], F32), A.alloc("Yb", [4, 128], F32)
        YTa, YTb = A.alloc("YTa", [4, 128], F32), A.alloc("YTb", [4, 128], F32)
        rhs = A.alloc("rhs", [4, 128], F32)
        ub = A.alloc("ub", [4, 128], BF16)
        PT_b = [A.alloc("PT", [4, 128], F32) for _ in range(2)]
        attnT_b = [A.alloc("attnT", [4, 128], BF16) for _ in range(2)]
        qdT_b = [A.alloc("qdT", [4, 128], BF16) for _ in range(2)]
        kdec_b = [A.alloc("kdec", [2, 4, 128], BF16) for _ in range(2)]
        vb_b = [A.alloc("vb", [4, 128], F32) for _ in range(2)]
        if hs:
            A.reset(off2)
            Eall = A.alloc("Eall", [16, 8], F32)
            Bde = A.alloc("Bde", [16, 8], F32)
            kTm = A.alloc("kTm", [8, 16], F32)
            qTm = A.alloc("qTm", [8, 16], F32)
            ktS = A.alloc("ktS", [1024], F32)
            vbS = A.alloc("vbS", [1024], F32)
            um = A.alloc("um", [1024], F32)
            tS = A.alloc("tS", [1024], F32)
            SsB = A.alloc("SsB", [1024], F32)
            oacc = A.alloc("oacc", [1024], F32)

        def gates(m, rows, col0, sm=sm):
            R = lambda i: sm.t[:rows, i, :]
            bab = 6
            for k in range(8):
                self.mm(ps.t[:rows, bab, 0:16], xT.t[:, k, col0:col0 + rows], wab.t[:, k, :], k == 0, k == 7, [(xT, m), wab], [(ps, bab)])
            P.act(lambda e: e.activation(R(7), ps.t[:rows, bab, 8:16], AF.Exp, scale=-1.0), [(ps, bab)], [sm])
            P.act(lambda e: e.activation(R(6), R(7), AF.Ln, bias=1.0), [sm], [sm])
            P.act(lambda e: e.activation(R(6), R(6), AF.Copy, scale=-1.0), [sm], [sm])
            P.act(lambda e: e.activation(R(0), R(6), AF.Exp), [sm], [sm])
            P.dve(lambda e: e.tensor_tensor(R(7), ps.t[:rows, bab, 0:8], dtb.t[:rows, :], ALU.add), [(ps, bab), dtb], [sm])
            P.act(lambda e: e.activation(R(7), R(7), AF.Exp), [sm], [sm])
            P.act(lambda e: e.activation(R(7), R(7), AF.Ln, bias=1.0), [sm], [sm])
            P.dve(lambda e: e.tensor_tensor(R(1), R(7), Ab.t[:rows, :], ALU.mult), [sm, Ab], [sm])

        v4 = lambda ap: ap.rearrange("p (h i) -> p h i", h=4)

        def tile_front(m, col0, sm, cumT):
            R = lambda i: sm.t[:, i, :]
            gates(m, 128, col0, sm)
            bc = 6
            self.mm(ps.t[:, bc, 32:40], U64, R(1), True, True, [masks, sm], [(ps, bc)])
            self.mm(ps.t[:, bc, 40:48], masks.t[:, 6, :], R(1), True, True, [masks, sm], [(ps, bc)])
            self.mm(ps.t[:, bc, 48:56], masks.t[:, 7, :], R(1), True, True, [masks, sm], [(ps, bc)])
            self.mm(ps.t[:, bc, 56:64], ONES, R(1), True, True, [masks, sm], [(ps, bc)])
            P.act(lambda e: e.activation(R(2), ps.t[:, bc, 32:40], AF.Copy), [(ps, bc)], [sm])
            P.act(lambda e: e.activation(R(3), ps.t[:, bc, 32:40], AF.Exp), [(ps, bc)], [sm])
            P.dve(lambda e: e.tensor_tensor(R(5), ps.t[:, bc, 40:48], R(2), ALU.subtract), [(ps, bc), sm], [sm])
            P.act(lambda e: e.activation(R(5), R(5), AF.Exp), [sm], [sm])
            P.dve(lambda e: e.tensor_copy(sm.t[:, 12:14, :], sm.t[:, 5:6, :].broadcast_to([128, 2, 8])), [sm], [sm])
            P.dve(lambda e: e.memset(sm.t[64:128, 12, :], 0.0), [sm], [sm])
            P.dve(lambda e: e.memset(sm.t[0:64, 13, :], 0.0), [sm], [sm])
            P.act(lambda e: e.activation(R(8), ps.t[:, bc, 48:56], AF.Exp), [(ps, bc)], [sm])
            P.act(lambda e: e.activation(R(7), ps.t[:, bc, 48:56], AF.Copy), [(ps, bc)], [sm])
            P.dve(lambda e: e.tensor_tensor(R(9), ps.t[:, bc, 56:64], R(7), ALU.subtract), [(ps, bc), sm], [sm])
            P.act(lambda e: e.activation(R(9), R(9), AF.Exp), [sm], [sm])
            P.dve(lambda e: e.scalar_tensor_tensor(R(4), R(0), -1.0, R(3), ALU.mult, ALU.mult), [sm], [sm])
            P.dve(lambda e: e.tensor_tensor(R(10), R(2), R(6), ALU.add), [sm], [sm])
            self.mm(ps.t[0:8, bc, 64:192], R(2), self.identf.t[:, :], True, True, [sm, self.identf], [(ps, bc)])
            self.mm(ps.t[0:8, bc, 192:320], R(10), self.identf.t[:, :], True, True, [sm, self.identf], [(ps, bc)])
            P.act(lambda e: e.activation(cumT.t[0:8, :, :].rearrange("p a i -> p (a i)"), ps.t[0:8, bc, 64:320], AF.Copy), [(ps, bc)], [cumT])

        def unit_gen(u, m, col0, g, part):
            par = u % 2
            BA, BB, BC = (2, 3, 4) if par == 0 else (0, 1, 5)
            sm, cumT = sm_b[m % 2], cumT_b[m % 2]
            PT, attnT, qdT, kdec, vb = PT_b[par], attnT_b[par], qdT_b[par], kdec_b[par], vb_b[par]
            R = lambda i: sm.t[:, i, :]
            hsl = slice(4 * g, 4 * g + 4)
            if part == "A":
                if g == 0:
                    tile_front(m, col0, sm, cumT)
                    yield
                hsl = slice(4 * g, 4 * g + 4)
                eye_g = self.eyep.t[0:8, 4 * g:4 * g + 4].unsqueeze(2).broadcast_to([8, 4, 128])
                bdf = Bd1.t[0:8, :, :].rearrange("p h i -> p (h i)")
                cum_b = R(2)[:, hsl].unsqueeze(2).broadcast_to([128, 4, 128])
                P.dve(lambda e: e.tensor_tensor(Bd1.t[0:8, :, :], cumT.t[0:8, 0, :].unsqueeze(1).broadcast_to([8, 4, 128]), eye_g, ALU.mult),
                      [cumT, self.eyep], [Bd1])
                self.mm(ps.t[:, BA, :], ONES[0:8, :], bdf, True, True, [masks, Bd1], [(ps, BA)])
                self.mm(ps.t[:, BB, :], ONES[0:8, :], bdf, True, False, [masks, Bd1], [(ps, BB)])
                for hq in range(4):
                    self.mm(ps.t[:, BB, hq * 128:(hq + 1) * 128], self.identf.t[:, :], MBI, False, hq == 3, [masks, self.identf], [(ps, BB)])
                for h in range(4):
                    hh = 4 * g + h
                    self.mm(ps.t[:, BC, h * 128:(h + 1) * 128], qkv.t[:, 8 + hh, col0:col0 + 128], qkv.t[:, hh, col0:col0 + 128], True, True,
                            [(qkv, [hh, 8 + hh])], [(ps, BC)])
                yield
                P.act(lambda e: e.activation(dtmp.t[:, :, :].rearrange("p h i -> p (h i)"), ps.t[:, BA, :], AF.Exp), [(ps, BA)], [dtmp])
                P.dve(lambda e: e.tensor_tensor(f1.t[:, :].rearrange("p (h i) -> p h i", h=4), v4(ps.t[:, BB, :]), cum_b, ALU.subtract), [(ps, BB), sm], [f1])
                yield
                P.dve(lambda e: e.tensor_tensor(qdT.t[:, :, :], qkv.t[:, 4 * g:4 * g + 4, col0:col0 + 128], dtmp.t[:, :, :], ALU.mult),
                      [(qkv, list(range(4 * g, 4 * g + 4))), dtmp], [qdT])
                P.act(lambda e: e.activation(f1.t[:, :], f1.t[:, :], AF.Exp), [f1], [f1])
                yield
                P.dve(lambda e: e.tensor_tensor(attnT.t[:, :, :], v4(ps.t[:, BC, :]), f1.t[:, :].rearrange("p (h i) -> p h i", h=4), ALU.mult),
                      [(ps, BC), f1], [attnT])
                P.dve(lambda e: e.tensor_tensor(Bd1.t[0:8, :, :], cumT.t[0:8, 1, :].unsqueeze(1).broadcast_to([8, 4, 128]), eye_g, ALU.mult),
                      [cumT, self.eyep], [Bd1])
                self.mm(ps.t[:, BB, :], ONES[0:8, :], bdf, True, False, [masks, Bd1], [(ps, BB)])
                for hq in range(4):
                    self.mm(ps.t[:, BB, hq * 128:(hq + 1) * 128], self.identf.t[:, :], MBS, False, hq == 3, [masks, self.identf], [(ps, BB)])
                for h in range(4):
                    hh = 4 * g + h
                    self.mm(ps.t[:, BA, h * 128:(h + 1) * 128], qkv.t[:, 8 + hh, col0:col0 + 128], qkv.t[:, 8 + hh, col0:col0 + 128], True, True,
                            [(qkv, 8 + hh)], [(ps, BA)])
                yield
                P.dve(lambda e: e.tensor_tensor(dtmp.t[:, :, :], v4(ps.t[:, BB, :]), cum_b, ALU.subtract), [(ps, BB), sm], [dtmp])
                yield
                P.act(lambda e: e.activation(dtmp.t[:, :, :], dtmp.t[:, :, :], AF.Exp), [dtmp], [dtmp])
                yield
                P.dve(lambda e: e.scalar_tensor_tensor(YTa.t[:, :, :], v4(ps.t[:, BA, :]), -1.0, dtmp.t[:, :, :], ALU.mult, ALU.mult),
                      [(ps, BA), dtmp], [YTa])
                for h in range(4):
                    P.pe(lambda e, h=h: e.transpose(ps.t[:, BB, h * 128:(h + 1) * 128], YTa.t[:, h, :], self.identf.t[:, :]),
                         [YTa, self.identf], [(ps, BB)])
                yield
                P.act(lambda e: e.activation(Ya.t[:, :, :], v4(ps.t[:, BB, :]), AF.Copy), [(ps, BB)], [Ya])
                P.dve(lambda e: e.tensor_tensor(PT.t[:, :, :], YTa.t[:, :, :], self.identf.t[:, :].unsqueeze(1).broadcast_to([128, 4, 128]), ALU.add),
                      [YTa, self.identf], [PT])
                def ev_k(pst, bt):
                    P.act(lambda e: e.activation(ktok.t[:, :, :], pst[:, 0:4, :], AF.Copy), [(ps, bt)], [ktok])
                self.transposes_T(ev_k, lambda k: qkv.t[:, 8 + 4 * g + k, col0:col0 + 128], 4, 128, (qkv, list(range(8 + 4 * g, 12 + 4 * g))), bt=7)
                yield
                def level(lev, Y, YT, Yn, YTn):
                    for h in range(4):
                        self.mm(ps.t[:, BA, h * 128:(h + 1) * 128], YT.t[:, h, :], Y.t[:, h, :], True, True, [YT, Y], [(ps, BA)])
                    if lev < 5:
                        for h in range(4):
                            self.mm(ps.t[:, BB, h * 128:(h + 1) * 128], Y.t[:, h, :], YT.t[:, h, :], True, True, [YT, Y], [(ps, BB)])
                    yield
                    P.act(lambda e: e.activation(Yn.t[:, :, :], v4(ps.t[:, BA, :]), AF.Copy), [(ps, BA)], [Yn])
                    if lev < 5:
                        P.act(lambda e: e.activation(YTn.t[:, :, :], v4(ps.t[:, BB, :]), AF.Copy), [(ps, BB)], [YTn])
                    yield
                    for h in range(4):
                        self.mm(ps.t[:, BC, h * 128:(h + 1) * 128], Yn.t[:, h, :], PT.t[:, h, :], True, True, [Yn, PT], [(ps, BC)])
                    yield
                    P.dve(lambda e: e.tensor_tensor(PT.t[:, :, :], PT.t[:, :, :], v4(ps.t[:, BC, :]), ALU.add), [PT, (ps, BC)], [PT])
                Y, YT, Yn, YTn = Ya, YTa, Yb, YTb
                for lev in range(1, 6):
                    yield from level(lev, Y, YT, Yn, YTn)
                    if lev == 1:
                        for cq in range(2):
                            P.dve(lambda e, cq=cq: e.tensor_tensor(kdec.t[:, cq, :, :], ktok.t[:, :, :],
                                                                   R(12 + cq)[:, hsl].unsqueeze(2).broadcast_to([128, 4, 128]), ALU.mult),
                                  [ktok, sm], [kdec])
                    if lev == 2:
                        def ev_v(pst, bt):
                            P.dve(lambda e: e.tensor_tensor(vb.t[:, :, :], pst[:, 0:4, :], R(0)[:, hsl].unsqueeze(2).broadcast_to([128, 4, 128]), ALU.mult),
                                  [(ps, bt), sm], [vb])
                        self.transposes_T(ev_v, lambda k: qkv.t[:, 16 + 4 * g + k, col0:col0 + 128], 4, 128,
                                          (qkv, list(range(16 + 4 * g, 20 + 4 * g))), bt=7)
                    Y, YT, Yn, YTn = Yn, YTn, Y, YT
                    yield
                return
            Sg = (S, list(range(8 * g, 8 * g + 8)))
            Sbg = (Sb, list(range(8 * g, 8 * g + 8)))

            def chunk(ch):
                rs = slice(64 * ch, 64 * ch + 64)
                rw = slice(0, 128) if ch == 0 else rs
                nrw = 128 if ch == 0 else 64
                for h in range(4):
                    hh = 4 * g + h
                    self.mm(ps.t[:, BA, h * 128:(h + 1) * 128], qkv.t[:, 8 + hh, col0:col0 + 128], Sb.t[:, hh * 128:(hh + 1) * 128], True, True,
                            [(qkv, 8 + hh), Sbg], [(ps, BA)])
                yield
                nbe_b = R(4)[rw, hsl].unsqueeze(2).broadcast_to([nrw, 4, 128])
                P.dve(lambda e: e.tensor_tensor(rhs.t[rw, :, :], v4(ps.t[rw, BA, :]), nbe_b, ALU.mult), [(ps, BA), sm], [rhs])
                P.dve(lambda e: e.tensor_tensor(rhs.t[rw, :, :], rhs.t[rw, :, :], vb.t[rw, :, :], ALU.add), [rhs, vb], [rhs])
                yield
                for h in range(4):
                    self.mm(ps.t[:, BB, h * 128:(h + 1) * 128], PT.t[:, h, :], rhs.t[:, h, :], True, True, [PT, rhs], [(ps, BB)])
                yield
                P.act(lambda e: e.activation(ub.t[rw, :, :], v4(ps.t[rw, BB, :]), AF.Copy), [(ps, BB)], [ub])
                yield
                for h in range(4):
                    hh = 4 * g + h
                    self.mm(ps.t[:, BC, h * 128:(h + 1) * 128], qdT.t[:, h, :], Sb.t[:, hh * 128:(hh + 1) * 128], True, False, [qdT, Sbg], [(ps, BC)])
                    self.mm(ps.t[:, BC, h * 128:(h + 1) * 128], attnT.t[:, h, :], ub.t[:, h, :], False, True, [attnT, ub], [(ps, BC)])
                for h in range(4):
                    self.mm(ps.t[:, BA, h * 128:(h + 1) * 128], kdec.t[:, ch, h, :], ub.t[:, h, :], True, True, [kdec, ub], [(ps, BA)])
                yield
                P.act(lambda e: e.activation(osb.t[rs, g * 512:(g + 1) * 512], ps.t[rs, BC, :], AF.Copy), [(ps, BC)], [(osb, g)])
                el_b = sm.t[:, 8 + ch, hsl].unsqueeze(2).broadcast_to([128, 4, 128])
                P.dve(lambda e: e.tensor_tensor(v4(S.t[:, g * 512:(g + 1) * 512]), v4(S.t[:, g * 512:(g + 1) * 512]), el_b, ALU.mult), [Sg, sm], [Sg])
                P.dve(lambda e: e.tensor_tensor(S.t[:, g * 512:(g + 1) * 512], S.t[:, g * 512:(g + 1) * 512], ps.t[:, BA, :], ALU.add), [Sg, (ps, BA)], [Sg])
                yield
                P.act(lambda e: e.activation(Sb.t[:, g * 512:(g + 1) * 512], S.t[:, g * 512:(g + 1) * 512], AF.Copy), [Sg], [Sbg])
                yield

            for ch in range(2):
                yield from chunk(ch)
            if g == 1:
                post(m, 128, col0, osb)

        def post(m, rows, col0, o_buf):
            v8 = lambda ap: ap.rearrange("p (h d) -> p h d", h=8)
            P.dve(lambda e: e.tensor_tensor(f3.t[:rows, :], o_buf.t[:rows, :], o_buf.t[:rows, :], ALU.mult), [o_buf], [f3])
            P.dve(lambda e: e.tensor_reduce(nst.t[:rows, 0, :], v8(f3.t[:rows, :]), mybir.AxisListType.X, ALU.add), [f3], [nst])
            P.pool(lambda e: e.tensor_tensor(nst.t[:rows, 1, :], nst.t[:rows, 0, :], self.cst.t[:rows, 5:6].broadcast_to([rows, 8]), ALU.mult),
                   [nst, self.cst], [nst])
            P.pool(lambda e: e.tensor_tensor(nst.t[:rows, 1, :], nst.t[:rows, 1, :], self.cst.t[:rows, 1:2].broadcast_to([rows, 8]), ALU.add),
                   [nst, self.cst], [nst])
            P.pool(lambda e: e.tensor_tensor(nst.t[:rows, 2, :], nst.t[:rows, 1, :], self.cst.t[:rows, 2:3].broadcast_to([rows, 8]), ALU.pow),
                   [nst, self.cst], [nst])
            P.dve(lambda e: e.tensor_tensor(v8(o_buf.t[:rows, :]), v8(o_buf.t[:rows, :]), nst.t[:rows, 2, :].unsqueeze(2).broadcast_to([rows, 8, 128]),
                                            ALU.mult), [o_buf, nst], [o_buf])
            P.dve(lambda e: e.tensor_tensor(v8(o_buf.t[:rows, :]), v8(o_buf.t[:rows, :]), nwb.t[:rows, :].unsqueeze(1).broadcast_to([rows, 8, 128]),
                                            ALU.mult), [o_buf, nwb], [o_buf])
            pz = 4
            for o in range(2):
                for k in range(8):
                    self.mm(ps.t[:rows, pz + o, :], xT.t[:, k, col0:col0 + rows], Wz[o].t[:, k, :], k == 0, k == 7, [(xT, m), Wz[o]], [(ps, pz + o)])
            pzv = ps.t[:rows, pz:pz + 2, :].rearrange("p b n -> p (b n)")
            self.sigmoid_chain(f3.t[:rows, :], pzv, [(ps, pz), (ps, pz + 1)], f3)
            P.dve(lambda e: e.tensor_tensor(f3.t[:rows, :], pzv, f3.t[:rows, :], ALU.mult), [(ps, pz), (ps, pz + 1), f3], [f3])
            P.dve(lambda e: e.tensor_tensor(ogb.t[:rows, :], o_buf.t[:rows, :], f3.t[:rows, :], ALU.mult), [o_buf, f3], [ogb])

            def evac(pst, bt):
                P.act(lambda e: e.activation(hT.t[:, 0:8, col0:col0 + rows], pst[:, :, :rows], AF.Copy), [(ps, bt)], [(hT, list(range(8)))])

            self.transposes_to(evac, lambda k: ogb.t[:rows, k * 128:(k + 1) * 128], 8, rows, ogb, None, bt=7)

        def sample_body(m, col0):
            R = lambda i: sm.t[0:16, i, :]
            gates(m, 16, col0)
            P.act(lambda e: e.activation(R(3), R(1), AF.Exp), [sm], [sm])
            P.dve(lambda e: e.scalar_tensor_tensor(R(4), R(0), -1.0, R(3), ALU.mult, ALU.mult), [sm], [sm])
            P.dve(lambda e: e.tensor_tensor(Bde.t[0:16, :, :], R(3).unsqueeze(1).broadcast_to([16, 16, 8]),
                                            self.eyep.t[:, :].unsqueeze(2).broadcast_to([16, 16, 8]), ALU.mult), [sm, self.eyep], [Bde])
            self.mm(ps.t[:, 6, 0:128], ONES[0:16, :], Bde.t[0:16, :, :].rearrange("p s h -> p (s h)"), True, True, [masks, Bde], [(ps, 6)])
            P.act(lambda e: e.activation(Eall.t[:, :, :].rearrange("p s h -> p (s h)"), ps.t[:, 6, 0:128], AF.Copy), [(ps, 6)], [Eall])
            for h in range(8):
                P.pe(lambda e, h=h: e.transpose(ps.t[0:16, 2 + h // 4, (h % 4) * 128:(h % 4 + 1) * 128], qkvS.t[:, 8 + h, :], self.identf.t[:, :]),
                     [qkvS, self.identf], [(ps, 2 + h // 4)])
            P.act(lambda e: e.activation(ktS.t[0:16, :], ps.t[0:16, 2:4, :].rearrange("p b n -> p (b n)"), AF.Copy), [(ps, 2), (ps, 3)], [ktS])
            for h in range(8):
                P.pe(lambda e, h=h: e.transpose(ps.t[0:16, 4 + h // 4, (h % 4) * 128:(h % 4 + 1) * 128], qkvS.t[:, 16 + h, :], self.identf.t[:, :]),
                     [qkvS, self.identf], [(ps, 4 + h // 4)])
            P.dve(lambda e: e.tensor_tensor(vbS.t[0:16, :].rearrange("p (h d) -> p h d", h=8),
                                            ps.t[0:16, 4:6, :].rearrange("p b (h d) -> p (b h) d", h=4),
                                            R(0).unsqueeze(2).broadcast_to([16, 8, 128]), ALU.mult), [(ps, 4), (ps, 5), sm], [vbS])
            Ss_b = [osb, SsB]
            P.dve(lambda e: e.memset(oacc.t[0:16, :], 0.0), [], [oacc])
            for s_ in range(NS):
                Ss = Ss_b[s_ % 2]
                P.dma("sp", Ss.t[:, :].rearrange("p (h v) -> p h v", h=8), I["st_gdn"][l, s_].rearrange("h k v -> k h v"), [], [Ss], Ss.sem())
                P.dve(lambda e, s_=s_: e.tensor_tensor(kTm.t[:, :, :], qkvS.t[:, 8:16, :],
                                                       self.eyef.t[:, s_, :].unsqueeze(1).broadcast_to([128, 8, 16]), ALU.mult),
                      [qkvS, self.eyef], [kTm])
                P.dve(lambda e, s_=s_: e.tensor_tensor(qTm.t[:, :, :], qkvS.t[:, 0:8, :],
                                                       self.eyef.t[:, s_, :].unsqueeze(1).broadcast_to([128, 8, 16]), ALU.mult),
                      [qkvS, self.eyef], [qTm])
                for h in range(8):
                    self.mm(ps.t[0:16, 2 + h // 4, (h % 4) * 128:(h % 4 + 1) * 128], kTm.t[:, h, :], Ss.t[:, h * 128:(h + 1) * 128], True, True,
                            [kTm, Ss], [(ps, 2 + h // 4)])
                P.dve(lambda e: e.tensor_tensor(tS.t[0:16, :].rearrange("p (h d) -> p h d", h=8),
                                                ps.t[0:16, 2:4, :].rearrange("p b (h d) -> p (b h) d", h=4),
                                                R(4).unsqueeze(2).broadcast_to([16, 8, 128]), ALU.mult), [(ps, 2), (ps, 3), sm], [tS])
                P.dve(lambda e, s_=s_: e.scalar_tensor_tensor(um.t[0:16, :], vbS.t[0:16, :], self.eyep.t[0:16, s_:s_ + 1], tS.t[0:16, :],
                                                              ALU.mult, ALU.add), [vbS, self.eyep, tS], [um])
                for h in range(8):
                    self.mm(ps.t[:, 4 + h // 4, (h % 4) * 128:(h % 4 + 1) * 128], ktS.t[0:16, h * 128:(h + 1) * 128], um.t[0:16, h * 128:(h + 1) * 128],
                            True, True, [ktS, um], [(ps, 4 + h // 4)])
                P.dve(lambda e, s_=s_, Ss=Ss: e.tensor_tensor(tS.t[:, :].rearrange("p (h d) -> p h d", h=8), Ss.t[:, :].rearrange("p (h d) -> p h d", h=8),
                                                              Eall.t[:, s_, :].unsqueeze(2).broadcast_to([128, 8, 128]), ALU.mult), [Ss, Eall], [tS])
                P.dve(lambda e, Ss=Ss: e.tensor_tensor(Ss.t[:, :], tS.t[:, :], ps.t[:, 4:6, :].rearrange("p b n -> p (b n)"), ALU.add),
                      [tS, (ps, 4), (ps, 5)], [Ss])
                P.dma("sp", self.o["gdn_s"][l, s_].rearrange("h k v -> k h v"), Ss.t[:, :].rearrange("p (h v) -> p h v", h=8), [Ss], [], Ss.sem(),
                      is_output=True)
                for h in range(8):
                    self.mm(ps.t[0:16, h // 4, (h % 4) * 128:(h % 4 + 1) * 128], qTm.t[:, h, :], Ss.t[:, h * 128:(h + 1) * 128],
                            True, True, [qTm, Ss], [(ps, h // 4)])
                P.dve(lambda e: e.tensor_tensor(oacc.t[0:16, :], oacc.t[0:16, :], ps.t[0:16, 0:2, :].rearrange("p b n -> p (b n)"), ALU.add),
                      [oacc, (ps, 0), (ps, 1)], [oacc])
            post(m, 16, col0, oacc)

        units = [(m, col0, g) for (m, rows, col0) in self.tiles() if rows == 128 for g in range(2)]
        self.run_pipelined([unit_gen(0, *units[0], "A")], 1)
        for u in range(len(units)):
            gens = [unit_gen(u, *units[u], "B")]
            if u + 1 < len(units):
                gens.append(unit_gen(u + 1, *units[u + 1], "A"))
            self.run_pipelined(gens, 2)
        for (m, rows, col0) in self.tiles():
            if rows != 128:
                sample_body(m, col0)
        self.state_store("gdn_p", l)

    def transposes_T(self, evac, src_fn, n, rows, src_buf, bt=None):
        P, ps = self.P, self.ps
        bt = self.bank() if bt is None else bt
        pst = ps.t[:, bt, :].bitcast(BF16).rearrange("p (k n) -> p k n", k=8)
        for k in range(n):
            P.pe(lambda e, k=k: e.transpose(pst[:rows, k, :], src_fn(k), self.identb.t[:, :]), [src_buf, self.identb], [(ps, bt)])
        evac(pst, bt)

    def token_mix(self, l):
        I = self.i
        if "ret" in self.cfg.mix:
            self.retention(l)
            self.finale(l, I["w_ret_out"][l], C_M1)
        if "ssd" in self.cfg.mix:
            self.ssd(l)
            self.finale(l, I["w_ssm_out"][l], C_M2)
        if "gdn" in self.cfg.mix:
            self.gdn(l)
            self.finale(l, I["w_gdn_out"][l], C_M3)

    def build(self):
        c = self.cfg
        self.pT = [self.P.sbuf(f"pT{i}", [128, 2, 128], BF16) for i in range(2)]
        self.alloc_mix()
        for half in range(c.NH):
            self.half = half
            self.has_sample = c.sample and half == c.NH - 1
            self.load_x()
            for l in range(c.layers):
                last = (l == c.layers - 1)
                if c.ffn:
                    self.ffn(l, 0)
                self.layer_norm(l, 0)
                self.token_mix(l)
                self.layer_norm(l, 1)
                if c.ffn:
                    self.ffn(l, 1)
                self.layer_norm(l, 2)
                if c.pegate:
                    self.pe_gate(l)
                self.layer_norm(l, 3, final=last)
        return self.P.finish()


def make_consts(T):
    c = {}
    c["c_ident"] = np.eye(128, dtype=np.float32)
    half = 64
    inv = (np.float32(10000.0) ** (-np.arange(half, dtype=np.float32) / np.float32(half))).astype(np.float32)
    pos = np.concatenate([np.arange(T, dtype=np.float32), np.full(NS, PAST_LEN, np.float32)])
    ang = (pos[:, None] * inv[None, :]).astype(np.float32).astype(np.float64)
    rope = np.zeros((T + NS, 4, 64), np.float32)
    rope[:, 0] = np.cos(ang)
    rope[:, 1] = np.sin(ang)
    rope[:, 2] = np.cos(ang) * 128 ** -0.5
    rope[:, 3] = np.sin(ang) * 128 ** -0.5
    c["c_rope"] = rope
    gam = 1.0 - 2.0 ** (-5.0 - np.arange(4))
    lg = np.log1p(-(2.0 ** (-5.0 - np.arange(4)))).astype(np.float32).astype(np.float64)
    i = np.arange(128)
    dm = i[None, :] - i[:, None]
    mask = np.zeros((4, 128, 128), np.float32)
    row = np.zeros((4, 3, 128), np.float32)
    for h in range(4):
        mask[h] = np.where(dm >= 0, np.exp(lg[h] * np.maximum(dm, 0)), 0.0)
        row[h, 0] = np.exp(lg[h] * (i + 1))
        row[h, 1] = np.exp(lg[h] * (127 - i))
        row[h, 2, 0] = np.exp(lg[h] * 128)
        row[h, 2, 1] = np.exp(lg[h])
    c["c_retmask"] = mask
    c["c_retrow"] = row
    mk = np.zeros((8, 128, 128), np.float32)
    jj, ii = np.meshgrid(np.arange(128), np.arange(128), indexing="ij")
    blk = (jj // 64) == (ii // 64)
    mk[0] = (jj <= ii)
    mk[1] = 1.0
    mk[2] = np.where(jj <= ii, 0.0, -30000.0)
    mk[3] = (jj <= ii) & blk
    mk[4] = np.where((jj <= ii) & blk, 0.0, -30000.0)
    mk[5] = np.where((jj < ii) & blk, 0.0, -30000.0)
    mk[6] = blk
    mk[7] = (jj < 64)
    c["c_masks"] = mk
    return c


_CACHE = {}


def kernel(**inputs):
    NC = 8
    cfg = Cfg(NH=4, NTH=4, sample=True, layers=2)
    mk = MK(cfg)
    mk.build()
    T = cfg.T
    consts = make_consts(T)
    f = lambda a: np.ascontiguousarray(np.asarray(a, dtype=np.float32))
    wnames = ["ln_g", "ln_b", "ffn_wg", "ffn_wu", "ffn_wd", "w_in", "ssm_conv_w", "ssm_conv_b", "ssm_dt_bias",
              "ssm_a_log", "ssm_d", "ssm_norm_w", "gdn_conv_w", "gdn_dt_bias", "gdn_a_log", "gdn_norm_w",
              "w_ret_out", "w_ssm_out", "w_gdn_out", "w_o", "pe_proj", "pe_gate"]
    W = {k: f(inputs[k]) for k in wnames}
    xp, xs = np.asarray(inputs["x_prompt"]), np.asarray(inputs["x_sample"])
    pp, ps_ = np.asarray(inputs["p_prompt"]), np.asarray(inputs["p_sample"])
    in_maps = []
    for c in range(NC):
        sl = slice(NS * c, NS * (c + 1))
        m = dict(W)
        m.update(consts)
        m["xp"] = f(xp[c])
        m["pp"] = f(pp[:, c])
        m["xs"] = f(xs[sl, 0])
        m["ps"] = f(ps_[:, sl, 0])
        m["st_ret"] = f(np.asarray(inputs["state_ret"])[:, sl])
        m["st_ssm"] = f(np.asarray(inputs["state_ssm"])[:, sl])
        m["st_ssm_conv"] = f(np.asarray(inputs["state_ssm_conv"])[:, sl])
        m["st_gdn"] = f(np.asarray(inputs["state_gdn"])[:, sl])
        m["st_gdn_conv"] = f(np.asarray(inputs["state_gdn_conv"])[:, sl])
        in_maps.append({k: v for k, v in m.items() if k in mk.i})
    res = run_bass_kernel_spmd(mk.nc, in_maps, core_ids=list(range(NC)))
    R = res.results
    cat0 = lambda k: np.stack([R[c][k] for c in range(NC)], axis=0)
    y_p = cat0("y_p")
    y_s = np.concatenate([R[c]["y_s"] for c in range(NC)], 0)[:, None, :]
    outs = [y_p, y_s]
    for k in ("ret_p", "ssm_p", "ssm_conv_p", "gdn_p", "gdn_conv_p"):
        outs.append(np.stack([R[c][k] for c in range(NC)], axis=1))
    for k in ("ret_s", "ssm_s", "ssm_conv_s", "gdn_s", "gdn_conv_s"):
        outs.append(np.concatenate([R[c][k] for c in range(NC)], axis=1))
    return tuple(np.ascontiguousarray(o, dtype=np.float32) for o in outs)
```

```python
import numpy as np
from contextlib import ExitStack
import concourse.bass as bass
import concourse.mybir as mybir
from concourse.bass_utils import run_bass_kernel_spmd

F32, BF16 = mybir.dt.float32, mybir.dt.bfloat16
AF = mybir.ActivationFunctionType
ALU = mybir.AluOpType

D = 1024
DEPTH = 2
FFN = 2048
PLE = 256
IN_DIM = 12832
DN_ALPHA = (2 * DEPTH) ** 0.25
LN_EPS = 1e-5
NORM_EPS = 1e-6
PAST_LEN = 16384
NS = 16

C_RQ, C_RK, C_RV, C_RG = 0, 512, 1024, 2048
C_MZ, C_MXBC, C_MDT = 3072, 4096, 5632
C_GQKV, C_GZ, C_GA, C_GB = 5648, 8720, 9744, 9752
C_M1, C_M2, C_M3 = 9760, 10784, 11808


class Sem:
    def __init__(self, name):
        self.name = name
        self.count = 0
        self.h = None


class Reg:
    __slots__ = ("writers", "readers")

    def __init__(self):
        self.writers = {}
        self.readers = {}


class Buf:
    def __init__(self, prog, name, t, nreg=1):
        self.prog, self.name, self.t = prog, name, t
        self.regs = [[Reg()] for _ in range(nreg)]
        self.dsem = None

    def sem(self):
        if self.dsem is None:
            self.dsem = self.prog.named_sem("d_" + getattr(self, "sem_name", self.name))
        return self.dsem

    def __getitem__(self, k):
        return self.t[k]


class Op:
    __slots__ = ("eng", "fn", "deps", "needed", "is_dma", "sem", "val", "waits", "dmawaits")

    def __init__(self, eng, fn, is_dma):
        self.eng, self.fn, self.is_dma = eng, fn, is_dma
        self.deps = []
        self.dmawaits = []
        self.needed = False
        self.sem = None
        self.val = 0


def _regs(spec):
    out = []
    for s in spec:
        if isinstance(s, Buf):
            for g in s.regs:
                out.extend(g)
        else:
            b, idx = s
            if isinstance(idx, int):
                out.extend(b.regs[idx])
            else:
                for i in idx:
                    out.extend(b.regs[i])
    return out


class Arena:
    GRAN = 256

    def __init__(self, prog, name, nbytes):
        self.prog = prog
        self.nbytes = nbytes
        self.base = prog.sbuf(name, [128, nbytes // 2], BF16)
        self.gr = [Reg() for _ in range(nbytes // self.GRAN)]
        self.off = 0
        self.n = 0

    def reset(self, off=0):
        self.off = off

    def alloc(self, name, free_shape, dt, nreg=1):
        esz = 2 if dt == BF16 else 4
        nel = int(np.prod(free_shape))
        nb = nel * esz
        nb_al = (nb + self.GRAN - 1) // self.GRAN * self.GRAN
        assert self.off + nb_al <= self.nbytes, f"arena overflow allocating {name}: {self.off}+{nb_al}>{self.nbytes}"
        o2 = self.off // 2
        v = self.base.t[:, o2:o2 + nb // 2]
        if dt != BF16:
            v = v.bitcast(dt)
        if len(free_shape) > 1:
            names = " ".join(f"d{i}" for i in range(len(free_shape)))
            kw = {f"d{i}": int(free_shape[i]) for i in range(len(free_shape))}
            v = v.rearrange(f"p ({names}) -> p {names}", **kw)
        self.n += 1
        b = Buf(self.prog, f"{name}_{self.n}", v, 1)
        b.sem_name = f"a_{name}_{self.off}"
        g0 = self.off // self.GRAN
        ng = nb_al // self.GRAN
        grs = self.gr[g0:g0 + ng]
        if nreg == 1:
            b.regs = [grs]
        else:
            assert ng % nreg == 0, (name, ng, nreg)
            k = ng // nreg
            b.regs = [grs[i * k:(i + 1) * k] for i in range(nreg)]
        self.off += nb_al
        return b


class Prog:
    ENGS = ("pe", "act", "dve", "pool", "sp")
    ATTR = {"pe": "tensor", "act": "scalar", "dve": "vector", "pool": "gpsimd", "sp": "sync"}

    def __init__(self, nc):
        self.nc = nc
        self.es = ExitStack()
        self.ops = {e: [] for e in self.ENGS}
        self.sems = []
        self.esem = {e: self.new_sem("e_" + e) for e in ("pe", "act", "dve", "pool")}
        self.nbuf = 0
        self.out_sems = set()

    def new_sem(self, name):
        s = Sem(name)
        self.sems.append(s)
        return s

    def named_sem(self, name):
        d = self.__dict__.setdefault("_named", {})
        if name not in d:
            d[name] = self.new_sem(name)
        return d[name]

    def sbuf(self, name, shape, dt, nreg=1):
        nb = int(np.prod(shape[1:])) * (2 if dt == BF16 else 4)
        self.sb_bytes = getattr(self, "sb_bytes", 0) + nb
        self.sb_log = getattr(self, "sb_log", []) + [(name, nb)]
        t = self.es.enter_context(self.nc.sbuf_tensor("s_" + name, list(shape), dt))
        return Buf(self, name, t, nreg)

    def psum(self, name, shape, dt, nreg=1):
        t = self.es.enter_context(self.nc.psum_tensor("p_" + name, list(shape), dt))
        return Buf(self, name, t, nreg)

    def dram(self, name, shape, dt, kind, nreg=1):
        t = self.nc.dram_tensor(name, list(shape), dt, kind=kind)
        return Buf(self, name, t.ap(), nreg)

    def _dep(self, c, p):
        if p is None or p is c:
            return
        if p.is_dma:
            c.dmawaits.append((p.sem, p.sem.count))
            return
        p.needed = True
        c.deps.append(p)

    def op(self, eng, fn, reads=(), writes=(), dma_sem=None):
        is_dma = dma_sem is not None
        o = Op(eng, fn, is_dma)
        st = self.__dict__.setdefault("tagstat", {})
        key = (getattr(self, "tag", "-"), eng)
        st[key] = st.get(key, 0) + 1
        rr, ww = _regs(reads), _regs(writes)
        for r in rr:
            for e, p in r.writers.items():
                if (not is_dma) and (not p.is_dma) and e == eng and eng == "pe":
                    continue
                self._dep(o, p)
        for r in ww:
            for e, p in r.readers.items():
                if (not is_dma) and (not p.is_dma) and e == eng and eng == "pe":
                    continue
                self._dep(o, p)
            for e, p in r.writers.items():
                if (not is_dma) and (not p.is_dma) and e == eng and eng == "pe":
                    continue
                if is_dma and p.is_dma and p.sem is dma_sem:
                    continue
                self._dep(o, p)
        key = ("dma", id(o)) if is_dma else eng
        if is_dma:
            dma_sem.count += 16
            o.sem, o.val = dma_sem, dma_sem.count
        for r in rr:
            r.readers[key] = o
        for r in ww:
            r.writers = {key: o}
            r.readers = {}
        self.ops[eng].append(o)
        return o

    def pe(self, fn, reads, writes):
        return self.op("pe", fn, reads, writes)

    def act(self, fn, reads, writes):
        return self.op("act", fn, reads, writes)

    def dve(self, fn, reads, writes):
        return self.op("dve", fn, reads, writes)

    def pool(self, fn, reads, writes):
        return self.op("pool", fn, reads, writes)

    def dma(self, q, out, in_, reads, writes, sem, is_output=False, **kw):
        if is_output:
            self.out_sems.add(sem)
        return self.op(q, lambda e: e.dma_start(out=out, in_=in_, **kw), reads, writes, dma_sem=sem)

    def finish(self):
        nc = self.nc
        fin = Op("sp", None, False)
        for s in self.sems:
            if s.name.startswith("d_") and s.count > 0:
                fin.dmawaits.append((s, s.count))
        self.ops["sp"].append(fin)
        for e in ("pe", "act", "dve", "pool"):
            n = 0
            for o in self.ops[e]:
                if o.is_dma:
                    continue
                if o.needed:
                    n += 1
                    o.sem, o.val = self.esem[e], n
            self.esem[e].count = n
        for s in self.sems:
            if s.count > 0:
                s.h = self.es.enter_context(nc.semaphore(s.name))
        block = self.es.enter_context(nc.Block())
        stats = {}
        for e in self.ENGS:
            ops = self.ops[e]

            def body(eng, ops=ops, e=e):
                seen = {}
                nw = 0
                for o in ops:
                    ws = {}
                    for p in o.deps:
                        ws[p.sem] = max(ws.get(p.sem, 0), p.val)
                    for s, v in o.dmawaits:
                        ws[s] = max(ws.get(s, 0), v)
                    for s, v in ws.items():
                        if seen.get(s, 0) >= v:
                            continue
                        seen[s] = v
                        eng.wait_ge(s.h, v)
                        nw += 1
                    if o.fn is None:
                        continue
                    ins = o.fn(eng)
                    if o.is_dma:
                        ins.then_inc(o.sem.h, 16)
                    elif o.needed:
                        ins.then_inc(o.sem.h, 1)
                stats[e] = (len(ops), nw)

            getattr(block, self.ATTR[e])(body)
        self.es.close()
        return stats


class Cfg:
    def __init__(self, NH=2, NTH=8, sample=True, layers=2, mix=("ret", "ssd", "gdn"), pegate=True, ffn=True):
        self.NH, self.NTH, self.sample, self.layers = NH, NTH, sample, layers
        self.mix, self.pegate, self.ffn = mix, pegate, ffn
        self.dbg = {}
        self.T = NH * NTH * 128


class MK:
    def __init__(self, cfg):
        self.cfg = cfg
        nc = bass.Bass("TRN2", target_bir_lowering=False)
        self.nc = nc
        self.P = Prog(nc)
        self.declare_io()
        self.alloc()

    def din(self, name, shape):
        return self.nc.dram_tensor(name, list(shape), F32, kind="ExternalInput").ap()

    def dout(self, name, shape):
        return self.nc.dram_tensor(name, list(shape), F32, kind="ExternalOutput").ap()

    def declare_io(self):
        c = self.cfg
        T = c.T
        L = DEPTH
        self.i = {}
        I = self.i
        I["xp"] = self.din("xp", [T, D])
        I["pp"] = self.din("pp", [L, T, PLE])
        if c.sample:
            I["xs"] = self.din("xs", [NS, D])
            I["ps"] = self.din("ps", [L, NS, PLE])
            I["st_ret"] = self.din("st_ret", [L, NS, 4, 128, 256])
            I["st_ssm"] = self.din("st_ssm", [L, NS, 16, 128, 64])
            I["st_ssm_conv"] = self.din("st_ssm_conv", [L, NS, 3, 1536])
            I["st_gdn"] = self.din("st_gdn", [L, NS, 8, 128, 128])
            I["st_gdn_conv"] = self.din("st_gdn_conv", [L, NS, 3, 3072])
        for nm, shp in [("ln_g", [L, 4, D]), ("ln_b", [L, 4, D]), ("ffn_wg", [L, 2, D, FFN]),
                        ("ffn_wu", [L, 2, D, FFN]), ("ffn_wd", [L, 2, FFN, D]), ("w_in", [L, D, IN_DIM]),
                        ("ssm_conv_w", [L, 4, 1536]), ("ssm_conv_b", [L, 1536]), ("ssm_dt_bias", [L, 16]),
                        ("ssm_a_log", [L, 16]), ("ssm_d", [L, 16]), ("ssm_norm_w", [L, 1024]),
                        ("gdn_conv_w", [L, 4, 3072]), ("gdn_dt_bias", [L, 8]), ("gdn_a_log", [L, 8]),
                        ("gdn_norm_w", [L, 128]), ("w_ret_out", [L, D, D]), ("w_ssm_out", [L, D, D]),
                        ("w_gdn_out", [L, D, D]), ("w_o", [L, D, D]), ("pe_proj", [L, PLE, D]),
                        ("pe_gate", [L, D, D])]:
            I[nm] = self.din(nm, shp)
        I["c_ident"] = self.din("c_ident", [128, 128])
        I["c_rope"] = self.din("c_rope", [T + NS, 4, 64])
        I["c_retmask"] = self.din("c_retmask", [4, 128, 128])
        I["c_retrow"] = self.din("c_retrow", [4, 3, 128])
        I["c_masks"] = self.din("c_masks", [8, 128, 128])
        self.o = {}
        O = self.o
        O["y_p"] = self.dout("y_p", [T, D])
        O["ret_p"] = self.dout("ret_p", [L, 4, 128, 256])
        O["ssm_p"] = self.dout("ssm_p", [L, 16, 128, 64])
        O["ssm_conv_p"] = self.dout("ssm_conv_p", [L, 3, 1536])
        O["gdn_p"] = self.dout("gdn_p", [L, 8, 128, 128])
        O["gdn_conv_p"] = self.dout("gdn_conv_p", [L, 3, 3072])
        if c.sample:
            O["y_s"] = self.dout("y_s", [NS, D])
            O["ret_s"] = self.dout("ret_s", [L, NS, 4, 128, 256])
            O["ssm_s"] = self.dout("ssm_s", [L, NS, 16, 128, 64])
            O["ssm_conv_s"] = self.dout("ssm_conv_s", [L, NS, 3, 1536])
            O["gdn_s"] = self.dout("gdn_s", [L, NS, 8, 128, 128])
            O["gdn_conv_s"] = self.dout("gdn_conv_s", [L, NS, 3, 3072])

    def alloc(self):
        c, P = self.cfg, self.P
        self.NTT = c.NTH + (1 if c.sample else 0)
        self.TS = c.NTH * 128 + (NS if c.sample else 0)
        NTT, TS = self.NTT, self.TS
        self.xa = P.sbuf("xa", [128, NTT, D], F32, nreg=NTT * 2)
        self.xT = P.sbuf("xT", [128, 8, TS], BF16, nreg=NTT)
        self.hT = P.sbuf("hT", [128, 16, TS], BF16, nreg=16)
        self.NSLOT = 6
        self.wslots = [P.sbuf(f"w{i}", [128, 8, 512], BF16) for i in range(self.NSLOT)]
        self.wi = 0
        self.lnp = P.sbuf("lnp", [128, 2, D], F32)
        self.identb = P.sbuf("identb", [128, 128], BF16)
        self.identf = P.sbuf("identf", [128, 128], F32)
        self.tmpA = [P.sbuf(f"tmpA{i}", [128, D], F32) for i in range(2)]
        self.tmpB = [P.sbuf(f"tmpB{i}", [128, D], F32) for i in range(2)]
        self.xb = [P.sbuf(f"xb{i}", [128, D], BF16) for i in range(1)]
        self.lnst = [P.sbuf(f"lnst{i}", [128, 2, 6], F32) for i in range(2)]
        self.lnmv = [P.sbuf(f"lnmv{i}", [128, 4], F32) for i in range(2)]
        self.ps = P.psum("ps", [128, 8, 512], F32, nreg=8)
        self.psi = 0
        self.rr = {}
        P.dma("pool", self.identb.t[:], self.i["c_ident"], [], [self.identb], self.identb.sem())
        P.dma("sp", self.identf.t[:], self.i["c_ident"], [], [self.identf], self.identf.sem())

    def rot(self, key, lst):
        i = self.rr.get(key, 0)
        self.rr[key] = i + 1
        return lst[i % len(lst)]

    def bank(self):
        b = 2 + self.psi
        self.psi = (self.psi + 1) % 6
        return b

    def bank2(self):
        if self.psi % 2:
            self.psi = (self.psi + 1) % 6
        b = 2 + self.psi
        self.psi = (self.psi + 2) % 6
        return b

    def wslot(self):
        s = self.wslots[self.wi % self.NSLOT]
        self.wi += 1
        return s

    def wload(self, slot, c0, src2d, K=1024):
        ncols = src2d.shape[1]
        kc = K // 128
        self.P.dma("pool", slot.t[:, 0:kc, c0:c0 + ncols], src2d.rearrange("(k p) c -> p k c", p=128),
                   [], [slot], slot.sem())

    def tiles(self):
        c = self.cfg
        out = [(m, 128, m * 128) for m in range(c.NTH)]
        if self.has_sample:
            out.append((c.NTH, NS, c.NTH * 128))
        return out

    def nblocks(self):
        c = self.cfg
        TP = c.NTH * 128
        out = [(n0, min(512, TP - n0)) for n0 in range(0, TP, 512)]
        if self.has_sample:
            out.append((TP, NS))
        return out

    def xT_regs(self, n0, nsz):
        return (self.xT, list(range(n0 // 128, (n0 + nsz - 1) // 128 + 1)))

    @staticmethod
    def run_pipelined(gens, depth):
        it = iter(gens)
        active = []
        done = False
        while True:
            if not done and len(active) < depth:
                try:
                    active.append(next(it))
                except StopIteration:
                    done = True
            if not active:
                if done:
                    break
                continue
            for g in list(active):
                try:
                    next(g)
                except StopIteration:
                    active.remove(g)

    def mm(self, out, lhsT, rhs, start, stop, reads, writes):
        n = int(np.prod(rhs.shape[1:]))
        cyc = max(64, n) * (4 if rhs.dtype == F32 else 1)
        pc = self.P.__dict__.setdefault("pecost", {})
        t = getattr(self.P, "tag", "-")
        pc[t] = pc.get(t, 0) + cyc
        self.P.pe(lambda e: e.matmul(out, lhsT, rhs, start=start, stop=stop), reads, writes)

    def emit_xT(self, m, rows, col0, src, final_out=None):
        P = self.P
        xa, xT, ps = self.xa, self.xT, self.ps
        xb = self.rot("xb", self.xb)
        P.act(lambda e: e.activation(xa.t[:rows, m, :], src.t[:rows, :], AF.Copy, scale=float(DN_ALPHA)),
              [src], [(xa, [2 * m, 2 * m + 1])])
        P.dve(lambda e: e.tensor_copy(xb.t[:rows, :], src.t[:rows, :]), [src], [xb])
        b = self.bank()
        pst = ps.t[:, b, :].bitcast(BF16).rearrange("p (k n) -> p k n", k=8)
        for k in range(8):
            P.pe(lambda e, k=k: e.transpose(pst[:, k, :rows], xb.t[:rows, k * 128:(k + 1) * 128],
                                            self.identb.t[:rows, :rows]),
                 [xb, self.identb], [(ps, b)])
        P.act(lambda e: e.activation(xT.t[:, :, col0:col0 + rows], pst[:, :, :rows], AF.Copy),
              [(ps, b)], [(xT, m)])

    def layer_norm(self, l, idx, final=False):
        self.P.tag = "ln"
        P = self.P
        I = self.i
        lnp = self.lnp
        P.dma("sp", lnp.t[:, 0, :], I["ln_g"][l, idx, :].partition_broadcast(128), [], [lnp], lnp.sem())
        P.dma("sp", lnp.t[:, 1, :], I["ln_b"][l, idx, :].partition_broadcast(128), [], [lnp], lnp.sem())
        xa = self.xa

        def tile_gen(i, m, rows, col0):
            par = i % 2
            st, mv, tA, tB = self.lnst[par], self.lnmv[par], self.tmpA[par], self.tmpB[par]
            xr = (xa, [2 * m, 2 * m + 1])
            P.dve(lambda e: e.bn_stats(st.t[:rows, 0, :], xa.t[:rows, m, 0:512]), [xr], [st])
            P.dve(lambda e: e.bn_stats(st.t[:rows, 1, :], xa.t[:rows, m, 512:1024]), [xr], [st])
            P.dve(lambda e: e.bn_aggr(mv.t[:rows, 0:2], st.t[:rows, :, :]), [st], [mv])
            yield
            P.act(lambda e: e.activation(mv.t[:rows, 2:3], mv.t[:rows, 1:2], AF.Ln, bias=float(LN_EPS)), [mv], [mv])
            P.act(lambda e: e.activation(mv.t[:rows, 3:4], mv.t[:rows, 2:3], AF.Exp, scale=-0.5), [mv], [mv])
            yield
            P.dve(lambda e: e.tensor_scalar(tA.t[:rows, :], xa.t[:rows, m, :], mv.t[:rows, 0:1], mv.t[:rows, 3:4],
                                            ALU.subtract, ALU.mult), [xr, mv], [tA])
            P.dve(lambda e: e.tensor_tensor(tA.t[:rows, :], tA.t[:rows, :], lnp.t[:rows, 0, :], ALU.mult), [tA, lnp], [tA])
            P.dve(lambda e: e.tensor_tensor(tB.t[:rows, :], tA.t[:rows, :], lnp.t[:rows, 1, :], ALU.add), [tA, lnp], [tB])
            yield
            if final:
                self.store_y(m, rows, tB)
            else:
                self.emit_xT(m, rows, col0, tB)

        self.run_pipelined((tile_gen(i, m, rows, col0) for i, (m, rows, col0) in enumerate(self.tiles())), 2)

    def store_y(self, m, rows, src):
        c = self.cfg
        if rows == 128:
            t0 = (self.half * c.NTH + m) * 128
            dst = self.o["y_p"][t0:t0 + 128, :]
        else:
            dst = self.o["y_s"][:, :]
        self.P.dma("sp", dst, src.t[:rows, :], [src], [], src.sem(), is_output=True)

    def load_x(self):
        c = self.cfg
        for (m, rows, col0) in self.tiles():
            tB = self.rot("tmpB", self.tmpB)
            if rows == 128:
                t0 = (self.half * c.NTH + m) * 128
                src = self.i["xp"][t0:t0 + 128, :]
            else:
                src = self.i["xs"][:, :]
            self.P.dma("sp", tB.t[:rows, :], src, [], [tB], tB.sem())
            self.emit_xT(m, rows, col0, tB)

    def ffn(self, l, idx):
        self.P.tag = "ffn"
        P, I = self.P, self.i
        ps, xT, hT, xa = self.ps, self.xT, self.hT, self.xa
        wg, wu, wd = I["ffn_wg"][l, idx], I["ffn_wu"][l, idx], I["ffn_wd"][l, idx]
        slots = []

        def loadA(hb):
            s = self.wslot()
            self.wload(s, 0, wg[:, hb * 256:(hb + 1) * 256])
            self.wload(s, 256, wu[:, hb * 256:(hb + 1) * 256])
            return s

        nxt = loadA(0)
        for hb in range(8):
            cur = nxt
            if hb + 1 < 8:
                nxt = loadA(hb + 1)
            for jj in range(2):
                j = hb * 2 + jj
                for (n0, nsz) in self.nblocks():
                    bg, bu = self.bank(), self.bank()
                    xr = self.xT_regs(n0, nsz)
                    for k in range(8):
                        self.mm(ps.t[:, bg, :nsz], cur.t[:, k, jj * 128:(jj + 1) * 128], xT.t[:, k, n0:n0 + nsz],
                                k == 0, k == 7, [cur, xr], [(ps, bg)])
                    for k in range(8):
                        self.mm(ps.t[:, bu, :nsz], cur.t[:, k, 256 + jj * 128:256 + (jj + 1) * 128],
                                xT.t[:, k, n0:n0 + nsz], k == 0, k == 7, [cur, xr], [(ps, bu)])
                    tA = self.rot("tmpA", self.tmpA)
                    P.act(lambda e, bg=bg, nsz=nsz, tA=tA: e.activation(tA.t[:, :nsz], ps.t[:, bg, :nsz], AF.Silu),
                          [(ps, bg)], [tA])
                    P.dve(lambda e, bu=bu, nsz=nsz, n0=n0, j=j, tA=tA: e.tensor_tensor(
                        hT.t[:, j, n0:n0 + nsz], ps.t[:, bu, :nsz], tA.t[:, :nsz], ALU.mult),
                        [(ps, bu), tA], [(hT, j)])
        def loadB(o):
            s0, s1 = self.wslot(), self.wslot()
            self.P.dma("pool", s0.t[:, :, :], wd[0:1024, o * 512:(o + 1) * 512].rearrange("(k p) c -> p k c", p=128),
                       [], [s0], s0.sem())
            self.P.dma("pool", s1.t[:, :, :], wd[1024:2048, o * 512:(o + 1) * 512].rearrange("(k p) c -> p k c", p=128),
                       [], [s1], s1.sem())
            return (s0, s1)

        nxt = loadB(0)
        for o in range(2):
            cur = nxt
            if o == 0:
                nxt = loadB(1)
            for (m, rows, col0) in self.tiles():
                b = self.bank()
                for j in range(16):
                    s = cur[j // 8]
                    self.mm(ps.t[:rows, b, :], hT.t[:, j, col0:col0 + rows], s.t[:, j % 8, :], j == 0, j == 15,
                            [(hT, j), s], [(ps, b)])
                P.dve(lambda e, m=m, rows=rows, b=b, o=o: e.scalar_tensor_tensor(
                    xa.t[:rows, m, o * 512:(o + 1) * 512], ps.t[:rows, b, :], 0.5,
                    xa.t[:rows, m, o * 512:(o + 1) * 512], ALU.mult, ALU.add),
                    [(ps, b), (xa, 2 * m + o)], [(xa, 2 * m + o)])

    def pe_gate(self, l):
        self.P.tag = "pegate"
        P, I = self.P, self.i
        c = self.cfg
        ps, xT, xa = self.ps, self.xT, self.xa
        sg0, sg1, sp = self.wslot(), self.wslot(), self.wslot()
        self.wload(sg0, 0, I["pe_gate"][l][:, 0:512])
        self.wload(sg1, 0, I["pe_gate"][l][:, 512:1024])
        for o in range(2):
            P.dma("pool", sp.t[:, 2 * o:2 * o + 2, :],
                  I["pe_proj"][l][:, o * 512:(o + 1) * 512].rearrange("(k p) c -> p k c", p=128), [], [sp], sp.sem())
        sg = (sg0, sg1)

        def tile_body(m, rows, col0):
            pb = self.rot("xb", self.xb)
            if rows == 128:
                t0 = (self.half * c.NTH + m) * 128
                src = I["pp"][l, t0:t0 + 128, :]
            else:
                src = I["ps"][l, :, :]
            P.dma("pool", pb.t[:rows, 0:PLE], src, [], [pb], pb.sem())
            b = self.bank()
            pst = ps.t[:, b, :].bitcast(BF16).rearrange("p (k n) -> p k n", k=8)
            for k in range(2):
                P.pe(lambda e, k=k, rows=rows, pb=pb: e.transpose(pst[:, k, :rows], pb.t[:rows, k * 128:(k + 1) * 128],
                                                                self.identb.t[:rows, :rows]),
                     [pb, self.identb], [(ps, b)])
            pT = self.rot("pT", self.pT)
            P.dve(lambda e, rows=rows, pT=pT: e.tensor_copy(pT.t[:, :, :rows], pst[:, 0:2, :rows]), [(ps, b)], [pT])
            tA = self.rot("tmpA", self.tmpA)
            for o in range(2):
                bg, bp = self.bank(), self.bank()
                for k in range(8):
                    self.mm(ps.t[:rows, bg, :], xT.t[:, k, col0:col0 + rows], sg[o].t[:, k, :], k == 0, k == 7,
                            [(xT, m), sg[o]], [(ps, bg)])
                for k in range(2):
                    self.mm(ps.t[:rows, bp, :], pT.t[:, k, :rows], sp.t[:, 2 * o + k, :], k == 0, k == 1,
                            [pT, sp], [(ps, bp)])
                P.act(lambda e, rows=rows, bg=bg, o=o, tA=tA: e.activation(
                    tA.t[:rows, o * 512:(o + 1) * 512], ps.t[:rows, bg, :], AF.Sigmoid), [(ps, bg)], [tA])
                P.dve(lambda e, rows=rows, bp=bp, o=o, tA=tA: e.tensor_tensor(
                    tA.t[:rows, o * 512:(o + 1) * 512], ps.t[:rows, bp, :], tA.t[:rows, o * 512:(o + 1) * 512], ALU.mult),
                    [(ps, bp), tA], [tA])
                P.dve(lambda e, m=m, rows=rows, o=o, tA=tA: e.tensor_tensor(
                    xa.t[:rows, m, o * 512:(o + 1) * 512], xa.t[:rows, m, o * 512:(o + 1) * 512],
                    tA.t[:rows, o * 512:(o + 1) * 512], ALU.add),
                    [tA, (xa, 2 * m + o)], [(xa, 2 * m + o)])

        for (m, rows, col0) in self.tiles():
            tile_body(m, rows, col0)


    def alloc_mix(self):
        P, I = self.P, self.i
        c = self.cfg
        self.S = P.sbuf("S", [128, 1024], F32, nreg=16)
        self.Sb = P.sbuf("Sb", [128, 1024], BF16, nreg=16)
        self.cst = P.sbuf("cst", [128, 8], F32)
        P.dve(lambda e: e.memset(self.cst.t[:, 0:1], float(LN_EPS)), [], [self.cst])
        P.dve(lambda e: e.memset(self.cst.t[:, 1:2], float(NORM_EPS)), [], [self.cst])
        P.dve(lambda e: e.memset(self.cst.t[:, 2:3], -0.5), [], [self.cst])
        P.dve(lambda e: e.memset(self.cst.t[:, 3:4], 1.0), [], [self.cst])
        P.dve(lambda e: e.memset(self.cst.t[:, 4:5], 1.0 / 512.0), [], [self.cst])
        P.dve(lambda e: e.memset(self.cst.t[:, 5:6], 1.0 / 128.0), [], [self.cst])
        self.eyeb = P.sbuf("eyeb", [128, 16, 16], BF16)
        self.eyep = P.sbuf("eyep", [16, 16], F32)
        self.eyef = P.sbuf("eyef", [128, 16, 16], F32)
        P.dma("sp", self.eyef.t[:], I["c_ident"][0:16, 0:16].unsqueeze(0).broadcast_to([128, 16, 16]), [], [self.eyef], self.eyef.sem())
        P.dma("pool", self.eyeb.t[:], I["c_ident"][0:16, 0:16].unsqueeze(0).broadcast_to([128, 16, 16]),
              [], [self.eyeb], self.eyeb.sem())
        P.dma("sp", self.eyep.t[:], I["c_ident"][0:16, 0:16], [], [self.eyep], self.eyep.sem())
        self.masks = P.sbuf("masks", [128, 8, 128], F32)
        P.dma("sp", self.masks.t[:], I["c_masks"].rearrange("m j i -> j m i"), [], [self.masks], self.masks.sem())
        self.A = Arena(P, "arena", 74 * 1024)
        self.tail_ssm = [P.sbuf(f"tail_ssm{l}", [128, 12, 3], F32, nreg=12) for l in range(DEPTH)]
        self.tail_gdn = [P.sbuf(f"tail_gdn{l}", [128, 24, 3], F32, nreg=24) for l in range(DEPTH)]
        L = DEPTH
        self.d_state = {}
        for nm in ("ret_p", "ssm_p", "gdn_p"):
            self.d_state[nm] = Buf(P, "dst_" + nm, self.o[nm], nreg=L)
        self.state_view = {
            "ret_p": lambda ap: ap.rearrange("h k v -> k h v"),
            "ssm_p": lambda ap: ap.rearrange("h n d -> n h d"),
            "gdn_p": lambda ap: ap.rearrange("h k v -> k h v"),
        }

    def state_load(self, nm, l):
        P, S, Sb = self.P, self.S, self.Sb
        db = self.d_state[nm]
        hd = {"ret_p": 4, "ssm_p": 16, "gdn_p": 8}[nm]
        if self.half == 0:
            P.dve(lambda e: e.memset(S.t[:], 0.0), [], [S])
        else:
            P.dma("sp", S.t[:].rearrange("p (h v) -> p h v", h=hd), self.state_view[nm](db.t[l]), [(db, l)], [S], S.sem())
        P.act(lambda e: e.activation(Sb.t[:], S.t[:], AF.Copy), [S], [Sb])

    def state_store(self, nm, l):
        P, S = self.P, self.S
        db = self.d_state[nm]
        hd = {"ret_p": 4, "ssm_p": 16, "gdn_p": 8}[nm]
        P.dma("sp", self.state_view[nm](db.t[l]), S.t[:].rearrange("p (h v) -> p h v", h=hd), [S], [(db, l)], S.sem(),
              is_output=True)

    def sigmoid_chain(self, tmp_ap, src_ap, reads, tmp_buf, scale_in=-1.0, bias_in=0.0):
        P = self.P
        if isinstance(bias_in, float) and bias_in == 0.0:
            P.act(lambda e: e.activation(tmp_ap, src_ap, AF.Exp, scale=scale_in), reads, [tmp_buf])
        else:
            P.act(lambda e: e.activation(tmp_ap, src_ap, AF.Exp, scale=scale_in, bias=bias_in), reads, [tmp_buf])
        P.act(lambda e: e.activation(tmp_ap, tmp_ap, AF.Ln, bias=1.0), [tmp_buf], [tmp_buf])
        P.act(lambda e: e.activation(tmp_ap, tmp_ap, AF.Exp, scale=-1.0), [tmp_buf], [tmp_buf])

    def rstd_pool(self, nst, rows, eps_col, scale_col=None):
        P, cst = self.P, self.cst
        src = nst.t[:rows, 1:2]
        if scale_col is not None:
            P.pool(lambda e: e.tensor_tensor(nst.t[:rows, 2:3], src, cst.t[:rows, scale_col:scale_col + 1], ALU.mult),
                   [nst, cst], [nst])
            src = nst.t[:rows, 2:3]
        P.pool(lambda e: e.tensor_tensor(nst.t[:rows, 2:3], src, cst.t[:rows, eps_col:eps_col + 1], ALU.add),
               [nst, cst], [nst])
        P.pool(lambda e: e.tensor_tensor(nst.t[:rows, 3:4], nst.t[:rows, 2:3], cst.t[:rows, 2:3], ALU.pow),
               [nst, cst], [nst])

    def transposes_to(self, dst_ap_fn, src_fn, n, rows, src_buf, dst_writes, bt=None):
        P, ps = self.P, self.ps
        bt = self.bank() if bt is None else bt
        pst = ps.t[:, bt, :].bitcast(BF16).rearrange("p (k n) -> p k n", k=8)
        for k in range(n):
            P.pe(lambda e, k=k: e.transpose(pst[:, k, :rows], src_fn(k), self.identb.t[:rows, :rows]),
                 [src_buf, self.identb], [(ps, bt)])
        dst_ap_fn(pst, bt)

    def retention(self, l):
        self.P.tag = "ret"
        P, I, c = self.P, self.i, self.cfg
        ps, xT, hT = self.ps, self.xT, self.hT
        w_in = I["w_in"][l]
        S, Sb, A = self.S, self.Sb, self.A
        A.reset()
        retmask = A.alloc("retmask", [4, 128], F32)
        retrow = A.alloc("retrow", [4, 128], F32)
        retcol = A.alloc("retcol", [4], F32)
        P.dma("sp", retmask.t[:], I["c_retmask"].rearrange("h j i -> j h i"), [], [retmask], retmask.sem())
        for h in range(4):
            P.dma("sp", retrow.t[:, h, :], I["c_retrow"][h, 0, :].partition_broadcast(128), [], [retrow], retrow.sem())
        P.dma("sp", retcol.t[:], I["c_retrow"][:, 1, :].rearrange("h j -> j h"), [], [retcol], retcol.sem(),
              allow_slow_non_contiguous=True)
        rope_b = [A.alloc("rope", [4, 64], F32) for _ in range(2)]
        qkr_b = [A.alloc("qkr", [2, 2, 128], BF16) for _ in range(2)]
        rt_b = [A.alloc("rt", [4, 256], F32, nreg=4) for _ in range(2)]
        qkT_b = [A.alloc("qkT", [4, 128], BF16) for _ in range(2)]
        qdT_b = [A.alloc("qdT", [128], BF16) for _ in range(4)]
        kdec_b = [A.alloc("kdec", [128], BF16) for _ in range(4)]
        vbf_b = [A.alloc("vbf", [256], BF16) for _ in range(4)]
        sgt_b = [A.alloc("sgt", [256], F32) for _ in range(4)]
        scm_b = [A.alloc("scm", [128], BF16) for _ in range(4)]
        ogt_b = [A.alloc("ogt", [256], F32) for _ in range(4)]
        ogb_b = [A.alloc("ogb", [256], BF16) for _ in range(2)]
        nst_b = [A.alloc("nst", [8], F32) for _ in range(4)]
        st6_b = [A.alloc("st6", [6], F32) for _ in range(4)]
        if self.has_sample:
            qTm = A.alloc("qTm", [16, 16], BF16)
            ktm = A.alloc("ktm", [16, 128], BF16)
            Ss_b = [A.alloc("Ss", [2, 256], F32) for _ in range(2)]
            Ssb_b = [A.alloc("Ssb", [256], BF16) for _ in range(2)]
        self.state_load("ret_p", l)
        lg = [float(np.float64(np.log1p(-np.float32(2.0) ** np.float32(-5.0 - h)).astype(np.float32))) for h in range(4)]

        def load_pair(pr):
            sA, sB, sC = self.wslot(), self.wslot(), self.wslot()
            h0, h1 = 2 * pr, 2 * pr + 1
            for i, h in enumerate((h0, h1)):
                self.wload(sA, i * 256, w_in[:, C_RQ + h * 128:C_RQ + (h + 1) * 128])
                self.wload(sA, i * 256 + 128, w_in[:, C_RK + h * 128:C_RK + (h + 1) * 128])
            for s_, h in ((sB, h0), (sC, h1)):
                self.wload(s_, 0, w_in[:, C_RV + h * 256:C_RV + (h + 1) * 256])
                self.wload(s_, 256, w_in[:, C_RG + h * 256:C_RG + (h + 1) * 256])
            return (sA, sB, sC)

        def tile_gen(i, pr, W, m, rows, col0):
            par = i % 2
            X, Y, Z = 2 + 3 * par, 3 + 3 * par, 4 + 3 * par
            sA = W[0]
            is_s = rows != 128
            t0 = (self.half * c.NTH * 128 + col0) if not is_s else c.T
            rope, rt, qkr, qkT = rope_b[par], rt_b[par], qkr_b[par], qkT_b[par]
            P.dma("sp", rope.t[:rows], I["c_rope"][t0:t0 + rows], [], [rope], rope.sem())
            for k in range(8):
                self.mm(ps.t[:rows, X, :], xT.t[:, k, col0:col0 + rows], sA.t[:, k, :], k == 0, k == 7,
                        [(xT, m), sA], [(ps, X)])
            qk5 = ps.t[:rows, X, :].rearrange("p (h a b f) -> p h a b f", h=2, a=2, b=2)
            x1, x2 = qk5[:, :, :, 0, :], qk5[:, :, :, 1, :]
            rp = rope.t[:rows].rearrange("p (a b) f -> p a b f", a=2)
            cos = rp[:, :, 0, :].unsqueeze(1).broadcast_to([rows, 2, 2, 64])
            sin = rp[:, :, 1, :].unsqueeze(1).broadcast_to([rows, 2, 2, 64])
            tv = [rt.t[:rows, j, :].rearrange("p (h a f) -> p h a f", h=2, a=2) for j in range(4)]
            P.dve(lambda e: e.tensor_tensor(tv[0], x1, cos, ALU.mult), [(ps, X), rope], [(rt, 0)])
            P.dve(lambda e: e.tensor_tensor(tv[1], x2, sin, ALU.mult), [(ps, X), rope], [(rt, 1)])
            P.dve(lambda e: e.tensor_tensor(tv[2], x1, sin, ALU.mult), [(ps, X), rope], [(rt, 2)])
            P.dve(lambda e: e.tensor_tensor(tv[3], x2, cos, ALU.mult), [(ps, X), rope], [(rt, 3)])
            P.dve(lambda e: e.tensor_tensor(qkr.t[:rows, :, :, 0:64], tv[0], tv[1], ALU.subtract), [(rt, 0), (rt, 1)], [qkr])
            P.dve(lambda e: e.tensor_tensor(qkr.t[:rows, :, :, 64:128], tv[2], tv[3], ALU.add), [(rt, 2), (rt, 3)], [qkr])
            yield

            def evac(pst, bt):
                P.act(lambda e: e.activation(qkT.t[:, :, :rows], pst[:, 0:4, :rows], AF.Copy), [(ps, bt)], [qkT])

            self.transposes_to(evac, lambda k: qkr.t[:rows, k // 2, k % 2, :], 4, rows, qkr, None, bt=Y)
            yield
            for hh in range(2):
                yield from head_gen(par, (X, Y, Z), pr, W, m, rows, col0, hh, qkr, qkT)

        def head_gen(par, banks, pr, W, m, rows, col0, hh, qkr, qkT):
            X, Y, Z = banks
            h = 2 * pr + hh
            sV = W[1 + hh]
            is_s = rows != 128
            gam = float(np.exp(lg[h]))
            bi = 2 * par + hh
            vbf, sgt, qdT, kdec, scm, ogt, nst, st6 = (vbf_b[bi], sgt_b[bi], qdT_b[bi], kdec_b[bi], scm_b[bi], ogt_b[bi],
                                                      nst_b[bi], st6_b[bi])
            ogb = ogb_b[par]
            for k in range(8):
                self.mm(ps.t[:rows, X, :], xT.t[:, k, col0:col0 + rows], sV.t[:, k, :], k == 0, k == 7, [(xT, m), sV], [(ps, X)])
            P.act(lambda e: e.activation(vbf.t[:rows, :], ps.t[:rows, X, 0:256], AF.Copy), [(ps, X)], [vbf])
            self.sigmoid_chain(sgt.t[:rows, :], ps.t[:rows, X, 256:512], [(ps, X)], sgt)
            yield
            P.dve(lambda e: e.tensor_tensor(sgt.t[:rows, :], ps.t[:rows, X, 256:512], sgt.t[:rows, :], ALU.mult), [(ps, X), sgt], [sgt])
            bo = Z if not is_s else 0
            Sr = (S, list(range(4 * h, 4 * h + 4)))
            Sbr = (Sb, list(range(4 * h, 4 * h + 4)))
            if not is_s:
                P.dve(lambda e: e.tensor_tensor(qdT.t[:, :], qkT.t[:, 2 * hh, :], retrow.t[:, h, :], ALU.mult), [qkT, retrow], [qdT])
                P.dve(lambda e: e.tensor_scalar(kdec.t[:, :], qkr.t[:, hh, 1, :], retcol.t[:, h:h + 1], None, ALU.mult), [qkr, retcol], [kdec])
                self.mm(ps.t[:, Y, 0:128], qkT.t[:, 2 * hh + 1, :], qkT.t[:, 2 * hh, :], True, True, [qkT], [(ps, Y)])
                yield
                P.dve(lambda e: e.tensor_tensor(scm.t[:, :], ps.t[:, Y, 0:128], retmask.t[:, h, :], ALU.mult), [(ps, Y), retmask], [scm])
                yield
                self.mm(ps.t[:, bo, 0:256], scm.t[:, :], vbf.t[:, :], True, False, [scm, vbf], [(ps, bo)])
                self.mm(ps.t[:, bo, 0:256], qdT.t[:, :], Sb.t[:, h * 256:(h + 1) * 256], False, True, [qdT, Sbr], [(ps, bo)])
                self.mm(ps.t[:, Y, 0:256], kdec.t[:, :], vbf.t[:, :], True, True, [kdec, vbf], [(ps, Y)])
                yield
                P.dve(lambda e: e.scalar_tensor_tensor(S.t[:, h * 256:(h + 1) * 256], S.t[:, h * 256:(h + 1) * 256],
                                                       float(np.exp(lg[h] * 128)), ps.t[:, Y, 0:256], ALU.mult, ALU.add), [Sr, (ps, Y)], [Sr])
                P.act(lambda e: e.activation(Sb.t[:, h * 256:(h + 1) * 256], S.t[:, h * 256:(h + 1) * 256], AF.Copy), [Sr], [Sbr])
            else:
                P.dve(lambda e: e.tensor_tensor(qTm.t[:], qkT.t[:, 2 * hh, 0:16].unsqueeze(1).broadcast_to([128, 16, 16]),
                                                self.eyeb.t[:], ALU.mult), [qkT, self.eyeb], [qTm])
                P.dve(lambda e: e.tensor_tensor(ktm.t[0:16], qkr.t[0:16, hh, 1, :].unsqueeze(1).broadcast_to([16, 16, 128]),
                                                self.eyep.t[:, :].unsqueeze(2).broadcast_to([16, 16, 128]), ALU.mult), [qkr, self.eyep], [ktm])
                self.run_pipelined((self.ret_sample(l, h, s_, gam, vbf, bo, Ss_b[s_ % 2], Ssb_b[s_ % 2], qTm, ktm, (Y, X)[s_ % 2])
                                    for s_ in range(NS)), 2)
            P.dve(lambda e: e.bn_stats(st6.t[:rows, :], ps.t[:rows, bo, 0:256]), [(ps, bo)], [st6])
            P.dve(lambda e: e.bn_aggr(nst.t[:rows, 0:2], st6.t[:rows, :]), [st6], [nst])
            yield
            self.rstd_pool(nst, rows, 0)
            yield
            P.dve(lambda e: e.tensor_scalar(ogt.t[:rows, :], ps.t[:rows, bo, 0:256], nst.t[:rows, 0:1], nst.t[:rows, 3:4],
                                            ALU.subtract, ALU.mult), [(ps, bo), nst], [ogt])
            P.dve(lambda e: e.tensor_tensor(ogb.t[:rows, :], ogt.t[:rows, :], sgt.t[:rows, :], ALU.mult), [ogt, sgt], [ogb])
            yield

            def evac(pst, bt):
                P.act(lambda e: e.activation(hT.t[:, 2 * h:2 * h + 2, col0:col0 + rows], pst[:, 0:2, :rows], AF.Copy),
                      [(ps, bt)], [(hT, [2 * h, 2 * h + 1])])

            self.transposes_to(evac, lambda k: ogb.t[:rows, k * 128:(k + 1) * 128], 2, rows, ogb, None, bt=X)
            yield

        nxt = load_pair(0)
        for pr in range(2):
            W = nxt
            if pr == 0:
                nxt = load_pair(1)
            self.run_pipelined((tile_gen(i, pr, W, m, rows, col0) for i, (m, rows, col0) in enumerate(self.tiles())), 2)
        self.state_store("ret_p", l)

    def ret_sample(self, l, h, s_, gam, vbf, bo, Ss, Ssb, qTm, ktm, bd):
        P, I, ps = self.P, self.i, self.ps
        P.dma("sp", Ss.t[:, 0, :], I["st_ret"][l, s_, h], [], [Ss], Ss.sem())
        self.mm(ps.t[:, bd, 0:256], ktm.t[0:16, s_, :], vbf.t[0:16, :], True, True, [ktm, vbf], [(ps, bd)])
        yield
        P.dve(lambda e: e.scalar_tensor_tensor(Ss.t[:, 1, :], Ss.t[:, 0, :], gam, ps.t[:, bd, 0:256], ALU.mult, ALU.add),
              [Ss, (ps, bd)], [Ss])
        yield
        P.dma("sp", self.o["ret_s"][l, s_, h], Ss.t[:, 1, :], [Ss], [], Ss.sem(), is_output=True)
        P.act(lambda e: e.activation(Ssb.t[:, :], Ss.t[:, 1, :], AF.Copy), [Ss], [Ssb])
        yield
        self.mm(ps.t[0:16, bo, 0:256], qTm.t[:, s_, :], Ssb.t[:, :], s_ == 0, s_ == NS - 1, [qTm, Ssb], [(ps, bo)])

    def finale(self, l, w_out, mcol):
        self.P.tag = "finale"
        P, I = self.P, self.i
        ps, xT, hT, xa = self.ps, self.xT, self.hT, self.xa
        w_in, w_o = I["w_in"][l], I["w_o"][l]
        A = self.A
        A.reset()
        gT_b = [A.alloc("gT", [8, 128], BF16) for _ in range(2)]
        gtok_b = [A.alloc("gtok", [D], BF16) for _ in range(2)]
        Wout = (self.wslot(), self.wslot())
        Wm = (self.wslot(), self.wslot())
        Wo = (self.wslot(), self.wslot())
        for o in range(2):
            self.wload(Wout[o], 0, w_out[:, o * 512:(o + 1) * 512])
            self.wload(Wm[o], 0, w_in[:, mcol + o * 512:mcol + (o + 1) * 512])
            self.wload(Wo[o], 0, w_o[:, o * 512:(o + 1) * 512])

        def tile_gen(i, m, rows, col0):
            par = i % 2
            by, bm = 4 * par, 4 * par + 2
            tA, gtok, gT = self.tmpA[par], gtok_b[par], gT_b[par]
            for o in range(2):
                for k in range(8):
                    self.mm(ps.t[:rows, bm + o, :], xT.t[:, k, col0:col0 + rows], Wm[o].t[:, k, :], k == 0, k == 7,
                            [(xT, m), Wm[o]], [(ps, bm + o)])
            for o in range(2):
                for k in range(8):
                    self.mm(ps.t[:rows, by + o, :], hT.t[:, k, col0:col0 + rows], Wout[o].t[:, k, :], k == 0, k == 7,
                            [(hT, k), Wout[o]], [(ps, by + o)])
            pm = ps.t[:rows, bm:bm + 2, :].rearrange("p b n -> p (b n)")
            py = ps.t[:rows, by:by + 2, :].rearrange("p b n -> p (b n)")
            self.sigmoid_chain(tA.t[:rows, :], pm, [(ps, bm), (ps, bm + 1)], tA)
            yield
            P.dve(lambda e: e.tensor_tensor(gtok.t[:rows, :], py, tA.t[:rows, :], ALU.mult), [(ps, by), (ps, by + 1), tA], [gtok])
            yield

            def evac(pst, bt):
                P.act(lambda e: e.activation(gT.t[:, :, :rows], pst[:, :, :rows], AF.Copy), [(ps, bt)], [gT])

            self.transposes_to(evac, lambda k: gtok.t[:rows, k * 128:(k + 1) * 128], 8, rows, gtok, None, bt=bm)
            yield
            for o in range(2):
                bo = by + o
                for k in range(8):
                    self.mm(ps.t[:rows, bo, :], gT.t[:, k, :rows], Wo[o].t[:, k, :], k == 0, k == 7, [gT, Wo[o]], [(ps, bo)])
            yield
            for o in range(2):
                bo = by + o
                P.dve(lambda e, o=o, bo=bo: e.tensor_tensor(xa.t[:rows, m, o * 512:(o + 1) * 512],
                                                            xa.t[:rows, m, o * 512:(o + 1) * 512], ps.t[:rows, bo, :], ALU.add),
                      [(ps, bo), (xa, 2 * m + o)], [(xa, 2 * m + o)])

        self.run_pipelined((tile_gen(i, m, rows, col0) for i, (m, rows, col0) in enumerate(self.tiles())), 2)

    def colvecs(self, dst_ap, tk, r, c, dst_buf):
        P, ps = self.P, self.ps
        b = self.bank()
        for cc in range(c):
            P.pe(lambda e, cc=cc: e.transpose(ps.t[:, b, cc * r:(cc + 1) * r], tk.t[:r, cc * 128:(cc + 1) * 128],
                                              self.identf.t[:r, :r]), [tk, self.identf], [(ps, b)])
        P.act(lambda e: e.activation(dst_ap, ps.t[:, b, 0:c * r], AF.Copy), [(ps, b)], [dst_buf])

    def ssd(self, l):
        self.P.tag = "ssd"
        P, I, c = self.P, self.i, self.cfg
        ps, xT, hT, S, Sb, A = self.ps, self.xT, self.hT, self.S, self.Sb, self.A
        w_in = I["w_in"][l]
        TP = c.NTH * 128
        assert TP <= 512
        hs = self.has_sample
        masks = self.masks
        U, ONES, MB = masks.t[:, 0, :], masks.t[:, 1, :], masks.t[:, 2, :]
        A.reset()
        cwb = A.alloc("cwb", [12, 5], F32)
        negb = A.alloc("negb", [12], F32)
        nwT = A.alloc("nwT", [8], F32)
        dtb = A.alloc("dtb", [16], F32)
        Ab = A.alloc("Ab", [16], F32)
        Db = A.alloc("Db", [16], F32)
        wdt = A.alloc("wdt", [8, 16], BF16)
        off0 = A.off
        tk = A.alloc("tk", [1536], F32)
        tk2 = A.alloc("tk2", [1024], F32)
        P.dma("sp", tk.t[0:4, :], I["ssm_conv_w"][l], [], [tk], tk.sem())
        P.dma("sp", tk.t[4:5, :], I["ssm_conv_b"][l].unsqueeze(0), [], [tk], tk.sem())
        self.colvecs(cwb.t[:].rearrange("p c j -> p (c j)"), tk, 5, 12, cwb)
        P.act(lambda e: e.activation(negb.t[:, :], cwb.t[:, :, 4], AF.Copy, scale=-1.0), [cwb], [negb])
        P.dma("sp", tk2.t[0:1, :], I["ssm_norm_w"][l].unsqueeze(0), [], [tk2], tk2.sem())
        self.colvecs(nwT.t[:, :], tk2, 1, 8, nwT)
        P.dma("sp", dtb.t[:], I["ssm_dt_bias"][l].partition_broadcast(128), [], [dtb], dtb.sem())
        P.dma("sp", Ab.t[:], I["ssm_a_log"][l].partition_broadcast(128), [], [Ab], Ab.sem())
        P.dma("sp", Db.t[:], I["ssm_d"][l].partition_broadcast(128), [], [Db], Db.sem())
        P.act(lambda e: e.activation(Ab.t[:], Ab.t[:], AF.Exp), [Ab], [Ab])
        P.act(lambda e: e.activation(Ab.t[:], Ab.t[:], AF.Copy, scale=-1.0), [Ab], [Ab])
        P.dma("pool", wdt.t[:], w_in[:, C_MDT:C_MDT + 16].rearrange("(k p) c -> p k c", p=128), [], [wdt], wdt.sem())
        A.reset(off0)
        xbc = A.alloc("xbc", [12, TP], BF16, nreg=12)
        xraw_b = [A.alloc("xraw", [3 + TP], F32) for _ in range(2)]
        acc_b = [A.alloc("acc", [TP], F32) for _ in range(2)]
        sgm_b = [A.alloc("sgm", [TP], F32) for _ in range(2)]
        f1 = A.alloc("f1", [1024], F32)
        f2 = A.alloc("f2", [1024], F32)
        f3 = A.alloc("f3", [1024], F32)
        xs_sb = A.alloc("xs_sb", [1024], BF16)
        v = A.alloc("v", [1024], BF16)
        vdec = A.alloc("vdec", [1024], BF16)
        Mh_b = [A.alloc("Mh", [8, 128], BF16) for _ in range(2)]
        Bd_b = [A.alloc("Bd", [8, 128], F32) for _ in range(2)]
        Btok = A.alloc("Btok", [2, 128], BF16)
        ogb = A.alloc("ogb", [1024], BF16)
        sm = A.alloc("sm", [8, 16], F32)
        cumT = A.alloc("cumT", [128], F32)
        nst = A.alloc("nst", [8], F32)
        stg = A.alloc("stg", [512], F32)
        if hs:
            xbcS = A.alloc("xbcS", [12, 16], BF16)
            stc_b = [A.alloc("stc", [3, 128], F32) for _ in range(2)]
            xrs_b = [A.alloc("xrs", [4, 16], F32) for _ in range(2)]
            accS_b = [A.alloc("accS", [16], F32) for _ in range(2)]
            sgS_b = [A.alloc("sgS", [16], F32) for _ in range(2)]
            Eall = A.alloc("Eall", [16, 16], F32)
            Bde = A.alloc("Bde", [16, 16], F32)
            Bm_b = [A.alloc("Bm", [256], BF16) for _ in range(2)]
            Cm_b = [A.alloc("Cm", [2, 16], BF16) for _ in range(2)]
        tail = self.tail_ssm[l]
        if self.half == 0:
            P.dve(lambda e: e.memset(tail.t[:], 0.0), [], [tail])
        self.state_load("ssm_p", l)
        Wx = [self.wslot() for _ in range(3)]
        for i in range(3):
            self.wload(Wx[i], 0, w_in[:, C_MXBC + i * 512:C_MXBC + (i + 1) * 512])
        Wz = (self.wslot(), self.wslot())
        for o in range(2):
            self.wload(Wz[o], 0, w_in[:, C_MZ + o * 512:C_MZ + (o + 1) * 512])

        def conv_chunk(cc):
            slot = Wx[cc // 4]
            cs = (cc % 4) * 128
            par = cc % 2
            acc, sgm = acc_b[par], sgm_b[par]
            b = 2 + par
            for k in range(8):
                self.mm(ps.t[:, b, :TP], slot.t[:, k, cs:cs + 128], xT.t[:, k, 0:TP], k == 0, k == 7,
                        [slot, (xT, list(range(c.NTH)))], [(ps, b)])
            xraw = xraw_b[par]
            P.act(lambda e: e.activation(xraw.t[:, 0:3], tail.t[:, cc, :], AF.Copy), [(tail, cc)], [xraw])
            P.act(lambda e: e.activation(xraw.t[:, 3:3 + TP], ps.t[:, b, :TP], AF.Copy), [(ps, b)], [xraw])
            P.act(lambda e: e.activation(tail.t[:, cc, :], xraw.t[:, TP:TP + 3], AF.Copy), [xraw], [(tail, cc)])
            yield
            P.dve(lambda e: e.tensor_scalar(acc.t[:, :], xraw.t[:, 0:TP], cwb.t[:, cc, 0:1], None, ALU.mult), [xraw, cwb], [acc])
            for j in range(1, 4):
                P.dve(lambda e, j=j: e.scalar_tensor_tensor(acc.t[:, :], xraw.t[:, j:j + TP], cwb.t[:, cc, j:j + 1], acc.t[:, :],
                                                            ALU.mult, ALU.add), [xraw, cwb, acc], [acc])
            yield
            self.sigmoid_chain(sgm.t[:, :], acc.t[:, :], [acc, negb], sgm, scale_in=-1.0, bias_in=negb.t[:, cc:cc + 1])
            yield
            P.dve(lambda e: e.scalar_tensor_tensor(xbc.t[:, cc, :], acc.t[:, :], cwb.t[:, cc, 4:5], sgm.t[:, :], ALU.add, ALU.mult),
                  [acc, cwb, sgm], [(xbc, cc)])
            if hs:
                yield
                b2 = 4 + par
                for k in range(8):
                    self.mm(ps.t[:, b2, 0:16], slot.t[:, k, cs:cs + 128], xT.t[:, k, TP:TP + 16], k == 0, k == 7,
                            [slot, (xT, c.NTH)], [(ps, b2)])
                stc = stc_b[par]
                P.dma("sp", stc.t[0:16, :, :], I["st_ssm_conv"][l, :, :, cc * 128:(cc + 1) * 128], [], [stc], stc.sem())
                for j in range(3):
                    P.pe(lambda e, j=j: e.transpose(ps.t[:, b2, 16 + 16 * j:32 + 16 * j], stc.t[0:16, j, :], self.identf.t[0:16, 0:16]),
                         [stc, self.identf], [(ps, b2)])
                xrs = xrs_b[par]
                P.act(lambda e: e.activation(xrs.t[:, 0:3, :], ps.t[:, b2, 16:64].rearrange("p (j s) -> p j s", j=3), AF.Copy),
                      [(ps, b2)], [xrs])
                P.act(lambda e: e.activation(xrs.t[:, 3, :], ps.t[:, b2, 0:16], AF.Copy), [(ps, b2)], [xrs])
                accS, sgS = accS_b[par], sgS_b[par]
                P.dve(lambda e: e.tensor_scalar(accS.t[:, :], xrs.t[:, 0, :], cwb.t[:, cc, 0:1], None, ALU.mult), [xrs, cwb], [accS])
                for j in range(1, 4):
                    P.dve(lambda e, j=j: e.scalar_tensor_tensor(accS.t[:, :], xrs.t[:, j, :], cwb.t[:, cc, j:j + 1], accS.t[:, :],
                                                                ALU.mult, ALU.add), [xrs, cwb, accS], [accS])
                yield
                self.sigmoid_chain(sgS.t[:, :], accS.t[:, :], [accS, negb], sgS, scale_in=-1.0, bias_in=negb.t[:, cc:cc + 1])
                yield
                P.dve(lambda e: e.scalar_tensor_tensor(xbcS.t[:, cc, :], accS.t[:, :], cwb.t[:, cc, 4:5], sgS.t[:, :], ALU.add, ALU.mult),
                      [accS, cwb, sgS], [xbcS])

        self.run_pipelined((conv_chunk(cc) for cc in range(12)), 2)
        def conv_rows(c0, n, dst_fn):
            for i in range(3):
                b = self.bank()
                for k in range(8):
                    self.mm(ps.t[:n, b, :], xT.t[:, k, c0:c0 + n], Wx[i].t[:, k, :], k == 0, k == 7,
                            [Wx[i], (xT, list(range(self.NTT)))], [(ps, b)])
                P.act(lambda e, b=b: e.activation(stg.t[:n, :], ps.t[:n, b, :], AF.Copy), [(ps, b)], [stg])
                P.dma("sp", dst_fn(i), stg.t[:n, :], [stg], [], stg.sem(), is_output=True)

        if self.half == c.NH - 1:
            conv_rows(TP - 3, 3, lambda i: self.o["ssm_conv_p"][l, :, i * 512:(i + 1) * 512])
        if hs:
            conv_rows(TP, 16, lambda i: self.o["ssm_conv_s"][l, :, 2, i * 512:(i + 1) * 512])
            P.dma("sp", self.o["ssm_conv_s"][l, :, 0:2, :], I["st_ssm_conv"][l, :, 1:3, :], [], [], stg.sem(), is_output=True)

        def tile_body(m, rows, col0):
            is_s = rows != 128
            src = xbcS if is_s else xbc
            sc0 = 0 if is_s else col0
            DT, LA, CUM, ETOK, ELAST, DECL, T16 = [sm.t[:rows, i, :] for i in range(7)]
            bdt = 6
            for k in range(8):
                self.mm(ps.t[:rows, bdt, 0:16], xT.t[:, k, col0:col0 + rows], wdt.t[:, k, :], k == 0, k == 7, [(xT, m), wdt], [(ps, bdt)])
            P.dve(lambda e: e.tensor_tensor(T16, ps.t[:rows, bdt, 0:16], dtb.t[:rows, :], ALU.add), [(ps, bdt), dtb], [sm])
            P.act(lambda e: e.activation(T16, T16, AF.Exp), [sm], [sm])
            P.act(lambda e: e.activation(DT, T16, AF.Ln, bias=1.0), [sm], [sm])
            P.dve(lambda e: e.tensor_tensor(LA, DT, Ab.t[:rows, :], ALU.mult), [sm, Ab], [sm])
            def evac_xs(pst, bt):
                P.act(lambda e: e.activation(xs_sb.t[:rows, :], pst[:rows, :, :].rearrange("p k n -> p (k n)"), AF.Copy), [(ps, bt)], [xs_sb])
            self.transposes_T(evac_xs, lambda k: src.t[:, k, sc0:sc0 + rows], 8, rows, src, bt=7)
            if is_s and l == 0 and self.cfg.dbg.get("ssd_dump"):
                dx = self.nc.dram_tensor("dbg_xs", [16, 1024], F32, kind="ExternalOutput").ap()
                dd = self.nc.dram_tensor("dbg_dt", [16, 16], F32, kind="ExternalOutput").ap()
                P.dma("pool", dx, xs_sb.t[0:16, :], [xs_sb], [], xs_sb.sem(), is_output=True)
                P.dma("sp", dd, sm.t[0:16, 0, :], [sm], [], sm.sem(), is_output=True)
            dt_b = DT.unsqueeze(2).broadcast_to([rows, 16, 64])
            P.dve(lambda e: e.tensor_tensor(v.t[:rows, :].rearrange("p (h d) -> p h d", h=16),
                                            xs_sb.t[:rows, :].rearrange("p (h d) -> p h d", h=16), dt_b, ALU.mult), [xs_sb, sm], [v])
            def evac_b(pst, bt):
                P.act(lambda e: e.activation(Btok.t[:rows, :, :], pst[:rows, 0:2, :], AF.Copy), [(ps, bt)], [Btok])
            self.transposes_T(evac_b, lambda k: src.t[:, 8 + k, sc0:sc0 + rows], 2, rows, src, bt=7)
            if not is_s:
                bc = 6
                self.mm(ps.t[:, bc, 32:48], U, LA, True, True, [masks, sm], [(ps, bc)])
                self.mm(ps.t[:, bc, 48:64], ONES, LA, True, True, [masks, sm], [(ps, bc)])
                self.mm(ps.t[0:16, bc, 64:192], LA, U, True, True, [masks, sm], [(ps, bc)])
                P.act(lambda e: e.activation(CUM, ps.t[:, bc, 32:48], AF.Copy), [(ps, bc)], [sm])
                P.act(lambda e: e.activation(ETOK, ps.t[:, bc, 32:48], AF.Exp), [(ps, bc)], [sm])
                P.act(lambda e: e.activation(ELAST, ps.t[:, bc, 48:64], AF.Exp), [(ps, bc)], [sm])
                P.dve(lambda e: e.tensor_tensor(DECL, ps.t[:, bc, 48:64], CUM, ALU.subtract), [(ps, bc), sm], [sm])
                P.act(lambda e: e.activation(DECL, DECL, AF.Exp), [sm], [sm])
                P.act(lambda e: e.activation(cumT.t[0:16, :], ps.t[0:16, bc, 64:192], AF.Copy), [(ps, bc)], [cumT])
                P.dve(lambda e: e.tensor_tensor(vdec.t[:, :].rearrange("p (h d) -> p h d", h=16),
                                                v.t[:, :].rearrange("p (h d) -> p h d", h=16),
                                                DECL.unsqueeze(2).broadcast_to([128, 16, 64]), ALU.mult), [v, sm], [vdec])
                bsc = 6
                for g in range(2):
                    self.mm(ps.t[:, bsc, 256 + g * 128:256 + (g + 1) * 128], xbc.t[:, 8 + g, col0:col0 + 128], xbc.t[:, 10 + g, col0:col0 + 128],
                            True, True, [xbc], [(ps, bsc)])
                po, pcs, pds = 0, 4, 2
                for g in range(2):
                    self.mm(ps.t[:, pcs + g, :], xbc.t[:, 10 + g, col0:col0 + 128], Sb.t[:, g * 512:(g + 1) * 512], True, True,
                            [xbc, (Sb, list(range(8 * g, 8 * g + 8)))], [(ps, pcs + g)])
                P.dve(lambda e: e.tensor_tensor(f2.t[:, :].rearrange("p (h d) -> p h d", h=16),
                                                ps.t[:, pcs:pcs + 2, :].rearrange("p b (h d) -> p (b h) d", h=8),
                                                ETOK.unsqueeze(2).broadcast_to([128, 16, 64]), ALU.mult), [(ps, pcs), (ps, pcs + 1), sm], [f2])
                def grp(g):
                    Bdg, fg, Mh = Bd_b[g], (f1 if g == 0 else f3), Mh_b[g]
                    pa = 2 + 2 * g
                    P.dve(lambda e: e.tensor_tensor(Bdg.t[0:16, :, :], cumT.t[0:16, :].unsqueeze(1).broadcast_to([16, 8, 128]),
                                                    self.eyep.t[:, 8 * g:8 * g + 8].unsqueeze(2).broadcast_to([16, 8, 128]), ALU.mult),
                          [cumT, self.eyep], [Bdg])
                    yield
                    for hf in range(2):
                        self.mm(ps.t[:, pa + hf, :], ONES[0:16, :], Bdg.t[0:16, 4 * hf:4 * hf + 4, :].rearrange("p h i -> p (h i)"),
                                True, False, [masks, Bdg], [(ps, pa + hf)])
                        for hq in range(4):
                            self.mm(ps.t[:, pa + hf, hq * 128:(hq + 1) * 128], self.identf.t[:, :], MB, False, hq == 3,
                                    [masks, self.identf], [(ps, pa + hf)])
                    yield
                    pav = ps.t[:, pa:pa + 2, :].rearrange("p b (h i) -> p (b h) i", h=4)
                    P.dve(lambda e: e.tensor_tensor(fg.t[:, :].rearrange("p (h i) -> p h i", h=8), pav,
                                                    CUM[:, 8 * g:8 * g + 8].unsqueeze(2).broadcast_to([128, 8, 128]),
                                                    ALU.subtract), [(ps, pa), (ps, pa + 1), sm], [fg])
                    yield
                    P.act(lambda e: e.activation(fg.t[:, :], fg.t[:, :], AF.Exp), [fg], [fg])
                    yield
                    P.dve(lambda e: e.tensor_tensor(Mh.t[:, :, :], fg.t[:, :].rearrange("p (h i) -> p h i", h=8),
                                                    ps.t[:, bsc, 256 + g * 128:256 + (g + 1) * 128].unsqueeze(1).broadcast_to([128, 8, 128]),
                                                    ALU.mult), [fg, (ps, bsc)], [Mh])
                    yield
                    for h in range(8):
                        hg = 8 * g + h
                        self.mm(ps.t[:, po + g, h * 64:(h + 1) * 64], Mh.t[:, h, :], v.t[:, hg * 64:(hg + 1) * 64], True, True,
                                [Mh, v], [(ps, po + g)])

                self.run_pipelined([grp(0), grp(1)], 2)
                P.dve(lambda e: e.tensor_tensor(f2.t[:, :], f2.t[:, :], ps.t[:, po:po + 2, :].rearrange("p b n -> p (b n)"), ALU.add),
                      [f2, (ps, po), (ps, po + 1)], [f2])
                for g in range(2):
                    self.mm(ps.t[:, pds + g, :], Btok.t[:, g, :], vdec.t[:, g * 512:(g + 1) * 512], True, True, [Btok, vdec], [(ps, pds + g)])
                P.dve(lambda e: e.tensor_tensor(f1.t[:, :].rearrange("p (h d) -> p h d", h=16), S.t[:, :].rearrange("p (h d) -> p h d", h=16),
                                                ELAST.unsqueeze(2).broadcast_to([128, 16, 64]), ALU.mult), [S, sm], [f1])
                P.dve(lambda e: e.tensor_tensor(S.t[:, :], f1.t[:, :], ps.t[:, pds:pds + 2, :].rearrange("p b n -> p (b n)"), ALU.add),
                      [f1, (ps, pds), (ps, pds + 1)], [S])
                P.act(lambda e: e.activation(Sb.t[:, :], S.t[:, :], AF.Copy), [S], [Sb])
            else:
                ELA = ELAST
                P.act(lambda e: e.activation(ELA, LA, AF.Exp), [sm], [sm])
                P.dve(lambda e: e.tensor_tensor(Bde.t[0:16, :, :], ELA.unsqueeze(1).broadcast_to([16, 16, 16]),
                                                self.eyep.t[:, :].unsqueeze(2).broadcast_to([16, 16, 16]), ALU.mult), [sm, self.eyep], [Bde])
                be = 6
                self.mm(ps.t[:, be, 0:256], ONES[0:16, :], Bde.t[0:16, :, :].rearrange("p s h -> p (s h)"), True, True, [masks, Bde], [(ps, be)])
                P.act(lambda e: e.activation(Eall.t[:, :, :].rearrange("p s h -> p (s h)"), ps.t[:, be, 0:256], AF.Copy), [(ps, be)], [Eall])
                Ss_b = [f2, f3]
                Ssb_b = [vdec, ogb]

                def smp(s_):
                    par = s_ % 2
                    Ss, Ssb, Bm, Cm = Ss_b[par], Ssb_b[par], Bm_b[par], Cm_b[par]
                    pds = 2 + 2 * par
                    P.dma("sp", Ss.t[:, :].rearrange("p (h d) -> p h d", h=16), I["st_ssm"][l, s_].rearrange("h n d -> n h d"), [], [Ss], Ss.sem())
                    P.dve(lambda e: e.tensor_scalar(Bm.t[0:16, :], Btok.t[0:16, :, :].rearrange("p g n -> p (g n)"),
                                                    self.eyep.t[:, s_:s_ + 1], None, ALU.mult), [Btok, self.eyep], [Bm])
                    P.dve(lambda e: e.tensor_tensor(Cm.t[:, :, :], xbcS.t[:, 10:12, :],
                                                    self.eyeb.t[:, s_, :].unsqueeze(1).broadcast_to([128, 2, 16]), ALU.mult),
                          [xbcS, self.eyeb], [Cm])
                    yield
                    for g in range(2):
                        self.mm(ps.t[:, pds + g, :], Bm.t[0:16, g * 128:(g + 1) * 128], v.t[0:16, g * 512:(g + 1) * 512], True, True,
                                [Bm, v], [(ps, pds + g)])
                    yield
                    P.dve(lambda e: e.tensor_tensor(Ss.t[:, :].rearrange("p (h d) -> p h d", h=16), Ss.t[:, :].rearrange("p (h d) -> p h d", h=16),
                                                    Eall.t[:, s_, :].unsqueeze(2).broadcast_to([128, 16, 64]), ALU.mult), [Ss, Eall], [Ss])
                    P.dve(lambda e: e.tensor_tensor(Ss.t[:, :], Ss.t[:, :], ps.t[:, pds:pds + 2, :].rearrange("p b n -> p (b n)"), ALU.add),
                          [Ss, (ps, pds), (ps, pds + 1)], [Ss])
                    yield
                    P.dma("sp", self.o["ssm_s"][l, s_].rearrange("h n d -> n h d"), Ss.t[:, :].rearrange("p (h d) -> p h d", h=16),
                          [Ss], [], Ss.sem(), is_output=True)
                    P.act(lambda e: e.activation(Ssb.t[:, :], Ss.t[:, :], AF.Copy), [Ss], [Ssb])
                    yield
                    for g in range(2):
                        self.mm(ps.t[0:16, g, :], Cm.t[:, g, :], Ssb.t[:, g * 512:(g + 1) * 512], s_ == 0, s_ == NS - 1, [Cm, Ssb], [(ps, g)])

                self.run_pipelined((smp(s_) for s_ in range(NS)), 2)
                P.act(lambda e: e.activation(f2.t[0:16, :], ps.t[0:16, 0:2, :].rearrange("p b n -> p (b n)"), AF.Copy), [(ps, 0), (ps, 1)], [f2])
            P.dve(lambda e: e.tensor_tensor(f3.t[:rows, :].rearrange("p (h d) -> p h d", h=16),
                                            xs_sb.t[:rows, :].rearrange("p (h d) -> p h d", h=16),
                                            Db.t[:rows, :].unsqueeze(2).broadcast_to([rows, 16, 64]), ALU.mult), [xs_sb, Db], [f3])
            P.dve(lambda e: e.tensor_tensor(f2.t[:rows, :], f2.t[:rows, :], f3.t[:rows, :], ALU.add), [f2, f3], [f2])
            pz = 4
            for o in range(2):
                for k in range(8):
                    self.mm(ps.t[:rows, pz + o, :], xT.t[:, k, col0:col0 + rows], Wz[o].t[:, k, :], k == 0, k == 7, [(xT, m), Wz[o]], [(ps, pz + o)])
            pzv = ps.t[:rows, pz:pz + 2, :].rearrange("p b n -> p (b n)")
            self.sigmoid_chain(f3.t[:rows, :], pzv, [(ps, pz), (ps, pz + 1)], f3)
            P.dve(lambda e: e.tensor_tensor(f3.t[:rows, :], pzv, f3.t[:rows, :], ALU.mult), [(ps, pz), (ps, pz + 1), f3], [f3])
            P.dve(lambda e: e.tensor_tensor(f2.t[:rows, :], f2.t[:rows, :], f3.t[:rows, :], ALU.mult), [f2, f3], [f2])
            for g in range(2):
                P.act(lambda e, g=g: e.activation(f3.t[:rows, g * 512:(g + 1) * 512], f2.t[:rows, g * 512:(g + 1) * 512], AF.Square,
                                                  accum_out=nst.t[:rows, 4 + g:5 + g]), [f2], [f3, nst])
            P.pool(lambda e: e.tensor_tensor(nst.t[:rows, 0:2], nst.t[:rows, 4:6], self.cst.t[:rows, 4:5].broadcast_to([rows, 2]), ALU.mult),
                   [nst, self.cst], [nst])
            P.pool(lambda e: e.tensor_tensor(nst.t[:rows, 0:2], nst.t[:rows, 0:2], self.cst.t[:rows, 1:2].broadcast_to([rows, 2]), ALU.add),
                   [nst, self.cst], [nst])
            P.pool(lambda e: e.tensor_tensor(nst.t[:rows, 2:4], nst.t[:rows, 0:2], self.cst.t[:rows, 2:3].broadcast_to([rows, 2]), ALU.pow),
                   [nst, self.cst], [nst])
            for g in range(2):
                P.dve(lambda e, g=g: e.tensor_scalar(ogb.t[:rows, g * 512:(g + 1) * 512], f2.t[:rows, g * 512:(g + 1) * 512],
                                                     nst.t[:rows, 2 + g:3 + g], None, ALU.mult), [f2, nst], [ogb])

            def evac(pst, bt):
                P.dve(lambda e: e.tensor_tensor(hT.t[:, 0:8, col0:col0 + rows], pst[:, :, :rows],
                                                nwT.t[:, :].unsqueeze(2).broadcast_to([128, 8, rows]), ALU.mult),
                      [(ps, bt), nwT], [(hT, list(range(8)))])

            self.transposes_to(evac, lambda k: ogb.t[:rows, k * 128:(k + 1) * 128], 8, rows, ogb, None, bt=7)

        for (m, rows, col0) in self.tiles():
            tile_body(m, rows, col0)
        self.state_store("ssm_p", l)

    def gdn(self, l):
        self.P.tag = "gdn"
        P, I, c = self.P, self.i, self.cfg
        ps, xT, hT, S, Sb, A = self.ps, self.xT, self.hT, self.S, self.Sb, self.A
        w_in = I["w_in"][l]
        TP = c.NTH * 128
        hs = self.has_sample
        masks = self.masks
        ONES, MBI, MBS = masks.t[:, 1, :], masks.t[:, 4, :], masks.t[:, 5, :]
        U64 = masks.t[:, 3, :]
        A.reset()
        cw = A.alloc("cw", [24, 4], F32)
        dtb = A.alloc("dtb", [8], F32)
        Ab = A.alloc("Ab", [8], F32)
        nwb = A.alloc("nwb", [128], F32)
        wab = A.alloc("wab", [8, 16], BF16)
        off0 = A.off
        tk = A.alloc("tk", [3072], F32)
        P.dma("sp", tk.t[0:4, :], I["gdn_conv_w"][l], [], [tk], tk.sem())
        self.colvecs(cw.t[:].rearrange("p c j -> p (c j)"), tk, 4, 24, cw)
        P.dma("sp", dtb.t[:], I["gdn_dt_bias"][l].partition_broadcast(128), [], [dtb], dtb.sem())
        P.dma("sp", Ab.t[:], I["gdn_a_log"][l].partition_broadcast(128), [], [Ab], Ab.sem())
        P.dma("sp", nwb.t[:], I["gdn_norm_w"][l].partition_broadcast(128), [], [nwb], nwb.sem())
        P.act(lambda e: e.activation(Ab.t[:], Ab.t[:], AF.Exp), [Ab], [Ab])
        P.act(lambda e: e.activation(Ab.t[:], Ab.t[:], AF.Copy, scale=-1.0), [Ab], [Ab])
        P.dma("pool", wab.t[:], w_in[:, C_GA:C_GA + 16].rearrange("(k p) c -> p k c", p=128), [], [wab], wab.sem())
        A.reset(off0)
        qkv = A.alloc("qkv", [24, TP], BF16, nreg=24)
        if hs:
            qkvS = A.alloc("qkvS", [24, 16], F32)
        off1 = A.off
        xraw_b = [A.alloc("xraw", [3 + TP], F32) for _ in range(2)]
        NPAR = 2 if hs else 3
        xraw_b = xraw_b + [A.alloc("xraw", [3 + TP], F32) for _ in range(NPAR - 2)]
        acc_b = [A.alloc("acc", [TP], F32) for _ in range(NPAR)]
        sgm_b = [A.alloc("sgm", [TP], F32) for _ in range(NPAR)]
        sq_b = [A.alloc("sq", [TP], F32) for _ in range(NPAR)]
        stg = A.alloc("stg", [512], F32)
        if hs:
            stc_b = [A.alloc("stc", [3, 128], F32) for _ in range(2)]
            xrs_b = [A.alloc("xrs", [4, 16], F32) for _ in range(2)]
            accS_b = [A.alloc("accS", [16], F32) for _ in range(2)]
            sgS_b = [A.alloc("sgS", [16], F32) for _ in range(2)]
            sqS_b = [A.alloc("sqS", [16], F32) for _ in range(2)]
        tail = self.tail_gdn[l]
        if self.half == 0:
            P.dve(lambda e: e.memset(tail.t[:], 0.0), [], [tail])
        self.state_load("gdn_p", l)
        lnq = float(np.log(128.0 ** -0.5))

        def l2n_g(dst_ap, xin_ap, sq_ap, n, is_q, bufs_r, bufs_w, b):
            P.act(lambda e: e.activation(sq_ap, xin_ap, AF.Square), bufs_r, [bufs_w[0]])
            self.mm(ps.t[:, b, :n], self.masks.t[:, 1, :], sq_ap, True, True, [masks, bufs_w[0]], [(ps, b)])
            yield
            P.act(lambda e: e.activation(sq_ap, ps.t[:, b, :n], AF.Ln, bias=float(NORM_EPS)), [(ps, b)], [bufs_w[0]])
            if is_q:
                P.act(lambda e: e.activation(sq_ap, sq_ap, AF.Exp, scale=-0.5, bias=lnq), [bufs_w[0]], [bufs_w[0]])
            else:
                P.act(lambda e: e.activation(sq_ap, sq_ap, AF.Exp, scale=-0.5), [bufs_w[0]], [bufs_w[0]])
            yield
            P.dve(lambda e: e.tensor_tensor(dst_ap, xin_ap, sq_ap, ALU.mult), list(bufs_r) + [bufs_w[0]], [bufs_w[1]])

        def conv_chunk(cc, slot):
            cs = (cc % 4) * 128
            par = cc % NPAR
            acc, sgm, sq = acc_b[par], sgm_b[par], sq_b[par]
            b = 2 + 2 * par
            bl = 3 + 2 * par
            for k in range(8):
                self.mm(ps.t[:, b, :TP], slot.t[:, k, cs:cs + 128], xT.t[:, k, 0:TP], k == 0, k == 7,
                        [slot, (xT, list(range(c.NTH)))], [(ps, b)])
            xraw = xraw_b[par]
            P.act(lambda e: e.activation(xraw.t[:, 0:3], tail.t[:, cc, :], AF.Copy), [(tail, cc)], [xraw])
            P.act(lambda e: e.activation(xraw.t[:, 3:3 + TP], ps.t[:, b, :TP], AF.Copy), [(ps, b)], [xraw])
            P.act(lambda e: e.activation(tail.t[:, cc, :], xraw.t[:, TP:TP + 3], AF.Copy), [xraw], [(tail, cc)])
            yield
            P.dve(lambda e: e.tensor_scalar(acc.t[:, :], xraw.t[:, 0:TP], cw.t[:, cc, 0:1], None, ALU.mult), [xraw, cw], [acc])
            for j in range(1, 4):
                P.dve(lambda e, j=j: e.scalar_tensor_tensor(acc.t[:, :], xraw.t[:, j:j + TP], cw.t[:, cc, j:j + 1], acc.t[:, :],
                                                            ALU.mult, ALU.add), [xraw, cw, acc], [acc])
            yield
            self.sigmoid_chain(sgm.t[:, :], acc.t[:, :], [acc], sgm)
            yield
            if cc < 16:
                P.dve(lambda e: e.tensor_tensor(acc.t[:, :], acc.t[:, :], sgm.t[:, :], ALU.mult), [acc, sgm], [acc])
                yield from l2n_g(qkv.t[:, cc, :], acc.t[:, :], sq.t[:, :], TP, cc < 8, [acc], [sq, (qkv, cc)], bl)
            else:
                P.dve(lambda e: e.tensor_tensor(qkv.t[:, cc, :], acc.t[:, :], sgm.t[:, :], ALU.mult), [acc, sgm], [(qkv, cc)])
            if hs:
                yield
                b2 = 6 + par
                for k in range(8):
                    self.mm(ps.t[:, b2, 0:16], slot.t[:, k, cs:cs + 128], xT.t[:, k, TP:TP + 16], k == 0, k == 7,
                            [slot, (xT, c.NTH)], [(ps, b2)])
                stc = stc_b[par]
                P.dma("sp", stc.t[0:16, :, :], I["st_gdn_conv"][l, :, :, cc * 128:(cc + 1) * 128], [], [stc], stc.sem())
                for j in range(3):
                    P.pe(lambda e, j=j: e.transpose(ps.t[:, b2, 16 + 16 * j:32 + 16 * j], stc.t[0:16, j, :], self.identf.t[0:16, 0:16]),
                         [stc, self.identf], [(ps, b2)])
                xrs = xrs_b[par]
                P.act(lambda e: e.activation(xrs.t[:, 0:3, :], ps.t[:, b2, 16:64].rearrange("p (j s) -> p j s", j=3), AF.Copy),
                      [(ps, b2)], [xrs])
                P.act(lambda e: e.activation(xrs.t[:, 3, :], ps.t[:, b2, 0:16], AF.Copy), [(ps, b2)], [xrs])
                accS, sgS, sqS = accS_b[par], sgS_b[par], sqS_b[par]
                P.dve(lambda e: e.tensor_scalar(accS.t[:, :], xrs.t[:, 0, :], cw.t[:, cc, 0:1], None, ALU.mult), [xrs, cw], [accS])
                for j in range(1, 4):
                    P.dve(lambda e, j=j: e.scalar_tensor_tensor(accS.t[:, :], xrs.t[:, j, :], cw.t[:, cc, j:j + 1], accS.t[:, :],
                                                                ALU.mult, ALU.add), [xrs, cw, accS], [accS])
                yield
                self.sigmoid_chain(sgS.t[:, :], accS.t[:, :], [accS], sgS)
                yield
                if cc < 16:
                    P.dve(lambda e: e.tensor_tensor(accS.t[:, :], accS.t[:, :], sgS.t[:, :], ALU.mult), [accS, sgS], [accS])
                    yield from l2n_g(qkvS.t[:, cc, :], accS.t[:, :], sqS.t[:, :], 16, cc < 8, [accS], [sqS, qkvS], b2)
                else:
                    P.dve(lambda e: e.tensor_tensor(qkvS.t[:, cc, :], accS.t[:, :], sgS.t[:, :], ALU.mult), [accS, sgS], [qkvS])

        def conv_rows(slot, i, c0, n, dst):
            b = self.bank()
            for k in range(8):
                self.mm(ps.t[:n, b, :], xT.t[:, k, c0:c0 + n], slot.t[:, k, :], k == 0, k == 7,
                        [slot, (xT, list(range(self.NTT)))], [(ps, b)])
            P.act(lambda e: e.activation(stg.t[:n, :], ps.t[:n, b, :], AF.Copy), [(ps, b)], [stg])
            P.dma("sp", dst, stg.t[:n, :], [stg], [], stg.sem(), is_output=True)

        def load_slot(i):
            sl = self.wslot()
            self.wload(sl, 0, w_in[:, C_GQKV + i * 512:C_GQKV + (i + 1) * 512])
            return sl

        nxt = load_slot(0)
        for i in range(6):
            slot = nxt
            if i + 1 < 6:
                nxt = load_slot(i + 1)
            self.run_pipelined((conv_chunk(cc, slot) for cc in range(4 * i, 4 * i + 4)), NPAR)
            if self.half == c.NH - 1:
                conv_rows(slot, i, TP - 3, 3, self.o["gdn_conv_p"][l, :, i * 512:(i + 1) * 512])
            if hs:
                conv_rows(slot, i, TP, 16, self.o["gdn_conv_s"][l, :, 2, i * 512:(i + 1) * 512])
        if hs:
            P.dma("sp", self.o["gdn_conv_s"][l, :, 0:2, :], I["st_gdn_conv"][l, :, 1:3, :], [], [], stg.sem(), is_output=True)
        if self.cfg.dbg.get("gdn_stop", 9) <= 1:
            return
        Wz = (self.wslot(), self.wslot())
        for o in range(2):
            self.wload(Wz[o], 0, w_in[:, C_GZ + o * 512:C_GZ + (o + 1) * 512])

        A.reset(off1)
        sm = A.alloc("sm", [14, 8], F32)
        osb = A.alloc("osb", [1024], F32, nreg=2)
        f3 = A.alloc("f3", [1024], F32)
        ogb = A.alloc("ogb", [1024], BF16)
        nst = A.alloc("nst", [3, 8], F32)
        off2 = A.off
        sm_b = [sm, A.alloc("sm1", [14, 8], F32)]
        cumT_b = [A.alloc("cumT", [2, 128], F32) for _ in range(2)]
        Bd1 = A.alloc("Bd1", [4, 128], F32)
        o_d = A.off
        dtmp = A.alloc("dtmp", [4, 128], F32)
        A.reset(o_d)
        ktok = A.alloc("ktok", [4, 128], BF16)
        A.reset(o_d + 2048)
        f1 = A.alloc("f1", [512], F32)
        Ya, Yb = A.alloc("Ya", [4, 128], F32), A.alloc("Yb", [4, 128], F32)
        YTa, YTb = A.alloc("YTa", [4, 128], F32), A.alloc("YTb", [4, 128], F32)
        rhs = A.alloc("rhs", [4, 128], F32)
        ub = A.alloc("ub", [4, 128], BF16)
        PT_b = [A.alloc("PT", [4, 128], F32) for _ in range(2)]
        attnT_b = [A.alloc("attnT", [4, 128], BF16) for _ in range(2)]
        qdT_b = [A.alloc("qdT", [4, 128], BF16) for _ in range(2)]
        kdec_b = [A.alloc("kdec", [2, 4, 128], BF16) for _ in range(2)]
        vb_b = [A.alloc("vb", [4, 128], F32) for _ in range(2)]
        if hs:
            A.reset(off2)
            Eall = A.alloc("Eall", [16, 8], F32)
            Bde = A.alloc("Bde", [16, 8], F32)
            kTm_b = [A.alloc("kTm", [8, 16], F32) for _ in range(2)]
            qTm_b = [A.alloc("qTm", [8, 16], F32) for _ in range(2)]
            ktS = A.alloc("ktS", [1024], F32)
            vbS = A.alloc("vbS", [1024], F32)
            um_b = [A.alloc("um", [1024], F32) for _ in range(2)]
            SsB = A.alloc("SsB", [1024], F32)
            oacc = A.alloc("oacc", [1024], F32)

        def gates(m, rows, col0, sm=sm):
            R = lambda i: sm.t[:rows, i, :]
            bab = 6
            for k in range(8):
                self.mm(ps.t[:rows, bab, 0:16], xT.t[:, k, col0:col0 + rows], wab.t[:, k, :], k == 0, k == 7, [(xT, m), wab], [(ps, bab)])
            P.act(lambda e: e.activation(R(7), ps.t[:rows, bab, 8:16], AF.Exp, scale=-1.0), [(ps, bab)], [sm])
            P.act(lambda e: e.activation(R(6), R(7), AF.Ln, bias=1.0), [sm], [sm])
            P.act(lambda e: e.activation(R(6), R(6), AF.Copy, scale=-1.0), [sm], [sm])
            P.act(lambda e: e.activation(R(0), R(6), AF.Exp), [sm], [sm])
            P.dve(lambda e: e.tensor_tensor(R(7), ps.t[:rows, bab, 0:8], dtb.t[:rows, :], ALU.add), [(ps, bab), dtb], [sm])
            P.act(lambda e: e.activation(R(7), R(7), AF.Exp), [sm], [sm])
            P.act(lambda e: e.activation(R(7), R(7), AF.Ln, bias=1.0), [sm], [sm])
            P.dve(lambda e: e.tensor_tensor(R(1), R(7), Ab.t[:rows, :], ALU.mult), [sm, Ab], [sm])

        v4 = lambda ap: ap.rearrange("p (h i) -> p h i", h=4)

        def tile_front(m, col0, sm, cumT):
            R = lambda i: sm.t[:, i, :]
            gates(m, 128, col0, sm)
            bc = 6
            self.mm(ps.t[:, bc, 32:40], U64, R(1), True, True, [masks, sm], [(ps, bc)])
            self.mm(ps.t[:, bc, 40:48], masks.t[:, 6, :], R(1), True, True, [masks, sm], [(ps, bc)])
            self.mm(ps.t[:, bc, 48:56], masks.t[:, 7, :], R(1), True, True, [masks, sm], [(ps, bc)])
            self.mm(ps.t[:, bc, 56:64], ONES, R(1), True, True, [masks, sm], [(ps, bc)])
            P.act(lambda e: e.activation(R(2), ps.t[:, bc, 32:40], AF.Copy), [(ps, bc)], [sm])
            P.act(lambda e: e.activation(R(3), ps.t[:, bc, 32:40], AF.Exp), [(ps, bc)], [sm])
            P.dve(lambda e: e.tensor_tensor(R(5), ps.t[:, bc, 40:48], R(2), ALU.subtract), [(ps, bc), sm], [sm])
            P.act(lambda e: e.activation(R(5), R(5), AF.Exp), [sm], [sm])
            P.dve(lambda e: e.tensor_copy(sm.t[:, 12:14, :], sm.t[:, 5:6, :].broadcast_to([128, 2, 8])), [sm], [sm])
            P.dve(lambda e: e.memset(sm.t[64:128, 12, :], 0.0), [sm], [sm])
            P.dve(lambda e: e.memset(sm.t[0:64, 13, :], 0.0), [sm], [sm])
            P.act(lambda e: e.activation(R(8), ps.t[:, bc, 48:56], AF.Exp), [(ps, bc)], [sm])
            P.act(lambda e: e.activation(R(7), ps.t[:, bc, 48:56], AF.Copy), [(ps, bc)], [sm])
            P.dve(lambda e: e.tensor_tensor(R(9), ps.t[:, bc, 56:64], R(7), ALU.subtract), [(ps, bc), sm], [sm])
            P.act(lambda e: e.activation(R(9), R(9), AF.Exp), [sm], [sm])
            P.dve(lambda e: e.scalar_tensor_tensor(R(4), R(0), -1.0, R(3), ALU.mult, ALU.mult), [sm], [sm])
            P.dve(lambda e: e.tensor_tensor(R(10), R(2), R(6), ALU.add), [sm], [sm])
            self.mm(ps.t[0:8, bc, 64:192], R(2), self.identf.t[:, :], True, True, [sm, self.identf], [(ps, bc)])
            self.mm(ps.t[0:8, bc, 192:320], R(10), self.identf.t[:, :], True, True, [sm, self.identf], [(ps, bc)])
            P.act(lambda e: e.activation(cumT.t[0:8, :, :].rearrange("p a i -> p (a i)"), ps.t[0:8, bc, 64:320], AF.Copy), [(ps, bc)], [cumT])

        def unit_gen(u, m, col0, g, part):
            par = u % 2
            BA, BB, BC = (2, 3, 4) if par == 0 else (0, 1, 5)
            sm, cumT = sm_b[m % 2], cumT_b[m % 2]
            PT, attnT, qdT, kdec, vb = PT_b[par], attnT_b[par], qdT_b[par], kdec_b[par], vb_b[par]
            R = lambda i: sm.t[:, i, :]
            hsl = slice(4 * g, 4 * g + 4)
            if part == "A":
                if g == 0:
                    tile_front(m, col0, sm, cumT)
                    yield
                hsl = slice(4 * g, 4 * g + 4)
                eye_g = self.eyep.t[0:8, 4 * g:4 * g + 4].unsqueeze(2).broadcast_to([8, 4, 128])
                bdf = Bd1.t[0:8, :, :].rearrange("p h i -> p (h i)")
                cum_b = R(2)[:, hsl].unsqueeze(2).broadcast_to([128, 4, 128])
                P.dve(lambda e: e.tensor_tensor(Bd1.t[0:8, :, :], cumT.t[0:8, 0, :].unsqueeze(1).broadcast_to([8, 4, 128]), eye_g, ALU.mult),
                      [cumT, self.eyep], [Bd1])
                self.mm(ps.t[:, BA, :], ONES[0:8, :], bdf, True, True, [masks, Bd1], [(ps, BA)])
                self.mm(ps.t[:, BB, :], ONES[0:8, :], bdf, True, False, [masks, Bd1], [(ps, BB)])
                for hq in range(4):
                    self.mm(ps.t[:, BB, hq * 128:(hq + 1) * 128], self.identf.t[:, :], MBI, False, hq == 3, [masks, self.identf], [(ps, BB)])
                for h in range(4):
                    hh = 4 * g + h
                    self.mm(ps.t[:, BC, h * 128:(h + 1) * 128], qkv.t[:, 8 + hh, col0:col0 + 128], qkv.t[:, hh, col0:col0 + 128], True, True,
                            [(qkv, [hh, 8 + hh])], [(ps, BC)])
                yield
                P.act(lambda e: e.activation(dtmp.t[:, :, :].rearrange("p h i -> p (h i)"), ps.t[:, BA, :], AF.Exp), [(ps, BA)], [dtmp])
                P.dve(lambda e: e.tensor_tensor(f1.t[:, :].rearrange("p (h i) -> p h i", h=4), v4(ps.t[:, BB, :]), cum_b, ALU.subtract), [(ps, BB), sm], [f1])
                yield
                P.dve(lambda e: e.tensor_tensor(qdT.t[:, :, :], qkv.t[:, 4 * g:4 * g + 4, col0:col0 + 128], dtmp.t[:, :, :], ALU.mult),
                      [(qkv, list(range(4 * g, 4 * g + 4))), dtmp], [qdT])
                P.act(lambda e: e.activation(f1.t[:, :], f1.t[:, :], AF.Exp), [f1], [f1])
                yield
                P.dve(lambda e: e.tensor_tensor(attnT.t[:, :, :], v4(ps.t[:, BC, :]), f1.t[:, :].rearrange("p (h i) -> p h i", h=4), ALU.mult),
                      [(ps, BC), f1], [attnT])
                P.dve(lambda e: e.tensor_tensor(Bd1.t[0:8, :, :], cumT.t[0:8, 1, :].unsqueeze(1).broadcast_to([8, 4, 128]), eye_g, ALU.mult),
                      [cumT, self.eyep], [Bd1])
                self.mm(ps.t[:, BB, :], ONES[0:8, :], bdf, True, False, [masks, Bd1], [(ps, BB)])
                for hq in range(4):
                    self.mm(ps.t[:, BB, hq * 128:(hq + 1) * 128], self.identf.t[:, :], MBS, False, hq == 3, [masks, self.identf], [(ps, BB)])
                for h in range(4):
                    hh = 4 * g + h
                    self.mm(ps.t[:, BA, h * 128:(h + 1) * 128], qkv.t[:, 8 + hh, col0:col0 + 128], qkv.t[:, 8 + hh, col0:col0 + 128], True, True,
                            [(qkv, 8 + hh)], [(ps, BA)])
                yield
                P.dve(lambda e: e.tensor_tensor(dtmp.t[:, :, :], v4(ps.t[:, BB, :]), cum_b, ALU.subtract), [(ps, BB), sm], [dtmp])
                yield
                P.act(lambda e: e.activation(dtmp.t[:, :, :], dtmp.t[:, :, :], AF.Exp), [dtmp], [dtmp])
                yield
                P.dve(lambda e: e.scalar_tensor_tensor(YTa.t[:, :, :], v4(ps.t[:, BA, :]), -1.0, dtmp.t[:, :, :], ALU.mult, ALU.mult),
                      [(ps, BA), dtmp], [YTa])
                for h in range(4):
                    P.pe(lambda e, h=h: e.transpose(ps.t[:, BB, h * 128:(h + 1) * 128], YTa.t[:, h, :], self.identf.t[:, :]),
                         [YTa, self.identf], [(ps, BB)])
                yield
                P.act(lambda e: e.activation(Ya.t[:, :, :], v4(ps.t[:, BB, :]), AF.Copy), [(ps, BB)], [Ya])
                P.dve(lambda e: e.tensor_tensor(PT.t[:, :, :], YTa.t[:, :, :], self.identf.t[:, :].unsqueeze(1).broadcast_to([128, 4, 128]), ALU.add),
                      [YTa, self.identf], [PT])
                def ev_k(pst, bt):
                    P.act(lambda e: e.activation(ktok.t[:, :, :], pst[:, 0:4, :], AF.Copy), [(ps, bt)], [ktok])
                self.transposes_T(ev_k, lambda k: qkv.t[:, 8 + 4 * g + k, col0:col0 + 128], 4, 128, (qkv, list(range(8 + 4 * g, 12 + 4 * g))), bt=7)
                yield
                def level(lev, Y, YT, Yn, YTn):
                    for h in range(4):
                        self.mm(ps.t[:, BA, h * 128:(h + 1) * 128], YT.t[:, h, :], Y.t[:, h, :], True, True, [YT, Y], [(ps, BA)])
                    if lev < 5:
                        for h in range(4):
                            self.mm(ps.t[:, BB, h * 128:(h + 1) * 128], Y.t[:, h, :], YT.t[:, h, :], True, True, [YT, Y], [(ps, BB)])
                    yield
                    P.act(lambda e: e.activation(Yn.t[:, :, :], v4(ps.t[:, BA, :]), AF.Copy), [(ps, BA)], [Yn])
                    if lev < 5:
                        P.act(lambda e: e.activation(YTn.t[:, :, :], v4(ps.t[:, BB, :]), AF.Copy), [(ps, BB)], [YTn])
                    yield
                    for h in range(4):
                        self.mm(ps.t[:, BC, h * 128:(h + 1) * 128], Yn.t[:, h, :], PT.t[:, h, :], True, True, [Yn, PT], [(ps, BC)])
                    yield
                    P.dve(lambda e: e.tensor_tensor(PT.t[:, :, :], PT.t[:, :, :], v4(ps.t[:, BC, :]), ALU.add), [PT, (ps, BC)], [PT])
                Y, YT, Yn, YTn = Ya, YTa, Yb, YTb
                for lev in range(1, 6):
                    yield from level(lev, Y, YT, Yn, YTn)
                    if lev == 1:
                        for cq in range(2):
                            P.dve(lambda e, cq=cq: e.tensor_tensor(kdec.t[:, cq, :, :], ktok.t[:, :, :],
                                                                   R(12 + cq)[:, hsl].unsqueeze(2).broadcast_to([128, 4, 128]), ALU.mult),
                                  [ktok, sm], [kdec])
                    if lev == 2:
                        def ev_v(pst, bt):
                            P.dve(lambda e: e.tensor_tensor(vb.t[:, :, :], pst[:, 0:4, :], R(0)[:, hsl].unsqueeze(2).broadcast_to([128, 4, 128]), ALU.mult),
                                  [(ps, bt), sm], [vb])
                        self.transposes_T(ev_v, lambda k: qkv.t[:, 16 + 4 * g + k, col0:col0 + 128], 4, 128,
                                          (qkv, list(range(16 + 4 * g, 20 + 4 * g))), bt=7)
                    Y, YT, Yn, YTn = Yn, YTn, Y, YT
                    yield
                return
            Sg = (S, list(range(8 * g, 8 * g + 8)))
            Sbg = (Sb, list(range(8 * g, 8 * g + 8)))

            def chunk(ch):
                rs = slice(64 * ch, 64 * ch + 64)
                rw = slice(0, 128) if ch == 0 else rs
                nrw = 128 if ch == 0 else 64
                for h in range(4):
                    hh = 4 * g + h
                    self.mm(ps.t[:, BA, h * 128:(h + 1) * 128], qkv.t[:, 8 + hh, col0:col0 + 128], Sb.t[:, hh * 128:(hh + 1) * 128], True, True,
                            [(qkv, 8 + hh), Sbg], [(ps, BA)])
                yield
                nbe_b = R(4)[rw, hsl].unsqueeze(2).broadcast_to([nrw, 4, 128])
                P.dve(lambda e: e.tensor_tensor(rhs.t[rw, :, :], v4(ps.t[rw, BA, :]), nbe_b, ALU.mult), [(ps, BA), sm], [rhs])
                P.dve(lambda e: e.tensor_tensor(rhs.t[rw, :, :], rhs.t[rw, :, :], vb.t[rw, :, :], ALU.add), [rhs, vb], [rhs])
                yield
                for h in range(4):
                    self.mm(ps.t[:, BB, h * 128:(h + 1) * 128], PT.t[:, h, :], rhs.t[:, h, :], True, True, [PT, rhs], [(ps, BB)])
                yield
                P.act(lambda e: e.activation(ub.t[rw, :, :], v4(ps.t[rw, BB, :]), AF.Copy), [(ps, BB)], [ub])
                yield
                for h in range(4):
                    hh = 4 * g + h
                    self.mm(ps.t[:, BC, h * 128:(h + 1) * 128], qdT.t[:, h, :], Sb.t[:, hh * 128:(hh + 1) * 128], True, False, [qdT, Sbg], [(ps, BC)])
                    self.mm(ps.t[:, BC, h * 128:(h + 1) * 128], attnT.t[:, h, :], ub.t[:, h, :], False, True, [attnT, ub], [(ps, BC)])
                for h in range(4):
                    self.mm(ps.t[:, BA, h * 128:(h + 1) * 128], kdec.t[:, ch, h, :], ub.t[:, h, :], True, True, [kdec, ub], [(ps, BA)])
                yield
                P.act(lambda e: e.activation(osb.t[rs, g * 512:(g + 1) * 512], ps.t[rs, BC, :], AF.Copy), [(ps, BC)], [(osb, g)])
                el_b = sm.t[:, 8 + ch, hsl].unsqueeze(2).broadcast_to([128, 4, 128])
                P.dve(lambda e: e.tensor_tensor(v4(S.t[:, g * 512:(g + 1) * 512]), v4(S.t[:, g * 512:(g + 1) * 512]), el_b, ALU.mult), [Sg, sm], [Sg])
                P.dve(lambda e: e.tensor_tensor(S.t[:, g * 512:(g + 1) * 512], S.t[:, g * 512:(g + 1) * 512], ps.t[:, BA, :], ALU.add), [Sg, (ps, BA)], [Sg])
                yield
                P.act(lambda e: e.activation(Sb.t[:, g * 512:(g + 1) * 512], S.t[:, g * 512:(g + 1) * 512], AF.Copy), [Sg], [Sbg])
                yield

            for ch in range(2):
                yield from chunk(ch)
            if g == 1:
                post(m, 128, col0, osb)

        def post(m, rows, col0, o_buf):
            v8 = lambda ap: ap.rearrange("p (h d) -> p h d", h=8)
            P.dve(lambda e: e.tensor_tensor(f3.t[:rows, :], o_buf.t[:rows, :], o_buf.t[:rows, :], ALU.mult), [o_buf], [f3])
            P.dve(lambda e: e.tensor_reduce(nst.t[:rows, 0, :], v8(f3.t[:rows, :]), mybir.AxisListType.X, ALU.add), [f3], [nst])
            P.pool(lambda e: e.tensor_tensor(nst.t[:rows, 1, :], nst.t[:rows, 0, :], self.cst.t[:rows, 5:6].broadcast_to([rows, 8]), ALU.mult),
                   [nst, self.cst], [nst])
            P.pool(lambda e: e.tensor_tensor(nst.t[:rows, 1, :], nst.t[:rows, 1, :], self.cst.t[:rows, 1:2].broadcast_to([rows, 8]), ALU.add),
                   [nst, self.cst], [nst])
            P.pool(lambda e: e.tensor_tensor(nst.t[:rows, 2, :], nst.t[:rows, 1, :], self.cst.t[:rows, 2:3].broadcast_to([rows, 8]), ALU.pow),
                   [nst, self.cst], [nst])
            P.dve(lambda e: e.tensor_tensor(v8(o_buf.t[:rows, :]), v8(o_buf.t[:rows, :]), nst.t[:rows, 2, :].unsqueeze(2).broadcast_to([rows, 8, 128]),
                                            ALU.mult), [o_buf, nst], [o_buf])
            P.dve(lambda e: e.tensor_tensor(v8(o_buf.t[:rows, :]), v8(o_buf.t[:rows, :]), nwb.t[:rows, :].unsqueeze(1).broadcast_to([rows, 8, 128]),
                                            ALU.mult), [o_buf, nwb], [o_buf])
            pz = 4
            for o in range(2):
                for k in range(8):
                    self.mm(ps.t[:rows, pz + o, :], xT.t[:, k, col0:col0 + rows], Wz[o].t[:, k, :], k == 0, k == 7, [(xT, m), Wz[o]], [(ps, pz + o)])
            pzv = ps.t[:rows, pz:pz + 2, :].rearrange("p b n -> p (b n)")
            self.sigmoid_chain(f3.t[:rows, :], pzv, [(ps, pz), (ps, pz + 1)], f3)
            P.dve(lambda e: e.tensor_tensor(f3.t[:rows, :], pzv, f3.t[:rows, :], ALU.mult), [(ps, pz), (ps, pz + 1), f3], [f3])
            P.dve(lambda e: e.tensor_tensor(ogb.t[:rows, :], o_buf.t[:rows, :], f3.t[:rows, :], ALU.mult), [o_buf, f3], [ogb])

            def evac(pst, bt):
                P.act(lambda e: e.activation(hT.t[:, 0:8, col0:col0 + rows], pst[:, :, :rows], AF.Copy), [(ps, bt)], [(hT, list(range(8)))])

            self.transposes_to(evac, lambda k: ogb.t[:rows, k * 128:(k + 1) * 128], 8, rows, ogb, None, bt=7)

        def sample_body(m, col0):
            R = lambda i: sm.t[0:16, i, :]
            gates(m, 16, col0)
            P.act(lambda e: e.activation(R(3), R(1), AF.Exp), [sm], [sm])
            P.dve(lambda e: e.scalar_tensor_tensor(R(4), R(0), -1.0, R(3), ALU.mult, ALU.mult), [sm], [sm])
            P.dve(lambda e: e.tensor_tensor(Bde.t[0:16, :, :], R(3).unsqueeze(1).broadcast_to([16, 16, 8]),
                                            self.eyep.t[:, :].unsqueeze(2).broadcast_to([16, 16, 8]), ALU.mult), [sm, self.eyep], [Bde])
            self.mm(ps.t[:, 6, 0:128], ONES[0:16, :], Bde.t[0:16, :, :].rearrange("p s h -> p (s h)"), True, True, [masks, Bde], [(ps, 6)])
            P.act(lambda e: e.activation(Eall.t[:, :, :].rearrange("p s h -> p (s h)"), ps.t[:, 6, 0:128], AF.Copy), [(ps, 6)], [Eall])
            for h in range(8):
                P.pe(lambda e, h=h: e.transpose(ps.t[0:16, 2 + h // 4, (h % 4) * 128:(h % 4 + 1) * 128], qkvS.t[:, 8 + h, :], self.identf.t[:, :]),
                     [qkvS, self.identf], [(ps, 2 + h // 4)])
            P.act(lambda e: e.activation(ktS.t[0:16, :], ps.t[0:16, 2:4, :].rearrange("p b n -> p (b n)"), AF.Copy), [(ps, 2), (ps, 3)], [ktS])
            for h in range(8):
                P.pe(lambda e, h=h: e.transpose(ps.t[0:16, 4 + h // 4, (h % 4) * 128:(h % 4 + 1) * 128], qkvS.t[:, 16 + h, :], self.identf.t[:, :]),
                     [qkvS, self.identf], [(ps, 4 + h // 4)])
            P.dve(lambda e: e.tensor_tensor(vbS.t[0:16, :].rearrange("p (h d) -> p h d", h=8),
                                            ps.t[0:16, 4:6, :].rearrange("p b (h d) -> p (b h) d", h=4),
                                            R(0).unsqueeze(2).broadcast_to([16, 8, 128]), ALU.mult), [(ps, 4), (ps, 5), sm], [vbS])
            Ss_b = [osb, SsB]
            P.dve(lambda e: e.memset(oacc.t[0:16, :], 0.0), [], [oacc])
            v8 = lambda ap: ap.rearrange("p (h d) -> p h d", h=8)

            def smp(s_):
                par = s_ % 2
                Ss, kTm, qTm, um = Ss_b[par], kTm_b[par], qTm_b[par], um_b[par]
                pb = 2 + 2 * par
                pbv = ps.t[:, pb:pb + 2, :].rearrange("p b n -> p (b n)")
                P.dma("sp", Ss.t[:, :].rearrange("p (h v) -> p h v", h=8), I["st_gdn"][l, s_].rearrange("h k v -> k h v"), [], [Ss], Ss.sem())
                P.dve(lambda e: e.tensor_tensor(kTm.t[:, :, :], qkvS.t[:, 8:16, :],
                                                self.eyef.t[:, s_, :].unsqueeze(1).broadcast_to([128, 8, 16]), ALU.mult), [qkvS, self.eyef], [kTm])
                P.dve(lambda e: e.tensor_tensor(qTm.t[:, :, :], qkvS.t[:, 0:8, :],
                                                self.eyef.t[:, s_, :].unsqueeze(1).broadcast_to([128, 8, 16]), ALU.mult), [qkvS, self.eyef], [qTm])
                yield
                for h in range(8):
                    self.mm(ps.t[0:16, pb + h // 4, (h % 4) * 128:(h % 4 + 1) * 128], kTm.t[:, h, :], Ss.t[:, h * 128:(h + 1) * 128], True, True,
                            [kTm, Ss], [(ps, pb + h // 4)])
                yield
                P.dve(lambda e: e.tensor_tensor(v8(um.t[0:16, :]), v8(pbv[0:16, :]), R(4).unsqueeze(2).broadcast_to([16, 8, 128]), ALU.mult),
                      [(ps, pb), (ps, pb + 1), sm], [um])
                P.dve(lambda e: e.scalar_tensor_tensor(um.t[0:16, :], vbS.t[0:16, :], self.eyep.t[0:16, s_:s_ + 1], um.t[0:16, :],
                                                       ALU.mult, ALU.add), [vbS, self.eyep, um], [um])
                yield
                for h in range(8):
                    self.mm(ps.t[:, pb + h // 4, (h % 4) * 128:(h % 4 + 1) * 128], ktS.t[0:16, h * 128:(h + 1) * 128], um.t[0:16, h * 128:(h + 1) * 128],
                            True, True, [ktS, um], [(ps, pb + h // 4)])
                yield
                P.dve(lambda e: e.tensor_tensor(v8(Ss.t[:, :]), v8(Ss.t[:, :]), Eall.t[:, s_, :].unsqueeze(2).broadcast_to([128, 8, 128]), ALU.mult),
                      [Ss, Eall], [Ss])
                P.dve(lambda e: e.tensor_tensor(Ss.t[:, :], Ss.t[:, :], pbv, ALU.add), [Ss, (ps, pb), (ps, pb + 1)], [Ss])
                yield
                P.dma("sp", self.o["gdn_s"][l, s_].rearrange("h k v -> k h v"), Ss.t[:, :].rearrange("p (h v) -> p h v", h=8), [Ss], [], Ss.sem(),
                      is_output=True)
                for h in range(8):
                    self.mm(ps.t[0:16, pb + h // 4, (h % 4) * 128:(h % 4 + 1) * 128], qTm.t[:, h, :], Ss.t[:, h * 128:(h + 1) * 128],
                            True, True, [qTm, Ss], [(ps, pb + h // 4)])
                yield
                P.dve(lambda e: e.tensor_tensor(oacc.t[0:16, :], oacc.t[0:16, :], pbv[0:16, :], ALU.add), [oacc, (ps, pb), (ps, pb + 1)], [oacc])

            self.run_pipelined((smp(s_) for s_ in range(NS)), 2)
            post(m, 16, col0, oacc)

        units = [(m, col0, g) for (m, rows, col0) in self.tiles() if rows == 128 for g in range(2)]
        self.run_pipelined([unit_gen(0, *units[0], "A")], 1)
        for u in range(len(units)):
            gens = [unit_gen(u, *units[u], "B")]
            if u + 1 < len(units):
                gens.append(unit_gen(u + 1, *units[u + 1], "A"))
            self.run_pipelined(gens, 2)
        for (m, rows, col0) in self.tiles():
            if rows != 128:
                sample_body(m, col0)
        self.state_store("gdn_p", l)

    def transposes_T(self, evac, src_fn, n, rows, src_buf, bt=None):
        P, ps = self.P, self.ps
        bt = self.bank() if bt is None else bt
        pst = ps.t[:, bt, :].bitcast(BF16).rearrange("p (k n) -> p k n", k=8)
        for k in range(n):
            P.pe(lambda e, k=k: e.transpose(pst[:rows, k, :], src_fn(k), self.identb.t[:, :]), [src_buf, self.identb], [(ps, bt)])
        evac(pst, bt)

    def token_mix(self, l):
        I = self.i
        if "ret" in self.cfg.mix:
            self.retention(l)
            self.finale(l, I["w_ret_out"][l], C_M1)
        if "ssd" in self.cfg.mix:
            self.ssd(l)
            self.finale(l, I["w_ssm_out"][l], C_M2)
        if "gdn" in self.cfg.mix:
            self.gdn(l)
            self.finale(l, I["w_gdn_out"][l], C_M3)

    def build(self):
        c = self.cfg
        self.pT = [self.P.sbuf(f"pT{i}", [128, 2, 128], BF16) for i in range(2)]
        self.alloc_mix()
        for half in range(c.NH):
            self.half = half
            self.has_sample = c.sample and half == c.NH - 1
            self.load_x()
            for l in range(c.layers):
                last = (l == c.layers - 1)
                if c.ffn:
                    self.ffn(l, 0)
                self.layer_norm(l, 0)
                self.token_mix(l)
                self.layer_norm(l, 1)
                if c.ffn:
                    self.ffn(l, 1)
                self.layer_norm(l, 2)
                if c.pegate:
                    self.pe_gate(l)
                self.layer_norm(l, 3, final=last)
        return self.P.finish()


def make_consts(T):
    c = {}
    c["c_ident"] = np.eye(128, dtype=np.float32)
    half = 64
    inv = (np.float32(10000.0) ** (-np.arange(half, dtype=np.float32) / np.float32(half))).astype(np.float32)
    pos = np.concatenate([np.arange(T, dtype=np.float32), np.full(NS, PAST_LEN, np.float32)])
    ang = (pos[:, None] * inv[None, :]).astype(np.float32).astype(np.float64)
    rope = np.zeros((T + NS, 4, 64), np.float32)
    rope[:, 0] = np.cos(ang)
    rope[:, 1] = np.sin(ang)
    rope[:, 2] = np.cos(ang) * 128 ** -0.5
    rope[:, 3] = np.sin(ang) * 128 ** -0.5
    c["c_rope"] = rope
    gam = 1.0 - 2.0 ** (-5.0 - np.arange(4))
    lg = np.log1p(-(2.0 ** (-5.0 - np.arange(4)))).astype(np.float32).astype(np.float64)
    i = np.arange(128)
    dm = i[None, :] - i[:, None]
    mask = np.zeros((4, 128, 128), np.float32)
    row = np.zeros((4, 3, 128), np.float32)
    for h in range(4):
        mask[h] = np.where(dm >= 0, np.exp(lg[h] * np.maximum(dm, 0)), 0.0)
        row[h, 0] = np.exp(lg[h] * (i + 1))
        row[h, 1] = np.exp(lg[h] * (127 - i))
        row[h, 2, 0] = np.exp(lg[h] * 128)
        row[h, 2, 1] = np.exp(lg[h])
    c["c_retmask"] = mask
    c["c_retrow"] = row
    mk = np.zeros((8, 128, 128), np.float32)
    jj, ii = np.meshgrid(np.arange(128), np.arange(128), indexing="ij")
    blk = (jj // 64) == (ii // 64)
    mk[0] = (jj <= ii)
    mk[1] = 1.0
    mk[2] = np.where(jj <= ii, 0.0, -30000.0)
    mk[3] = (jj <= ii) & blk
    mk[4] = np.where((jj <= ii) & blk, 0.0, -30000.0)
    mk[5] = np.where((jj < ii) & blk, 0.0, -30000.0)
    mk[6] = blk
    mk[7] = (jj < 64)
    c["c_masks"] = mk
    return c


_CACHE = {}


def kernel(**inputs):
    NC = 8
    cfg = Cfg(NH=4, NTH=4, sample=True, layers=2)
    mk = MK(cfg)
    mk.build()
    T = cfg.T
    consts = make_consts(T)
    f = lambda a: np.ascontiguousarray(np.asarray(a, dtype=np.float32))
    wnames = ["ln_g", "ln_b", "ffn_wg", "ffn_wu", "ffn_wd", "w_in", "ssm_conv_w", "ssm_conv_b", "ssm_dt_bias",
              "ssm_a_log", "ssm_d", "ssm_norm_w", "gdn_conv_w", "gdn_dt_bias", "gdn_a_log", "gdn_norm_w",
              "w_ret_out", "w_ssm_out", "w_gdn_out", "w_o", "pe_proj", "pe_gate"]
    W = {k: f(inputs[k]) for k in wnames}
    xp, xs = np.asarray(inputs["x_prompt"]), np.asarray(inputs["x_sample"])
    pp, ps_ = np.asarray(inputs["p_prompt"]), np.asarray(inputs["p_sample"])
    in_maps = []
    for c in range(NC):
        sl = slice(NS * c, NS * (c + 1))
        m = dict(W)
        m.update(consts)
        m["xp"] = f(xp[c])
        m["pp"] = f(pp[:, c])
        m["xs"] = f(xs[sl, 0])
        m["ps"] = f(ps_[:, sl, 0])
        m["st_ret"] = f(np.asarray(inputs["state_ret"])[:, sl])
        m["st_ssm"] = f(np.asarray(inputs["state_ssm"])[:, sl])
        m["st_ssm_conv"] = f(np.asarray(inputs["state_ssm_conv"])[:, sl])
        m["st_gdn"] = f(np.asarray(inputs["state_gdn"])[:, sl])
        m["st_gdn_conv"] = f(np.asarray(inputs["state_gdn_conv"])[:, sl])
        in_maps.append({k: v for k, v in m.items() if k in mk.i})
    res = run_bass_kernel_spmd(mk.nc, in_maps, core_ids=list(range(NC)))
    R = res.results
    cat0 = lambda k: np.stack([R[c][k] for c in range(NC)], axis=0)
    y_p = cat0("y_p")
    y_s = np.concatenate([R[c]["y_s"] for c in range(NC)], 0)[:, None, :]
    outs = [y_p, y_s]
    for k in ("ret_p", "ssm_p", "ssm_conv_p", "gdn_p", "gdn_conv_p"):
        outs.append(np.stack([R[c][k] for c in range(NC)], axis=1))
    for k in ("ret_s", "ssm_s", "ssm_conv_s", "gdn_s", "gdn_conv_s"):
        outs.append(np.concatenate([R[c][k] for c in range(NC)], axis=1))
    return tuple(np.ascontiguousarray(o, dtype=np.float32) for o in outs)
```

```python
import numpy as np
from contextlib import ExitStack
import concourse.bass as bass
import concourse.mybir as mybir
from concourse.bass_utils import run_bass_kernel_spmd

F32, BF16 = mybir.dt.float32, mybir.dt.bfloat16
AF = mybir.ActivationFunctionType
ALU = mybir.AluOpType

D = 1024
DEPTH = 2
FFN = 2048
PLE = 256
IN_DIM = 12832
DN_ALPHA = (2 * DEPTH) ** 0.25
LN_EPS = 1e-5
NORM_EPS = 1e-6
PAST_LEN = 16384
NS = 16

C_RQ, C_RK, C_RV, C_RG = 0, 512, 1024, 2048
C_MZ, C_MXBC, C_MDT = 3072, 4096, 5632
C_GQKV, C_GZ, C_GA, C_GB = 5648, 8720, 9744, 9752
C_M1, C_M2, C_M3 = 9760, 10784, 11808


class Sem:
    def __init__(self, name):
        self.name = name
        self.count = 0
        self.h = None


class Reg:
    __slots__ = ("writers", "readers")

    def __init__(self):
        self.writers = {}
        self.readers = {}


class Buf:
    def __init__(self, prog, name, t, nreg=1):
        self.prog, self.name, self.t = prog, name, t
        self.regs = [[Reg()] for _ in range(nreg)]
        self.dsem = None

    def sem(self):
        if self.dsem is None:
            self.dsem = self.prog.named_sem("d_" + getattr(self, "sem_name", self.name))
        return self.dsem

    def __getitem__(self, k):
        return self.t[k]


class Op:
    __slots__ = ("eng", "fn", "deps", "needed", "is_dma", "sem", "val", "waits", "dmawaits")

    def __init__(self, eng, fn, is_dma):
        self.eng, self.fn, self.is_dma = eng, fn, is_dma
        self.deps = []
        self.dmawaits = []
        self.needed = False
        self.sem = None
        self.val = 0


def _regs(spec):
    out = []
    for s in spec:
        if isinstance(s, Buf):
            for g in s.regs:
                out.extend(g)
        else:
            b, idx = s
            if isinstance(idx, int):
                out.extend(b.regs[idx])
            else:
                for i in idx:
                    out.extend(b.regs[i])
    return out


class Arena:
    GRAN = 256

    def __init__(self, prog, name, nbytes):
        self.prog = prog
        self.nbytes = nbytes
        self.base = prog.sbuf(name, [128, nbytes // 2], BF16)
        self.gr = [Reg() for _ in range(nbytes // self.GRAN)]
        self.off = 0
        self.n = 0

    def reset(self, off=0):
        self.off = off

    def alloc(self, name, free_shape, dt, nreg=1):
        esz = 2 if dt == BF16 else 4
        nel = int(np.prod(free_shape))
        nb = nel * esz
        nb_al = (nb + self.GRAN - 1) // self.GRAN * self.GRAN
        assert self.off + nb_al <= self.nbytes, f"arena overflow allocating {name}: {self.off}+{nb_al}>{self.nbytes}"
        o2 = self.off // 2
        v = self.base.t[:, o2:o2 + nb // 2]
        if dt != BF16:
            v = v.bitcast(dt)
        if len(free_shape) > 1:
            names = " ".join(f"d{i}" for i in range(len(free_shape)))
            kw = {f"d{i}": int(free_shape[i]) for i in range(len(free_shape))}
            v = v.rearrange(f"p ({names}) -> p {names}", **kw)
        self.n += 1
        b = Buf(self.prog, f"{name}_{self.n}", v, 1)
        b.sem_name = f"a_{name}_{self.off}"
        g0 = self.off // self.GRAN
        ng = nb_al // self.GRAN
        grs = self.gr[g0:g0 + ng]
        if nreg == 1:
            b.regs = [grs]
        else:
            assert ng % nreg == 0, (name, ng, nreg)
            k = ng // nreg
            b.regs = [grs[i * k:(i + 1) * k] for i in range(nreg)]
        self.off += nb_al
        return b


class Prog:
    ENGS = ("pe", "act", "dve", "pool", "sp")
    ATTR = {"pe": "tensor", "act": "scalar", "dve": "vector", "pool": "gpsimd", "sp": "sync"}

    def __init__(self, nc):
        self.nc = nc
        self.es = ExitStack()
        self.ops = {e: [] for e in self.ENGS}
        self.sems = []
        self.esem = {e: self.new_sem("e_" + e) for e in ("pe", "act", "dve", "pool")}
        self.nbuf = 0
        self.out_sems = set()

    def new_sem(self, name):
        s = Sem(name)
        self.sems.append(s)
        return s

    def named_sem(self, name):
        d = self.__dict__.setdefault("_named", {})
        if name not in d:
            d[name] = self.new_sem(name)
        return d[name]

    def sbuf(self, name, shape, dt, nreg=1):
        nb = int(np.prod(shape[1:])) * (2 if dt == BF16 else 4)
        self.sb_bytes = getattr(self, "sb_bytes", 0) + nb
        self.sb_log = getattr(self, "sb_log", []) + [(name, nb)]
        t = self.es.enter_context(self.nc.sbuf_tensor("s_" + name, list(shape), dt))
        return Buf(self, name, t, nreg)

    def psum(self, name, shape, dt, nreg=1):
        t = self.es.enter_context(self.nc.psum_tensor("p_" + name, list(shape), dt))
        return Buf(self, name, t, nreg)

    def dram(self, name, shape, dt, kind, nreg=1):
        t = self.nc.dram_tensor(name, list(shape), dt, kind=kind)
        return Buf(self, name, t.ap(), nreg)

    def _dep(self, c, p):
        if p is None or p is c:
            return
        if p.is_dma:
            c.dmawaits.append((p.sem, p.sem.count))
            return
        p.needed = True
        c.deps.append(p)

    def op(self, eng, fn, reads=(), writes=(), dma_sem=None):
        is_dma = dma_sem is not None
        o = Op(eng, fn, is_dma)
        st = self.__dict__.setdefault("tagstat", {})
        key = (getattr(self, "tag", "-"), eng)
        st[key] = st.get(key, 0) + 1
        rr, ww = _regs(reads), _regs(writes)
        for r in rr:
            for e, p in r.writers.items():
                if (not is_dma) and (not p.is_dma) and e == eng and eng == "pe":
                    continue
                self._dep(o, p)
        for r in ww:
            for e, p in r.readers.items():
                if (not is_dma) and (not p.is_dma) and e == eng and eng == "pe":
                    continue
                self._dep(o, p)
            for e, p in r.writers.items():
                if (not is_dma) and (not p.is_dma) and e == eng and eng == "pe":
                    continue
                if is_dma and p.is_dma and p.sem is dma_sem:
                    continue
                self._dep(o, p)
        key = ("dma", id(o)) if is_dma else eng
        if is_dma:
            dma_sem.count += 16
            o.sem, o.val = dma_sem, dma_sem.count
        for r in rr:
            r.readers[key] = o
        for r in ww:
            r.writers = {key: o}
            r.readers = {}
        self.ops[eng].append(o)
        return o

    def pe(self, fn, reads, writes):
        return self.op("pe", fn, reads, writes)

    def act(self, fn, reads, writes):
        return self.op("act", fn, reads, writes)

    def dve(self, fn, reads, writes):
        return self.op("dve", fn, reads, writes)

    def pool(self, fn, reads, writes):
        return self.op("pool", fn, reads, writes)

    def dma(self, q, out, in_, reads, writes, sem, is_output=False, **kw):
        if is_output:
            self.out_sems.add(sem)
        return self.op(q, lambda e: e.dma_start(out=out, in_=in_, **kw), reads, writes, dma_sem=sem)

    def finish(self):
        nc = self.nc
        fin = Op("sp", None, False)
        for s in self.sems:
            if s.name.startswith("d_") and s.count > 0:
                fin.dmawaits.append((s, s.count))
        self.ops["sp"].append(fin)
        for e in ("pe", "act", "dve", "pool"):
            n = 0
            for o in self.ops[e]:
                if o.is_dma:
                    continue
                if o.needed:
                    n += 1
                    o.sem, o.val = self.esem[e], n
            self.esem[e].count = n
        for s in self.sems:
            if s.count > 0:
                s.h = self.es.enter_context(nc.semaphore(s.name))
        block = self.es.enter_context(nc.Block())
        stats = {}
        for e in self.ENGS:
            ops = self.ops[e]

            def body(eng, ops=ops, e=e):
                seen = {}
                nw = 0
                for o in ops:
                    ws = {}
                    for p in o.deps:
                        ws[p.sem] = max(ws.get(p.sem, 0), p.val)
                    for s, v in o.dmawaits:
                        ws[s] = max(ws.get(s, 0), v)
                    for s, v in ws.items():
                        if seen.get(s, 0) >= v:
                            continue
                        seen[s] = v
                        eng.wait_ge(s.h, v)
                        nw += 1
                    if o.fn is None:
                        continue
                    ins = o.fn(eng)
                    if o.is_dma:
                        ins.then_inc(o.sem.h, 16)
                    elif o.needed:
                        ins.then_inc(o.sem.h, 1)
                stats[e] = (len(ops), nw)

            getattr(block, self.ATTR[e])(body)
        self.es.close()
        return stats


class Cfg:
    def __init__(self, NH=2, NTH=8, sample=True, layers=2, mix=("ret", "ssd", "gdn"), pegate=True, ffn=True):
        self.NH, self.NTH, self.sample, self.layers = NH, NTH, sample, layers
        self.mix, self.pegate, self.ffn = mix, pegate, ffn
        self.dbg = {}
        self.T = NH * NTH * 128


class MK:
    def __init__(self, cfg):
        self.cfg = cfg
        nc = bass.Bass("TRN2", target_bir_lowering=False)
        self.nc = nc
        self.P = Prog(nc)
        self.declare_io()
        self.alloc()

    def din(self, name, shape):
        return self.nc.dram_tensor(name, list(shape), F32, kind="ExternalInput").ap()

    def dout(self, name, shape):
        return self.nc.dram_tensor(name, list(shape), F32, kind="ExternalOutput").ap()

    def declare_io(self):
        c = self.cfg
        T = c.T
        L = DEPTH
        self.i = {}
        I = self.i
        I["xp"] = self.din("xp", [T, D])
        I["pp"] = self.din("pp", [L, T, PLE])
        if c.sample:
            I["xs"] = self.din("xs", [NS, D])
            I["ps"] = self.din("ps", [L, NS, PLE])
            I["st_ret"] = self.din("st_ret", [L, NS, 4, 128, 256])
            I["st_ssm"] = self.din("st_ssm", [L, NS, 16, 128, 64])
            I["st_ssm_conv"] = self.din("st_ssm_conv", [L, NS, 3, 1536])
            I["st_gdn"] = self.din("st_gdn", [L, NS, 8, 128, 128])
            I["st_gdn_conv"] = self.din("st_gdn_conv", [L, NS, 3, 3072])
        for nm, shp in [("ln_g", [L, 4, D]), ("ln_b", [L, 4, D]), ("ffn_wg", [L, 2, D, FFN]),
                        ("ffn_wu", [L, 2, D, FFN]), ("ffn_wd", [L, 2, FFN, D]), ("w_in", [L, D, IN_DIM]),
                        ("ssm_conv_w", [L, 4, 1536]), ("ssm_conv_b", [L, 1536]), ("ssm_dt_bias", [L, 16]),
                        ("ssm_a_log", [L, 16]), ("ssm_d", [L, 16]), ("ssm_norm_w", [L, 1024]),
                        ("gdn_conv_w", [L, 4, 3072]), ("gdn_dt_bias", [L, 8]), ("gdn_a_log", [L, 8]),
                        ("gdn_norm_w", [L, 128]), ("w_ret_out", [L, D, D]), ("w_ssm_out", [L, D, D]),
                        ("w_gdn_out", [L, D, D]), ("w_o", [L, D, D]), ("pe_proj", [L, PLE, D]),
                        ("pe_gate", [L, D, D])]:
            I[nm] = self.din(nm, shp)
        I["c_ident"] = self.din("c_ident", [128, 128])
        I["c_rope"] = self.din("c_rope", [T + NS, 4, 64])
        I["c_retmask"] = self.din("c_retmask", [4, 128, 128])
        I["c_retrow"] = self.din("c_retrow", [4, 3, 128])
        I["c_masks"] = self.din("c_masks", [8, 128, 128])
        self.o = {}
        O = self.o
        O["y_p"] = self.dout("y_p", [T, D])
        O["ret_p"] = self.dout("ret_p", [L, 4, 128, 256])
        O["ssm_p"] = self.dout("ssm_p", [L, 16, 128, 64])
        O["ssm_conv_p"] = self.dout("ssm_conv_p", [L, 3, 1536])
        O["gdn_p"] = self.dout("gdn_p", [L, 8, 128, 128])
        O["gdn_conv_p"] = self.dout("gdn_conv_p", [L, 3, 3072])
        if c.sample:
            O["y_s"] = self.dout("y_s", [NS, D])
            O["ret_s"] = self.dout("ret_s", [L, NS, 4, 128, 256])
            O["ssm_s"] = self.dout("ssm_s", [L, NS, 16, 128, 64])
            O["ssm_conv_s"] = self.dout("ssm_conv_s", [L, NS, 3, 1536])
            O["gdn_s"] = self.dout("gdn_s", [L, NS, 8, 128, 128])
            O["gdn_conv_s"] = self.dout("gdn_conv_s", [L, NS, 3, 3072])

    def alloc(self):
        c, P = self.cfg, self.P
        self.NTT = c.NTH + (1 if c.sample else 0)
        self.TS = c.NTH * 128 + (NS if c.sample else 0)
        NTT, TS = self.NTT, self.TS
        self.xa = P.sbuf("xa", [128, NTT, D], F32, nreg=NTT * 2)
        self.xT = P.sbuf("xT", [128, 8, TS], BF16, nreg=NTT)
        self.hT = P.sbuf("hT", [128, 16, TS], BF16, nreg=16)
        self.NSLOT = 6
        self.wslots = [P.sbuf(f"w{i}", [128, 8, 512], BF16) for i in range(self.NSLOT)]
        self.wi = 0
        self.lnp = P.sbuf("lnp", [128, 2, D], F32)
        self.identb = P.sbuf("identb", [128, 128], BF16)
        self.identf = P.sbuf("identf", [128, 128], F32)
        self.tmpA = [P.sbuf(f"tmpA{i}", [128, D], F32) for i in range(2)]
        self.tmpB = [P.sbuf(f"tmpB{i}", [128, D], F32) for i in range(2)]
        self.xb = [P.sbuf(f"xb{i}", [128, D], BF16) for i in range(1)]
        self.lnst = [P.sbuf(f"lnst{i}", [128, 2, 6], F32) for i in range(2)]
        self.lnmv = [P.sbuf(f"lnmv{i}", [128, 4], F32) for i in range(2)]
        self.ps = P.psum("ps", [128, 8, 512], F32, nreg=8)
        self.psi = 0
        self.rr = {}
        P.dma("pool", self.identb.t[:], self.i["c_ident"], [], [self.identb], self.identb.sem())
        P.dma("sp", self.identf.t[:], self.i["c_ident"], [], [self.identf], self.identf.sem())

    def rot(self, key, lst):
        i = self.rr.get(key, 0)
        self.rr[key] = i + 1
        return lst[i % len(lst)]

    def bank(self):
        b = 2 + self.psi
        self.psi = (self.psi + 1) % 6
        return b

    def bank2(self):
        if self.psi % 2:
            self.psi = (self.psi + 1) % 6
        b = 2 + self.psi
        self.psi = (self.psi + 2) % 6
        return b

    def wslot(self):
        s = self.wslots[self.wi % self.NSLOT]
        self.wi += 1
        return s

    def wload(self, slot, c0, src2d, K=1024):
        ncols = src2d.shape[1]
        kc = K // 128
        self.P.dma("pool", slot.t[:, 0:kc, c0:c0 + ncols], src2d.rearrange("(k p) c -> p k c", p=128),
                   [], [slot], slot.sem())

    def tiles(self):
        c = self.cfg
        out = [(m, 128, m * 128) for m in range(c.NTH)]
        if self.has_sample:
            out.append((c.NTH, NS, c.NTH * 128))
        return out

    def nblocks(self):
        c = self.cfg
        TP = c.NTH * 128
        out = [(n0, min(512, TP - n0)) for n0 in range(0, TP, 512)]
        if self.has_sample:
            out.append((TP, NS))
        return out

    def xT_regs(self, n0, nsz):
        return (self.xT, list(range(n0 // 128, (n0 + nsz - 1) // 128 + 1)))

    @staticmethod
    def run_pipelined(gens, depth):
        it = iter(gens)
        active = []
        done = False
        while True:
            if not done and len(active) < depth:
                try:
                    active.append(next(it))
                except StopIteration:
                    done = True
            if not active:
                if done:
                    break
                continue
            for g in list(active):
                try:
                    next(g)
                except StopIteration:
                    active.remove(g)

    def mm(self, out, lhsT, rhs, start, stop, reads, writes):
        n = int(np.prod(rhs.shape[1:]))
        cyc = max(64, n) * (4 if rhs.dtype == F32 else 1)
        pc = self.P.__dict__.setdefault("pecost", {})
        t = getattr(self.P, "tag", "-")
        pc[t] = pc.get(t, 0) + cyc
        self.P.pe(lambda e: e.matmul(out, lhsT, rhs, start=start, stop=stop), reads, writes)

    def emit_xT(self, m, rows, col0, src, final_out=None):
        P = self.P
        xa, xT, ps = self.xa, self.xT, self.ps
        xb = self.rot("xb", self.xb)
        P.act(lambda e: e.activation(xa.t[:rows, m, :], src.t[:rows, :], AF.Copy, scale=float(DN_ALPHA)),
              [src], [(xa, [2 * m, 2 * m + 1])])
        P.dve(lambda e: e.tensor_copy(xb.t[:rows, :], src.t[:rows, :]), [src], [xb])
        b = self.bank()
        pst = ps.t[:, b, :].bitcast(BF16).rearrange("p (k n) -> p k n", k=8)
        for k in range(8):
            P.pe(lambda e, k=k: e.transpose(pst[:, k, :rows], xb.t[:rows, k * 128:(k + 1) * 128],
                                            self.identb.t[:rows, :rows]),
                 [xb, self.identb], [(ps, b)])
        P.act(lambda e: e.activation(xT.t[:, :, col0:col0 + rows], pst[:, :, :rows], AF.Copy),
              [(ps, b)], [(xT, m)])

    def layer_norm(self, l, idx, final=False):
        self.P.tag = "ln"
        P = self.P
        I = self.i
        lnp = self.lnp
        P.dma("sp", lnp.t[:, 0, :], I["ln_g"][l, idx, :].partition_broadcast(128), [], [lnp], lnp.sem())
        P.dma("sp", lnp.t[:, 1, :], I["ln_b"][l, idx, :].partition_broadcast(128), [], [lnp], lnp.sem())
        xa = self.xa

        def tile_gen(i, m, rows, col0):
            par = i % 2
            st, mv, tA, tB = self.lnst[par], self.lnmv[par], self.tmpA[par], self.tmpB[par]
            xr = (xa, [2 * m, 2 * m + 1])
            P.dve(lambda e: e.bn_stats(st.t[:rows, 0, :], xa.t[:rows, m, 0:512]), [xr], [st])
            P.dve(lambda e: e.bn_stats(st.t[:rows, 1, :], xa.t[:rows, m, 512:1024]), [xr], [st])
            P.dve(lambda e: e.bn_aggr(mv.t[:rows, 0:2], st.t[:rows, :, :]), [st], [mv])
            yield
            P.act(lambda e: e.activation(mv.t[:rows, 2:3], mv.t[:rows, 1:2], AF.Ln, bias=float(LN_EPS)), [mv], [mv])
            P.act(lambda e: e.activation(mv.t[:rows, 3:4], mv.t[:rows, 2:3], AF.Exp, scale=-0.5), [mv], [mv])
            yield
            P.dve(lambda e: e.tensor_scalar(tA.t[:rows, :], xa.t[:rows, m, :], mv.t[:rows, 0:1], mv.t[:rows, 3:4],
                                            ALU.subtract, ALU.mult), [xr, mv], [tA])
            P.dve(lambda e: e.tensor_tensor(tA.t[:rows, :], tA.t[:rows, :], lnp.t[:rows, 0, :], ALU.mult), [tA, lnp], [tA])
            P.dve(lambda e: e.tensor_tensor(tB.t[:rows, :], tA.t[:rows, :], lnp.t[:rows, 1, :], ALU.add), [tA, lnp], [tB])
            yield
            if final:
                self.store_y(m, rows, tB)
            else:
                self.emit_xT(m, rows, col0, tB)

        self.run_pipelined((tile_gen(i, m, rows, col0) for i, (m, rows, col0) in enumerate(self.tiles())), 2)

    def store_y(self, m, rows, src):
        c = self.cfg
        if rows == 128:
            t0 = (self.half * c.NTH + m) * 128
            dst = self.o["y_p"][t0:t0 + 128, :]
        else:
            dst = self.o["y_s"][:, :]
        self.P.dma("sp", dst, src.t[:rows, :], [src], [], src.sem(), is_output=True)

    def load_x(self):
        c = self.cfg
        for (m, rows, col0) in self.tiles():
            tB = self.rot("tmpB", self.tmpB)
            if rows == 128:
                t0 = (self.half * c.NTH + m) * 128
                src = self.i["xp"][t0:t0 + 128, :]
            else:
                src = self.i["xs"][:, :]
            self.P.dma("sp", tB.t[:rows, :], src, [], [tB], tB.sem())
            self.emit_xT(m, rows, col0, tB)

    def ffn(self, l, idx):
        self.P.tag = "ffn"
        P, I = self.P, self.i
        ps, xT, hT, xa = self.ps, self.xT, self.hT, self.xa
        wg, wu, wd = I["ffn_wg"][l, idx], I["ffn_wu"][l, idx], I["ffn_wd"][l, idx]
        slots = []

        def loadA(hb):
            s = self.wslot()
            self.wload(s, 0, wg[:, hb * 256:(hb + 1) * 256])
            self.wload(s, 256, wu[:, hb * 256:(hb + 1) * 256])
            return s

        nxt = loadA(0)
        for hb in range(8):
            cur = nxt
            if hb + 1 < 8:
                nxt = loadA(hb + 1)
            for jj in range(2):
                j = hb * 2 + jj
                for (n0, nsz) in self.nblocks():
                    bg, bu = self.bank(), self.bank()
                    xr = self.xT_regs(n0, nsz)
                    for k in range(8):
                        self.mm(ps.t[:, bg, :nsz], cur.t[:, k, jj * 128:(jj + 1) * 128], xT.t[:, k, n0:n0 + nsz],
                                k == 0, k == 7, [cur, xr], [(ps, bg)])
                    for k in range(8):
                        self.mm(ps.t[:, bu, :nsz], cur.t[:, k, 256 + jj * 128:256 + (jj + 1) * 128],
                                xT.t[:, k, n0:n0 + nsz], k == 0, k == 7, [cur, xr], [(ps, bu)])
                    tA = self.rot("tmpA", self.tmpA)
                    P.act(lambda e, bg=bg, nsz=nsz, tA=tA: e.activation(tA.t[:, :nsz], ps.t[:, bg, :nsz], AF.Silu),
                          [(ps, bg)], [tA])
                    P.dve(lambda e, bu=bu, nsz=nsz, n0=n0, j=j, tA=tA: e.tensor_tensor(
                        hT.t[:, j, n0:n0 + nsz], ps.t[:, bu, :nsz], tA.t[:, :nsz], ALU.mult),
                        [(ps, bu), tA], [(hT, j)])
        def loadB(o):
            s0, s1 = self.wslot(), self.wslot()
            self.P.dma("pool", s0.t[:, :, :], wd[0:1024, o * 512:(o + 1) * 512].rearrange("(k p) c -> p k c", p=128),
                       [], [s0], s0.sem())
            self.P.dma("pool", s1.t[:, :, :], wd[1024:2048, o * 512:(o + 1) * 512].rearrange("(k p) c -> p k c", p=128),
                       [], [s1], s1.sem())
            return (s0, s1)

        nxt = loadB(0)
        for o in range(2):
            cur = nxt
            if o == 0:
                nxt = loadB(1)
            for (m, rows, col0) in self.tiles():
                b = self.bank()
                for j in range(16):
                    s = cur[j // 8]
                    self.mm(ps.t[:rows, b, :], hT.t[:, j, col0:col0 + rows], s.t[:, j % 8, :], j == 0, j == 15,
                            [(hT, j), s], [(ps, b)])
                P.dve(lambda e, m=m, rows=rows, b=b, o=o: e.scalar_tensor_tensor(
                    xa.t[:rows, m, o * 512:(o + 1) * 512], ps.t[:rows, b, :], 0.5,
                    xa.t[:rows, m, o * 512:(o + 1) * 512], ALU.mult, ALU.add),
                    [(ps, b), (xa, 2 * m + o)], [(xa, 2 * m + o)])

    def pe_gate(self, l):
        self.P.tag = "pegate"
        P, I = self.P, self.i
        c = self.cfg
        ps, xT, xa = self.ps, self.xT, self.xa
        sg0, sg1, sp = self.wslot(), self.wslot(), self.wslot()
        self.wload(sg0, 0, I["pe_gate"][l][:, 0:512])
        self.wload(sg1, 0, I["pe_gate"][l][:, 512:1024])
        for o in range(2):
            P.dma("pool", sp.t[:, 2 * o:2 * o + 2, :],
                  I["pe_proj"][l][:, o * 512:(o + 1) * 512].rearrange("(k p) c -> p k c", p=128), [], [sp], sp.sem())
        sg = (sg0, sg1)

        def tile_body(m, rows, col0):
            pb = self.rot("xb", self.xb)
            if rows == 128:
                t0 = (self.half * c.NTH + m) * 128
                src = I["pp"][l, t0:t0 + 128, :]
            else:
                src = I["ps"][l, :, :]
            P.dma("pool", pb.t[:rows, 0:PLE], src, [], [pb], pb.sem())
            b = self.bank()
            pst = ps.t[:, b, :].bitcast(BF16).rearrange("p (k n) -> p k n", k=8)
            for k in range(2):
                P.pe(lambda e, k=k, rows=rows, pb=pb: e.transpose(pst[:, k, :rows], pb.t[:rows, k * 128:(k + 1) * 128],
                                                                self.identb.t[:rows, :rows]),
                     [pb, self.identb], [(ps, b)])
            pT = self.rot("pT", self.pT)
            P.dve(lambda e, rows=rows, pT=pT: e.tensor_copy(pT.t[:, :, :rows], pst[:, 0:2, :rows]), [(ps, b)], [pT])
            tA = self.rot("tmpA", self.tmpA)
            for o in range(2):
                bg, bp = self.bank(), self.bank()
                for k in range(8):
                    self.mm(ps.t[:rows, bg, :], xT.t[:, k, col0:col0 + rows], sg[o].t[:, k, :], k == 0, k == 7,
                            [(xT, m), sg[o]], [(ps, bg)])
                for k in range(2):
                    self.mm(ps.t[:rows, bp, :], pT.t[:, k, :rows], sp.t[:, 2 * o + k, :], k == 0, k == 1,
                            [pT, sp], [(ps, bp)])
                P.act(lambda e, rows=rows, bg=bg, o=o, tA=tA: e.activation(
                    tA.t[:rows, o * 512:(o + 1) * 512], ps.t[:rows, bg, :], AF.Sigmoid), [(ps, bg)], [tA])
                P.dve(lambda e, rows=rows, bp=bp, o=o, tA=tA: e.tensor_tensor(
                    tA.t[:rows, o * 512:(o + 1) * 512], ps.t[:rows, bp, :], tA.t[:rows, o * 512:(o + 1) * 512], ALU.mult),
                    [(ps, bp), tA], [tA])
                P.dve(lambda e, m=m, rows=rows, o=o, tA=tA: e.tensor_tensor(
                    xa.t[:rows, m, o * 512:(o + 1) * 512], xa.t[:rows, m, o * 512:(o + 1) * 512],
                    tA.t[:rows, o * 512:(o + 1) * 512], ALU.add),
                    [tA, (xa, 2 * m + o)], [(xa, 2 * m + o)])

        for (m, rows, col0) in self.tiles():
            tile_body(m, rows, col0)


    def alloc_mix(self):
        P, I = self.P, self.i
        c = self.cfg
        self.S = P.sbuf("S", [128, 1024], F32, nreg=16)
        self.Sb = P.sbuf("Sb", [128, 1024], BF16, nreg=16)
        self.cst = P.sbuf("cst", [128, 8], F32)
        P.dve(lambda e: e.memset(self.cst.t[:, 0:1], float(LN_EPS)), [], [self.cst])
        P.dve(lambda e: e.memset(self.cst.t[:, 1:2], float(NORM_EPS)), [], [self.cst])
        P.dve(lambda e: e.memset(self.cst.t[:, 2:3], -0.5), [], [self.cst])
        P.dve(lambda e: e.memset(self.cst.t[:, 3:4], 1.0), [], [self.cst])
        P.dve(lambda e: e.memset(self.cst.t[:, 4:5], 1.0 / 512.0), [], [self.cst])
        P.dve(lambda e: e.memset(self.cst.t[:, 5:6], 1.0 / 128.0), [], [self.cst])
        self.eyeb = P.sbuf("eyeb", [128, 16, 16], BF16)
        self.eyep = P.sbuf("eyep", [16, 16], F32)
        self.eyef = P.sbuf("eyef", [128, 16, 16], F32)
        P.dma("sp", self.eyef.t[:], I["c_ident"][0:16, 0:16].unsqueeze(0).broadcast_to([128, 16, 16]), [], [self.eyef], self.eyef.sem())
        P.dma("pool", self.eyeb.t[:], I["c_ident"][0:16, 0:16].unsqueeze(0).broadcast_to([128, 16, 16]),
              [], [self.eyeb], self.eyeb.sem())
        P.dma("sp", self.eyep.t[:], I["c_ident"][0:16, 0:16], [], [self.eyep], self.eyep.sem())
        self.masks = P.sbuf("masks", [128, 8, 128], F32)
        P.dma("sp", self.masks.t[:], I["c_masks"].rearrange("m j i -> j m i"), [], [self.masks], self.masks.sem())
        self.A = Arena(P, "arena", 74 * 1024)
        self.tail_ssm = [P.sbuf(f"tail_ssm{l}", [128, 12, 3], F32, nreg=12) for l in range(DEPTH)]
        self.tail_gdn = [P.sbuf(f"tail_gdn{l}", [128, 24, 3], F32, nreg=24) for l in range(DEPTH)]
        L = DEPTH
        self.d_state = {}
        for nm in ("ret_p", "ssm_p", "gdn_p"):
            self.d_state[nm] = Buf(P, "dst_" + nm, self.o[nm], nreg=L)
        self.state_view = {
            "ret_p": lambda ap: ap.rearrange("h k v -> k h v"),
            "ssm_p": lambda ap: ap.rearrange("h n d -> n h d"),
            "gdn_p": lambda ap: ap.rearrange("h k v -> k h v"),
        }

    def state_load(self, nm, l):
        P, S, Sb = self.P, self.S, self.Sb
        db = self.d_state[nm]
        hd = {"ret_p": 4, "ssm_p": 16, "gdn_p": 8}[nm]
        if self.half == 0:
            P.dve(lambda e: e.memset(S.t[:], 0.0), [], [S])
        else:
            P.dma("sp", S.t[:].rearrange("p (h v) -> p h v", h=hd), self.state_view[nm](db.t[l]), [(db, l)], [S], S.sem())
        P.act(lambda e: e.activation(Sb.t[:], S.t[:], AF.Copy), [S], [Sb])

    def state_store(self, nm, l):
        P, S = self.P, self.S
        db = self.d_state[nm]
        hd = {"ret_p": 4, "ssm_p": 16, "gdn_p": 8}[nm]
        P.dma("sp", self.state_view[nm](db.t[l]), S.t[:].rearrange("p (h v) -> p h v", h=hd), [S], [(db, l)], S.sem(),
              is_output=True)

    def sigmoid_chain(self, tmp_ap, src_ap, reads, tmp_buf, scale_in=-1.0, bias_in=0.0):
        P = self.P
        if isinstance(bias_in, float) and bias_in == 0.0:
            P.act(lambda e: e.activation(tmp_ap, src_ap, AF.Exp, scale=scale_in), reads, [tmp_buf])
        else:
            P.act(lambda e: e.activation(tmp_ap, src_ap, AF.Exp, scale=scale_in, bias=bias_in), reads, [tmp_buf])
        P.act(lambda e: e.activation(tmp_ap, tmp_ap, AF.Ln, bias=1.0), [tmp_buf], [tmp_buf])
        P.act(lambda e: e.activation(tmp_ap, tmp_ap, AF.Exp, scale=-1.0), [tmp_buf], [tmp_buf])

    def rstd_pool(self, nst, rows, eps_col, scale_col=None):
        P = self.P
        eps = float(LN_EPS) if eps_col == 0 else float(NORM_EPS)
        P.act(lambda e: e.activation(nst.t[:rows, 2:3], nst.t[:rows, 1:2], AF.Ln, bias=eps), [nst], [nst])
        P.act(lambda e: e.activation(nst.t[:rows, 3:4], nst.t[:rows, 2:3], AF.Exp, scale=-0.5), [nst], [nst])

    def transposes_to(self, dst_ap_fn, src_fn, n, rows, src_buf, dst_writes, bt=None):
        P, ps = self.P, self.ps
        bt = self.bank() if bt is None else bt
        pst = ps.t[:, bt, :].bitcast(BF16).rearrange("p (k n) -> p k n", k=8)
        for k in range(n):
            P.pe(lambda e, k=k: e.transpose(pst[:, k, :rows], src_fn(k), self.identb.t[:rows, :rows]),
                 [src_buf, self.identb], [(ps, bt)])
        dst_ap_fn(pst, bt)

    def retention(self, l):
        self.P.tag = "ret"
        P, I, c = self.P, self.i, self.cfg
        ps, xT, hT = self.ps, self.xT, self.hT
        w_in = I["w_in"][l]
        S, Sb, A = self.S, self.Sb, self.A
        A.reset()
        retmask = A.alloc("retmask", [4, 128], F32)
        retrow = A.alloc("retrow", [4, 128], F32)
        retcol = A.alloc("retcol", [4], F32)
        P.dma("sp", retmask.t[:], I["c_retmask"].rearrange("h j i -> j h i"), [], [retmask], retmask.sem())
        for h in range(4):
            P.dma("sp", retrow.t[:, h, :], I["c_retrow"][h, 0, :].partition_broadcast(128), [], [retrow], retrow.sem())
        P.dma("sp", retcol.t[:], I["c_retrow"][:, 1, :].rearrange("h j -> j h"), [], [retcol], retcol.sem(),
              allow_slow_non_contiguous=True)
        rope_b = [A.alloc("rope", [4, 64], F32) for _ in range(2)]
        qkr_b = [A.alloc("qkr", [2, 2, 128], BF16) for _ in range(2)]
        rt_b = [A.alloc("rt", [4, 256], F32, nreg=4) for _ in range(2)]
        qkT_b = [A.alloc("qkT", [4, 128], BF16) for _ in range(2)]
        qdT_b = [A.alloc("qdT", [128], BF16) for _ in range(4)]
        kdec_b = [A.alloc("kdec", [128], BF16) for _ in range(4)]
        vbf_b = [A.alloc("vbf", [256], BF16) for _ in range(4)]
        sgt_b = [A.alloc("sgt", [256], F32) for _ in range(4)]
        scm_b = [A.alloc("scm", [128], BF16) for _ in range(4)]
        ogt_b = [A.alloc("ogt", [256], F32) for _ in range(4)]
        ogb_b = [A.alloc("ogb", [256], BF16) for _ in range(2)]
        nst_b = [A.alloc("nst", [8], F32) for _ in range(4)]
        st6_b = [A.alloc("st6", [6], F32) for _ in range(4)]
        if self.has_sample:
            qTm = A.alloc("qTm", [16, 16], BF16)
            ktm = A.alloc("ktm", [16, 128], BF16)
            Ss_b = [A.alloc("Ss", [2, 256], F32) for _ in range(2)]
            Ssb_b = [A.alloc("Ssb", [256], BF16) for _ in range(2)]
        self.state_load("ret_p", l)
        lg = [float(np.float64(np.log1p(-np.float32(2.0) ** np.float32(-5.0 - h)).astype(np.float32))) for h in range(4)]

        def load_pair(pr):
            sA, sB, sC = self.wslot(), self.wslot(), self.wslot()
            h0, h1 = 2 * pr, 2 * pr + 1
            for i, h in enumerate((h0, h1)):
                self.wload(sA, i * 256, w_in[:, C_RQ + h * 128:C_RQ + (h + 1) * 128])
                self.wload(sA, i * 256 + 128, w_in[:, C_RK + h * 128:C_RK + (h + 1) * 128])
            for s_, h in ((sB, h0), (sC, h1)):
                self.wload(s_, 0, w_in[:, C_RV + h * 256:C_RV + (h + 1) * 256])
                self.wload(s_, 256, w_in[:, C_RG + h * 256:C_RG + (h + 1) * 256])
            return (sA, sB, sC)

        def tile_gen(i, pr, W, m, rows, col0):
            par = i % 2
            X, Y, Z = 2 + 3 * par, 3 + 3 * par, 4 + 3 * par
            sA = W[0]
            is_s = rows != 128
            t0 = (self.half * c.NTH * 128 + col0) if not is_s else c.T
            rope, rt, qkr, qkT = rope_b[par], rt_b[par], qkr_b[par], qkT_b[par]
            P.dma("sp", rope.t[:rows], I["c_rope"][t0:t0 + rows], [], [rope], rope.sem())
            for k in range(8):
                self.mm(ps.t[:rows, X, :], xT.t[:, k, col0:col0 + rows], sA.t[:, k, :], k == 0, k == 7,
                        [(xT, m), sA], [(ps, X)])
            qk5 = ps.t[:rows, X, :].rearrange("p (h a b f) -> p h a b f", h=2, a=2, b=2)
            x1, x2 = qk5[:, :, :, 0, :], qk5[:, :, :, 1, :]
            rp = rope.t[:rows].rearrange("p (a b) f -> p a b f", a=2)
            cos = rp[:, :, 0, :].unsqueeze(1).broadcast_to([rows, 2, 2, 64])
            sin = rp[:, :, 1, :].unsqueeze(1).broadcast_to([rows, 2, 2, 64])
            tv = [rt.t[:rows, j, :].rearrange("p (h a f) -> p h a f", h=2, a=2) for j in range(4)]
            P.dve(lambda e: e.tensor_tensor(tv[0], x1, cos, ALU.mult), [(ps, X), rope], [(rt, 0)])
            P.dve(lambda e: e.tensor_tensor(tv[1], x2, sin, ALU.mult), [(ps, X), rope], [(rt, 1)])
            P.dve(lambda e: e.tensor_tensor(tv[2], x1, sin, ALU.mult), [(ps, X), rope], [(rt, 2)])
            P.dve(lambda e: e.tensor_tensor(tv[3], x2, cos, ALU.mult), [(ps, X), rope], [(rt, 3)])
            P.dve(lambda e: e.tensor_tensor(qkr.t[:rows, :, :, 0:64], tv[0], tv[1], ALU.subtract), [(rt, 0), (rt, 1)], [qkr])
            P.dve(lambda e: e.tensor_tensor(qkr.t[:rows, :, :, 64:128], tv[2], tv[3], ALU.add), [(rt, 2), (rt, 3)], [qkr])
            yield

            def evac(pst, bt):
                P.act(lambda e: e.activation(qkT.t[:, :, :rows], pst[:, 0:4, :rows], AF.Copy), [(ps, bt)], [qkT])

            self.transposes_to(evac, lambda k: qkr.t[:rows, k // 2, k % 2, :], 4, rows, qkr, None, bt=Y)
            yield
            for hh in range(2):
                yield from head_gen(par, (X, Y, Z), pr, W, m, rows, col0, hh, qkr, qkT)

        def head_gen(par, banks, pr, W, m, rows, col0, hh, qkr, qkT):
            X, Y, Z = banks
            h = 2 * pr + hh
            sV = W[1 + hh]
            is_s = rows != 128
            gam = float(np.exp(lg[h]))
            bi = 2 * par + hh
            vbf, sgt, qdT, kdec, scm, ogt, nst, st6 = (vbf_b[bi], sgt_b[bi], qdT_b[bi], kdec_b[bi], scm_b[bi], ogt_b[bi],
                                                      nst_b[bi], st6_b[bi])
            ogb = ogb_b[par]
            for k in range(8):
                self.mm(ps.t[:rows, X, :], xT.t[:, k, col0:col0 + rows], sV.t[:, k, :], k == 0, k == 7, [(xT, m), sV], [(ps, X)])
            P.act(lambda e: e.activation(vbf.t[:rows, :], ps.t[:rows, X, 0:256], AF.Copy), [(ps, X)], [vbf])
            self.sigmoid_chain(sgt.t[:rows, :], ps.t[:rows, X, 256:512], [(ps, X)], sgt)
            yield
            P.dve(lambda e: e.tensor_tensor(sgt.t[:rows, :], ps.t[:rows, X, 256:512], sgt.t[:rows, :], ALU.mult), [(ps, X), sgt], [sgt])
            bo = Z if not is_s else 0
            Sr = (S, list(range(4 * h, 4 * h + 4)))
            Sbr = (Sb, list(range(4 * h, 4 * h + 4)))
            if not is_s:
                P.dve(lambda e: e.tensor_tensor(qdT.t[:, :], qkT.t[:, 2 * hh, :], retrow.t[:, h, :], ALU.mult), [qkT, retrow], [qdT])
                P.dve(lambda e: e.tensor_scalar(kdec.t[:, :], qkr.t[:, hh, 1, :], retcol.t[:, h:h + 1], None, ALU.mult), [qkr, retcol], [kdec])
                self.mm(ps.t[:, Y, 0:128], qkT.t[:, 2 * hh + 1, :], qkT.t[:, 2 * hh, :], True, True, [qkT], [(ps, Y)])
                yield
                P.dve(lambda e: e.tensor_tensor(scm.t[:, :], ps.t[:, Y, 0:128], retmask.t[:, h, :], ALU.mult), [(ps, Y), retmask], [scm])
                yield
                self.mm(ps.t[:, bo, 0:256], scm.t[:, :], vbf.t[:, :], True, False, [scm, vbf], [(ps, bo)])
                self.mm(ps.t[:, bo, 0:256], qdT.t[:, :], Sb.t[:, h * 256:(h + 1) * 256], False, True, [qdT, Sbr], [(ps, bo)])
                self.mm(ps.t[:, Y, 0:256], kdec.t[:, :], vbf.t[:, :], True, True, [kdec, vbf], [(ps, Y)])
                yield
                P.dve(lambda e: e.scalar_tensor_tensor(S.t[:, h * 256:(h + 1) * 256], S.t[:, h * 256:(h + 1) * 256],
                                                       float(np.exp(lg[h] * 128)), ps.t[:, Y, 0:256], ALU.mult, ALU.add), [Sr, (ps, Y)], [Sr])
                P.act(lambda e: e.activation(Sb.t[:, h * 256:(h + 1) * 256], S.t[:, h * 256:(h + 1) * 256], AF.Copy), [Sr], [Sbr])
            else:
                P.dve(lambda e: e.tensor_tensor(qTm.t[:], qkT.t[:, 2 * hh, 0:16].unsqueeze(1).broadcast_to([128, 16, 16]),
                                                self.eyeb.t[:], ALU.mult), [qkT, self.eyeb], [qTm])
                P.dve(lambda e: e.tensor_tensor(ktm.t[0:16], qkr.t[0:16, hh, 1, :].unsqueeze(1).broadcast_to([16, 16, 128]),
                                                self.eyep.t[:, :].unsqueeze(2).broadcast_to([16, 16, 128]), ALU.mult), [qkr, self.eyep], [ktm])
                self.run_pipelined((self.ret_sample(l, h, s_, gam, vbf, bo, Ss_b[s_ % 2], Ssb_b[s_ % 2], qTm, ktm, (Y, X)[s_ % 2])
                                    for s_ in range(NS)), 2)
            P.dve(lambda e: e.bn_stats(st6.t[:rows, :], ps.t[:rows, bo, 0:256]), [(ps, bo)], [st6])
            P.dve(lambda e: e.bn_aggr(nst.t[:rows, 0:2], st6.t[:rows, :]), [st6], [nst])
            yield
            self.rstd_pool(nst, rows, 0)
            yield
            P.dve(lambda e: e.tensor_scalar(ogt.t[:rows, :], ps.t[:rows, bo, 0:256], nst.t[:rows, 0:1], nst.t[:rows, 3:4],
                                            ALU.subtract, ALU.mult), [(ps, bo), nst], [ogt])
            P.dve(lambda e: e.tensor_tensor(ogb.t[:rows, :], ogt.t[:rows, :], sgt.t[:rows, :], ALU.mult), [ogt, sgt], [ogb])
            yield

            def evac(pst, bt):
                P.act(lambda e: e.activation(hT.t[:, 2 * h:2 * h + 2, col0:col0 + rows], pst[:, 0:2, :rows], AF.Copy),
                      [(ps, bt)], [(hT, [2 * h, 2 * h + 1])])

            self.transposes_to(evac, lambda k: ogb.t[:rows, k * 128:(k + 1) * 128], 2, rows, ogb, None, bt=X)
            yield

        nxt = load_pair(0)
        for pr in range(2):
            W = nxt
            if pr == 0:
                nxt = load_pair(1)
            self.run_pipelined((tile_gen(i, pr, W, m, rows, col0) for i, (m, rows, col0) in enumerate(self.tiles())), 2)
        self.state_store("ret_p", l)

    def ret_sample(self, l, h, s_, gam, vbf, bo, Ss, Ssb, qTm, ktm, bd):
        P, I, ps = self.P, self.i, self.ps
        P.dma("sp", Ss.t[:, 0, :], I["st_ret"][l, s_, h], [], [Ss], Ss.sem())
        self.mm(ps.t[:, bd, 0:256], ktm.t[0:16, s_, :], vbf.t[0:16, :], True, True, [ktm, vbf], [(ps, bd)])
        yield
        P.dve(lambda e: e.scalar_tensor_tensor(Ss.t[:, 1, :], Ss.t[:, 0, :], gam, ps.t[:, bd, 0:256], ALU.mult, ALU.add),
              [Ss, (ps, bd)], [Ss])
        yield
        P.dma("sp", self.o["ret_s"][l, s_, h], Ss.t[:, 1, :], [Ss], [], Ss.sem(), is_output=True)
        P.act(lambda e: e.activation(Ssb.t[:, :], Ss.t[:, 1, :], AF.Copy), [Ss], [Ssb])
        yield
        self.mm(ps.t[0:16, bo, 0:256], qTm.t[:, s_, :], Ssb.t[:, :], s_ == 0, s_ == NS - 1, [qTm, Ssb], [(ps, bo)])

    def finale(self, l, w_out, mcol):
        self.P.tag = "finale"
        P, I = self.P, self.i
        ps, xT, hT, xa = self.ps, self.xT, self.hT, self.xa
        w_in, w_o = I["w_in"][l], I["w_o"][l]
        A = self.A
        A.reset()
        gT_b = [A.alloc("gT", [8, 128], BF16) for _ in range(2)]
        gtok_b = [A.alloc("gtok", [D], BF16) for _ in range(2)]
        Wout = (self.wslot(), self.wslot())
        Wm = (self.wslot(), self.wslot())
        Wo = (self.wslot(), self.wslot())
        for o in range(2):
            self.wload(Wout[o], 0, w_out[:, o * 512:(o + 1) * 512])
            self.wload(Wm[o], 0, w_in[:, mcol + o * 512:mcol + (o + 1) * 512])
            self.wload(Wo[o], 0, w_o[:, o * 512:(o + 1) * 512])

        def tile_gen(i, m, rows, col0):
            par = i % 2
            by, bm = 4 * par, 4 * par + 2
            tA, gtok, gT = self.tmpA[par], gtok_b[par], gT_b[par]
            for o in range(2):
                for k in range(8):
                    self.mm(ps.t[:rows, bm + o, :], xT.t[:, k, col0:col0 + rows], Wm[o].t[:, k, :], k == 0, k == 7,
                            [(xT, m), Wm[o]], [(ps, bm + o)])
            for o in range(2):
                for k in range(8):
                    self.mm(ps.t[:rows, by + o, :], hT.t[:, k, col0:col0 + rows], Wout[o].t[:, k, :], k == 0, k == 7,
                            [(hT, k), Wout[o]], [(ps, by + o)])
            pm = ps.t[:rows, bm:bm + 2, :].rearrange("p b n -> p (b n)")
            py = ps.t[:rows, by:by + 2, :].rearrange("p b n -> p (b n)")
            self.sigmoid_chain(tA.t[:rows, :], pm, [(ps, bm), (ps, bm + 1)], tA)
            yield
            P.dve(lambda e: e.tensor_tensor(gtok.t[:rows, :], py, tA.t[:rows, :], ALU.mult), [(ps, by), (ps, by + 1), tA], [gtok])
            yield

            def evac(pst, bt):
                P.act(lambda e: e.activation(gT.t[:, :, :rows], pst[:, :, :rows], AF.Copy), [(ps, bt)], [gT])

            self.transposes_to(evac, lambda k: gtok.t[:rows, k * 128:(k + 1) * 128], 8, rows, gtok, None, bt=bm)
            yield
            for o in range(2):
                bo = by + o
                for k in range(8):
                    self.mm(ps.t[:rows, bo, :], gT.t[:, k, :rows], Wo[o].t[:, k, :], k == 0, k == 7, [gT, Wo[o]], [(ps, bo)])
            yield
            for o in range(2):
                bo = by + o
                P.dve(lambda e, o=o, bo=bo: e.tensor_tensor(xa.t[:rows, m, o * 512:(o + 1) * 512],
                                                            xa.t[:rows, m, o * 512:(o + 1) * 512], ps.t[:rows, bo, :], ALU.add),
                      [(ps, bo), (xa, 2 * m + o)], [(xa, 2 * m + o)])

        self.run_pipelined((tile_gen(i, m, rows, col0) for i, (m, rows, col0) in enumerate(self.tiles())), 2)

    def colvecs(self, dst_ap, tk, r, c, dst_buf):
        P, ps = self.P, self.ps
        b = self.bank()
        for cc in range(c):
            P.pe(lambda e, cc=cc: e.transpose(ps.t[:, b, cc * r:(cc + 1) * r], tk.t[:r, cc * 128:(cc + 1) * 128],
                                              self.identf.t[:r, :r]), [tk, self.identf], [(ps, b)])
        P.act(lambda e: e.activation(dst_ap, ps.t[:, b, 0:c * r], AF.Copy), [(ps, b)], [dst_buf])

    def ssd(self, l):
        self.P.tag = "ssd"
        P, I, c = self.P, self.i, self.cfg
        ps, xT, hT, S, Sb, A = self.ps, self.xT, self.hT, self.S, self.Sb, self.A
        w_in = I["w_in"][l]
        TP = c.NTH * 128
        assert TP <= 512
        hs = self.has_sample
        masks = self.masks
        U, ONES, MB = masks.t[:, 0, :], masks.t[:, 1, :], masks.t[:, 2, :]
        A.reset()
        cwb = A.alloc("cwb", [12, 5], F32)
        negb = A.alloc("negb", [12], F32)
        nwT = A.alloc("nwT", [8], F32)
        dtb = A.alloc("dtb", [16], F32)
        Ab = A.alloc("Ab", [16], F32)
        Db = A.alloc("Db", [16], F32)
        wdt = A.alloc("wdt", [8, 16], BF16)
        off0 = A.off
        tk = A.alloc("tk", [1536], F32)
        tk2 = A.alloc("tk2", [1024], F32)
        P.dma("sp", tk.t[0:4, :], I["ssm_conv_w"][l], [], [tk], tk.sem())
        P.dma("sp", tk.t[4:5, :], I["ssm_conv_b"][l].unsqueeze(0), [], [tk], tk.sem())
        self.colvecs(cwb.t[:].rearrange("p c j -> p (c j)"), tk, 5, 12, cwb)
        P.act(lambda e: e.activation(negb.t[:, :], cwb.t[:, :, 4], AF.Copy, scale=-1.0), [cwb], [negb])
        P.dma("sp", tk2.t[0:1, :], I["ssm_norm_w"][l].unsqueeze(0), [], [tk2], tk2.sem())
        self.colvecs(nwT.t[:, :], tk2, 1, 8, nwT)
        P.dma("sp", dtb.t[:], I["ssm_dt_bias"][l].partition_broadcast(128), [], [dtb], dtb.sem())
        P.dma("sp", Ab.t[:], I["ssm_a_log"][l].partition_broadcast(128), [], [Ab], Ab.sem())
        P.dma("sp", Db.t[:], I["ssm_d"][l].partition_broadcast(128), [], [Db], Db.sem())
        P.act(lambda e: e.activation(Ab.t[:], Ab.t[:], AF.Exp), [Ab], [Ab])
        P.act(lambda e: e.activation(Ab.t[:], Ab.t[:], AF.Copy, scale=-1.0), [Ab], [Ab])
        P.dma("pool", wdt.t[:], w_in[:, C_MDT:C_MDT + 16].rearrange("(k p) c -> p k c", p=128), [], [wdt], wdt.sem())
        A.reset(off0)
        xbc = A.alloc("xbc", [12, TP], BF16, nreg=12)
        xraw_b = [A.alloc("xraw", [3 + TP], F32) for _ in range(2)]
        acc_b = [A.alloc("acc", [TP], F32) for _ in range(2)]
        sgm_b = [A.alloc("sgm", [TP], F32) for _ in range(2)]
        f1 = A.alloc("f1", [1024], F32)
        f2 = A.alloc("f2", [1024], F32)
        f3 = A.alloc("f3", [1024], F32)
        xs_sb = A.alloc("xs_sb", [1024], BF16)
        v = A.alloc("v", [1024], BF16)
        vdec = A.alloc("vdec", [1024], BF16)
        Mh_b = [A.alloc("Mh", [8, 128], BF16) for _ in range(2)]
        Bd_b = [A.alloc("Bd", [8, 128], F32) for _ in range(2)]
        Btok = A.alloc("Btok", [2, 128], BF16)
        ogb = A.alloc("ogb", [1024], BF16)
        sm = A.alloc("sm", [8, 16], F32)
        cumT = A.alloc("cumT", [128], F32)
        nst = A.alloc("nst", [8], F32)
        stg = A.alloc("stg", [512], F32)
        if hs:
            xbcS = A.alloc("xbcS", [12, 16], BF16)
            stc_b = [A.alloc("stc", [3, 128], F32) for _ in range(2)]
            xrs_b = [A.alloc("xrs", [4, 16], F32) for _ in range(2)]
            accS_b = [A.alloc("accS", [16], F32) for _ in range(2)]
            sgS_b = [A.alloc("sgS", [16], F32) for _ in range(2)]
            Eall = A.alloc("Eall", [16, 16], F32)
            Bde = A.alloc("Bde", [16, 16], F32)
            Bm_b = [A.alloc("Bm", [256], BF16) for _ in range(2)]
            Cm_b = [A.alloc("Cm", [2, 16], BF16) for _ in range(2)]
        tail = self.tail_ssm[l]
        if self.half == 0:
            P.dve(lambda e: e.memset(tail.t[:], 0.0), [], [tail])
        self.state_load("ssm_p", l)
        Wx = [self.wslot() for _ in range(3)]
        for i in range(3):
            self.wload(Wx[i], 0, w_in[:, C_MXBC + i * 512:C_MXBC + (i + 1) * 512])
        Wz = (self.wslot(), self.wslot())
        for o in range(2):
            self.wload(Wz[o], 0, w_in[:, C_MZ + o * 512:C_MZ + (o + 1) * 512])

        def conv_chunk(cc):
            slot = Wx[cc // 4]
            cs = (cc % 4) * 128
            par = cc % 2
            acc, sgm = acc_b[par], sgm_b[par]
            b = 2 + par
            for k in range(8):
                self.mm(ps.t[:, b, :TP], slot.t[:, k, cs:cs + 128], xT.t[:, k, 0:TP], k == 0, k == 7,
                        [slot, (xT, list(range(c.NTH)))], [(ps, b)])
            xraw = xraw_b[par]
            P.act(lambda e: e.activation(xraw.t[:, 0:3], tail.t[:, cc, :], AF.Copy), [(tail, cc)], [xraw])
            P.act(lambda e: e.activation(xraw.t[:, 3:3 + TP], ps.t[:, b, :TP], AF.Copy), [(ps, b)], [xraw])
            P.act(lambda e: e.activation(tail.t[:, cc, :], xraw.t[:, TP:TP + 3], AF.Copy), [xraw], [(tail, cc)])
            yield
            P.dve(lambda e: e.tensor_scalar(acc.t[:, :], xraw.t[:, 0:TP], cwb.t[:, cc, 0:1], None, ALU.mult), [xraw, cwb], [acc])
            for j in range(1, 4):
                P.dve(lambda e, j=j: e.scalar_tensor_tensor(acc.t[:, :], xraw.t[:, j:j + TP], cwb.t[:, cc, j:j + 1], acc.t[:, :],
                                                            ALU.mult, ALU.add), [xraw, cwb, acc], [acc])
            yield
            self.sigmoid_chain(sgm.t[:, :], acc.t[:, :], [acc, negb], sgm, scale_in=-1.0, bias_in=negb.t[:, cc:cc + 1])
            yield
            P.dve(lambda e: e.scalar_tensor_tensor(xbc.t[:, cc, :], acc.t[:, :], cwb.t[:, cc, 4:5], sgm.t[:, :], ALU.add, ALU.mult),
                  [acc, cwb, sgm], [(xbc, cc)])
            if hs:
                yield
                b2 = 4 + par
                for k in range(8):
                    self.mm(ps.t[:, b2, 0:16], slot.t[:, k, cs:cs + 128], xT.t[:, k, TP:TP + 16], k == 0, k == 7,
                            [slot, (xT, c.NTH)], [(ps, b2)])
                stc = stc_b[par]
                P.dma("sp", stc.t[0:16, :, :], I["st_ssm_conv"][l, :, :, cc * 128:(cc + 1) * 128], [], [stc], stc.sem())
                for j in range(3):
                    P.pe(lambda e, j=j: e.transpose(ps.t[:, b2, 16 + 16 * j:32 + 16 * j], stc.t[0:16, j, :], self.identf.t[0:16, 0:16]),
                         [stc, self.identf], [(ps, b2)])
                xrs = xrs_b[par]
                P.act(lambda e: e.activation(xrs.t[:, 0:3, :], ps.t[:, b2, 16:64].rearrange("p (j s) -> p j s", j=3), AF.Copy),
                      [(ps, b2)], [xrs])
                P.act(lambda e: e.activation(xrs.t[:, 3, :], ps.t[:, b2, 0:16], AF.Copy), [(ps, b2)], [xrs])
                accS, sgS = accS_b[par], sgS_b[par]
                P.dve(lambda e: e.tensor_scalar(accS.t[:, :], xrs.t[:, 0, :], cwb.t[:, cc, 0:1], None, ALU.mult), [xrs, cwb], [accS])
                for j in range(1, 4):
                    P.dve(lambda e, j=j: e.scalar_tensor_tensor(accS.t[:, :], xrs.t[:, j, :], cwb.t[:, cc, j:j + 1], accS.t[:, :],
                                                                ALU.mult, ALU.add), [xrs, cwb, accS], [accS])
                yield
                self.sigmoid_chain(sgS.t[:, :], accS.t[:, :], [accS, negb], sgS, scale_in=-1.0, bias_in=negb.t[:, cc:cc + 1])
                yield
                P.dve(lambda e: e.scalar_tensor_tensor(xbcS.t[:, cc, :], accS.t[:, :], cwb.t[:, cc, 4:5], sgS.t[:, :], ALU.add, ALU.mult),
                      [accS, cwb, sgS], [xbcS])

        self.run_pipelined((conv_chunk(cc) for cc in range(12)), 2)
        def conv_rows(c0, n, dst_fn):
            for i in range(3):
                b = self.bank()
                for k in range(8):
                    self.mm(ps.t[:n, b, :], xT.t[:, k, c0:c0 + n], Wx[i].t[:, k, :], k == 0, k == 7,
                            [Wx[i], (xT, list(range(self.NTT)))], [(ps, b)])
                P.act(lambda e, b=b: e.activation(stg.t[:n, :], ps.t[:n, b, :], AF.Copy), [(ps, b)], [stg])
                P.dma("sp", dst_fn(i), stg.t[:n, :], [stg], [], stg.sem(), is_output=True)

        if self.half == c.NH - 1:
            conv_rows(TP - 3, 3, lambda i: self.o["ssm_conv_p"][l, :, i * 512:(i + 1) * 512])
        if hs:
            conv_rows(TP, 16, lambda i: self.o["ssm_conv_s"][l, :, 2, i * 512:(i + 1) * 512])
            P.dma("sp", self.o["ssm_conv_s"][l, :, 0:2, :], I["st_ssm_conv"][l, :, 1:3, :], [], [], stg.sem(), is_output=True)

        def tile_body(m, rows, col0):
            is_s = rows != 128
            src = xbcS if is_s else xbc
            sc0 = 0 if is_s else col0
            DT, LA, CUM, ETOK, ELAST, DECL, T16 = [sm.t[:rows, i, :] for i in range(7)]
            bdt = 6
            for k in range(8):
                self.mm(ps.t[:rows, bdt, 0:16], xT.t[:, k, col0:col0 + rows], wdt.t[:, k, :], k == 0, k == 7, [(xT, m), wdt], [(ps, bdt)])
            P.dve(lambda e: e.tensor_tensor(T16, ps.t[:rows, bdt, 0:16], dtb.t[:rows, :], ALU.add), [(ps, bdt), dtb], [sm])
            P.act(lambda e: e.activation(T16, T16, AF.Exp), [sm], [sm])
            P.act(lambda e: e.activation(DT, T16, AF.Ln, bias=1.0), [sm], [sm])
            P.dve(lambda e: e.tensor_tensor(LA, DT, Ab.t[:rows, :], ALU.mult), [sm, Ab], [sm])
            def evac_xs(pst, bt):
                P.act(lambda e: e.activation(xs_sb.t[:rows, :], pst[:rows, :, :].rearrange("p k n -> p (k n)"), AF.Copy), [(ps, bt)], [xs_sb])
            self.transposes_T(evac_xs, lambda k: src.t[:, k, sc0:sc0 + rows], 8, rows, src, bt=7)
            if is_s and l == 0 and self.cfg.dbg.get("ssd_dump"):
                dx = self.nc.dram_tensor("dbg_xs", [16, 1024], F32, kind="ExternalOutput").ap()
                dd = self.nc.dram_tensor("dbg_dt", [16, 16], F32, kind="ExternalOutput").ap()
                P.dma("pool", dx, xs_sb.t[0:16, :], [xs_sb], [], xs_sb.sem(), is_output=True)
                P.dma("sp", dd, sm.t[0:16, 0, :], [sm], [], sm.sem(), is_output=True)
            dt_b = DT.unsqueeze(2).broadcast_to([rows, 16, 64])
            P.dve(lambda e: e.tensor_tensor(v.t[:rows, :].rearrange("p (h d) -> p h d", h=16),
                                            xs_sb.t[:rows, :].rearrange("p (h d) -> p h d", h=16), dt_b, ALU.mult), [xs_sb, sm], [v])
            def evac_b(pst, bt):
                P.act(lambda e: e.activation(Btok.t[:rows, :, :], pst[:rows, 0:2, :], AF.Copy), [(ps, bt)], [Btok])
            self.transposes_T(evac_b, lambda k: src.t[:, 8 + k, sc0:sc0 + rows], 2, rows, src, bt=7)
            if not is_s:
                bc = 6
                self.mm(ps.t[:, bc, 32:48], U, LA, True, True, [masks, sm], [(ps, bc)])
                self.mm(ps.t[:, bc, 48:64], ONES, LA, True, True, [masks, sm], [(ps, bc)])
                self.mm(ps.t[0:16, bc, 64:192], LA, U, True, True, [masks, sm], [(ps, bc)])
                P.act(lambda e: e.activation(CUM, ps.t[:, bc, 32:48], AF.Copy), [(ps, bc)], [sm])
                P.act(lambda e: e.activation(ETOK, ps.t[:, bc, 32:48], AF.Exp), [(ps, bc)], [sm])
                P.act(lambda e: e.activation(ELAST, ps.t[:, bc, 48:64], AF.Exp), [(ps, bc)], [sm])
                P.dve(lambda e: e.tensor_tensor(DECL, ps.t[:, bc, 48:64], CUM, ALU.subtract), [(ps, bc), sm], [sm])
                P.act(lambda e: e.activation(DECL, DECL, AF.Exp), [sm], [sm])
                P.act(lambda e: e.activation(cumT.t[0:16, :], ps.t[0:16, bc, 64:192], AF.Copy), [(ps, bc)], [cumT])
                P.dve(lambda e: e.tensor_tensor(vdec.t[:, :].rearrange("p (h d) -> p h d", h=16),
                                                v.t[:, :].rearrange("p (h d) -> p h d", h=16),
                                                DECL.unsqueeze(2).broadcast_to([128, 16, 64]), ALU.mult), [v, sm], [vdec])
                bsc = 6
                for g in range(2):
                    self.mm(ps.t[:, bsc, 256 + g * 128:256 + (g + 1) * 128], xbc.t[:, 8 + g, col0:col0 + 128], xbc.t[:, 10 + g, col0:col0 + 128],
                            True, True, [xbc], [(ps, bsc)])
                po, pcs, pds = 0, 4, 2
                for g in range(2):
                    self.mm(ps.t[:, pcs + g, :], xbc.t[:, 10 + g, col0:col0 + 128], Sb.t[:, g * 512:(g + 1) * 512], True, True,
                            [xbc, (Sb, list(range(8 * g, 8 * g + 8)))], [(ps, pcs + g)])
                P.dve(lambda e: e.tensor_tensor(f2.t[:, :].rearrange("p (h d) -> p h d", h=16),
                                                ps.t[:, pcs:pcs + 2, :].rearrange("p b (h d) -> p (b h) d", h=8),
                                                ETOK.unsqueeze(2).broadcast_to([128, 16, 64]), ALU.mult), [(ps, pcs), (ps, pcs + 1), sm], [f2])
                def grp(g):
                    Bdg, fg, Mh = Bd_b[g], (f1 if g == 0 else f3), Mh_b[g]
                    pa = 2 + 2 * g
                    P.dve(lambda e: e.tensor_tensor(Bdg.t[0:16, :, :], cumT.t[0:16, :].unsqueeze(1).broadcast_to([16, 8, 128]),
                                                    self.eyep.t[:, 8 * g:8 * g + 8].unsqueeze(2).broadcast_to([16, 8, 128]), ALU.mult),
                          [cumT, self.eyep], [Bdg])
                    yield
                    for hf in range(2):
                        self.mm(ps.t[:, pa + hf, :], ONES[0:16, :], Bdg.t[0:16, 4 * hf:4 * hf + 4, :].rearrange("p h i -> p (h i)"),
                                True, False, [masks, Bdg], [(ps, pa + hf)])
                        for hq in range(4):
                            self.mm(ps.t[:, pa + hf, hq * 128:(hq + 1) * 128], self.identf.t[:, :], MB, False, hq == 3,
                                    [masks, self.identf], [(ps, pa + hf)])
                    yield
                    pav = ps.t[:, pa:pa + 2, :].rearrange("p b (h i) -> p (b h) i", h=4)
                    P.dve(lambda e: e.tensor_tensor(fg.t[:, :].rearrange("p (h i) -> p h i", h=8), pav,
                                                    CUM[:, 8 * g:8 * g + 8].unsqueeze(2).broadcast_to([128, 8, 128]),
                                                    ALU.subtract), [(ps, pa), (ps, pa + 1), sm], [fg])
                    yield
                    P.act(lambda e: e.activation(fg.t[:, :], fg.t[:, :], AF.Exp), [fg], [fg])
                    yield
                    P.dve(lambda e: e.tensor_tensor(Mh.t[:, :, :], fg.t[:, :].rearrange("p (h i) -> p h i", h=8),
                                                    ps.t[:, bsc, 256 + g * 128:256 + (g + 1) * 128].unsqueeze(1).broadcast_to([128, 8, 128]),
                                                    ALU.mult), [fg, (ps, bsc)], [Mh])
                    yield
                    for h in range(8):
                        hg = 8 * g + h
                        self.mm(ps.t[:, po + g, h * 64:(h + 1) * 64], Mh.t[:, h, :], v.t[:, hg * 64:(hg + 1) * 64], True, True,
                                [Mh, v], [(ps, po + g)])

                self.run_pipelined([grp(0), grp(1)], 2)
                P.dve(lambda e: e.tensor_tensor(f2.t[:, :], f2.t[:, :], ps.t[:, po:po + 2, :].rearrange("p b n -> p (b n)"), ALU.add),
                      [f2, (ps, po), (ps, po + 1)], [f2])
                for g in range(2):
                    self.mm(ps.t[:, pds + g, :], Btok.t[:, g, :], vdec.t[:, g * 512:(g + 1) * 512], True, True, [Btok, vdec], [(ps, pds + g)])
                P.dve(lambda e: e.tensor_tensor(f1.t[:, :].rearrange("p (h d) -> p h d", h=16), S.t[:, :].rearrange("p (h d) -> p h d", h=16),
                                                ELAST.unsqueeze(2).broadcast_to([128, 16, 64]), ALU.mult), [S, sm], [f1])
                P.dve(lambda e: e.tensor_tensor(S.t[:, :], f1.t[:, :], ps.t[:, pds:pds + 2, :].rearrange("p b n -> p (b n)"), ALU.add),
                      [f1, (ps, pds), (ps, pds + 1)], [S])
                P.act(lambda e: e.activation(Sb.t[:, :], S.t[:, :], AF.Copy), [S], [Sb])
            else:
                ELA = ELAST
                P.act(lambda e: e.activation(ELA, LA, AF.Exp), [sm], [sm])
                P.dve(lambda e: e.tensor_tensor(Bde.t[0:16, :, :], ELA.unsqueeze(1).broadcast_to([16, 16, 16]),
                                                self.eyep.t[:, :].unsqueeze(2).broadcast_to([16, 16, 16]), ALU.mult), [sm, self.eyep], [Bde])
                be = 6
                self.mm(ps.t[:, be, 0:256], ONES[0:16, :], Bde.t[0:16, :, :].rearrange("p s h -> p (s h)"), True, True, [masks, Bde], [(ps, be)])
                P.act(lambda e: e.activation(Eall.t[:, :, :].rearrange("p s h -> p (s h)"), ps.t[:, be, 0:256], AF.Copy), [(ps, be)], [Eall])
                Ss_b = [f2, f3]
                Ssb_b = [vdec, ogb]

                def smp(s_):
                    par = s_ % 2
                    Ss, Ssb, Bm, Cm = Ss_b[par], Ssb_b[par], Bm_b[par], Cm_b[par]
                    pds = 2 + 2 * par
                    P.dma("sp", Ss.t[:, :].rearrange("p (h d) -> p h d", h=16), I["st_ssm"][l, s_].rearrange("h n d -> n h d"), [], [Ss], Ss.sem())
                    P.dve(lambda e: e.tensor_scalar(Bm.t[0:16, :], Btok.t[0:16, :, :].rearrange("p g n -> p (g n)"),
                                                    self.eyep.t[:, s_:s_ + 1], None, ALU.mult), [Btok, self.eyep], [Bm])
                    P.dve(lambda e: e.tensor_tensor(Cm.t[:, :, :], xbcS.t[:, 10:12, :],
                                                    self.eyeb.t[:, s_, :].unsqueeze(1).broadcast_to([128, 2, 16]), ALU.mult),
                          [xbcS, self.eyeb], [Cm])
                    yield
                    for g in range(2):
                        self.mm(ps.t[:, pds + g, :], Bm.t[0:16, g * 128:(g + 1) * 128], v.t[0:16, g * 512:(g + 1) * 512], True, True,
                                [Bm, v], [(ps, pds + g)])
                    yield
                    P.dve(lambda e: e.tensor_tensor(Ss.t[:, :].rearrange("p (h d) -> p h d", h=16), Ss.t[:, :].rearrange("p (h d) -> p h d", h=16),
                                                    Eall.t[:, s_, :].unsqueeze(2).broadcast_to([128, 16, 64]), ALU.mult), [Ss, Eall], [Ss])
                    P.dve(lambda e: e.tensor_tensor(Ss.t[:, :], Ss.t[:, :], ps.t[:, pds:pds + 2, :].rearrange("p b n -> p (b n)"), ALU.add),
                          [Ss, (ps, pds), (ps, pds + 1)], [Ss])
                    yield
                    P.dma("sp", self.o["ssm_s"][l, s_].rearrange("h n d -> n h d"), Ss.t[:, :].rearrange("p (h d) -> p h d", h=16),
                          [Ss], [], Ss.sem(), is_output=True)
                    P.act(lambda e: e.activation(Ssb.t[:, :], Ss.t[:, :], AF.Copy), [Ss], [Ssb])
                    yield
                    for g in range(2):
                        self.mm(ps.t[0:16, g, :], Cm.t[:, g, :], Ssb.t[:, g * 512:(g + 1) * 512], s_ == 0, s_ == NS - 1, [Cm, Ssb], [(ps, g)])

                self.run_pipelined((smp(s_) for s_ in range(NS)), 2)
                P.act(lambda e: e.activation(f2.t[0:16, :], ps.t[0:16, 0:2, :].rearrange("p b n -> p (b n)"), AF.Copy), [(ps, 0), (ps, 1)], [f2])
            P.dve(lambda e: e.tensor_tensor(f3.t[:rows, :].rearrange("p (h d) -> p h d", h=16),
                                            xs_sb.t[:rows, :].rearrange("p (h d) -> p h d", h=16),
                                            Db.t[:rows, :].unsqueeze(2).broadcast_to([rows, 16, 64]), ALU.mult), [xs_sb, Db], [f3])
            P.dve(lambda e: e.tensor_tensor(f2.t[:rows, :], f2.t[:rows, :], f3.t[:rows, :], ALU.add), [f2, f3], [f2])
            pz = 4
            for o in range(2):
                for k in range(8):
                    self.mm(ps.t[:rows, pz + o, :], xT.t[:, k, col0:col0 + rows], Wz[o].t[:, k, :], k == 0, k == 7, [(xT, m), Wz[o]], [(ps, pz + o)])
            pzv = ps.t[:rows, pz:pz + 2, :].rearrange("p b n -> p (b n)")
            self.sigmoid_chain(f3.t[:rows, :], pzv, [(ps, pz), (ps, pz + 1)], f3)
            P.dve(lambda e: e.tensor_tensor(f3.t[:rows, :], pzv, f3.t[:rows, :], ALU.mult), [(ps, pz), (ps, pz + 1), f3], [f3])
            P.dve(lambda e: e.tensor_tensor(f2.t[:rows, :], f2.t[:rows, :], f3.t[:rows, :], ALU.mult), [f2, f3], [f2])
            for g in range(2):
                P.act(lambda e, g=g: e.activation(f3.t[:rows, g * 512:(g + 1) * 512], f2.t[:rows, g * 512:(g + 1) * 512], AF.Square,
                                                  accum_out=nst.t[:rows, 4 + g:5 + g]), [f2], [f3, nst])
            P.act(lambda e: e.activation(nst.t[:rows, 0:2], nst.t[:rows, 4:6], AF.Ln, scale=1.0 / 512.0, bias=float(NORM_EPS)), [nst], [nst])
            P.act(lambda e: e.activation(nst.t[:rows, 2:4], nst.t[:rows, 0:2], AF.Exp, scale=-0.5), [nst], [nst])
            for g in range(2):
                P.dve(lambda e, g=g: e.tensor_scalar(ogb.t[:rows, g * 512:(g + 1) * 512], f2.t[:rows, g * 512:(g + 1) * 512],
                                                     nst.t[:rows, 2 + g:3 + g], None, ALU.mult), [f2, nst], [ogb])

            def evac(pst, bt):
                P.dve(lambda e: e.tensor_tensor(hT.t[:, 0:8, col0:col0 + rows], pst[:, :, :rows],
                                                nwT.t[:, :].unsqueeze(2).broadcast_to([128, 8, rows]), ALU.mult),
                      [(ps, bt), nwT], [(hT, list(range(8)))])

            self.transposes_to(evac, lambda k: ogb.t[:rows, k * 128:(k + 1) * 128], 8, rows, ogb, None, bt=7)

        for (m, rows, col0) in self.tiles():
            tile_body(m, rows, col0)
        self.state_store("ssm_p", l)

    def gdn(self, l):
        self.P.tag = "gdn"
        P, I, c = self.P, self.i, self.cfg
        ps, xT, hT, S, Sb, A = self.ps, self.xT, self.hT, self.S, self.Sb, self.A
        w_in = I["w_in"][l]
        TP = c.NTH * 128
        hs = self.has_sample
        masks = self.masks
        ONES, MBI, MBS = masks.t[:, 1, :], masks.t[:, 4, :], masks.t[:, 5, :]
        U64 = masks.t[:, 3, :]
        A.reset()
        cw = A.alloc("cw", [24, 4], F32)
        dtb = A.alloc("dtb", [8], F32)
        Ab = A.alloc("Ab", [8], F32)
        nwb = A.alloc("nwb", [128], F32)
        wab = A.alloc("wab", [8, 16], BF16)
        off0 = A.off
        tk = A.alloc("tk", [3072], F32)
        P.dma("sp", tk.t[0:4, :], I["gdn_conv_w"][l], [], [tk], tk.sem())
        self.colvecs(cw.t[:].rearrange("p c j -> p (c j)"), tk, 4, 24, cw)
        P.dma("sp", dtb.t[:], I["gdn_dt_bias"][l].partition_broadcast(128), [], [dtb], dtb.sem())
        P.dma("sp", Ab.t[:], I["gdn_a_log"][l].partition_broadcast(128), [], [Ab], Ab.sem())
        P.dma("sp", nwb.t[:], I["gdn_norm_w"][l].partition_broadcast(128), [], [nwb], nwb.sem())
        P.act(lambda e: e.activation(Ab.t[:], Ab.t[:], AF.Exp), [Ab], [Ab])
        P.act(lambda e: e.activation(Ab.t[:], Ab.t[:], AF.Copy, scale=-1.0), [Ab], [Ab])
        P.dma("pool", wab.t[:], w_in[:, C_GA:C_GA + 16].rearrange("(k p) c -> p k c", p=128), [], [wab], wab.sem())
        A.reset(off0)
        qkv = A.alloc("qkv", [24, TP], BF16, nreg=24)
        if hs:
            qkvS = A.alloc("qkvS", [24, 16], F32)
        off1 = A.off
        xraw_b = [A.alloc("xraw", [3 + TP], F32) for _ in range(2)]
        NPAR = 2 if hs else 3
        xraw_b = xraw_b + [A.alloc("xraw", [3 + TP], F32) for _ in range(NPAR - 2)]
        acc_b = [A.alloc("acc", [TP], F32) for _ in range(NPAR)]
        sgm_b = [A.alloc("sgm", [TP], F32) for _ in range(NPAR)]
        sq_b = [A.alloc("sq", [TP], F32) for _ in range(NPAR)]
        stg = A.alloc("stg", [512], F32)
        if hs:
            stc_b = [A.alloc("stc", [3, 128], F32) for _ in range(2)]
            xrs_b = [A.alloc("xrs", [4, 16], F32) for _ in range(2)]
            accS_b = [A.alloc("accS", [16], F32) for _ in range(2)]
            sgS_b = [A.alloc("sgS", [16], F32) for _ in range(2)]
            sqS_b = [A.alloc("sqS", [16], F32) for _ in range(2)]
        tail = self.tail_gdn[l]
        if self.half == 0:
            P.dve(lambda e: e.memset(tail.t[:], 0.0), [], [tail])
        self.state_load("gdn_p", l)
        lnq = float(np.log(128.0 ** -0.5))

        def l2n_g(dst_ap, xin_ap, sq_ap, n, is_q, bufs_r, bufs_w, b):
            P.act(lambda e: e.activation(sq_ap, xin_ap, AF.Square), bufs_r, [bufs_w[0]])
            self.mm(ps.t[:, b, :n], self.masks.t[:, 1, :], sq_ap, True, True, [masks, bufs_w[0]], [(ps, b)])
            yield
            P.act(lambda e: e.activation(sq_ap, ps.t[:, b, :n], AF.Ln, bias=float(NORM_EPS)), [(ps, b)], [bufs_w[0]])
            if is_q:
                P.act(lambda e: e.activation(sq_ap, sq_ap, AF.Exp, scale=-0.5, bias=lnq), [bufs_w[0]], [bufs_w[0]])
            else:
                P.act(lambda e: e.activation(sq_ap, sq_ap, AF.Exp, scale=-0.5), [bufs_w[0]], [bufs_w[0]])
            yield
            P.dve(lambda e: e.tensor_tensor(dst_ap, xin_ap, sq_ap, ALU.mult), list(bufs_r) + [bufs_w[0]], [bufs_w[1]])

        def conv_chunk(cc, slot):
            cs = (cc % 4) * 128
            par = cc % NPAR
            acc, sgm, sq = acc_b[par], sgm_b[par], sq_b[par]
            b = 2 + 2 * par
            bl = 3 + 2 * par
            for k in range(8):
                self.mm(ps.t[:, b, :TP], slot.t[:, k, cs:cs + 128], xT.t[:, k, 0:TP], k == 0, k == 7,
                        [slot, (xT, list(range(c.NTH)))], [(ps, b)])
            xraw = xraw_b[par]
            P.act(lambda e: e.activation(xraw.t[:, 0:3], tail.t[:, cc, :], AF.Copy), [(tail, cc)], [xraw])
            P.act(lambda e: e.activation(xraw.t[:, 3:3 + TP], ps.t[:, b, :TP], AF.Copy), [(ps, b)], [xraw])
            P.act(lambda e: e.activation(tail.t[:, cc, :], xraw.t[:, TP:TP + 3], AF.Copy), [xraw], [(tail, cc)])
            yield
            P.dve(lambda e: e.tensor_scalar(acc.t[:, :], xraw.t[:, 0:TP], cw.t[:, cc, 0:1], None, ALU.mult), [xraw, cw], [acc])
            for j in range(1, 4):
                P.dve(lambda e, j=j: e.scalar_tensor_tensor(acc.t[:, :], xraw.t[:, j:j + TP], cw.t[:, cc, j:j + 1], acc.t[:, :],
                                                            ALU.mult, ALU.add), [xraw, cw, acc], [acc])
            yield
            self.sigmoid_chain(sgm.t[:, :], acc.t[:, :], [acc], sgm)
            yield
            if cc < 16:
                P.dve(lambda e: e.tensor_tensor(acc.t[:, :], acc.t[:, :], sgm.t[:, :], ALU.mult), [acc, sgm], [acc])
                yield from l2n_g(qkv.t[:, cc, :], acc.t[:, :], sq.t[:, :], TP, cc < 8, [acc], [sq, (qkv, cc)], bl)
            else:
                P.dve(lambda e: e.tensor_tensor(qkv.t[:, cc, :], acc.t[:, :], sgm.t[:, :], ALU.mult), [acc, sgm], [(qkv, cc)])
            if hs:
                yield
                b2 = 6 + par
                for k in range(8):
                    self.mm(ps.t[:, b2, 0:16], slot.t[:, k, cs:cs + 128], xT.t[:, k, TP:TP + 16], k == 0, k == 7,
                            [slot, (xT, c.NTH)], [(ps, b2)])
                stc = stc_b[par]
                P.dma("sp", stc.t[0:16, :, :], I["st_gdn_conv"][l, :, :, cc * 128:(cc + 1) * 128], [], [stc], stc.sem())
                for j in range(3):
                    P.pe(lambda e, j=j: e.transpose(ps.t[:, b2, 16 + 16 * j:32 + 16 * j], stc.t[0:16, j, :], self.identf.t[0:16, 0:16]),
                         [stc, self.identf], [(ps, b2)])
                xrs = xrs_b[par]
                P.act(lambda e: e.activation(xrs.t[:, 0:3, :], ps.t[:, b2, 16:64].rearrange("p (j s) -> p j s", j=3), AF.Copy),
                      [(ps, b2)], [xrs])
                P.act(lambda e: e.activation(xrs.t[:, 3, :], ps.t[:, b2, 0:16], AF.Copy), [(ps, b2)], [xrs])
                accS, sgS, sqS = accS_b[par], sgS_b[par], sqS_b[par]
                P.dve(lambda e: e.tensor_scalar(accS.t[:, :], xrs.t[:, 0, :], cw.t[:, cc, 0:1], None, ALU.mult), [xrs, cw], [accS])
                for j in range(1, 4):
                    P.dve(lambda e, j=j: e.scalar_tensor_tensor(accS.t[:, :], xrs.t[:, j, :], cw.t[:, cc, j:j + 1], accS.t[:, :],
                                                                ALU.mult, ALU.add), [xrs, cw, accS], [accS])
                yield
                self.sigmoid_chain(sgS.t[:, :], accS.t[:, :], [accS], sgS)
                yield
                if cc < 16:
                    P.dve(lambda e: e.tensor_tensor(accS.t[:, :], accS.t[:, :], sgS.t[:, :], ALU.mult), [accS, sgS], [accS])
                    yield from l2n_g(qkvS.t[:, cc, :], accS.t[:, :], sqS.t[:, :], 16, cc < 8, [accS], [sqS, qkvS], b2)
                else:
                    P.dve(lambda e: e.tensor_tensor(qkvS.t[:, cc, :], accS.t[:, :], sgS.t[:, :], ALU.mult), [accS, sgS], [qkvS])

        def conv_rows(slot, i, c0, n, dst):
            b = self.bank()
            for k in range(8):
                self.mm(ps.t[:n, b, :], xT.t[:, k, c0:c0 + n], slot.t[:, k, :], k == 0, k == 7,
                        [slot, (xT, list(range(self.NTT)))], [(ps, b)])
            P.act(lambda e: e.activation(stg.t[:n, :], ps.t[:n, b, :], AF.Copy), [(ps, b)], [stg])
            P.dma("sp", dst, stg.t[:n, :], [stg], [], stg.sem(), is_output=True)

        def load_slot(i):
            sl = self.wslot()
            self.wload(sl, 0, w_in[:, C_GQKV + i * 512:C_GQKV + (i + 1) * 512])
            return sl

        nxt = load_slot(0)
        for i in range(6):
            slot = nxt
            if i + 1 < 6:
                nxt = load_slot(i + 1)
            self.run_pipelined((conv_chunk(cc, slot) for cc in range(4 * i, 4 * i + 4)), NPAR)
            if self.half == c.NH - 1:
                conv_rows(slot, i, TP - 3, 3, self.o["gdn_conv_p"][l, :, i * 512:(i + 1) * 512])
            if hs:
                conv_rows(slot, i, TP, 16, self.o["gdn_conv_s"][l, :, 2, i * 512:(i + 1) * 512])
        if hs:
            P.dma("sp", self.o["gdn_conv_s"][l, :, 0:2, :], I["st_gdn_conv"][l, :, 1:3, :], [], [], stg.sem(), is_output=True)
        if self.cfg.dbg.get("gdn_stop", 9) <= 1:
            return
        Wz = (self.wslot(), self.wslot())
        for o in range(2):
            self.wload(Wz[o], 0, w_in[:, C_GZ + o * 512:C_GZ + (o + 1) * 512])

        A.reset(off1)
        sm = A.alloc("sm", [14, 8], F32)
        osb = A.alloc("osb", [1024], F32, nreg=2)
        f3 = A.alloc("f3", [1024], F32)
        ogb = A.alloc("ogb", [1024], BF16)
        nst = A.alloc("nst", [3, 8], F32)
        off2 = A.off
        sm_b = [sm, A.alloc("sm1", [14, 8], F32)]
        cumT_b = [A.alloc("cumT", [2, 128], F32) for _ in range(2)]
        Bd1 = A.alloc("Bd1", [4, 128], F32)
        o_d = A.off
        dtmp = A.alloc("dtmp", [4, 128], F32)
        A.reset(o_d)
        ktok = A.alloc("ktok", [4, 128], BF16)
        A.reset(o_d + 2048)
        f1 = A.alloc("f1", [512], F32)
        Ya, Yb = A.alloc("Ya", [4, 128], F32), A.alloc("Yb", [4, 128], F32)
        YTa, YTb = A.alloc("YTa", [4, 128], F32), A.alloc("YTb", [4, 128], F32)
        rhs = A.alloc("rhs", [4, 128], F32)
        ub = A.alloc("ub", [4, 128], BF16)
        PT_b = [A.alloc("PT", [4, 128], F32) for _ in range(2)]
        attnT_b = [A.alloc("attnT", [4, 128], BF16) for _ in range(2)]
        qdT_b = [A.alloc("qdT", [4, 128], BF16) for _ in range(2)]
        kdec_b = [A.alloc("kdec", [2, 4, 128], BF16) for _ in range(2)]
        vb_b = [A.alloc("vb", [4, 128], F32) for _ in range(2)]
        if hs:
            A.reset(off2)
            Eall = A.alloc("Eall", [16, 8], F32)
            Bde = A.alloc("Bde", [16, 8], F32)
            kTm_b = [A.alloc("kTm", [8, 16], F32) for _ in range(2)]
            qTm_b = [A.alloc("qTm", [8, 16], F32) for _ in range(2)]
            ktS = A.alloc("ktS", [1024], F32)
            vbS = A.alloc("vbS", [1024], F32)
            um_b = [A.alloc("um", [1024], F32) for _ in range(2)]
            SsB = A.alloc("SsB", [1024], F32)
            oacc = A.alloc("oacc", [1024], F32)

        def gates(m, rows, col0, sm=sm):
            R = lambda i: sm.t[:rows, i, :]
            bab = 6
            for k in range(8):
                self.mm(ps.t[:rows, bab, 0:16], xT.t[:, k, col0:col0 + rows], wab.t[:, k, :], k == 0, k == 7, [(xT, m), wab], [(ps, bab)])
            P.act(lambda e: e.activation(R(7), ps.t[:rows, bab, 8:16], AF.Exp, scale=-1.0), [(ps, bab)], [sm])
            P.act(lambda e: e.activation(R(6), R(7), AF.Ln, bias=1.0), [sm], [sm])
            P.act(lambda e: e.activation(R(6), R(6), AF.Copy, scale=-1.0), [sm], [sm])
            P.act(lambda e: e.activation(R(0), R(6), AF.Exp), [sm], [sm])
            P.dve(lambda e: e.tensor_tensor(R(7), ps.t[:rows, bab, 0:8], dtb.t[:rows, :], ALU.add), [(ps, bab), dtb], [sm])
            P.act(lambda e: e.activation(R(7), R(7), AF.Exp), [sm], [sm])
            P.act(lambda e: e.activation(R(7), R(7), AF.Ln, bias=1.0), [sm], [sm])
            P.dve(lambda e: e.tensor_tensor(R(1), R(7), Ab.t[:rows, :], ALU.mult), [sm, Ab], [sm])

        v4 = lambda ap: ap.rearrange("p (h i) -> p h i", h=4)

        def tile_front(m, col0, sm, cumT):
            R = lambda i: sm.t[:, i, :]
            gates(m, 128, col0, sm)
            bc = 6
            self.mm(ps.t[:, bc, 32:40], U64, R(1), True, True, [masks, sm], [(ps, bc)])
            self.mm(ps.t[:, bc, 40:48], masks.t[:, 6, :], R(1), True, True, [masks, sm], [(ps, bc)])
            self.mm(ps.t[:, bc, 48:56], masks.t[:, 7, :], R(1), True, True, [masks, sm], [(ps, bc)])
            self.mm(ps.t[:, bc, 56:64], ONES, R(1), True, True, [masks, sm], [(ps, bc)])
            P.act(lambda e: e.activation(R(2), ps.t[:, bc, 32:40], AF.Copy), [(ps, bc)], [sm])
            P.act(lambda e: e.activation(R(3), ps.t[:, bc, 32:40], AF.Exp), [(ps, bc)], [sm])
            P.dve(lambda e: e.tensor_tensor(R(5), ps.t[:, bc, 40:48], R(2), ALU.subtract), [(ps, bc), sm], [sm])
            P.act(lambda e: e.activation(R(5), R(5), AF.Exp), [sm], [sm])
            P.dve(lambda e: e.tensor_copy(sm.t[:, 12:14, :], sm.t[:, 5:6, :].broadcast_to([128, 2, 8])), [sm], [sm])
            P.dve(lambda e: e.memset(sm.t[64:128, 12, :], 0.0), [sm], [sm])
            P.dve(lambda e: e.memset(sm.t[0:64, 13, :], 0.0), [sm], [sm])
            P.act(lambda e: e.activation(R(8), ps.t[:, bc, 48:56], AF.Exp), [(ps, bc)], [sm])
            P.act(lambda e: e.activation(R(7), ps.t[:, bc, 48:56], AF.Copy), [(ps, bc)], [sm])
            P.dve(lambda e: e.tensor_tensor(R(9), ps.t[:, bc, 56:64], R(7), ALU.subtract), [(ps, bc), sm], [sm])
            P.act(lambda e: e.activation(R(9), R(9), AF.Exp), [sm], [sm])
            P.dve(lambda e: e.scalar_tensor_tensor(R(4), R(0), -1.0, R(3), ALU.mult, ALU.mult), [sm], [sm])
            P.dve(lambda e: e.tensor_tensor(R(10), R(2), R(6), ALU.add), [sm], [sm])
            self.mm(ps.t[0:8, bc, 64:192], R(2), self.identf.t[:, :], True, True, [sm, self.identf], [(ps, bc)])
            self.mm(ps.t[0:8, bc, 192:320], R(10), self.identf.t[:, :], True, True, [sm, self.identf], [(ps, bc)])
            P.act(lambda e: e.activation(cumT.t[0:8, :, :].rearrange("p a i -> p (a i)"), ps.t[0:8, bc, 64:320], AF.Copy), [(ps, bc)], [cumT])

        def unit_gen(u, m, col0, g, part):
            par = u % 2
            BA, BB, BC = (2, 3, 4) if par == 0 else (0, 1, 5)
            sm, cumT = sm_b[m % 2], cumT_b[m % 2]
            PT, attnT, qdT, kdec, vb = PT_b[par], attnT_b[par], qdT_b[par], kdec_b[par], vb_b[par]
            R = lambda i: sm.t[:, i, :]
            hsl = slice(4 * g, 4 * g + 4)
            if part == "A":
                if g == 0:
                    tile_front(m, col0, sm, cumT)
                    yield
                hsl = slice(4 * g, 4 * g + 4)
                eye_g = self.eyep.t[0:8, 4 * g:4 * g + 4].unsqueeze(2).broadcast_to([8, 4, 128])
                bdf = Bd1.t[0:8, :, :].rearrange("p h i -> p (h i)")
                cum_b = R(2)[:, hsl].unsqueeze(2).broadcast_to([128, 4, 128])
                P.dve(lambda e: e.tensor_tensor(Bd1.t[0:8, :, :], cumT.t[0:8, 0, :].unsqueeze(1).broadcast_to([8, 4, 128]), eye_g, ALU.mult),
                      [cumT, self.eyep], [Bd1])
                self.mm(ps.t[:, BA, :], ONES[0:8, :], bdf, True, True, [masks, Bd1], [(ps, BA)])
                self.mm(ps.t[:, BB, :], ONES[0:8, :], bdf, True, False, [masks, Bd1], [(ps, BB)])
                for hq in range(4):
                    self.mm(ps.t[:, BB, hq * 128:(hq + 1) * 128], self.identf.t[:, :], MBI, False, hq == 3, [masks, self.identf], [(ps, BB)])
                for h in range(4):
                    hh = 4 * g + h
                    self.mm(ps.t[:, BC, h * 128:(h + 1) * 128], qkv.t[:, 8 + hh, col0:col0 + 128], qkv.t[:, hh, col0:col0 + 128], True, True,
                            [(qkv, [hh, 8 + hh])], [(ps, BC)])
                yield
                P.act(lambda e: e.activation(dtmp.t[:, :, :].rearrange("p h i -> p (h i)"), ps.t[:, BA, :], AF.Exp), [(ps, BA)], [dtmp])
                P.dve(lambda e: e.tensor_tensor(f1.t[:, :].rearrange("p (h i) -> p h i", h=4), v4(ps.t[:, BB, :]), cum_b, ALU.subtract), [(ps, BB), sm], [f1])
                yield
                P.dve(lambda e: e.tensor_tensor(qdT.t[:, :, :], qkv.t[:, 4 * g:4 * g + 4, col0:col0 + 128], dtmp.t[:, :, :], ALU.mult),
                      [(qkv, list(range(4 * g, 4 * g + 4))), dtmp], [qdT])
                P.act(lambda e: e.activation(f1.t[:, :], f1.t[:, :], AF.Exp), [f1], [f1])
                yield
                P.dve(lambda e: e.tensor_tensor(attnT.t[:, :, :], v4(ps.t[:, BC, :]), f1.t[:, :].rearrange("p (h i) -> p h i", h=4), ALU.mult),
                      [(ps, BC), f1], [attnT])
                P.dve(lambda e: e.tensor_tensor(Bd1.t[0:8, :, :], cumT.t[0:8, 1, :].unsqueeze(1).broadcast_to([8, 4, 128]), eye_g, ALU.mult),
                      [cumT, self.eyep], [Bd1])
                self.mm(ps.t[:, BB, :], ONES[0:8, :], bdf, True, False, [masks, Bd1], [(ps, BB)])
                for hq in range(4):
                    self.mm(ps.t[:, BB, hq * 128:(hq + 1) * 128], self.identf.t[:, :], MBS, False, hq == 3, [masks, self.identf], [(ps, BB)])
                for h in range(4):
                    hh = 4 * g + h
                    self.mm(ps.t[:, BA, h * 128:(h + 1) * 128], qkv.t[:, 8 + hh, col0:col0 + 128], qkv.t[:, 8 + hh, col0:col0 + 128], True, True,
                            [(qkv, 8 + hh)], [(ps, BA)])
                yield
                P.dve(lambda e: e.tensor_tensor(dtmp.t[:, :, :], v4(ps.t[:, BB, :]), cum_b, ALU.subtract), [(ps, BB), sm], [dtmp])
                yield
                P.act(lambda e: e.activation(dtmp.t[:, :, :], dtmp.t[:, :, :], AF.Exp), [dtmp], [dtmp])
                yield
                P.dve(lambda e: e.scalar_tensor_tensor(YTa.t[:, :, :], v4(ps.t[:, BA, :]), -1.0, dtmp.t[:, :, :], ALU.mult, ALU.mult),
                      [(ps, BA), dtmp], [YTa])
                for h in range(4):
                    P.pe(lambda e, h=h: e.transpose(ps.t[:, BB, h * 128:(h + 1) * 128], YTa.t[:, h, :], self.identf.t[:, :]),
                         [YTa, self.identf], [(ps, BB)])
                yield
                P.act(lambda e: e.activation(Ya.t[:, :, :], v4(ps.t[:, BB, :]), AF.Copy), [(ps, BB)], [Ya])
                P.dve(lambda e: e.tensor_tensor(PT.t[:, :, :], YTa.t[:, :, :], self.identf.t[:, :].unsqueeze(1).broadcast_to([128, 4, 128]), ALU.add),
                      [YTa, self.identf], [PT])
                def ev_k(pst, bt):
                    P.act(lambda e: e.activation(ktok.t[:, :, :], pst[:, 0:4, :], AF.Copy), [(ps, bt)], [ktok])
                self.transposes_T(ev_k, lambda k: qkv.t[:, 8 + 4 * g + k, col0:col0 + 128], 4, 128, (qkv, list(range(8 + 4 * g, 12 + 4 * g))), bt=7)
                yield
                def level(lev, Y, YT, Yn, YTn):
                    for h in range(4):
                        self.mm(ps.t[:, BA, h * 128:(h + 1) * 128], YT.t[:, h, :], Y.t[:, h, :], True, True, [YT, Y], [(ps, BA)])
                    if lev < 5:
                        for h in range(4):
                            self.mm(ps.t[:, BB, h * 128:(h + 1) * 128], Y.t[:, h, :], YT.t[:, h, :], True, True, [YT, Y], [(ps, BB)])
                    yield
                    P.act(lambda e: e.activation(Yn.t[:, :, :], v4(ps.t[:, BA, :]), AF.Copy), [(ps, BA)], [Yn])
                    if lev < 5:
                        P.act(lambda e: e.activation(YTn.t[:, :, :], v4(ps.t[:, BB, :]), AF.Copy), [(ps, BB)], [YTn])
                    yield
                    for h in range(4):
                        self.mm(ps.t[:, BC, h * 128:(h + 1) * 128], Yn.t[:, h, :], PT.t[:, h, :], True, True, [Yn, PT], [(ps, BC)])
                    yield
                    P.dve(lambda e: e.tensor_tensor(PT.t[:, :, :], PT.t[:, :, :], v4(ps.t[:, BC, :]), ALU.add), [PT, (ps, BC)], [PT])
                Y, YT, Yn, YTn = Ya, YTa, Yb, YTb
                for lev in range(1, 6):
                    yield from level(lev, Y, YT, Yn, YTn)
                    if lev == 1:
                        for cq in range(2):
                            P.dve(lambda e, cq=cq: e.tensor_tensor(kdec.t[:, cq, :, :], ktok.t[:, :, :],
                                                                   R(12 + cq)[:, hsl].unsqueeze(2).broadcast_to([128, 4, 128]), ALU.mult),
                                  [ktok, sm], [kdec])
                    if lev == 2:
                        def ev_v(pst, bt):
                            P.dve(lambda e: e.tensor_tensor(vb.t[:, :, :], pst[:, 0:4, :], R(0)[:, hsl].unsqueeze(2).broadcast_to([128, 4, 128]), ALU.mult),
                                  [(ps, bt), sm], [vb])
                        self.transposes_T(ev_v, lambda k: qkv.t[:, 16 + 4 * g + k, col0:col0 + 128], 4, 128,
                                          (qkv, list(range(16 + 4 * g, 20 + 4 * g))), bt=7)
                    Y, YT, Yn, YTn = Yn, YTn, Y, YT
                    yield
                return
            Sg = (S, list(range(8 * g, 8 * g + 8)))
            Sbg = (Sb, list(range(8 * g, 8 * g + 8)))

            def chunk(ch):
                rs = slice(64 * ch, 64 * ch + 64)
                rw = slice(0, 128) if ch == 0 else rs
                nrw = 128 if ch == 0 else 64
                for h in range(4):
                    hh = 4 * g + h
                    self.mm(ps.t[:, BA, h * 128:(h + 1) * 128], qkv.t[:, 8 + hh, col0:col0 + 128], Sb.t[:, hh * 128:(hh + 1) * 128], True, True,
                            [(qkv, 8 + hh), Sbg], [(ps, BA)])
                yield
                nbe_b = R(4)[rw, hsl].unsqueeze(2).broadcast_to([nrw, 4, 128])
                P.dve(lambda e: e.tensor_tensor(rhs.t[rw, :, :], v4(ps.t[rw, BA, :]), nbe_b, ALU.mult), [(ps, BA), sm], [rhs])
                P.dve(lambda e: e.tensor_tensor(rhs.t[rw, :, :], rhs.t[rw, :, :], vb.t[rw, :, :], ALU.add), [rhs, vb], [rhs])
                yield
                for h in range(4):
                    self.mm(ps.t[:, BB, h * 128:(h + 1) * 128], PT.t[:, h, :], rhs.t[:, h, :], True, True, [PT, rhs], [(ps, BB)])
                yield
                P.act(lambda e: e.activation(ub.t[rw, :, :], v4(ps.t[rw, BB, :]), AF.Copy), [(ps, BB)], [ub])
                yield
                for h in range(4):
                    hh = 4 * g + h
                    self.mm(ps.t[:, BC, h * 128:(h + 1) * 128], qdT.t[:, h, :], Sb.t[:, hh * 128:(hh + 1) * 128], True, False, [qdT, Sbg], [(ps, BC)])
                    self.mm(ps.t[:, BC, h * 128:(h + 1) * 128], attnT.t[:, h, :], ub.t[:, h, :], False, True, [attnT, ub], [(ps, BC)])
                for h in range(4):
                    self.mm(ps.t[:, BA, h * 128:(h + 1) * 128], kdec.t[:, ch, h, :], ub.t[:, h, :], True, True, [kdec, ub], [(ps, BA)])
                yield
                P.act(lambda e: e.activation(osb.t[rs, g * 512:(g + 1) * 512], ps.t[rs, BC, :], AF.Copy), [(ps, BC)], [(osb, g)])
                el_b = sm.t[:, 8 + ch, hsl].unsqueeze(2).broadcast_to([128, 4, 128])
                P.dve(lambda e: e.tensor_tensor(v4(S.t[:, g * 512:(g + 1) * 512]), v4(S.t[:, g * 512:(g + 1) * 512]), el_b, ALU.mult), [Sg, sm], [Sg])
                P.dve(lambda e: e.tensor_tensor(S.t[:, g * 512:(g + 1) * 512], S.t[:, g * 512:(g + 1) * 512], ps.t[:, BA, :], ALU.add), [Sg, (ps, BA)], [Sg])
                yield
                P.act(lambda e: e.activation(Sb.t[:, g * 512:(g + 1) * 512], S.t[:, g * 512:(g + 1) * 512], AF.Copy), [Sg], [Sbg])
                yield

            for ch in range(2):
                yield from chunk(ch)
            if g == 1:
                post(m, 128, col0, osb)

        def post(m, rows, col0, o_buf):
            v8 = lambda ap: ap.rearrange("p (h d) -> p h d", h=8)
            P.dve(lambda e: e.tensor_tensor(f3.t[:rows, :], o_buf.t[:rows, :], o_buf.t[:rows, :], ALU.mult), [o_buf], [f3])
            P.dve(lambda e: e.tensor_reduce(nst.t[:rows, 0, :], v8(f3.t[:rows, :]), mybir.AxisListType.X, ALU.add), [f3], [nst])
            P.act(lambda e: e.activation(nst.t[:rows, 1, :], nst.t[:rows, 0, :], AF.Ln, scale=1.0 / 128.0, bias=float(NORM_EPS)), [nst], [nst])
            P.act(lambda e: e.activation(nst.t[:rows, 2, :], nst.t[:rows, 1, :], AF.Exp, scale=-0.5), [nst], [nst])
            P.dve(lambda e: e.tensor_tensor(v8(o_buf.t[:rows, :]), v8(o_buf.t[:rows, :]), nst.t[:rows, 2, :].unsqueeze(2).broadcast_to([rows, 8, 128]),
                                            ALU.mult), [o_buf, nst], [o_buf])
            P.dve(lambda e: e.tensor_tensor(v8(o_buf.t[:rows, :]), v8(o_buf.t[:rows, :]), nwb.t[:rows, :].unsqueeze(1).broadcast_to([rows, 8, 128]),
                                            ALU.mult), [o_buf, nwb], [o_buf])
            pz = 4
            for o in range(2):
                for k in range(8):
                    self.mm(ps.t[:rows, pz + o, :], xT.t[:, k, col0:col0 + rows], Wz[o].t[:, k, :], k == 0, k == 7, [(xT, m), Wz[o]], [(ps, pz + o)])
            pzv = ps.t[:rows, pz:pz + 2, :].rearrange("p b n -> p (b n)")
            self.sigmoid_chain(f3.t[:rows, :], pzv, [(ps, pz), (ps, pz + 1)], f3)
            P.dve(lambda e: e.tensor_tensor(f3.t[:rows, :], pzv, f3.t[:rows, :], ALU.mult), [(ps, pz), (ps, pz + 1), f3], [f3])
            P.dve(lambda e: e.tensor_tensor(ogb.t[:rows, :], o_buf.t[:rows, :], f3.t[:rows, :], ALU.mult), [o_buf, f3], [ogb])

            def evac(pst, bt):
                P.act(lambda e: e.activation(hT.t[:, 0:8, col0:col0 + rows], pst[:, :, :rows], AF.Copy), [(ps, bt)], [(hT, list(range(8)))])

            self.transposes_to(evac, lambda k: ogb.t[:rows, k * 128:(k + 1) * 128], 8, rows, ogb, None, bt=7)

        def sample_body(m, col0):
            R = lambda i: sm.t[0:16, i, :]
            gates(m, 16, col0)
            P.act(lambda e: e.activation(R(3), R(1), AF.Exp), [sm], [sm])
            P.dve(lambda e: e.scalar_tensor_tensor(R(4), R(0), -1.0, R(3), ALU.mult, ALU.mult), [sm], [sm])
            P.dve(lambda e: e.tensor_tensor(Bde.t[0:16, :, :], R(3).unsqueeze(1).broadcast_to([16, 16, 8]),
                                            self.eyep.t[:, :].unsqueeze(2).broadcast_to([16, 16, 8]), ALU.mult), [sm, self.eyep], [Bde])
            self.mm(ps.t[:, 6, 0:128], ONES[0:16, :], Bde.t[0:16, :, :].rearrange("p s h -> p (s h)"), True, True, [masks, Bde], [(ps, 6)])
            P.act(lambda e: e.activation(Eall.t[:, :, :].rearrange("p s h -> p (s h)"), ps.t[:, 6, 0:128], AF.Copy), [(ps, 6)], [Eall])
            for h in range(8):
                P.pe(lambda e, h=h: e.transpose(ps.t[0:16, 2 + h // 4, (h % 4) * 128:(h % 4 + 1) * 128], qkvS.t[:, 8 + h, :], self.identf.t[:, :]),
                     [qkvS, self.identf], [(ps, 2 + h // 4)])
            P.act(lambda e: e.activation(ktS.t[0:16, :], ps.t[0:16, 2:4, :].rearrange("p b n -> p (b n)"), AF.Copy), [(ps, 2), (ps, 3)], [ktS])
            for h in range(8):
                P.pe(lambda e, h=h: e.transpose(ps.t[0:16, 4 + h // 4, (h % 4) * 128:(h % 4 + 1) * 128], qkvS.t[:, 16 + h, :], self.identf.t[:, :]),
                     [qkvS, self.identf], [(ps, 4 + h // 4)])
            P.dve(lambda e: e.tensor_tensor(vbS.t[0:16, :].rearrange("p (h d) -> p h d", h=8),
                                            ps.t[0:16, 4:6, :].rearrange("p b (h d) -> p (b h) d", h=4),
                                            R(0).unsqueeze(2).broadcast_to([16, 8, 128]), ALU.mult), [(ps, 4), (ps, 5), sm], [vbS])
            Ss_b = [osb, SsB]
            P.dve(lambda e: e.memset(oacc.t[0:16, :], 0.0), [], [oacc])
            v8 = lambda ap: ap.rearrange("p (h d) -> p h d", h=8)

            def smp(s_):
                par = s_ % 2
                Ss, kTm, qTm, um = Ss_b[par], kTm_b[par], qTm_b[par], um_b[par]
                pb = 2 + 2 * par
                pbv = ps.t[:, pb:pb + 2, :].rearrange("p b n -> p (b n)")
                P.dma("sp", Ss.t[:, :].rearrange("p (h v) -> p h v", h=8), I["st_gdn"][l, s_].rearrange("h k v -> k h v"), [], [Ss], Ss.sem())
                P.dve(lambda e: e.tensor_tensor(kTm.t[:, :, :], qkvS.t[:, 8:16, :],
                                                self.eyef.t[:, s_, :].unsqueeze(1).broadcast_to([128, 8, 16]), ALU.mult), [qkvS, self.eyef], [kTm])
                P.dve(lambda e: e.tensor_tensor(qTm.t[:, :, :], qkvS.t[:, 0:8, :],
                                                self.eyef.t[:, s_, :].unsqueeze(1).broadcast_to([128, 8, 16]), ALU.mult), [qkvS, self.eyef], [qTm])
                yield
                for h in range(8):
                    self.mm(ps.t[0:16, pb + h // 4, (h % 4) * 128:(h % 4 + 1) * 128], kTm.t[:, h, :], Ss.t[:, h * 128:(h + 1) * 128], True, True,
                            [kTm, Ss], [(ps, pb + h // 4)])
                yield
                P.dve(lambda e: e.tensor_tensor(v8(um.t[0:16, :]), v8(pbv[0:16, :]), R(4).unsqueeze(2).broadcast_to([16, 8, 128]), ALU.mult),
                      [(ps, pb), (ps, pb + 1), sm], [um])
                P.dve(lambda e: e.scalar_tensor_tensor(um.t[0:16, :], vbS.t[0:16, :], self.eyep.t[0:16, s_:s_ + 1], um.t[0:16, :],
                                                       ALU.mult, ALU.add), [vbS, self.eyep, um], [um])
                yield
                for h in range(8):
                    self.mm(ps.t[:, pb + h // 4, (h % 4) * 128:(h % 4 + 1) * 128], ktS.t[0:16, h * 128:(h + 1) * 128], um.t[0:16, h * 128:(h + 1) * 128],
                            True, True, [ktS, um], [(ps, pb + h // 4)])
                yield
                P.dve(lambda e: e.tensor_tensor(v8(Ss.t[:, :]), v8(Ss.t[:, :]), Eall.t[:, s_, :].unsqueeze(2).broadcast_to([128, 8, 128]), ALU.mult),
                      [Ss, Eall], [Ss])
                P.dve(lambda e: e.tensor_tensor(Ss.t[:, :], Ss.t[:, :], pbv, ALU.add), [Ss, (ps, pb), (ps, pb + 1)], [Ss])
                yield
                P.dma("sp", self.o["gdn_s"][l, s_].rearrange("h k v -> k h v"), Ss.t[:, :].rearrange("p (h v) -> p h v", h=8), [Ss], [], Ss.sem(),
                      is_output=True)
                for h in range(8):
                    self.mm(ps.t[0:16, pb + h // 4, (h % 4) * 128:(h % 4 + 1) * 128], qTm.t[:, h, :], Ss.t[:, h * 128:(h + 1) * 128],
                            True, True, [qTm, Ss], [(ps, pb + h // 4)])
                yield
                P.dve(lambda e: e.tensor_tensor(oacc.t[0:16, :], oacc.t[0:16, :], pbv[0:16, :], ALU.add), [oacc, (ps, pb), (ps, pb + 1)], [oacc])

            self.run_pipelined((smp(s_) for s_ in range(NS)), 2)
            post(m, 16, col0, oacc)

        units = [(m, col0, g) for (m, rows, col0) in self.tiles() if rows == 128 for g in range(2)]
        self.run_pipelined([unit_gen(0, *units[0], "A")], 1)
        for u in range(len(units)):
            gens = [unit_gen(u, *units[u], "B")]
            if u + 1 < len(units):
                gens.append(unit_gen(u + 1, *units[u + 1], "A"))
            self.run_pipelined(gens, 2)
        for (m, rows, col0) in self.tiles():
            if rows != 128:
                sample_body(m, col0)
        self.state_store("gdn_p", l)

    def transposes_T(self, evac, src_fn, n, rows, src_buf, bt=None):
        P, ps = self.P, self.ps
        bt = self.bank() if bt is None else bt
        pst = ps.t[:, bt, :].bitcast(BF16).rearrange("p (k n) -> p k n", k=8)
        for k in range(n):
            P.pe(lambda e, k=k: e.transpose(pst[:rows, k, :], src_fn(k), self.identb.t[:, :]), [src_buf, self.identb], [(ps, bt)])
        evac(pst, bt)

    def token_mix(self, l):
        I = self.i
        if "ret" in self.cfg.mix:
            self.retention(l)
            self.finale(l, I["w_ret_out"][l], C_M1)
        if "ssd" in self.cfg.mix:
            self.ssd(l)
            self.finale(l, I["w_ssm_out"][l], C_M2)
        if "gdn" in self.cfg.mix:
            self.gdn(l)
            self.finale(l, I["w_gdn_out"][l], C_M3)

    def build(self):
        c = self.cfg
        self.pT = [self.P.sbuf(f"pT{i}", [128, 2, 128], BF16) for i in range(2)]
        self.alloc_mix()
        for half in range(c.NH):
            self.half = half
            self.has_sample = c.sample and half == c.NH - 1
            self.load_x()
            for l in range(c.layers):
                last = (l == c.layers - 1)
                if c.ffn:
                    self.ffn(l, 0)
                self.layer_norm(l, 0)
                self.token_mix(l)
                self.layer_norm(l, 1)
                if c.ffn:
                    self.ffn(l, 1)
                self.layer_norm(l, 2)
                if c.pegate:
                    self.pe_gate(l)
                self.layer_norm(l, 3, final=last)
        return self.P.finish()


def make_consts(T):
    c = {}
    c["c_ident"] = np.eye(128, dtype=np.float32)
    half = 64
    inv = (np.float32(10000.0) ** (-np.arange(half, dtype=np.float32) / np.float32(half))).astype(np.float32)
    pos = np.concatenate([np.arange(T, dtype=np.float32), np.full(NS, PAST_LEN, np.float32)])
    ang = (pos[:, None] * inv[None, :]).astype(np.float32).astype(np.float64)
    rope = np.zeros((T + NS, 4, 64), np.float32)
    rope[:, 0] = np.cos(ang)
    rope[:, 1] = np.sin(ang)
    rope[:, 2] = np.cos(ang) * 128 ** -0.5
    rope[:, 3] = np.sin(ang) * 128 ** -0.5
    c["c_rope"] = rope
    gam = 1.0 - 2.0 ** (-5.0 - np.arange(4))
    lg = np.log1p(-(2.0 ** (-5.0 - np.arange(4)))).astype(np.float32).astype(np.float64)
    i = np.arange(128)
    dm = i[None, :] - i[:, None]
    mask = np.zeros((4, 128, 128), np.float32)
    row = np.zeros((4, 3, 128), np.float32)
    for h in range(4):
        mask[h] = np.where(dm >= 0, np.exp(lg[h] * np.maximum(dm, 0)), 0.0)
        row[h, 0] = np.exp(lg[h] * (i + 1))
        row[h, 1] = np.exp(lg[h] * (127 - i))
        row[h, 2, 0] = np.exp(lg[h] * 128)
        row[h, 2, 1] = np.exp(lg[h])
    c["c_retmask"] = mask
    c["c_retrow"] = row
    mk = np.zeros((8, 128, 128), np.float32)
    jj, ii = np.meshgrid(np.arange(128), np.arange(128), indexing="ij")
    blk = (jj // 64) == (ii // 64)
    mk[0] = (jj <= ii)
    mk[1] = 1.0
    mk[2] = np.where(jj <= ii, 0.0, -30000.0)
    mk[3] = (jj <= ii) & blk
    mk[4] = np.where((jj <= ii) & blk, 0.0, -30000.0)
    mk[5] = np.where((jj < ii) & blk, 0.0, -30000.0)
    mk[6] = blk
    mk[7] = (jj < 64)
    c["c_masks"] = mk
    return c


_CACHE = {}


def kernel(**inputs):
    NC = 8
    cfg = Cfg(NH=4, NTH=4, sample=True, layers=2)
    mk = MK(cfg)
    mk.build()
    T = cfg.T
    consts = make_consts(T)
    f = lambda a: np.ascontiguousarray(np.asarray(a, dtype=np.float32))
    wnames = ["ln_g", "ln_b", "ffn_wg", "ffn_wu", "ffn_wd", "w_in", "ssm_conv_w", "ssm_conv_b", "ssm_dt_bias",
              "ssm_a_log", "ssm_d", "ssm_norm_w", "gdn_conv_w", "gdn_dt_bias", "gdn_a_log", "gdn_norm_w",
              "w_ret_out", "w_ssm_out", "w_gdn_out", "w_o", "pe_proj", "pe_gate"]
    W = {k: f(inputs[k]) for k in wnames}
    xp, xs = np.asarray(inputs["x_prompt"]), np.asarray(inputs["x_sample"])
    pp, ps_ = np.asarray(inputs["p_prompt"]), np.asarray(inputs["p_sample"])
    in_maps = []
    for c in range(NC):
        sl = slice(NS * c, NS * (c + 1))
        m = dict(W)
        m.update(consts)
        m["xp"] = f(xp[c])
        m["pp"] = f(pp[:, c])
        m["xs"] = f(xs[sl, 0])
        m["ps"] = f(ps_[:, sl, 0])
        m["st_ret"] = f(np.asarray(inputs["state_ret"])[:, sl])
        m["st_ssm"] = f(np.asarray(inputs["state_ssm"])[:, sl])
        m["st_ssm_conv"] = f(np.asarray(inputs["state_ssm_conv"])[:, sl])
        m["st_gdn"] = f(np.asarray(inputs["state_gdn"])[:, sl])
        m["st_gdn_conv"] = f(np.asarray(inputs["state_gdn_conv"])[:, sl])
        in_maps.append({k: v for k, v in m.items() if k in mk.i})
    res = run_bass_kernel_spmd(mk.nc, in_maps, core_ids=list(range(NC)))
    R = res.results
    cat0 = lambda k: np.stack([R[c][k] for c in range(NC)], axis=0)
    y_p = cat0("y_p")
    y_s = np.concatenate([R[c]["y_s"] for c in range(NC)], 0)[:, None, :]
    outs = [y_p, y_s]
    for k in ("ret_p", "ssm_p", "ssm_conv_p", "gdn_p", "gdn_conv_p"):
        outs.append(np.stack([R[c][k] for c in range(NC)], axis=1))
    for k in ("ret_s", "ssm_s", "ssm_conv_s", "gdn_s", "gdn_conv_s"):
        outs.append(np.concatenate([R[c][k] for c in range(NC)], axis=1))
    return tuple(np.ascontiguousarray(o, dtype=np.float32) for o in outs)
```

```python
import numpy as np
from contextlib import ExitStack
import concourse.bass as bass
import concourse.mybir as mybir
from concourse.bass_utils import run_bass_kernel_spmd

F32, BF16 = mybir.dt.float32, mybir.dt.bfloat16
AF = mybir.ActivationFunctionType
ALU = mybir.AluOpType

D = 1024
DEPTH = 2
FFN = 2048
PLE = 256
IN_DIM = 12832
DN_ALPHA = (2 * DEPTH) ** 0.25
LN_EPS = 1e-5
NORM_EPS = 1e-6
PAST_LEN = 16384
NS = 16

C_RQ, C_RK, C_RV, C_RG = 0, 512, 1024, 2048
C_MZ, C_MXBC, C_MDT = 3072, 4096, 5632
C_GQKV, C_GZ, C_GA, C_GB = 5648, 8720, 9744, 9752
C_M1, C_M2, C_M3 = 9760, 10784, 11808


class Sem:
    def __init__(self, name):
        self.name = name
        self.count = 0
        self.h = None


class Reg:
    __slots__ = ("writers", "readers")

    def __init__(self):
        self.writers = {}
        self.readers = {}


class Buf:
    def __init__(self, prog, name, t, nreg=1):
        self.prog, self.name, self.t = prog, name, t
        self.regs = [[Reg()] for _ in range(nreg)]
        self.dsem = None

    def sem(self):
        if self.dsem is None:
            self.dsem = self.prog.named_sem("d_" + getattr(self, "sem_name", self.name))
        return self.dsem

    def __getitem__(self, k):
        return self.t[k]


class Op:
    __slots__ = ("eng", "fn", "deps", "needed", "is_dma", "sem", "val", "waits", "dmawaits")

    def __init__(self, eng, fn, is_dma):
        self.eng, self.fn, self.is_dma = eng, fn, is_dma
        self.deps = []
        self.dmawaits = []
        self.needed = False
        self.sem = None
        self.val = 0


def _regs(spec):
    out = []
    for s in spec:
        if isinstance(s, Buf):
            for g in s.regs:
                out.extend(g)
        else:
            b, idx = s
            if isinstance(idx, int):
                out.extend(b.regs[idx])
            else:
                for i in idx:
                    out.extend(b.regs[i])
    return out


class Arena:
    GRAN = 256

    def __init__(self, prog, name, nbytes):
        self.prog = prog
        self.nbytes = nbytes
        self.base = prog.sbuf(name, [128, nbytes // 2], BF16)
        self.gr = [Reg() for _ in range(nbytes // self.GRAN)]
        self.off = 0
        self.n = 0

    def reset(self, off=0):
        self.off = off

    def alloc(self, name, free_shape, dt, nreg=1):
        esz = 2 if dt == BF16 else 4
        nel = int(np.prod(free_shape))
        nb = nel * esz
        nb_al = (nb + self.GRAN - 1) // self.GRAN * self.GRAN
        assert self.off + nb_al <= self.nbytes, f"arena overflow allocating {name}: {self.off}+{nb_al}>{self.nbytes}"
        o2 = self.off // 2
        v = self.base.t[:, o2:o2 + nb // 2]
        if dt != BF16:
            v = v.bitcast(dt)
        if len(free_shape) > 1:
            names = " ".join(f"d{i}" for i in range(len(free_shape)))
            kw = {f"d{i}": int(free_shape[i]) for i in range(len(free_shape))}
            v = v.rearrange(f"p ({names}) -> p {names}", **kw)
        self.n += 1
        b = Buf(self.prog, f"{name}_{self.n}", v, 1)
        b.sem_name = f"a_{name}_{self.off}"
        g0 = self.off // self.GRAN
        ng = nb_al // self.GRAN
        grs = self.gr[g0:g0 + ng]
        if nreg == 1:
            b.regs = [grs]
        else:
            assert ng % nreg == 0, (name, ng, nreg)
            k = ng // nreg
            b.regs = [grs[i * k:(i + 1) * k] for i in range(nreg)]
        self.off += nb_al
        return b


class Prog:
    ENGS = ("pe", "act", "dve", "pool", "sp")
    ATTR = {"pe": "tensor", "act": "scalar", "dve": "vector", "pool": "gpsimd", "sp": "sync"}

    def __init__(self, nc):
        self.nc = nc
        self.es = ExitStack()
        self.ops = {e: [] for e in self.ENGS}
        self.sems = []
        self.esem = {e: self.new_sem("e_" + e) for e in ("pe", "act", "dve", "pool")}
        self.nbuf = 0
        self.out_sems = set()

    def new_sem(self, name):
        s = Sem(name)
        self.sems.append(s)
        return s

    def named_sem(self, name):
        d = self.__dict__.setdefault("_named", {})
        if name not in d:
            d[name] = self.new_sem(name)
        return d[name]

    def sbuf(self, name, shape, dt, nreg=1):
        nb = int(np.prod(shape[1:])) * (2 if dt == BF16 else 4)
        self.sb_bytes = getattr(self, "sb_bytes", 0) + nb
        self.sb_log = getattr(self, "sb_log", []) + [(name, nb)]
        t = self.es.enter_context(self.nc.sbuf_tensor("s_" + name, list(shape), dt))
        return Buf(self, name, t, nreg)

    def psum(self, name, shape, dt, nreg=1):
        t = self.es.enter_context(self.nc.psum_tensor("p_" + name, list(shape), dt))
        return Buf(self, name, t, nreg)

    def dram(self, name, shape, dt, kind, nreg=1):
        t = self.nc.dram_tensor(name, list(shape), dt, kind=kind)
        return Buf(self, name, t.ap(), nreg)

    def _dep(self, c, p):
        if p is None or p is c:
            return
        if p.is_dma:
            c.dmawaits.append((p.sem, p.sem.count))
            return
        p.needed = True
        c.deps.append(p)

    def op(self, eng, fn, reads=(), writes=(), dma_sem=None):
        is_dma = dma_sem is not None
        o = Op(eng, fn, is_dma)
        st = self.__dict__.setdefault("tagstat", {})
        key = (getattr(self, "tag", "-"), eng)
        st[key] = st.get(key, 0) + 1
        rr, ww = _regs(reads), _regs(writes)
        for r in rr:
            for e, p in r.writers.items():
                if (not is_dma) and (not p.is_dma) and e == eng and eng == "pe":
                    continue
                self._dep(o, p)
        for r in ww:
            for e, p in r.readers.items():
                if (not is_dma) and (not p.is_dma) and e == eng and eng == "pe":
                    continue
                self._dep(o, p)
            for e, p in r.writers.items():
                if (not is_dma) and (not p.is_dma) and e == eng and eng == "pe":
                    continue
                if is_dma and p.is_dma and p.sem is dma_sem:
                    continue
                self._dep(o, p)
        key = ("dma", id(o)) if is_dma else eng
        if is_dma:
            dma_sem.count += 16
            o.sem, o.val = dma_sem, dma_sem.count
        for r in rr:
            r.readers[key] = o
        for r in ww:
            r.writers = {key: o}
            r.readers = {}
        self.ops[eng].append(o)
        return o

    def pe(self, fn, reads, writes):
        return self.op("pe", fn, reads, writes)

    def act(self, fn, reads, writes):
        return self.op("act", fn, reads, writes)

    def dve(self, fn, reads, writes):
        return self.op("dve", fn, reads, writes)

    def pool(self, fn, reads, writes):
        return self.op("pool", fn, reads, writes)

    def dma(self, q, out, in_, reads, writes, sem, is_output=False, **kw):
        if is_output:
            self.out_sems.add(sem)
        return self.op(q, lambda e: e.dma_start(out=out, in_=in_, **kw), reads, writes, dma_sem=sem)

    def finish(self):
        nc = self.nc
        fin = Op("sp", None, False)
        for s in self.sems:
            if s.name.startswith("d_") and s.count > 0:
                fin.dmawaits.append((s, s.count))
        self.ops["sp"].append(fin)
        for e in ("pe", "act", "dve", "pool"):
            n = 0
            for o in self.ops[e]:
                if o.is_dma:
                    continue
                if o.needed:
                    n += 1
                    o.sem, o.val = self.esem[e], n
            self.esem[e].count = n
        for s in self.sems:
            if s.count > 0:
                s.h = self.es.enter_context(nc.semaphore(s.name))
        block = self.es.enter_context(nc.Block())
        stats = {}
        for e in self.ENGS:
            ops = self.ops[e]

            def body(eng, ops=ops, e=e):
                seen = {}
                nw = 0
                for o in ops:
                    ws = {}
                    for p in o.deps:
                        ws[p.sem] = max(ws.get(p.sem, 0), p.val)
                    for s, v in o.dmawaits:
                        ws[s] = max(ws.get(s, 0), v)
                    for s, v in ws.items():
                        if seen.get(s, 0) >= v:
                            continue
                        seen[s] = v
                        eng.wait_ge(s.h, v)
                        nw += 1
                    if o.fn is None:
                        continue
                    ins = o.fn(eng)
                    if o.is_dma:
                        ins.then_inc(o.sem.h, 16)
                    elif o.needed:
                        ins.then_inc(o.sem.h, 1)
                stats[e] = (len(ops), nw)

            getattr(block, self.ATTR[e])(body)
        self.es.close()
        return stats


class Cfg:
    def __init__(self, NH=2, NTH=8, sample=True, layers=2, mix=("ret", "ssd", "gdn"), pegate=True, ffn=True):
        self.NH, self.NTH, self.sample, self.layers = NH, NTH, sample, layers
        self.mix, self.pegate, self.ffn = mix, pegate, ffn
        self.dbg = {}
        self.T = NH * NTH * 128


class MK:
    def __init__(self, cfg):
        self.cfg = cfg
        nc = bass.Bass("TRN2", target_bir_lowering=False)
        self.nc = nc
        self.P = Prog(nc)
        self.declare_io()
        self.alloc()

    def din(self, name, shape):
        return self.nc.dram_tensor(name, list(shape), F32, kind="ExternalInput").ap()

    def dout(self, name, shape):
        return self.nc.dram_tensor(name, list(shape), F32, kind="ExternalOutput").ap()

    def declare_io(self):
        c = self.cfg
        T = c.T
        L = DEPTH
        self.i = {}
        I = self.i
        I["xp"] = self.din("xp", [T, D])
        I["pp"] = self.din("pp", [L, T, PLE])
        if c.sample:
            I["xs"] = self.din("xs", [NS, D])
            I["ps"] = self.din("ps", [L, NS, PLE])
            I["st_ret"] = self.din("st_ret", [L, NS, 4, 128, 256])
            I["st_ssm"] = self.din("st_ssm", [L, NS, 16, 128, 64])
            I["st_ssm_conv"] = self.din("st_ssm_conv", [L, NS, 3, 1536])
            I["st_gdn"] = self.din("st_gdn", [L, NS, 8, 128, 128])
            I["st_gdn_conv"] = self.din("st_gdn_conv", [L, NS, 3, 3072])
        for nm, shp in [("ln_g", [L, 4, D]), ("ln_b", [L, 4, D]), ("ffn_wg", [L, 2, D, FFN]),
                        ("ffn_wu", [L, 2, D, FFN]), ("ffn_wd", [L, 2, FFN, D]), ("w_in", [L, D, IN_DIM]),
                        ("ssm_conv_w", [L, 4, 1536]), ("ssm_conv_b", [L, 1536]), ("ssm_dt_bias", [L, 16]),
                        ("ssm_a_log", [L, 16]), ("ssm_d", [L, 16]), ("ssm_norm_w", [L, 1024]),
                        ("gdn_conv_w", [L, 4, 3072]), ("gdn_dt_bias", [L, 8]), ("gdn_a_log", [L, 8]),
                        ("gdn_norm_w", [L, 128]), ("w_ret_out", [L, D, D]), ("w_ssm_out", [L, D, D]),
                        ("w_gdn_out", [L, D, D]), ("w_o", [L, D, D]), ("pe_proj", [L, PLE, D]),
                        ("pe_gate", [L, D, D])]:
            I[nm] = self.din(nm, shp)
        I["c_ident"] = self.din("c_ident", [128, 128])
        I["c_rope"] = self.din("c_rope", [T + NS, 4, 64])
        I["c_retmask"] = self.din("c_retmask", [4, 128, 128])
        I["c_retrow"] = self.din("c_retrow", [4, 3, 128])
        I["c_masks"] = self.din("c_masks", [8, 128, 128])
        self.o = {}
        O = self.o
        O["y_p"] = self.dout("y_p", [T, D])
        O["ret_p"] = self.dout("ret_p", [L, 4, 128, 256])
        O["ssm_p"] = self.dout("ssm_p", [L, 16, 128, 64])
        O["ssm_conv_p"] = self.dout("ssm_conv_p", [L, 3, 1536])
        O["gdn_p"] = self.dout("gdn_p", [L, 8, 128, 128])
        O["gdn_conv_p"] = self.dout("gdn_conv_p", [L, 3, 3072])
        if c.sample:
            O["y_s"] = self.dout("y_s", [NS, D])
            O["ret_s"] = self.dout("ret_s", [L, NS, 4, 128, 256])
            O["ssm_s"] = self.dout("ssm_s", [L, NS, 16, 128, 64])
            O["ssm_conv_s"] = self.dout("ssm_conv_s", [L, NS, 3, 1536])
            O["gdn_s"] = self.dout("gdn_s", [L, NS, 8, 128, 128])
            O["gdn_conv_s"] = self.dout("gdn_conv_s", [L, NS, 3, 3072])

    def alloc(self):
        c, P = self.cfg, self.P
        self.NTT = c.NTH + (1 if c.sample else 0)
        self.TS = c.NTH * 128 + (NS if c.sample else 0)
        NTT, TS = self.NTT, self.TS
        self.xa = P.sbuf("xa", [128, NTT, D], F32, nreg=NTT * 2)
        self.xT = P.sbuf("xT", [128, 8, TS], BF16, nreg=NTT)
        self.hT = P.sbuf("hT", [128, 16, TS], BF16, nreg=16)
        self.NSLOT = 6
        self.wslots = [P.sbuf(f"w{i}", [128, 8, 512], BF16) for i in range(self.NSLOT)]
        self.wi = 0
        self.lnp = P.sbuf("lnp", [128, 2, D], F32)
        self.identb = P.sbuf("identb", [128, 128], BF16)
        self.identf = P.sbuf("identf", [128, 128], F32)
        self.tmpA = [P.sbuf(f"tmpA{i}", [128, D], F32) for i in range(2)]
        self.tmpB = [P.sbuf(f"tmpB{i}", [128, D], F32) for i in range(2)]
        self.xb = [P.sbuf(f"xb{i}", [128, D], BF16) for i in range(1)]
        self.lnst = [P.sbuf(f"lnst{i}", [128, 2, 6], F32) for i in range(2)]
        self.lnmv = [P.sbuf(f"lnmv{i}", [128, 4], F32) for i in range(2)]
        self.ps = P.psum("ps", [128, 8, 512], F32, nreg=8)
        self.psi = 0
        self.rr = {}
        P.dma("pool", self.identb.t[:], self.i["c_ident"], [], [self.identb], self.identb.sem())
        P.dma("sp", self.identf.t[:], self.i["c_ident"], [], [self.identf], self.identf.sem())

    def rot(self, key, lst):
        i = self.rr.get(key, 0)
        self.rr[key] = i + 1
        return lst[i % len(lst)]

    def bank(self):
        b = 2 + self.psi
        self.psi = (self.psi + 1) % 6
        return b

    def bank2(self):
        if self.psi % 2:
            self.psi = (self.psi + 1) % 6
        b = 2 + self.psi
        self.psi = (self.psi + 2) % 6
        return b

    def wslot(self):
        s = self.wslots[self.wi % self.NSLOT]
        self.wi += 1
        return s

    def wload(self, slot, c0, src2d, K=1024):
        ncols = src2d.shape[1]
        kc = K // 128
        self.P.dma("pool", slot.t[:, 0:kc, c0:c0 + ncols], src2d.rearrange("(k p) c -> p k c", p=128),
                   [], [slot], slot.sem())

    def tiles(self):
        c = self.cfg
        out = [(m, 128, m * 128) for m in range(c.NTH)]
        if self.has_sample:
            out.append((c.NTH, NS, c.NTH * 128))
        return out

    def nblocks(self):
        c = self.cfg
        TP = c.NTH * 128
        out = [(n0, min(512, TP - n0)) for n0 in range(0, TP, 512)]
        if self.has_sample:
            out.append((TP, NS))
        return out

    def xT_regs(self, n0, nsz):
        return (self.xT, list(range(n0 // 128, (n0 + nsz - 1) // 128 + 1)))

    @staticmethod
    def run_pipelined(gens, depth):
        it = iter(gens)
        active = []
        done = False
        while True:
            if not done and len(active) < depth:
                try:
                    active.append(next(it))
                except StopIteration:
                    done = True
            if not active:
                if done:
                    break
                continue
            for g in list(active):
                try:
                    next(g)
                except StopIteration:
                    active.remove(g)

    def mm(self, out, lhsT, rhs, start, stop, reads, writes):
        n = int(np.prod(rhs.shape[1:]))
        cyc = max(64, n) * (4 if rhs.dtype == F32 else 1)
        pc = self.P.__dict__.setdefault("pecost", {})
        t = getattr(self.P, "tag", "-")
        pc[t] = pc.get(t, 0) + cyc
        self.P.pe(lambda e: e.matmul(out, lhsT, rhs, start=start, stop=stop), reads, writes)

    def emit_xT(self, m, rows, col0, src, final_out=None):
        P = self.P
        xa, xT, ps = self.xa, self.xT, self.ps
        xb = self.rot("xb", self.xb)
        P.act(lambda e: e.activation(xa.t[:rows, m, :], src.t[:rows, :], AF.Copy, scale=float(DN_ALPHA)),
              [src], [(xa, [2 * m, 2 * m + 1])])
        P.dve(lambda e: e.tensor_copy(xb.t[:rows, :], src.t[:rows, :]), [src], [xb])
        b = self.bank()
        pst = ps.t[:, b, :].bitcast(BF16).rearrange("p (k n) -> p k n", k=8)
        for k in range(8):
            P.pe(lambda e, k=k: e.transpose(pst[:, k, :rows], xb.t[:rows, k * 128:(k + 1) * 128],
                                            self.identb.t[:rows, :rows]),
                 [xb, self.identb], [(ps, b)])
        P.act(lambda e: e.activation(xT.t[:, :, col0:col0 + rows], pst[:, :, :rows], AF.Copy),
              [(ps, b)], [(xT, m)])

    def layer_norm(self, l, idx, final=False):
        self.P.tag = "ln"
        P = self.P
        I = self.i
        lnp = self.lnp
        P.dma("sp", lnp.t[:, 0, :], I["ln_g"][l, idx, :].partition_broadcast(128), [], [lnp], lnp.sem())
        P.dma("sp", lnp.t[:, 1, :], I["ln_b"][l, idx, :].partition_broadcast(128), [], [lnp], lnp.sem())
        xa = self.xa

        def tile_gen(i, m, rows, col0):
            par = i % 2
            st, mv, tA, tB = self.lnst[par], self.lnmv[par], self.tmpA[par], self.tmpB[par]
            xr = (xa, [2 * m, 2 * m + 1])
            P.dve(lambda e: e.bn_stats(st.t[:rows, 0, :], xa.t[:rows, m, 0:512]), [xr], [st])
            P.dve(lambda e: e.bn_stats(st.t[:rows, 1, :], xa.t[:rows, m, 512:1024]), [xr], [st])
            P.dve(lambda e: e.bn_aggr(mv.t[:rows, 0:2], st.t[:rows, :, :]), [st], [mv])
            yield
            P.act(lambda e: e.activation(mv.t[:rows, 2:3], mv.t[:rows, 1:2], AF.Ln, bias=float(LN_EPS)), [mv], [mv])
            P.act(lambda e: e.activation(mv.t[:rows, 3:4], mv.t[:rows, 2:3], AF.Exp, scale=-0.5), [mv], [mv])
            yield
            P.dve(lambda e: e.tensor_scalar(tA.t[:rows, :], xa.t[:rows, m, :], mv.t[:rows, 0:1], mv.t[:rows, 3:4],
                                            ALU.subtract, ALU.mult), [xr, mv], [tA])
            P.dve(lambda e: e.tensor_tensor(tA.t[:rows, :], tA.t[:rows, :], lnp.t[:rows, 0, :], ALU.mult), [tA, lnp], [tA])
            P.dve(lambda e: e.tensor_tensor(tB.t[:rows, :], tA.t[:rows, :], lnp.t[:rows, 1, :], ALU.add), [tA, lnp], [tB])
            yield
            if final:
                self.store_y(m, rows, tB)
            else:
                self.emit_xT(m, rows, col0, tB)

        self.run_pipelined((tile_gen(i, m, rows, col0) for i, (m, rows, col0) in enumerate(self.tiles())), 2)

    def store_y(self, m, rows, src):
        c = self.cfg
        if rows == 128:
            t0 = (self.half * c.NTH + m) * 128
            dst = self.o["y_p"][t0:t0 + 128, :]
        else:
            dst = self.o["y_s"][:, :]
        self.P.dma("sp", dst, src.t[:rows, :], [src], [], src.sem(), is_output=True)

    def load_x(self):
        c = self.cfg
        for (m, rows, col0) in self.tiles():
            tB = self.rot("tmpB", self.tmpB)
            if rows == 128:
                t0 = (self.half * c.NTH + m) * 128
                src = self.i["xp"][t0:t0 + 128, :]
            else:
                src = self.i["xs"][:, :]
            self.P.dma("sp", tB.t[:rows, :], src, [], [tB], tB.sem())
            self.emit_xT(m, rows, col0, tB)

    def ffn(self, l, idx):
        self.P.tag = "ffn"
        P, I = self.P, self.i
        ps, xT, hT, xa = self.ps, self.xT, self.hT, self.xa
        wg, wu, wd = I["ffn_wg"][l, idx], I["ffn_wu"][l, idx], I["ffn_wd"][l, idx]
        slots = []

        def loadA(hb):
            s = self.wslot()
            self.wload(s, 0, wg[:, hb * 256:(hb + 1) * 256])
            self.wload(s, 256, wu[:, hb * 256:(hb + 1) * 256])
            return s

        nxt = loadA(0)
        for hb in range(8):
            cur = nxt
            if hb + 1 < 8:
                nxt = loadA(hb + 1)
            for jj in range(2):
                j = hb * 2 + jj
                for (n0, nsz) in self.nblocks():
                    bg, bu = self.bank(), self.bank()
                    xr = self.xT_regs(n0, nsz)
                    for k in range(8):
                        self.mm(ps.t[:, bg, :nsz], cur.t[:, k, jj * 128:(jj + 1) * 128], xT.t[:, k, n0:n0 + nsz],
                                k == 0, k == 7, [cur, xr], [(ps, bg)])
                    for k in range(8):
                        self.mm(ps.t[:, bu, :nsz], cur.t[:, k, 256 + jj * 128:256 + (jj + 1) * 128],
                                xT.t[:, k, n0:n0 + nsz], k == 0, k == 7, [cur, xr], [(ps, bu)])
                    tA = self.rot("tmpA", self.tmpA)
                    P.act(lambda e, bg=bg, nsz=nsz, tA=tA: e.activation(tA.t[:, :nsz], ps.t[:, bg, :nsz], AF.Silu),
                          [(ps, bg)], [tA])
                    P.dve(lambda e, bu=bu, nsz=nsz, n0=n0, j=j, tA=tA: e.tensor_tensor(
                        hT.t[:, j, n0:n0 + nsz], ps.t[:, bu, :nsz], tA.t[:, :nsz], ALU.mult),
                        [(ps, bu), tA], [(hT, j)])
        def loadB(o):
            s0, s1 = self.wslot(), self.wslot()
            self.P.dma("pool", s0.t[:, :, :], wd[0:1024, o * 512:(o + 1) * 512].rearrange("(k p) c -> p k c", p=128),
                       [], [s0], s0.sem())
            self.P.dma("pool", s1.t[:, :, :], wd[1024:2048, o * 512:(o + 1) * 512].rearrange("(k p) c -> p k c", p=128),
                       [], [s1], s1.sem())
            return (s0, s1)

        nxt = loadB(0)
        for o in range(2):
            cur = nxt
            if o == 0:
                nxt = loadB(1)
            for (m, rows, col0) in self.tiles():
                b = self.bank()
                for j in range(16):
                    s = cur[j // 8]
                    self.mm(ps.t[:rows, b, :], hT.t[:, j, col0:col0 + rows], s.t[:, j % 8, :], j == 0, j == 15,
                            [(hT, j), s], [(ps, b)])
                P.dve(lambda e, m=m, rows=rows, b=b, o=o: e.scalar_tensor_tensor(
                    xa.t[:rows, m, o * 512:(o + 1) * 512], ps.t[:rows, b, :], 0.5,
                    xa.t[:rows, m, o * 512:(o + 1) * 512], ALU.mult, ALU.add),
                    [(ps, b), (xa, 2 * m + o)], [(xa, 2 * m + o)])

    def pe_gate(self, l):
        self.P.tag = "pegate"
        P, I = self.P, self.i
        c = self.cfg
        ps, xT, xa = self.ps, self.xT, self.xa
        sg0, sg1, sp = self.wslot(), self.wslot(), self.wslot()
        self.wload(sg0, 0, I["pe_gate"][l][:, 0:512])
        self.wload(sg1, 0, I["pe_gate"][l][:, 512:1024])
        for o in range(2):
            P.dma("pool", sp.t[:, 2 * o:2 * o + 2, :],
                  I["pe_proj"][l][:, o * 512:(o + 1) * 512].rearrange("(k p) c -> p k c", p=128), [], [sp], sp.sem())
        sg = (sg0, sg1)

        def tile_body(m, rows, col0):
            pb = self.rot("xb", self.xb)
            if rows == 128:
                t0 = (self.half * c.NTH + m) * 128
                src = I["pp"][l, t0:t0 + 128, :]
            else:
                src = I["ps"][l, :, :]
            P.dma("pool", pb.t[:rows, 0:PLE], src, [], [pb], pb.sem())
            b = self.bank()
            pst = ps.t[:, b, :].bitcast(BF16).rearrange("p (k n) -> p k n", k=8)
            for k in range(2):
                P.pe(lambda e, k=k, rows=rows, pb=pb: e.transpose(pst[:, k, :rows], pb.t[:rows, k * 128:(k + 1) * 128],
                                                                self.identb.t[:rows, :rows]),
                     [pb, self.identb], [(ps, b)])
            pT = self.rot("pT", self.pT)
            P.dve(lambda e, rows=rows, pT=pT: e.tensor_copy(pT.t[:, :, :rows], pst[:, 0:2, :rows]), [(ps, b)], [pT])
            tA = self.rot("tmpA", self.tmpA)
            for o in range(2):
                bg, bp = self.bank(), self.bank()
                for k in range(8):
                    self.mm(ps.t[:rows, bg, :], xT.t[:, k, col0:col0 + rows], sg[o].t[:, k, :], k == 0, k == 7,
                            [(xT, m), sg[o]], [(ps, bg)])
                for k in range(2):
                    self.mm(ps.t[:rows, bp, :], pT.t[:, k, :rows], sp.t[:, 2 * o + k, :], k == 0, k == 1,
                            [pT, sp], [(ps, bp)])
                P.act(lambda e, rows=rows, bg=bg, o=o, tA=tA: e.activation(
                    tA.t[:rows, o * 512:(o + 1) * 512], ps.t[:rows, bg, :], AF.Sigmoid), [(ps, bg)], [tA])
                P.dve(lambda e, rows=rows, bp=bp, o=o, tA=tA: e.tensor_tensor(
                    tA.t[:rows, o * 512:(o + 1) * 512], ps.t[:rows, bp, :], tA.t[:rows, o * 512:(o + 1) * 512], ALU.mult),
                    [(ps, bp), tA], [tA])
                P.dve(lambda e, m=m, rows=rows, o=o, tA=tA: e.tensor_tensor(
                    xa.t[:rows, m, o * 512:(o + 1) * 512], xa.t[:rows, m, o * 512:(o + 1) * 512],
                    tA.t[:rows, o * 512:(o + 1) * 512], ALU.add),
                    [tA, (xa, 2 * m + o)], [(xa, 2 * m + o)])

        for (m, rows, col0) in self.tiles():
            tile_body(m, rows, col0)


    def alloc_mix(self):
        P, I = self.P, self.i
        c = self.cfg
        self.S = P.sbuf("S", [128, 1024], F32, nreg=16)
        self.Sb = P.sbuf("Sb", [128, 1024], BF16, nreg=16)
        self.cst = P.sbuf("cst", [128, 8], F32)
        P.dve(lambda e: e.memset(self.cst.t[:, 0:1], float(LN_EPS)), [], [self.cst])
        P.dve(lambda e: e.memset(self.cst.t[:, 1:2], float(NORM_EPS)), [], [self.cst])
        P.dve(lambda e: e.memset(self.cst.t[:, 2:3], -0.5), [], [self.cst])
        P.dve(lambda e: e.memset(self.cst.t[:, 3:4], 1.0), [], [self.cst])
        P.dve(lambda e: e.memset(self.cst.t[:, 4:5], 1.0 / 512.0), [], [self.cst])
        P.dve(lambda e: e.memset(self.cst.t[:, 5:6], 1.0 / 128.0), [], [self.cst])
        self.eyeb = P.sbuf("eyeb", [128, 16, 16], BF16)
        self.eyep = P.sbuf("eyep", [16, 16], F32)
        self.eyef = P.sbuf("eyef", [128, 16, 16], F32)
        P.dma("sp", self.eyef.t[:], I["c_ident"][0:16, 0:16].unsqueeze(0).broadcast_to([128, 16, 16]), [], [self.eyef], self.eyef.sem())
        P.dma("pool", self.eyeb.t[:], I["c_ident"][0:16, 0:16].unsqueeze(0).broadcast_to([128, 16, 16]),
              [], [self.eyeb], self.eyeb.sem())
        P.dma("sp", self.eyep.t[:], I["c_ident"][0:16, 0:16], [], [self.eyep], self.eyep.sem())
        self.masks = P.sbuf("masks", [128, 8, 128], F32)
        P.dma("sp", self.masks.t[:], I["c_masks"].rearrange("m j i -> j m i"), [], [self.masks], self.masks.sem())
        self.A = Arena(P, "arena", 74 * 1024)
        self.tail_ssm = [P.sbuf(f"tail_ssm{l}", [128, 12, 3], F32, nreg=12) for l in range(DEPTH)]
        self.tail_gdn = [P.sbuf(f"tail_gdn{l}", [128, 24, 3], F32, nreg=24) for l in range(DEPTH)]
        L = DEPTH
        self.d_state = {}
        for nm in ("ret_p", "ssm_p", "gdn_p"):
            self.d_state[nm] = Buf(P, "dst_" + nm, self.o[nm], nreg=L)
        self.state_view = {
            "ret_p": lambda ap: ap.rearrange("h k v -> k h v"),
            "ssm_p": lambda ap: ap.rearrange("h n d -> n h d"),
            "gdn_p": lambda ap: ap.rearrange("h k v -> k h v"),
        }

    def state_load(self, nm, l):
        P, S, Sb = self.P, self.S, self.Sb
        db = self.d_state[nm]
        hd = {"ret_p": 4, "ssm_p": 16, "gdn_p": 8}[nm]
        if self.half == 0:
            P.dve(lambda e: e.memset(S.t[:], 0.0), [], [S])
        else:
            P.dma("sp", S.t[:].rearrange("p (h v) -> p h v", h=hd), self.state_view[nm](db.t[l]), [(db, l)], [S], S.sem())
        P.act(lambda e: e.activation(Sb.t[:], S.t[:], AF.Copy), [S], [Sb])

    def state_store(self, nm, l):
        P, S = self.P, self.S
        db = self.d_state[nm]
        hd = {"ret_p": 4, "ssm_p": 16, "gdn_p": 8}[nm]
        P.dma("sp", self.state_view[nm](db.t[l]), S.t[:].rearrange("p (h v) -> p h v", h=hd), [S], [(db, l)], S.sem(),
              is_output=True)

    def sigmoid_chain(self, tmp_ap, src_ap, reads, tmp_buf, scale_in=-1.0, bias_in=0.0):
        P = self.P
        if isinstance(bias_in, float) and bias_in == 0.0:
            P.act(lambda e: e.activation(tmp_ap, src_ap, AF.Exp, scale=scale_in), reads, [tmp_buf])
        else:
            P.act(lambda e: e.activation(tmp_ap, src_ap, AF.Exp, scale=scale_in, bias=bias_in), reads, [tmp_buf])
        P.act(lambda e: e.activation(tmp_ap, tmp_ap, AF.Ln, bias=1.0), [tmp_buf], [tmp_buf])
        P.act(lambda e: e.activation(tmp_ap, tmp_ap, AF.Exp, scale=-1.0), [tmp_buf], [tmp_buf])

    def rstd_pool(self, nst, rows, eps_col, scale_col=None):
        P = self.P
        eps = float(LN_EPS) if eps_col == 0 else float(NORM_EPS)
        P.act(lambda e: e.activation(nst.t[:rows, 2:3], nst.t[:rows, 1:2], AF.Ln, bias=eps), [nst], [nst])
        P.act(lambda e: e.activation(nst.t[:rows, 3:4], nst.t[:rows, 2:3], AF.Exp, scale=-0.5), [nst], [nst])

    def transposes_to(self, dst_ap_fn, src_fn, n, rows, src_buf, dst_writes, bt=None):
        P, ps = self.P, self.ps
        bt = self.bank() if bt is None else bt
        pst = ps.t[:, bt, :].bitcast(BF16).rearrange("p (k n) -> p k n", k=8)
        for k in range(n):
            P.pe(lambda e, k=k: e.transpose(pst[:, k, :rows], src_fn(k), self.identb.t[:rows, :rows]),
                 [src_buf, self.identb], [(ps, bt)])
        dst_ap_fn(pst, bt)

    def retention(self, l):
        self.P.tag = "ret"
        P, I, c = self.P, self.i, self.cfg
        ps, xT, hT = self.ps, self.xT, self.hT
        w_in = I["w_in"][l]
        S, Sb, A = self.S, self.Sb, self.A
        A.reset()
        retmask = A.alloc("retmask", [4, 128], F32)
        retrow = A.alloc("retrow", [4, 128], F32)
        retcol = A.alloc("retcol", [4], F32)
        P.dma("sp", retmask.t[:], I["c_retmask"].rearrange("h j i -> j h i"), [], [retmask], retmask.sem())
        for h in range(4):
            P.dma("sp", retrow.t[:, h, :], I["c_retrow"][h, 0, :].partition_broadcast(128), [], [retrow], retrow.sem())
        P.dma("sp", retcol.t[:], I["c_retrow"][:, 1, :].rearrange("h j -> j h"), [], [retcol], retcol.sem(),
              allow_slow_non_contiguous=True)
        rope_b = [A.alloc("rope", [4, 64], F32) for _ in range(2)]
        qkr_b = [A.alloc("qkr", [2, 2, 128], BF16) for _ in range(2)]
        rt_b = [A.alloc("rt", [4, 256], F32, nreg=4) for _ in range(2)]
        qkT_b = [A.alloc("qkT", [4, 128], BF16) for _ in range(2)]
        qdT_b = [A.alloc("qdT", [128], BF16) for _ in range(4)]
        kdec_b = [A.alloc("kdec", [128], BF16) for _ in range(4)]
        vbf_b = [A.alloc("vbf", [256], BF16) for _ in range(4)]
        sgt_b = [A.alloc("sgt", [256], F32) for _ in range(4)]
        scm_b = [A.alloc("scm", [128], BF16) for _ in range(4)]
        ogt_b = [A.alloc("ogt", [256], F32) for _ in range(4)]
        ogb_b = [A.alloc("ogb", [256], BF16) for _ in range(2)]
        nst_b = [A.alloc("nst", [8], F32) for _ in range(4)]
        st6_b = [A.alloc("st6", [6], F32) for _ in range(4)]
        if self.has_sample:
            qTm = A.alloc("qTm", [16, 16], BF16)
            ktm = A.alloc("ktm", [16, 128], BF16)
            Ss_b = [A.alloc("Ss", [2, 256], F32) for _ in range(2)]
            Ssb_b = [A.alloc("Ssb", [256], BF16) for _ in range(2)]
        self.state_load("ret_p", l)
        lg = [float(np.float64(np.log1p(-np.float32(2.0) ** np.float32(-5.0 - h)).astype(np.float32))) for h in range(4)]

        def load_pair(pr):
            sA, sB, sC = self.wslot(), self.wslot(), self.wslot()
            h0, h1 = 2 * pr, 2 * pr + 1
            for i, h in enumerate((h0, h1)):
                self.wload(sA, i * 256, w_in[:, C_RQ + h * 128:C_RQ + (h + 1) * 128])
                self.wload(sA, i * 256 + 128, w_in[:, C_RK + h * 128:C_RK + (h + 1) * 128])
            for s_, h in ((sB, h0), (sC, h1)):
                self.wload(s_, 0, w_in[:, C_RV + h * 256:C_RV + (h + 1) * 256])
                self.wload(s_, 256, w_in[:, C_RG + h * 256:C_RG + (h + 1) * 256])
            return (sA, sB, sC)

        def tile_gen(i, pr, W, m, rows, col0):
            par = i % 2
            X, Y, Z = 2 + 3 * par, 3 + 3 * par, 4 + 3 * par
            sA = W[0]
            is_s = rows != 128
            t0 = (self.half * c.NTH * 128 + col0) if not is_s else c.T
            rope, rt, qkr, qkT = rope_b[par], rt_b[par], qkr_b[par], qkT_b[par]
            P.dma("sp", rope.t[:rows], I["c_rope"][t0:t0 + rows], [], [rope], rope.sem())
            for k in range(8):
                self.mm(ps.t[:rows, X, :], xT.t[:, k, col0:col0 + rows], sA.t[:, k, :], k == 0, k == 7,
                        [(xT, m), sA], [(ps, X)])
            qk5 = ps.t[:rows, X, :].rearrange("p (h a b f) -> p h a b f", h=2, a=2, b=2)
            x1, x2 = qk5[:, :, :, 0, :], qk5[:, :, :, 1, :]
            rp = rope.t[:rows].rearrange("p (a b) f -> p a b f", a=2)
            cos = rp[:, :, 0, :].unsqueeze(1).broadcast_to([rows, 2, 2, 64])
            sin = rp[:, :, 1, :].unsqueeze(1).broadcast_to([rows, 2, 2, 64])
            tv = [rt.t[:rows, j, :].rearrange("p (h a f) -> p h a f", h=2, a=2) for j in range(4)]
            P.dve(lambda e: e.tensor_tensor(tv[0], x1, cos, ALU.mult), [(ps, X), rope], [(rt, 0)])
            P.dve(lambda e: e.tensor_tensor(tv[1], x2, sin, ALU.mult), [(ps, X), rope], [(rt, 1)])
            P.dve(lambda e: e.tensor_tensor(tv[2], x1, sin, ALU.mult), [(ps, X), rope], [(rt, 2)])
            P.dve(lambda e: e.tensor_tensor(tv[3], x2, cos, ALU.mult), [(ps, X), rope], [(rt, 3)])
            P.dve(lambda e: e.tensor_tensor(qkr.t[:rows, :, :, 0:64], tv[0], tv[1], ALU.subtract), [(rt, 0), (rt, 1)], [qkr])
            P.dve(lambda e: e.tensor_tensor(qkr.t[:rows, :, :, 64:128], tv[2], tv[3], ALU.add), [(rt, 2), (rt, 3)], [qkr])
            yield

            def evac(pst, bt):
                P.act(lambda e: e.activation(qkT.t[:, :, :rows], pst[:, 0:4, :rows], AF.Copy), [(ps, bt)], [qkT])

            self.transposes_to(evac, lambda k: qkr.t[:rows, k // 2, k % 2, :], 4, rows, qkr, None, bt=Y)
            yield
            for hh in range(2):
                yield from head_gen(par, (X, Y, Z), pr, W, m, rows, col0, hh, qkr, qkT)

        def head_gen(par, banks, pr, W, m, rows, col0, hh, qkr, qkT):
            X, Y, Z = banks
            h = 2 * pr + hh
            sV = W[1 + hh]
            is_s = rows != 128
            gam = float(np.exp(lg[h]))
            bi = 2 * par + hh
            vbf, sgt, qdT, kdec, scm, ogt, nst, st6 = (vbf_b[bi], sgt_b[bi], qdT_b[bi], kdec_b[bi], scm_b[bi], ogt_b[bi],
                                                      nst_b[bi], st6_b[bi])
            ogb = ogb_b[par]
            for k in range(8):
                self.mm(ps.t[:rows, X, :], xT.t[:, k, col0:col0 + rows], sV.t[:, k, :], k == 0, k == 7, [(xT, m), sV], [(ps, X)])
            P.act(lambda e: e.activation(vbf.t[:rows, :], ps.t[:rows, X, 0:256], AF.Copy), [(ps, X)], [vbf])
            self.sigmoid_chain(sgt.t[:rows, :], ps.t[:rows, X, 256:512], [(ps, X)], sgt)
            yield
            P.dve(lambda e: e.tensor_tensor(sgt.t[:rows, :], ps.t[:rows, X, 256:512], sgt.t[:rows, :], ALU.mult), [(ps, X), sgt], [sgt])
            bo = Z if not is_s else 0
            Sr = (S, list(range(4 * h, 4 * h + 4)))
            Sbr = (Sb, list(range(4 * h, 4 * h + 4)))
            if not is_s:
                P.dve(lambda e: e.tensor_tensor(qdT.t[:, :], qkT.t[:, 2 * hh, :], retrow.t[:, h, :], ALU.mult), [qkT, retrow], [qdT])
                P.dve(lambda e: e.tensor_scalar(kdec.t[:, :], qkr.t[:, hh, 1, :], retcol.t[:, h:h + 1], None, ALU.mult), [qkr, retcol], [kdec])
                self.mm(ps.t[:, Y, 0:128], qkT.t[:, 2 * hh + 1, :], qkT.t[:, 2 * hh, :], True, True, [qkT], [(ps, Y)])
                yield
                P.dve(lambda e: e.tensor_tensor(scm.t[:, :], ps.t[:, Y, 0:128], retmask.t[:, h, :], ALU.mult), [(ps, Y), retmask], [scm])
                yield
                self.mm(ps.t[:, bo, 0:256], scm.t[:, :], vbf.t[:, :], True, False, [scm, vbf], [(ps, bo)])
                self.mm(ps.t[:, bo, 0:256], qdT.t[:, :], Sb.t[:, h * 256:(h + 1) * 256], False, True, [qdT, Sbr], [(ps, bo)])
                self.mm(ps.t[:, Y, 0:256], kdec.t[:, :], vbf.t[:, :], True, True, [kdec, vbf], [(ps, Y)])
                yield
                P.dve(lambda e: e.scalar_tensor_tensor(S.t[:, h * 256:(h + 1) * 256], S.t[:, h * 256:(h + 1) * 256],
                                                       float(np.exp(lg[h] * 128)), ps.t[:, Y, 0:256], ALU.mult, ALU.add), [Sr, (ps, Y)], [Sr])
                P.act(lambda e: e.activation(Sb.t[:, h * 256:(h + 1) * 256], S.t[:, h * 256:(h + 1) * 256], AF.Copy), [Sr], [Sbr])
            else:
                P.dve(lambda e: e.tensor_tensor(qTm.t[:], qkT.t[:, 2 * hh, 0:16].unsqueeze(1).broadcast_to([128, 16, 16]),
                                                self.eyeb.t[:], ALU.mult), [qkT, self.eyeb], [qTm])
                P.dve(lambda e: e.tensor_tensor(ktm.t[0:16], qkr.t[0:16, hh, 1, :].unsqueeze(1).broadcast_to([16, 16, 128]),
                                                self.eyep.t[:, :].unsqueeze(2).broadcast_to([16, 16, 128]), ALU.mult), [qkr, self.eyep], [ktm])
                self.run_pipelined((self.ret_sample(l, h, s_, gam, vbf, bo, Ss_b[s_ % 2], Ssb_b[s_ % 2], qTm, ktm, (Y, X)[s_ % 2])
                                    for s_ in range(NS)), 2)
            P.dve(lambda e: e.bn_stats(st6.t[:rows, :], ps.t[:rows, bo, 0:256]), [(ps, bo)], [st6])
            P.dve(lambda e: e.bn_aggr(nst.t[:rows, 0:2], st6.t[:rows, :]), [st6], [nst])
            yield
            self.rstd_pool(nst, rows, 0)
            yield
            P.dve(lambda e: e.tensor_scalar(ogt.t[:rows, :], ps.t[:rows, bo, 0:256], nst.t[:rows, 0:1], nst.t[:rows, 3:4],
                                            ALU.subtract, ALU.mult), [(ps, bo), nst], [ogt])
            P.dve(lambda e: e.tensor_tensor(ogb.t[:rows, :], ogt.t[:rows, :], sgt.t[:rows, :], ALU.mult), [ogt, sgt], [ogb])
            yield

            def evac(pst, bt):
                P.act(lambda e: e.activation(hT.t[:, 2 * h:2 * h + 2, col0:col0 + rows], pst[:, 0:2, :rows], AF.Copy),
                      [(ps, bt)], [(hT, [2 * h, 2 * h + 1])])

            self.transposes_to(evac, lambda k: ogb.t[:rows, k * 128:(k + 1) * 128], 2, rows, ogb, None, bt=X)
            yield

        nxt = load_pair(0)
        for pr in range(2):
            W = nxt
            if pr == 0:
                nxt = load_pair(1)
            self.run_pipelined((tile_gen(i, pr, W, m, rows, col0) for i, (m, rows, col0) in enumerate(self.tiles())), 2)
        self.state_store("ret_p", l)

    def ret_sample(self, l, h, s_, gam, vbf, bo, Ss, Ssb, qTm, ktm, bd):
        P, I, ps = self.P, self.i, self.ps
        P.dma("sp", Ss.t[:, 0, :], I["st_ret"][l, s_, h], [], [Ss], Ss.sem())
        self.mm(ps.t[:, bd, 0:256], ktm.t[0:16, s_, :], vbf.t[0:16, :], True, True, [ktm, vbf], [(ps, bd)])
        yield
        P.dve(lambda e: e.scalar_tensor_tensor(Ss.t[:, 1, :], Ss.t[:, 0, :], gam, ps.t[:, bd, 0:256], ALU.mult, ALU.add),
              [Ss, (ps, bd)], [Ss])
        yield
        P.dma("sp", self.o["ret_s"][l, s_, h], Ss.t[:, 1, :], [Ss], [], Ss.sem(), is_output=True)
        P.act(lambda e: e.activation(Ssb.t[:, :], Ss.t[:, 1, :], AF.Copy), [Ss], [Ssb])
        yield
        self.mm(ps.t[0:16, bo, 0:256], qTm.t[:, s_, :], Ssb.t[:, :], s_ == 0, s_ == NS - 1, [qTm, Ssb], [(ps, bo)])

    def finale(self, l, w_out, mcol):
        self.P.tag = "finale"
        P, I = self.P, self.i
        ps, xT, hT, xa = self.ps, self.xT, self.hT, self.xa
        w_in, w_o = I["w_in"][l], I["w_o"][l]
        A = self.A
        A.reset()
        gT_b = [A.alloc("gT", [8, 128], BF16) for _ in range(2)]
        gtok_b = [A.alloc("gtok", [D], BF16) for _ in range(2)]
        Wout = (self.wslot(), self.wslot())
        Wm = (self.wslot(), self.wslot())
        Wo = (self.wslot(), self.wslot())
        for o in range(2):
            self.wload(Wout[o], 0, w_out[:, o * 512:(o + 1) * 512])
            self.wload(Wm[o], 0, w_in[:, mcol + o * 512:mcol + (o + 1) * 512])
            self.wload(Wo[o], 0, w_o[:, o * 512:(o + 1) * 512])

        def tile_gen(i, m, rows, col0):
            par = i % 2
            by, bm = 4 * par, 4 * par + 2
            tA, gtok, gT = self.tmpA[par], gtok_b[par], gT_b[par]
            for o in range(2):
                for k in range(8):
                    self.mm(ps.t[:rows, bm + o, :], xT.t[:, k, col0:col0 + rows], Wm[o].t[:, k, :], k == 0, k == 7,
                            [(xT, m), Wm[o]], [(ps, bm + o)])
            for o in range(2):
                for k in range(8):
                    self.mm(ps.t[:rows, by + o, :], hT.t[:, k, col0:col0 + rows], Wout[o].t[:, k, :], k == 0, k == 7,
                            [(hT, k), Wout[o]], [(ps, by + o)])
            pm = ps.t[:rows, bm:bm + 2, :].rearrange("p b n -> p (b n)")
            py = ps.t[:rows, by:by + 2, :].rearrange("p b n -> p (b n)")
            P.act(lambda e: e.activation(tA.t[:rows, :], pm, AF.Sigmoid), [(ps, bm), (ps, bm + 1)], [tA])
            yield
            P.dve(lambda e: e.tensor_tensor(gtok.t[:rows, :], py, tA.t[:rows, :], ALU.mult), [(ps, by), (ps, by + 1), tA], [gtok])
            yield

            def evac(pst, bt):
                P.act(lambda e: e.activation(gT.t[:, :, :rows], pst[:, :, :rows], AF.Copy), [(ps, bt)], [gT])

            self.transposes_to(evac, lambda k: gtok.t[:rows, k * 128:(k + 1) * 128], 8, rows, gtok, None, bt=bm)
            yield
            for o in range(2):
                bo = by + o
                for k in range(8):
                    self.mm(ps.t[:rows, bo, :], gT.t[:, k, :rows], Wo[o].t[:, k, :], k == 0, k == 7, [gT, Wo[o]], [(ps, bo)])
            yield
            for o in range(2):
                bo = by + o
                P.dve(lambda e, o=o, bo=bo: e.tensor_tensor(xa.t[:rows, m, o * 512:(o + 1) * 512],
                                                            xa.t[:rows, m, o * 512:(o + 1) * 512], ps.t[:rows, bo, :], ALU.add),
                      [(ps, bo), (xa, 2 * m + o)], [(xa, 2 * m + o)])

        self.run_pipelined((tile_gen(i, m, rows, col0) for i, (m, rows, col0) in enumerate(self.tiles())), 2)

    def colvecs(self, dst_ap, tk, r, c, dst_buf):
        P, ps = self.P, self.ps
        b = self.bank()
        for cc in range(c):
            P.pe(lambda e, cc=cc: e.transpose(ps.t[:, b, cc * r:(cc + 1) * r], tk.t[:r, cc * 128:(cc + 1) * 128],
                                              self.identf.t[:r, :r]), [tk, self.identf], [(ps, b)])
        P.act(lambda e: e.activation(dst_ap, ps.t[:, b, 0:c * r], AF.Copy), [(ps, b)], [dst_buf])

    def ssd(self, l):
        self.P.tag = "ssd"
        P, I, c = self.P, self.i, self.cfg
        ps, xT, hT, S, Sb, A = self.ps, self.xT, self.hT, self.S, self.Sb, self.A
        w_in = I["w_in"][l]
        TP = c.NTH * 128
        assert TP <= 512
        hs = self.has_sample
        masks = self.masks
        U, ONES, MB = masks.t[:, 0, :], masks.t[:, 1, :], masks.t[:, 2, :]
        A.reset()
        cwb = A.alloc("cwb", [12, 5], F32)
        negb = A.alloc("negb", [12], F32)
        nwT = A.alloc("nwT", [8], F32)
        dtb = A.alloc("dtb", [16], F32)
        Ab = A.alloc("Ab", [16], F32)
        Db = A.alloc("Db", [16], F32)
        wdt = A.alloc("wdt", [8, 16], BF16)
        off0 = A.off
        tk = A.alloc("tk", [1536], F32)
        tk2 = A.alloc("tk2", [1024], F32)
        P.dma("sp", tk.t[0:4, :], I["ssm_conv_w"][l], [], [tk], tk.sem())
        P.dma("sp", tk.t[4:5, :], I["ssm_conv_b"][l].unsqueeze(0), [], [tk], tk.sem())
        self.colvecs(cwb.t[:].rearrange("p c j -> p (c j)"), tk, 5, 12, cwb)
        P.act(lambda e: e.activation(negb.t[:, :], cwb.t[:, :, 4], AF.Copy, scale=-1.0), [cwb], [negb])
        P.dma("sp", tk2.t[0:1, :], I["ssm_norm_w"][l].unsqueeze(0), [], [tk2], tk2.sem())
        self.colvecs(nwT.t[:, :], tk2, 1, 8, nwT)
        P.dma("sp", dtb.t[:], I["ssm_dt_bias"][l].partition_broadcast(128), [], [dtb], dtb.sem())
        P.dma("sp", Ab.t[:], I["ssm_a_log"][l].partition_broadcast(128), [], [Ab], Ab.sem())
        P.dma("sp", Db.t[:], I["ssm_d"][l].partition_broadcast(128), [], [Db], Db.sem())
        P.act(lambda e: e.activation(Ab.t[:], Ab.t[:], AF.Exp), [Ab], [Ab])
        P.act(lambda e: e.activation(Ab.t[:], Ab.t[:], AF.Copy, scale=-1.0), [Ab], [Ab])
        P.dma("pool", wdt.t[:], w_in[:, C_MDT:C_MDT + 16].rearrange("(k p) c -> p k c", p=128), [], [wdt], wdt.sem())
        A.reset(off0)
        xbc = A.alloc("xbc", [12, TP], BF16, nreg=12)
        xraw_b = [A.alloc("xraw", [3 + TP], F32) for _ in range(2)]
        acc_b = [A.alloc("acc", [TP], F32) for _ in range(2)]
        sgm_b = [A.alloc("sgm", [TP], F32) for _ in range(2)]
        f1 = A.alloc("f1", [1024], F32)
        f2 = A.alloc("f2", [1024], F32)
        f3 = A.alloc("f3", [1024], F32)
        xs_sb = A.alloc("xs_sb", [1024], BF16)
        v = A.alloc("v", [1024], BF16)
        vdec = A.alloc("vdec", [1024], BF16)
        Mh_b = [A.alloc("Mh", [8, 128], BF16) for _ in range(2)]
        Bd_b = [A.alloc("Bd", [8, 128], F32) for _ in range(2)]
        Btok = A.alloc("Btok", [2, 128], BF16)
        ogb = A.alloc("ogb", [1024], BF16)
        sm = A.alloc("sm", [8, 16], F32)
        cumT = A.alloc("cumT", [128], F32)
        nst = A.alloc("nst", [8], F32)
        stg = A.alloc("stg", [512], F32)
        if hs:
            xbcS = A.alloc("xbcS", [12, 16], BF16)
            stc_b = [A.alloc("stc", [3, 128], F32) for _ in range(2)]
            xrs_b = [A.alloc("xrs", [4, 16], F32) for _ in range(2)]
            accS_b = [A.alloc("accS", [16], F32) for _ in range(2)]
            sgS_b = [A.alloc("sgS", [16], F32) for _ in range(2)]
            Eall = A.alloc("Eall", [16, 16], F32)
            Bde = A.alloc("Bde", [16, 16], F32)
            Bm_b = [A.alloc("Bm", [256], BF16) for _ in range(2)]
            Cm_b = [A.alloc("Cm", [2, 16], BF16) for _ in range(2)]
        tail = self.tail_ssm[l]
        if self.half == 0:
            P.dve(lambda e: e.memset(tail.t[:], 0.0), [], [tail])
        self.state_load("ssm_p", l)
        Wx = [self.wslot() for _ in range(3)]
        for i in range(3):
            self.wload(Wx[i], 0, w_in[:, C_MXBC + i * 512:C_MXBC + (i + 1) * 512])
        Wz = (self.wslot(), self.wslot())
        for o in range(2):
            self.wload(Wz[o], 0, w_in[:, C_MZ + o * 512:C_MZ + (o + 1) * 512])

        def conv_chunk(cc):
            slot = Wx[cc // 4]
            cs = (cc % 4) * 128
            par = cc % 2
            acc, sgm = acc_b[par], sgm_b[par]
            b = 2 + par
            for k in range(8):
                self.mm(ps.t[:, b, :TP], slot.t[:, k, cs:cs + 128], xT.t[:, k, 0:TP], k == 0, k == 7,
                        [slot, (xT, list(range(c.NTH)))], [(ps, b)])
            xraw = xraw_b[par]
            P.act(lambda e: e.activation(xraw.t[:, 0:3], tail.t[:, cc, :], AF.Copy), [(tail, cc)], [xraw])
            P.act(lambda e: e.activation(xraw.t[:, 3:3 + TP], ps.t[:, b, :TP], AF.Copy), [(ps, b)], [xraw])
            P.act(lambda e: e.activation(tail.t[:, cc, :], xraw.t[:, TP:TP + 3], AF.Copy), [xraw], [(tail, cc)])
            yield
            P.dve(lambda e: e.tensor_scalar(acc.t[:, :], xraw.t[:, 0:TP], cwb.t[:, cc, 0:1], None, ALU.mult), [xraw, cwb], [acc])
            for j in range(1, 4):
                P.dve(lambda e, j=j: e.scalar_tensor_tensor(acc.t[:, :], xraw.t[:, j:j + TP], cwb.t[:, cc, j:j + 1], acc.t[:, :],
                                                            ALU.mult, ALU.add), [xraw, cwb, acc], [acc])
            yield
            P.act(lambda e: e.activation(xbc.t[:, cc, :], acc.t[:, :], AF.Silu, bias=cwb.t[:, cc, 4:5]), [acc, cwb], [(xbc, cc)])
            if hs:
                yield
                b2 = 4 + par
                for k in range(8):
                    self.mm(ps.t[:, b2, 0:16], slot.t[:, k, cs:cs + 128], xT.t[:, k, TP:TP + 16], k == 0, k == 7,
                            [slot, (xT, c.NTH)], [(ps, b2)])
                stc = stc_b[par]
                P.dma("sp", stc.t[0:16, :, :], I["st_ssm_conv"][l, :, :, cc * 128:(cc + 1) * 128], [], [stc], stc.sem())
                for j in range(3):
                    P.pe(lambda e, j=j: e.transpose(ps.t[:, b2, 16 + 16 * j:32 + 16 * j], stc.t[0:16, j, :], self.identf.t[0:16, 0:16]),
                         [stc, self.identf], [(ps, b2)])
                xrs = xrs_b[par]
                P.act(lambda e: e.activation(xrs.t[:, 0:3, :], ps.t[:, b2, 16:64].rearrange("p (j s) -> p j s", j=3), AF.Copy),
                      [(ps, b2)], [xrs])
                P.act(lambda e: e.activation(xrs.t[:, 3, :], ps.t[:, b2, 0:16], AF.Copy), [(ps, b2)], [xrs])
                accS, sgS = accS_b[par], sgS_b[par]
                P.dve(lambda e: e.tensor_scalar(accS.t[:, :], xrs.t[:, 0, :], cwb.t[:, cc, 0:1], None, ALU.mult), [xrs, cwb], [accS])
                for j in range(1, 4):
                    P.dve(lambda e, j=j: e.scalar_tensor_tensor(accS.t[:, :], xrs.t[:, j, :], cwb.t[:, cc, j:j + 1], accS.t[:, :],
                                                                ALU.mult, ALU.add), [xrs, cwb, accS], [accS])
                yield
                P.act(lambda e: e.activation(xbcS.t[:, cc, :], accS.t[:, :], AF.Silu, bias=cwb.t[:, cc, 4:5]), [accS, cwb], [xbcS])

        self.run_pipelined((conv_chunk(cc) for cc in range(12)), 2)
        def conv_rows(c0, n, dst_fn):
            for i in range(3):
                b = self.bank()
                for k in range(8):
                    self.mm(ps.t[:n, b, :], xT.t[:, k, c0:c0 + n], Wx[i].t[:, k, :], k == 0, k == 7,
                            [Wx[i], (xT, list(range(self.NTT)))], [(ps, b)])
                P.act(lambda e, b=b: e.activation(stg.t[:n, :], ps.t[:n, b, :], AF.Copy), [(ps, b)], [stg])
                P.dma("sp", dst_fn(i), stg.t[:n, :], [stg], [], stg.sem(), is_output=True)

        if self.half == c.NH - 1:
            conv_rows(TP - 3, 3, lambda i: self.o["ssm_conv_p"][l, :, i * 512:(i + 1) * 512])
        if hs:
            conv_rows(TP, 16, lambda i: self.o["ssm_conv_s"][l, :, 2, i * 512:(i + 1) * 512])
            P.dma("sp", self.o["ssm_conv_s"][l, :, 0:2, :], I["st_ssm_conv"][l, :, 1:3, :], [], [], stg.sem(), is_output=True)

        def tile_body(m, rows, col0):
            is_s = rows != 128
            src = xbcS if is_s else xbc
            sc0 = 0 if is_s else col0
            DT, LA, CUM, ETOK, ELAST, DECL, T16 = [sm.t[:rows, i, :] for i in range(7)]
            bdt = 6
            for k in range(8):
                self.mm(ps.t[:rows, bdt, 0:16], xT.t[:, k, col0:col0 + rows], wdt.t[:, k, :], k == 0, k == 7, [(xT, m), wdt], [(ps, bdt)])
            P.dve(lambda e: e.tensor_tensor(T16, ps.t[:rows, bdt, 0:16], dtb.t[:rows, :], ALU.add), [(ps, bdt), dtb], [sm])
            P.act(lambda e: e.activation(T16, T16, AF.Exp), [sm], [sm])
            P.act(lambda e: e.activation(DT, T16, AF.Ln, bias=1.0), [sm], [sm])
            P.dve(lambda e: e.tensor_tensor(LA, DT, Ab.t[:rows, :], ALU.mult), [sm, Ab], [sm])
            def evac_xs(pst, bt):
                P.act(lambda e: e.activation(xs_sb.t[:rows, :], pst[:rows, :, :].rearrange("p k n -> p (k n)"), AF.Copy), [(ps, bt)], [xs_sb])
            self.transposes_T(evac_xs, lambda k: src.t[:, k, sc0:sc0 + rows], 8, rows, src, bt=7)
            if is_s and l == 0 and self.cfg.dbg.get("ssd_dump"):
                dx = self.nc.dram_tensor("dbg_xs", [16, 1024], F32, kind="ExternalOutput").ap()
                dd = self.nc.dram_tensor("dbg_dt", [16, 16], F32, kind="ExternalOutput").ap()
                P.dma("pool", dx, xs_sb.t[0:16, :], [xs_sb], [], xs_sb.sem(), is_output=True)
                P.dma("sp", dd, sm.t[0:16, 0, :], [sm], [], sm.sem(), is_output=True)
            dt_b = DT.unsqueeze(2).broadcast_to([rows, 16, 64])
            P.dve(lambda e: e.tensor_tensor(v.t[:rows, :].rearrange("p (h d) -> p h d", h=16),
                                            xs_sb.t[:rows, :].rearrange("p (h d) -> p h d", h=16), dt_b, ALU.mult), [xs_sb, sm], [v])
            def evac_b(pst, bt):
                P.act(lambda e: e.activation(Btok.t[:rows, :, :], pst[:rows, 0:2, :], AF.Copy), [(ps, bt)], [Btok])
            self.transposes_T(evac_b, lambda k: src.t[:, 8 + k, sc0:sc0 + rows], 2, rows, src, bt=7)
            if not is_s:
                bc = 6
                self.mm(ps.t[:, bc, 32:48], U, LA, True, True, [masks, sm], [(ps, bc)])
                self.mm(ps.t[:, bc, 48:64], ONES, LA, True, True, [masks, sm], [(ps, bc)])
                self.mm(ps.t[0:16, bc, 64:192], LA, U, True, True, [masks, sm], [(ps, bc)])
                P.act(lambda e: e.activation(CUM, ps.t[:, bc, 32:48], AF.Copy), [(ps, bc)], [sm])
                P.act(lambda e: e.activation(ETOK, ps.t[:, bc, 32:48], AF.Exp), [(ps, bc)], [sm])
                P.act(lambda e: e.activation(ELAST, ps.t[:, bc, 48:64], AF.Exp), [(ps, bc)], [sm])
                P.dve(lambda e: e.tensor_tensor(DECL, ps.t[:, bc, 48:64], CUM, ALU.subtract), [(ps, bc), sm], [sm])
                P.act(lambda e: e.activation(DECL, DECL, AF.Exp), [sm], [sm])
                P.act(lambda e: e.activation(cumT.t[0:16, :], ps.t[0:16, bc, 64:192], AF.Copy), [(ps, bc)], [cumT])
                P.dve(lambda e: e.tensor_tensor(vdec.t[:, :].rearrange("p (h d) -> p h d", h=16),
                                                v.t[:, :].rearrange("p (h d) -> p h d", h=16),
                                                DECL.unsqueeze(2).broadcast_to([128, 16, 64]), ALU.mult), [v, sm], [vdec])
                bsc = 6
                for g in range(2):
                    self.mm(ps.t[:, bsc, 256 + g * 128:256 + (g + 1) * 128], xbc.t[:, 8 + g, col0:col0 + 128], xbc.t[:, 10 + g, col0:col0 + 128],
                            True, True, [xbc], [(ps, bsc)])
                po, pcs, pds = 0, 4, 2
                for g in range(2):
                    self.mm(ps.t[:, pcs + g, :], xbc.t[:, 10 + g, col0:col0 + 128], Sb.t[:, g * 512:(g + 1) * 512], True, True,
                            [xbc, (Sb, list(range(8 * g, 8 * g + 8)))], [(ps, pcs + g)])
                P.dve(lambda e: e.tensor_tensor(f2.t[:, :].rearrange("p (h d) -> p h d", h=16),
                                                ps.t[:, pcs:pcs + 2, :].rearrange("p b (h d) -> p (b h) d", h=8),
                                                ETOK.unsqueeze(2).broadcast_to([128, 16, 64]), ALU.mult), [(ps, pcs), (ps, pcs + 1), sm], [f2])
                def grp(g):
                    Bdg, fg, Mh = Bd_b[g], (f1 if g == 0 else f3), Mh_b[g]
                    pa = 2 + 2 * g
                    P.dve(lambda e: e.tensor_tensor(Bdg.t[0:16, :, :], cumT.t[0:16, :].unsqueeze(1).broadcast_to([16, 8, 128]),
                                                    self.eyep.t[:, 8 * g:8 * g + 8].unsqueeze(2).broadcast_to([16, 8, 128]), ALU.mult),
                          [cumT, self.eyep], [Bdg])
                    yield
                    for hf in range(2):
                        self.mm(ps.t[:, pa + hf, :], ONES[0:16, :], Bdg.t[0:16, 4 * hf:4 * hf + 4, :].rearrange("p h i -> p (h i)"),
                                True, False, [masks, Bdg], [(ps, pa + hf)])
                        for hq in range(4):
                            self.mm(ps.t[:, pa + hf, hq * 128:(hq + 1) * 128], self.identf.t[:, :], MB, False, hq == 3,
                                    [masks, self.identf], [(ps, pa + hf)])
                    yield
                    pav = ps.t[:, pa:pa + 2, :].rearrange("p b (h i) -> p (b h) i", h=4)
                    P.dve(lambda e: e.tensor_tensor(fg.t[:, :].rearrange("p (h i) -> p h i", h=8), pav,
                                                    CUM[:, 8 * g:8 * g + 8].unsqueeze(2).broadcast_to([128, 8, 128]),
                                                    ALU.subtract), [(ps, pa), (ps, pa + 1), sm], [fg])
                    yield
                    P.act(lambda e: e.activation(fg.t[:, :], fg.t[:, :], AF.Exp), [fg], [fg])
                    yield
                    P.dve(lambda e: e.tensor_tensor(Mh.t[:, :, :], fg.t[:, :].rearrange("p (h i) -> p h i", h=8),
                                                    ps.t[:, bsc, 256 + g * 128:256 + (g + 1) * 128].unsqueeze(1).broadcast_to([128, 8, 128]),
                                                    ALU.mult), [fg, (ps, bsc)], [Mh])
                    yield
                    for h in range(8):
                        hg = 8 * g + h
                        self.mm(ps.t[:, po + g, h * 64:(h + 1) * 64], Mh.t[:, h, :], v.t[:, hg * 64:(hg + 1) * 64], True, True,
                                [Mh, v], [(ps, po + g)])

                self.run_pipelined([grp(0), grp(1)], 2)
                P.dve(lambda e: e.tensor_tensor(f2.t[:, :], f2.t[:, :], ps.t[:, po:po + 2, :].rearrange("p b n -> p (b n)"), ALU.add),
                      [f2, (ps, po), (ps, po + 1)], [f2])
                for g in range(2):
                    self.mm(ps.t[:, pds + g, :], Btok.t[:, g, :], vdec.t[:, g * 512:(g + 1) * 512], True, True, [Btok, vdec], [(ps, pds + g)])
                P.dve(lambda e: e.tensor_tensor(f1.t[:, :].rearrange("p (h d) -> p h d", h=16), S.t[:, :].rearrange("p (h d) -> p h d", h=16),
                                                ELAST.unsqueeze(2).broadcast_to([128, 16, 64]), ALU.mult), [S, sm], [f1])
                P.dve(lambda e: e.tensor_tensor(S.t[:, :], f1.t[:, :], ps.t[:, pds:pds + 2, :].rearrange("p b n -> p (b n)"), ALU.add),
                      [f1, (ps, pds), (ps, pds + 1)], [S])
                P.act(lambda e: e.activation(Sb.t[:, :], S.t[:, :], AF.Copy), [S], [Sb])
            else:
                ELA = ELAST
                P.act(lambda e: e.activation(ELA, LA, AF.Exp), [sm], [sm])
                P.dve(lambda e: e.tensor_tensor(Bde.t[0:16, :, :], ELA.unsqueeze(1).broadcast_to([16, 16, 16]),
                                                self.eyep.t[:, :].unsqueeze(2).broadcast_to([16, 16, 16]), ALU.mult), [sm, self.eyep], [Bde])
                be = 6
                self.mm(ps.t[:, be, 0:256], ONES[0:16, :], Bde.t[0:16, :, :].rearrange("p s h -> p (s h)"), True, True, [masks, Bde], [(ps, be)])
                P.act(lambda e: e.activation(Eall.t[:, :, :].rearrange("p s h -> p (s h)"), ps.t[:, be, 0:256], AF.Copy), [(ps, be)], [Eall])
                Ss_b = [f2, f3]
                Ssb_b = [vdec, ogb]

                def smp(s_):
                    par = s_ % 2
                    Ss, Ssb, Bm, Cm = Ss_b[par], Ssb_b[par], Bm_b[par], Cm_b[par]
                    pds = 2 + 2 * par
                    P.dma("sp", Ss.t[:, :].rearrange("p (h d) -> p h d", h=16), I["st_ssm"][l, s_].rearrange("h n d -> n h d"), [], [Ss], Ss.sem())
                    P.dve(lambda e: e.tensor_scalar(Bm.t[0:16, :], Btok.t[0:16, :, :].rearrange("p g n -> p (g n)"),
                                                    self.eyep.t[:, s_:s_ + 1], None, ALU.mult), [Btok, self.eyep], [Bm])
                    P.dve(lambda e: e.tensor_tensor(Cm.t[:, :, :], xbcS.t[:, 10:12, :],
                                                    self.eyeb.t[:, s_, :].unsqueeze(1).broadcast_to([128, 2, 16]), ALU.mult),
                          [xbcS, self.eyeb], [Cm])
                    yield
                    for g in range(2):
                        self.mm(ps.t[:, pds + g, :], Bm.t[0:16, g * 128:(g + 1) * 128], v.t[0:16, g * 512:(g + 1) * 512], True, True,
                                [Bm, v], [(ps, pds + g)])
                    yield
                    P.dve(lambda e: e.tensor_tensor(Ss.t[:, :].rearrange("p (h d) -> p h d", h=16), Ss.t[:, :].rearrange("p (h d) -> p h d", h=16),
                                                    Eall.t[:, s_, :].unsqueeze(2).broadcast_to([128, 16, 64]), ALU.mult), [Ss, Eall], [Ss])
                    P.dve(lambda e: e.tensor_tensor(Ss.t[:, :], Ss.t[:, :], ps.t[:, pds:pds + 2, :].rearrange("p b n -> p (b n)"), ALU.add),
                          [Ss, (ps, pds), (ps, pds + 1)], [Ss])
                    yield
                    P.dma("sp", self.o["ssm_s"][l, s_].rearrange("h n d -> n h d"), Ss.t[:, :].rearrange("p (h d) -> p h d", h=16),
                          [Ss], [], Ss.sem(), is_output=True)
                    P.act(lambda e: e.activation(Ssb.t[:, :], Ss.t[:, :], AF.Copy), [Ss], [Ssb])
                    yield
                    for g in range(2):
                        self.mm(ps.t[0:16, g, :], Cm.t[:, g, :], Ssb.t[:, g * 512:(g + 1) * 512], s_ == 0, s_ == NS - 1, [Cm, Ssb], [(ps, g)])

                self.run_pipelined((smp(s_) for s_ in range(NS)), 2)
                P.act(lambda e: e.activation(f2.t[0:16, :], ps.t[0:16, 0:2, :].rearrange("p b n -> p (b n)"), AF.Copy), [(ps, 0), (ps, 1)], [f2])
            P.dve(lambda e: e.tensor_tensor(f3.t[:rows, :].rearrange("p (h d) -> p h d", h=16),
                                            xs_sb.t[:rows, :].rearrange("p (h d) -> p h d", h=16),
                                            Db.t[:rows, :].unsqueeze(2).broadcast_to([rows, 16, 64]), ALU.mult), [xs_sb, Db], [f3])
            P.dve(lambda e: e.tensor_tensor(f2.t[:rows, :], f2.t[:rows, :], f3.t[:rows, :], ALU.add), [f2, f3], [f2])
            pz = 4
            for o in range(2):
                for k in range(8):
                    self.mm(ps.t[:rows, pz + o, :], xT.t[:, k, col0:col0 + rows], Wz[o].t[:, k, :], k == 0, k == 7, [(xT, m), Wz[o]], [(ps, pz + o)])
            pzv = ps.t[:rows, pz:pz + 2, :].rearrange("p b n -> p (b n)")
            self.sigmoid_chain(f3.t[:rows, :], pzv, [(ps, pz), (ps, pz + 1)], f3)
            P.dve(lambda e: e.tensor_tensor(f3.t[:rows, :], pzv, f3.t[:rows, :], ALU.mult), [(ps, pz), (ps, pz + 1), f3], [f3])
            P.dve(lambda e: e.tensor_tensor(f2.t[:rows, :], f2.t[:rows, :], f3.t[:rows, :], ALU.mult), [f2, f3], [f2])
            for g in range(2):
                P.act(lambda e, g=g: e.activation(f3.t[:rows, g * 512:(g + 1) * 512], f2.t[:rows, g * 512:(g + 1) * 512], AF.Square,
                                                  accum_out=nst.t[:rows, 4 + g:5 + g]), [f2], [f3, nst])
            P.act(lambda e: e.activation(nst.t[:rows, 0:2], nst.t[:rows, 4:6], AF.Ln, scale=1.0 / 512.0, bias=float(NORM_EPS)), [nst], [nst])
            P.act(lambda e: e.activation(nst.t[:rows, 2:4], nst.t[:rows, 0:2], AF.Exp, scale=-0.5), [nst], [nst])
            for g in range(2):
                P.dve(lambda e, g=g: e.tensor_scalar(ogb.t[:rows, g * 512:(g + 1) * 512], f2.t[:rows, g * 512:(g + 1) * 512],
                                                     nst.t[:rows, 2 + g:3 + g], None, ALU.mult), [f2, nst], [ogb])

            def evac(pst, bt):
                P.dve(lambda e: e.tensor_tensor(hT.t[:, 0:8, col0:col0 + rows], pst[:, :, :rows],
                                                nwT.t[:, :].unsqueeze(2).broadcast_to([128, 8, rows]), ALU.mult),
                      [(ps, bt), nwT], [(hT, list(range(8)))])

            self.transposes_to(evac, lambda k: ogb.t[:rows, k * 128:(k + 1) * 128], 8, rows, ogb, None, bt=7)

        for (m, rows, col0) in self.tiles():
            tile_body(m, rows, col0)
        self.state_store("ssm_p", l)

    def gdn(self, l):
        self.P.tag = "gdn"
        P, I, c = self.P, self.i, self.cfg
        ps, xT, hT, S, Sb, A = self.ps, self.xT, self.hT, self.S, self.Sb, self.A
        w_in = I["w_in"][l]
        TP = c.NTH * 128
        hs = self.has_sample
        masks = self.masks
        ONES, MBI, MBS = masks.t[:, 1, :], masks.t[:, 4, :], masks.t[:, 5, :]
        U64 = masks.t[:, 3, :]
        A.reset()
        cw = A.alloc("cw", [24, 4], F32)
        dtb = A.alloc("dtb", [8], F32)
        Ab = A.alloc("Ab", [8], F32)
        nwb = A.alloc("nwb", [128], F32)
        wab = A.alloc("wab", [8, 16], BF16)
        off0 = A.off
        tk = A.alloc("tk", [3072], F32)
        P.dma("sp", tk.t[0:4, :], I["gdn_conv_w"][l], [], [tk], tk.sem())
        self.colvecs(cw.t[:].rearrange("p c j -> p (c j)"), tk, 4, 24, cw)
        P.dma("sp", dtb.t[:], I["gdn_dt_bias"][l].partition_broadcast(128), [], [dtb], dtb.sem())
        P.dma("sp", Ab.t[:], I["gdn_a_log"][l].partition_broadcast(128), [], [Ab], Ab.sem())
        P.dma("sp", nwb.t[:], I["gdn_norm_w"][l].partition_broadcast(128), [], [nwb], nwb.sem())
        P.act(lambda e: e.activation(Ab.t[:], Ab.t[:], AF.Exp), [Ab], [Ab])
        P.act(lambda e: e.activation(Ab.t[:], Ab.t[:], AF.Copy, scale=-1.0), [Ab], [Ab])
        P.dma("pool", wab.t[:], w_in[:, C_GA:C_GA + 16].rearrange("(k p) c -> p k c", p=128), [], [wab], wab.sem())
        A.reset(off0)
        qkv = A.alloc("qkv", [24, TP], BF16, nreg=24)
        if hs:
            qkvS = A.alloc("qkvS", [24, 16], F32)
        off1 = A.off
        xraw_b = [A.alloc("xraw", [3 + TP], F32) for _ in range(2)]
        NPAR = 2 if hs else 3
        xraw_b = xraw_b + [A.alloc("xraw", [3 + TP], F32) for _ in range(NPAR - 2)]
        acc_b = [A.alloc("acc", [TP], F32) for _ in range(NPAR)]
        sgm_b = [A.alloc("sgm", [TP], F32) for _ in range(NPAR)]
        sq_b = [A.alloc("sq", [TP], F32) for _ in range(NPAR)]
        stg = A.alloc("stg", [512], F32)
        if hs:
            stc_b = [A.alloc("stc", [3, 128], F32) for _ in range(2)]
            xrs_b = [A.alloc("xrs", [4, 16], F32) for _ in range(2)]
            accS_b = [A.alloc("accS", [16], F32) for _ in range(2)]
            sgS_b = [A.alloc("sgS", [16], F32) for _ in range(2)]
            sqS_b = [A.alloc("sqS", [16], F32) for _ in range(2)]
        tail = self.tail_gdn[l]
        if self.half == 0:
            P.dve(lambda e: e.memset(tail.t[:], 0.0), [], [tail])
        self.state_load("gdn_p", l)
        lnq = float(np.log(128.0 ** -0.5))

        def l2n_g(dst_ap, xin_ap, sq_ap, n, is_q, bufs_r, bufs_w, b):
            P.act(lambda e: e.activation(sq_ap, xin_ap, AF.Square), bufs_r, [bufs_w[0]])
            self.mm(ps.t[:, b, :n], self.masks.t[:, 1, :], sq_ap, True, True, [masks, bufs_w[0]], [(ps, b)])
            yield
            P.act(lambda e: e.activation(sq_ap, ps.t[:, b, :n], AF.Ln, bias=float(NORM_EPS)), [(ps, b)], [bufs_w[0]])
            if is_q:
                P.act(lambda e: e.activation(sq_ap, sq_ap, AF.Exp, scale=-0.5, bias=lnq), [bufs_w[0]], [bufs_w[0]])
            else:
                P.act(lambda e: e.activation(sq_ap, sq_ap, AF.Exp, scale=-0.5), [bufs_w[0]], [bufs_w[0]])
            yield
            P.dve(lambda e: e.tensor_tensor(dst_ap, xin_ap, sq_ap, ALU.mult), list(bufs_r) + [bufs_w[0]], [bufs_w[1]])

        def conv_chunk(cc, slot):
            cs = (cc % 4) * 128
            par = cc % NPAR
            acc, sgm, sq = acc_b[par], sgm_b[par], sq_b[par]
            b = 2 + 2 * par
            bl = 3 + 2 * par
            for k in range(8):
                self.mm(ps.t[:, b, :TP], slot.t[:, k, cs:cs + 128], xT.t[:, k, 0:TP], k == 0, k == 7,
                        [slot, (xT, list(range(c.NTH)))], [(ps, b)])
            xraw = xraw_b[par]
            P.act(lambda e: e.activation(xraw.t[:, 0:3], tail.t[:, cc, :], AF.Copy), [(tail, cc)], [xraw])
            P.act(lambda e: e.activation(xraw.t[:, 3:3 + TP], ps.t[:, b, :TP], AF.Copy), [(ps, b)], [xraw])
            P.act(lambda e: e.activation(tail.t[:, cc, :], xraw.t[:, TP:TP + 3], AF.Copy), [xraw], [(tail, cc)])
            yield
            P.dve(lambda e: e.tensor_scalar(acc.t[:, :], xraw.t[:, 0:TP], cw.t[:, cc, 0:1], None, ALU.mult), [xraw, cw], [acc])
            for j in range(1, 4):
                P.dve(lambda e, j=j: e.scalar_tensor_tensor(acc.t[:, :], xraw.t[:, j:j + TP], cw.t[:, cc, j:j + 1], acc.t[:, :],
                                                            ALU.mult, ALU.add), [xraw, cw, acc], [acc])
            yield
            if cc < 16:
                self.sigmoid_chain(sgm.t[:, :], acc.t[:, :], [acc], sgm)
                yield
                P.dve(lambda e: e.tensor_tensor(acc.t[:, :], acc.t[:, :], sgm.t[:, :], ALU.mult), [acc, sgm], [acc])
                yield from l2n_g(qkv.t[:, cc, :], acc.t[:, :], sq.t[:, :], TP, cc < 8, [acc], [sq, (qkv, cc)], bl)
            else:
                P.act(lambda e: e.activation(qkv.t[:, cc, :], acc.t[:, :], AF.Silu), [acc], [(qkv, cc)])
            if hs:
                yield
                b2 = 6 + par
                for k in range(8):
                    self.mm(ps.t[:, b2, 0:16], slot.t[:, k, cs:cs + 128], xT.t[:, k, TP:TP + 16], k == 0, k == 7,
                            [slot, (xT, c.NTH)], [(ps, b2)])
                stc = stc_b[par]
                P.dma("sp", stc.t[0:16, :, :], I["st_gdn_conv"][l, :, :, cc * 128:(cc + 1) * 128], [], [stc], stc.sem())
                for j in range(3):
                    P.pe(lambda e, j=j: e.transpose(ps.t[:, b2, 16 + 16 * j:32 + 16 * j], stc.t[0:16, j, :], self.identf.t[0:16, 0:16]),
                         [stc, self.identf], [(ps, b2)])
                xrs = xrs_b[par]
                P.act(lambda e: e.activation(xrs.t[:, 0:3, :], ps.t[:, b2, 16:64].rearrange("p (j s) -> p j s", j=3), AF.Copy),
                      [(ps, b2)], [xrs])
                P.act(lambda e: e.activation(xrs.t[:, 3, :], ps.t[:, b2, 0:16], AF.Copy), [(ps, b2)], [xrs])
                accS, sgS, sqS = accS_b[par], sgS_b[par], sqS_b[par]
                P.dve(lambda e: e.tensor_scalar(accS.t[:, :], xrs.t[:, 0, :], cw.t[:, cc, 0:1], None, ALU.mult), [xrs, cw], [accS])
                for j in range(1, 4):
                    P.dve(lambda e, j=j: e.scalar_tensor_tensor(accS.t[:, :], xrs.t[:, j, :], cw.t[:, cc, j:j + 1], accS.t[:, :],
                                                                ALU.mult, ALU.add), [xrs, cw, accS], [accS])
                yield
                if cc < 16:
                    self.sigmoid_chain(sgS.t[:, :], accS.t[:, :], [accS], sgS)
                    yield
                    P.dve(lambda e: e.tensor_tensor(accS.t[:, :], accS.t[:, :], sgS.t[:, :], ALU.mult), [accS, sgS], [accS])
                    yield from l2n_g(qkvS.t[:, cc, :], accS.t[:, :], sqS.t[:, :], 16, cc < 8, [accS], [sqS, qkvS], b2)
                else:
                    P.act(lambda e: e.activation(qkvS.t[:, cc, :], accS.t[:, :], AF.Silu), [accS], [qkvS])

        def conv_rows(slot, i, c0, n, dst):
            b = self.bank()
            for k in range(8):
                self.mm(ps.t[:n, b, :], xT.t[:, k, c0:c0 + n], slot.t[:, k, :], k == 0, k == 7,
                        [slot, (xT, list(range(self.NTT)))], [(ps, b)])
            P.act(lambda e: e.activation(stg.t[:n, :], ps.t[:n, b, :], AF.Copy), [(ps, b)], [stg])
            P.dma("sp", dst, stg.t[:n, :], [stg], [], stg.sem(), is_output=True)

        def load_slot(i):
            sl = self.wslot()
            self.wload(sl, 0, w_in[:, C_GQKV + i * 512:C_GQKV + (i + 1) * 512])
            return sl

        nxt = load_slot(0)
        for i in range(6):
            slot = nxt
            if i + 1 < 6:
                nxt = load_slot(i + 1)
            self.run_pipelined((conv_chunk(cc, slot) for cc in range(4 * i, 4 * i + 4)), NPAR)
            if self.half == c.NH - 1:
                conv_rows(slot, i, TP - 3, 3, self.o["gdn_conv_p"][l, :, i * 512:(i + 1) * 512])
            if hs:
                conv_rows(slot, i, TP, 16, self.o["gdn_conv_s"][l, :, 2, i * 512:(i + 1) * 512])
        if hs:
            P.dma("sp", self.o["gdn_conv_s"][l, :, 0:2, :], I["st_gdn_conv"][l, :, 1:3, :], [], [], stg.sem(), is_output=True)
        if self.cfg.dbg.get("gdn_stop", 9) <= 1:
            return
        Wz = (self.wslot(), self.wslot())
        for o in range(2):
            self.wload(Wz[o], 0, w_in[:, C_GZ + o * 512:C_GZ + (o + 1) * 512])

        A.reset(off1)
        sm = A.alloc("sm", [14, 8], F32)
        osb = A.alloc("osb", [1024], F32, nreg=2)
        f3 = A.alloc("f3", [1024], F32)
        ogb = A.alloc("ogb", [1024], BF16)
        nst = A.alloc("nst", [3, 8], F32)
        off2 = A.off
        sm_b = [sm, A.alloc("sm1", [14, 8], F32)]
        cumT_b = [A.alloc("cumT", [2, 128], F32) for _ in range(2)]
        Bd1 = A.alloc("Bd1", [4, 128], F32)
        o_d = A.off
        dtmp = A.alloc("dtmp", [4, 128], F32)
        A.reset(o_d)
        ktok = A.alloc("ktok", [4, 128], BF16)
        A.reset(o_d + 2048)
        f1 = A.alloc("f1", [512], F32)
        Ya, Yb = A.alloc("Ya", [4, 128], F32), A.alloc("Yb", [4, 128], F32)
        YTa, YTb = A.alloc("YTa", [4, 128], F32), A.alloc("YTb", [4, 128], F32)
        rhs = A.alloc("rhs", [4, 128], F32)
        ub = A.alloc("ub", [4, 128], BF16)
        PT_b = [A.alloc("PT", [4, 128], F32) for _ in range(2)]
        attnT_b = [A.alloc("attnT", [4, 128], BF16) for _ in range(2)]
        qdT_b = [A.alloc("qdT", [4, 128], BF16) for _ in range(2)]
        kdec_b = [A.alloc("kdec", [2, 4, 128], BF16) for _ in range(2)]
        vb_b = [A.alloc("vb", [4, 128], F32) for _ in range(2)]
        if hs:
            A.reset(off2)
            Eall = A.alloc("Eall", [16, 8], F32)
            Bde = A.alloc("Bde", [16, 8], F32)
            kTm_b = [A.alloc("kTm", [8, 16], F32) for _ in range(2)]
            qTm_b = [A.alloc("qTm", [8, 16], F32) for _ in range(2)]
            ktS = A.alloc("ktS", [1024], F32)
            vbS = A.alloc("vbS", [1024], F32)
            um_b = [A.alloc("um", [1024], F32) for _ in range(2)]
            SsB = A.alloc("SsB", [1024], F32)
            oacc = A.alloc("oacc", [1024], F32)

        def gates(m, rows, col0, sm=sm):
            R = lambda i: sm.t[:rows, i, :]
            bab = 6
            for k in range(8):
                self.mm(ps.t[:rows, bab, 0:16], xT.t[:, k, col0:col0 + rows], wab.t[:, k, :], k == 0, k == 7, [(xT, m), wab], [(ps, bab)])
            P.act(lambda e: e.activation(R(7), ps.t[:rows, bab, 8:16], AF.Exp, scale=-1.0), [(ps, bab)], [sm])
            P.act(lambda e: e.activation(R(6), R(7), AF.Ln, bias=1.0), [sm], [sm])
            P.act(lambda e: e.activation(R(6), R(6), AF.Copy, scale=-1.0), [sm], [sm])
            P.act(lambda e: e.activation(R(0), R(6), AF.Exp), [sm], [sm])
            P.dve(lambda e: e.tensor_tensor(R(7), ps.t[:rows, bab, 0:8], dtb.t[:rows, :], ALU.add), [(ps, bab), dtb], [sm])
            P.act(lambda e: e.activation(R(7), R(7), AF.Exp), [sm], [sm])
            P.act(lambda e: e.activation(R(7), R(7), AF.Ln, bias=1.0), [sm], [sm])
            P.dve(lambda e: e.tensor_tensor(R(1), R(7), Ab.t[:rows, :], ALU.mult), [sm, Ab], [sm])

        v4 = lambda ap: ap.rearrange("p (h i) -> p h i", h=4)

        def tile_front(m, col0, sm, cumT):
            R = lambda i: sm.t[:, i, :]
            gates(m, 128, col0, sm)
            bc = 6
            self.mm(ps.t[:, bc, 32:40], U64, R(1), True, True, [masks, sm], [(ps, bc)])
            self.mm(ps.t[:, bc, 40:48], masks.t[:, 6, :], R(1), True, True, [masks, sm], [(ps, bc)])
            self.mm(ps.t[:, bc, 48:56], masks.t[:, 7, :], R(1), True, True, [masks, sm], [(ps, bc)])
            self.mm(ps.t[:, bc, 56:64], ONES, R(1), True, True, [masks, sm], [(ps, bc)])
            P.act(lambda e: e.activation(R(2), ps.t[:, bc, 32:40], AF.Copy), [(ps, bc)], [sm])
            P.act(lambda e: e.activation(R(3), ps.t[:, bc, 32:40], AF.Exp), [(ps, bc)], [sm])
            P.dve(lambda e: e.tensor_tensor(R(5), ps.t[:, bc, 40:48], R(2), ALU.subtract), [(ps, bc), sm], [sm])
            P.act(lambda e: e.activation(R(5), R(5), AF.Exp), [sm], [sm])
            P.dve(lambda e: e.tensor_copy(sm.t[:, 12:14, :], sm.t[:, 5:6, :].broadcast_to([128, 2, 8])), [sm], [sm])
            P.dve(lambda e: e.memset(sm.t[64:128, 12, :], 0.0), [sm], [sm])
            P.dve(lambda e: e.memset(sm.t[0:64, 13, :], 0.0), [sm], [sm])
            P.act(lambda e: e.activation(R(8), ps.t[:, bc, 48:56], AF.Exp), [(ps, bc)], [sm])
            P.act(lambda e: e.activation(R(7), ps.t[:, bc, 48:56], AF.Copy), [(ps, bc)], [sm])
            P.dve(lambda e: e.tensor_tensor(R(9), ps.t[:, bc, 56:64], R(7), ALU.subtract), [(ps, bc), sm], [sm])
            P.act(lambda e: e.activation(R(9), R(9), AF.Exp), [sm], [sm])
            P.dve(lambda e: e.scalar_tensor_tensor(R(4), R(0), -1.0, R(3), ALU.mult, ALU.mult), [sm], [sm])
            P.dve(lambda e: e.tensor_tensor(R(10), R(2), R(6), ALU.add), [sm], [sm])
            self.mm(ps.t[0:8, bc, 64:192], R(2), self.identf.t[:, :], True, True, [sm, self.identf], [(ps, bc)])
            self.mm(ps.t[0:8, bc, 192:320], R(10), self.identf.t[:, :], True, True, [sm, self.identf], [(ps, bc)])
            P.act(lambda e: e.activation(cumT.t[0:8, :, :].rearrange("p a i -> p (a i)"), ps.t[0:8, bc, 64:320], AF.Copy), [(ps, bc)], [cumT])

        def unit_gen(u, m, col0, g, part):
            par = u % 2
            BA, BB, BC = (2, 3, 4) if par == 0 else (0, 1, 5)
            sm, cumT = sm_b[m % 2], cumT_b[m % 2]
            PT, attnT, qdT, kdec, vb = PT_b[par], attnT_b[par], qdT_b[par], kdec_b[par], vb_b[par]
            R = lambda i: sm.t[:, i, :]
            hsl = slice(4 * g, 4 * g + 4)
            if part == "A":
                if g == 0:
                    tile_front(m, col0, sm, cumT)
                    yield
                hsl = slice(4 * g, 4 * g + 4)
                eye_g = self.eyep.t[0:8, 4 * g:4 * g + 4].unsqueeze(2).broadcast_to([8, 4, 128])
                bdf = Bd1.t[0:8, :, :].rearrange("p h i -> p (h i)")
                cum_b = R(2)[:, hsl].unsqueeze(2).broadcast_to([128, 4, 128])
                P.dve(lambda e: e.tensor_tensor(Bd1.t[0:8, :, :], cumT.t[0:8, 0, :].unsqueeze(1).broadcast_to([8, 4, 128]), eye_g, ALU.mult),
                      [cumT, self.eyep], [Bd1])
                self.mm(ps.t[:, BA, :], ONES[0:8, :], bdf, True, True, [masks, Bd1], [(ps, BA)])
                self.mm(ps.t[:, BB, :], ONES[0:8, :], bdf, True, False, [masks, Bd1], [(ps, BB)])
                for hq in range(4):
                    self.mm(ps.t[:, BB, hq * 128:(hq + 1) * 128], self.identf.t[:, :], MBI, False, hq == 3, [masks, self.identf], [(ps, BB)])
                for h in range(4):
                    hh = 4 * g + h
                    self.mm(ps.t[:, BC, h * 128:(h + 1) * 128], qkv.t[:, 8 + hh, col0:col0 + 128], qkv.t[:, hh, col0:col0 + 128], True, True,
                            [(qkv, [hh, 8 + hh])], [(ps, BC)])
                yield
                P.act(lambda e: e.activation(dtmp.t[:, :, :].rearrange("p h i -> p (h i)"), ps.t[:, BA, :], AF.Exp), [(ps, BA)], [dtmp])
                P.dve(lambda e: e.tensor_tensor(f1.t[:, :].rearrange("p (h i) -> p h i", h=4), v4(ps.t[:, BB, :]), cum_b, ALU.subtract), [(ps, BB), sm], [f1])
                yield
                P.dve(lambda e: e.tensor_tensor(qdT.t[:, :, :], qkv.t[:, 4 * g:4 * g + 4, col0:col0 + 128], dtmp.t[:, :, :], ALU.mult),
                      [(qkv, list(range(4 * g, 4 * g + 4))), dtmp], [qdT])
                P.act(lambda e: e.activation(f1.t[:, :], f1.t[:, :], AF.Exp), [f1], [f1])
                yield
                P.dve(lambda e: e.tensor_tensor(attnT.t[:, :, :], v4(ps.t[:, BC, :]), f1.t[:, :].rearrange("p (h i) -> p h i", h=4), ALU.mult),
                      [(ps, BC), f1], [attnT])
                P.dve(lambda e: e.tensor_tensor(Bd1.t[0:8, :, :], cumT.t[0:8, 1, :].unsqueeze(1).broadcast_to([8, 4, 128]), eye_g, ALU.mult),
                      [cumT, self.eyep], [Bd1])
                self.mm(ps.t[:, BB, :], ONES[0:8, :], bdf, True, False, [masks, Bd1], [(ps, BB)])
                for hq in range(4):
                    self.mm(ps.t[:, BB, hq * 128:(hq + 1) * 128], self.identf.t[:, :], MBS, False, hq == 3, [masks, self.identf], [(ps, BB)])
                for h in range(4):
                    hh = 4 * g + h
                    self.mm(ps.t[:, BA, h * 128:(h + 1) * 128], qkv.t[:, 8 + hh, col0:col0 + 128], qkv.t[:, 8 + hh, col0:col0 + 128], True, True,
                            [(qkv, 8 + hh)], [(ps, BA)])
                yield
                P.dve(lambda e: e.tensor_tensor(dtmp.t[:, :, :], v4(ps.t[:, BB, :]), cum_b, ALU.subtract), [(ps, BB), sm], [dtmp])
                yield
                P.act(lambda e: e.activation(dtmp.t[:, :, :], dtmp.t[:, :, :], AF.Exp), [dtmp], [dtmp])
                yield
                P.dve(lambda e: e.scalar_tensor_tensor(YTa.t[:, :, :], v4(ps.t[:, BA, :]), -1.0, dtmp.t[:, :, :], ALU.mult, ALU.mult),
                      [(ps, BA), dtmp], [YTa])
                for h in range(4):
                    P.pe(lambda e, h=h: e.transpose(ps.t[:, BB, h * 128:(h + 1) * 128], YTa.t[:, h, :], self.identf.t[:, :]),
                         [YTa, self.identf], [(ps, BB)])
                yield
                P.act(lambda e: e.activation(Ya.t[:, :, :], v4(ps.t[:, BB, :]), AF.Copy), [(ps, BB)], [Ya])
                P.dve(lambda e: e.tensor_tensor(PT.t[:, :, :], YTa.t[:, :, :], self.identf.t[:, :].unsqueeze(1).broadcast_to([128, 4, 128]), ALU.add),
                      [YTa, self.identf], [PT])
                def ev_k(pst, bt):
                    P.act(lambda e: e.activation(ktok.t[:, :, :], pst[:, 0:4, :], AF.Copy), [(ps, bt)], [ktok])
                self.transposes_T(ev_k, lambda k: qkv.t[:, 8 + 4 * g + k, col0:col0 + 128], 4, 128, (qkv, list(range(8 + 4 * g, 12 + 4 * g))), bt=7)
                yield
                def level(lev, Y, YT, Yn, YTn):
                    for h in range(4):
                        self.mm(ps.t[:, BA, h * 128:(h + 1) * 128], YT.t[:, h, :], Y.t[:, h, :], True, True, [YT, Y], [(ps, BA)])
                    if lev < 5:
                        for h in range(4):
                            self.mm(ps.t[:, BB, h * 128:(h + 1) * 128], Y.t[:, h, :], YT.t[:, h, :], True, True, [YT, Y], [(ps, BB)])
                    yield
                    P.act(lambda e: e.activation(Yn.t[:, :, :], v4(ps.t[:, BA, :]), AF.Copy), [(ps, BA)], [Yn])
                    if lev < 5:
                        P.act(lambda e: e.activation(YTn.t[:, :, :], v4(ps.t[:, BB, :]), AF.Copy), [(ps, BB)], [YTn])
                    yield
                    for h in range(4):
                        self.mm(ps.t[:, BC, h * 128:(h + 1) * 128], Yn.t[:, h, :], PT.t[:, h, :], True, True, [Yn, PT], [(ps, BC)])
                    yield
                    P.dve(lambda e: e.tensor_tensor(PT.t[:, :, :], PT.t[:, :, :], v4(ps.t[:, BC, :]), ALU.add), [PT, (ps, BC)], [PT])
                Y, YT, Yn, YTn = Ya, YTa, Yb, YTb
                for lev in range(1, 6):
                    yield from level(lev, Y, YT, Yn, YTn)
                    if lev == 1:
                        for cq in range(2):
                            P.dve(lambda e, cq=cq: e.tensor_tensor(kdec.t[:, cq, :, :], ktok.t[:, :, :],
                                                                   R(12 + cq)[:, hsl].unsqueeze(2).broadcast_to([128, 4, 128]), ALU.mult),
                                  [ktok, sm], [kdec])
                    if lev == 2:
                        def ev_v(pst, bt):
                            P.dve(lambda e: e.tensor_tensor(vb.t[:, :, :], pst[:, 0:4, :], R(0)[:, hsl].unsqueeze(2).broadcast_to([128, 4, 128]), ALU.mult),
                                  [(ps, bt), sm], [vb])
                        self.transposes_T(ev_v, lambda k: qkv.t[:, 16 + 4 * g + k, col0:col0 + 128], 4, 128,
                                          (qkv, list(range(16 + 4 * g, 20 + 4 * g))), bt=7)
                    Y, YT, Yn, YTn = Yn, YTn, Y, YT
                    yield
                return
            Sg = (S, list(range(8 * g, 8 * g + 8)))
            Sbg = (Sb, list(range(8 * g, 8 * g + 8)))

            def chunk(ch):
                rs = slice(64 * ch, 64 * ch + 64)
                rw = slice(0, 128) if ch == 0 else rs
                nrw = 128 if ch == 0 else 64
                for h in range(4):
                    hh = 4 * g + h
                    self.mm(ps.t[:, BA, h * 128:(h + 1) * 128], qkv.t[:, 8 + hh, col0:col0 + 128], Sb.t[:, hh * 128:(hh + 1) * 128], True, True,
                            [(qkv, 8 + hh), Sbg], [(ps, BA)])
                yield
                nbe_b = R(4)[rw, hsl].unsqueeze(2).broadcast_to([nrw, 4, 128])
                P.dve(lambda e: e.tensor_tensor(rhs.t[rw, :, :], v4(ps.t[rw, BA, :]), nbe_b, ALU.mult), [(ps, BA), sm], [rhs])
                P.dve(lambda e: e.tensor_tensor(rhs.t[rw, :, :], rhs.t[rw, :, :], vb.t[rw, :, :], ALU.add), [rhs, vb], [rhs])
                yield
                for h in range(4):
                    self.mm(ps.t[:, BB, h * 128:(h + 1) * 128], PT.t[:, h, :], rhs.t[:, h, :], True, True, [PT, rhs], [(ps, BB)])
                yield
                P.act(lambda e: e.activation(ub.t[rw, :, :], v4(ps.t[rw, BB, :]), AF.Copy), [(ps, BB)], [ub])
                yield
                for h in range(4):
                    hh = 4 * g + h
                    self.mm(ps.t[:, BC, h * 128:(h + 1) * 128], qdT.t[:, h, :], Sb.t[:, hh * 128:(hh + 1) * 128], True, False, [qdT, Sbg], [(ps, BC)])
                    self.mm(ps.t[:, BC, h * 128:(h + 1) * 128], attnT.t[:, h, :], ub.t[:, h, :], False, True, [attnT, ub], [(ps, BC)])
                for h in range(4):
                    self.mm(ps.t[:, BA, h * 128:(h + 1) * 128], kdec.t[:, ch, h, :], ub.t[:, h, :], True, True, [kdec, ub], [(ps, BA)])
                yield
                P.act(lambda e: e.activation(osb.t[rs, g * 512:(g + 1) * 512], ps.t[rs, BC, :], AF.Copy), [(ps, BC)], [(osb, g)])
                el_b = sm.t[:, 8 + ch, hsl].unsqueeze(2).broadcast_to([128, 4, 128])
                P.dve(lambda e: e.tensor_tensor(v4(S.t[:, g * 512:(g + 1) * 512]), v4(S.t[:, g * 512:(g + 1) * 512]), el_b, ALU.mult), [Sg, sm], [Sg])
                P.dve(lambda e: e.tensor_tensor(S.t[:, g * 512:(g + 1) * 512], S.t[:, g * 512:(g + 1) * 512], ps.t[:, BA, :], ALU.add), [Sg, (ps, BA)], [Sg])
                yield
                P.act(lambda e: e.activation(Sb.t[:, g * 512:(g + 1) * 512], S.t[:, g * 512:(g + 1) * 512], AF.Copy), [Sg], [Sbg])
                yield

            for ch in range(2):
                yield from chunk(ch)
            if g == 1:
                post(m, 128, col0, osb)

        def post(m, rows, col0, o_buf):
            v8 = lambda ap: ap.rearrange("p (h d) -> p h d", h=8)
            P.dve(lambda e: e.tensor_tensor(f3.t[:rows, :], o_buf.t[:rows, :], o_buf.t[:rows, :], ALU.mult), [o_buf], [f3])
            P.dve(lambda e: e.tensor_reduce(nst.t[:rows, 0, :], v8(f3.t[:rows, :]), mybir.AxisListType.X, ALU.add), [f3], [nst])
            P.act(lambda e: e.activation(nst.t[:rows, 1, :], nst.t[:rows, 0, :], AF.Ln, scale=1.0 / 128.0, bias=float(NORM_EPS)), [nst], [nst])
            P.act(lambda e: e.activation(nst.t[:rows, 2, :], nst.t[:rows, 1, :], AF.Exp, scale=-0.5), [nst], [nst])
            P.dve(lambda e: e.tensor_tensor(v8(o_buf.t[:rows, :]), v8(o_buf.t[:rows, :]), nst.t[:rows, 2, :].unsqueeze(2).broadcast_to([rows, 8, 128]),
                                            ALU.mult), [o_buf, nst], [o_buf])
            P.dve(lambda e: e.tensor_tensor(v8(o_buf.t[:rows, :]), v8(o_buf.t[:rows, :]), nwb.t[:rows, :].unsqueeze(1).broadcast_to([rows, 8, 128]),
                                            ALU.mult), [o_buf, nwb], [o_buf])
            pz = 4
            for o in range(2):
                for k in range(8):
                    self.mm(ps.t[:rows, pz + o, :], xT.t[:, k, col0:col0 + rows], Wz[o].t[:, k, :], k == 0, k == 7, [(xT, m), Wz[o]], [(ps, pz + o)])
            pzv = ps.t[:rows, pz:pz + 2, :].rearrange("p b n -> p (b n)")
            self.sigmoid_chain(f3.t[:rows, :], pzv, [(ps, pz), (ps, pz + 1)], f3)
            P.dve(lambda e: e.tensor_tensor(f3.t[:rows, :], pzv, f3.t[:rows, :], ALU.mult), [(ps, pz), (ps, pz + 1), f3], [f3])
            P.dve(lambda e: e.tensor_tensor(ogb.t[:rows, :], o_buf.t[:rows, :], f3.t[:rows, :], ALU.mult), [o_buf, f3], [ogb])

            def evac(pst, bt):
                P.act(lambda e: e.activation(hT.t[:, 0:8, col0:col0 + rows], pst[:, :, :rows], AF.Copy), [(ps, bt)], [(hT, list(range(8)))])

            self.transposes_to(evac, lambda k: ogb.t[:rows, k * 128:(k + 1) * 128], 8, rows, ogb, None, bt=7)

        def sample_body(m, col0):
            R = lambda i: sm.t[0:16, i, :]
            gates(m, 16, col0)
            P.act(lambda e: e.activation(R(3), R(1), AF.Exp), [sm], [sm])
            P.dve(lambda e: e.scalar_tensor_tensor(R(4), R(0), -1.0, R(3), ALU.mult, ALU.mult), [sm], [sm])
            P.dve(lambda e: e.tensor_tensor(Bde.t[0:16, :, :], R(3).unsqueeze(1).broadcast_to([16, 16, 8]),
                                            self.eyep.t[:, :].unsqueeze(2).broadcast_to([16, 16, 8]), ALU.mult), [sm, self.eyep], [Bde])
            self.mm(ps.t[:, 6, 0:128], ONES[0:16, :], Bde.t[0:16, :, :].rearrange("p s h -> p (s h)"), True, True, [masks, Bde], [(ps, 6)])
            P.act(lambda e: e.activation(Eall.t[:, :, :].rearrange("p s h -> p (s h)"), ps.t[:, 6, 0:128], AF.Copy), [(ps, 6)], [Eall])
            for h in range(8):
                P.pe(lambda e, h=h: e.transpose(ps.t[0:16, 2 + h // 4, (h % 4) * 128:(h % 4 + 1) * 128], qkvS.t[:, 8 + h, :], self.identf.t[:, :]),
                     [qkvS, self.identf], [(ps, 2 + h // 4)])
            P.act(lambda e: e.activation(ktS.t[0:16, :], ps.t[0:16, 2:4, :].rearrange("p b n -> p (b n)"), AF.Copy), [(ps, 2), (ps, 3)], [ktS])
            for h in range(8):
                P.pe(lambda e, h=h: e.transpose(ps.t[0:16, 4 + h // 4, (h % 4) * 128:(h % 4 + 1) * 128], qkvS.t[:, 16 + h, :], self.identf.t[:, :]),
                     [qkvS, self.identf], [(ps, 4 + h // 4)])
            P.dve(lambda e: e.tensor_tensor(vbS.t[0:16, :].rearrange("p (h d) -> p h d", h=8),
                                            ps.t[0:16, 4:6, :].rearrange("p b (h d) -> p (b h) d", h=4),
                                            R(0).unsqueeze(2).broadcast_to([16, 8, 128]), ALU.mult), [(ps, 4), (ps, 5), sm], [vbS])
            Ss_b = [osb, SsB]
            P.dve(lambda e: e.memset(oacc.t[0:16, :], 0.0), [], [oacc])
            v8 = lambda ap: ap.rearrange("p (h d) -> p h d", h=8)

            def smp(s_):
                par = s_ % 2
                Ss, kTm, qTm, um = Ss_b[par], kTm_b[par], qTm_b[par], um_b[par]
                pb = 2 + 2 * par
                pbv = ps.t[:, pb:pb + 2, :].rearrange("p b n -> p (b n)")
                P.dma("sp", Ss.t[:, :].rearrange("p (h v) -> p h v", h=8), I["st_gdn"][l, s_].rearrange("h k v -> k h v"), [], [Ss], Ss.sem())
                P.dve(lambda e: e.tensor_tensor(kTm.t[:, :, :], qkvS.t[:, 8:16, :],
                                                self.eyef.t[:, s_, :].unsqueeze(1).broadcast_to([128, 8, 16]), ALU.mult), [qkvS, self.eyef], [kTm])
                P.dve(lambda e: e.tensor_tensor(qTm.t[:, :, :], qkvS.t[:, 0:8, :],
                                                self.eyef.t[:, s_, :].unsqueeze(1).broadcast_to([128, 8, 16]), ALU.mult), [qkvS, self.eyef], [qTm])
                yield
                for h in range(8):
                    self.mm(ps.t[0:16, pb + h // 4, (h % 4) * 128:(h % 4 + 1) * 128], kTm.t[:, h, :], Ss.t[:, h * 128:(h + 1) * 128], True, True,
                            [kTm, Ss], [(ps, pb + h // 4)])
                yield
                P.dve(lambda e: e.tensor_tensor(v8(um.t[0:16, :]), v8(pbv[0:16, :]), R(4).unsqueeze(2).broadcast_to([16, 8, 128]), ALU.mult),
                      [(ps, pb), (ps, pb + 1), sm], [um])
                P.dve(lambda e: e.scalar_tensor_tensor(um.t[0:16, :], vbS.t[0:16, :], self.eyep.t[0:16, s_:s_ + 1], um.t[0:16, :],
                                                       ALU.mult, ALU.add), [vbS, self.eyep, um], [um])
                yield
                for h in range(8):
                    self.mm(ps.t[:, pb + h // 4, (h % 4) * 128:(h % 4 + 1) * 128], ktS.t[0:16, h * 128:(h + 1) * 128], um.t[0:16, h * 128:(h + 1) * 128],
                            True, True, [ktS, um], [(ps, pb + h // 4)])
                yield
                P.dve(lambda e: e.tensor_tensor(v8(Ss.t[:, :]), v8(Ss.t[:, :]), Eall.t[:, s_, :].unsqueeze(2).broadcast_to([128, 8, 128]), ALU.mult),
                      [Ss, Eall], [Ss])
                P.dve(lambda e: e.tensor_tensor(Ss.t[:, :], Ss.t[:, :], pbv, ALU.add), [Ss, (ps, pb), (ps, pb + 1)], [Ss])
                yield
                P.dma("sp", self.o["gdn_s"][l, s_].rearrange("h k v -> k h v"), Ss.t[:, :].rearrange("p (h v) -> p h v", h=8), [Ss], [], Ss.sem(),
                      is_output=True)
                for h in range(8):
                    self.mm(ps.t[0:16, pb + h // 4, (h % 4) * 128:(h % 4 + 1) * 128], qTm.t[:, h, :], Ss.t[:, h * 128:(h + 1) * 128],
                            True, True, [qTm, Ss], [(ps, pb + h // 4)])
                yield
                P.dve(lambda e: e.tensor_tensor(oacc.t[0:16, :], oacc.t[0:16, :], pbv[0:16, :], ALU.add), [oacc, (ps, pb), (ps, pb + 1)], [oacc])

            self.run_pipelined((smp(s_) for s_ in range(NS)), 2)
            post(m, 16, col0, oacc)

        units = [(m, col0, g) for (m, rows, col0) in self.tiles() if rows == 128 for g in range(2)]
        self.run_pipelined([unit_gen(0, *units[0], "A")], 1)
        for u in range(len(units)):
            gens = [unit_gen(u, *units[u], "B")]
            if u + 1 < len(units):
                gens.append(unit_gen(u + 1, *units[u + 1], "A"))
            self.run_pipelined(gens, 2)
        for (m, rows, col0) in self.tiles():
            if rows != 128:
                sample_body(m, col0)
        self.state_store("gdn_p", l)

    def transposes_T(self, evac, src_fn, n, rows, src_buf, bt=None):
        P, ps = self.P, self.ps
        bt = self.bank() if bt is None else bt
        pst = ps.t[:, bt, :].bitcast(BF16).rearrange("p (k n) -> p k n", k=8)
        for k in range(n):
            P.pe(lambda e, k=k: e.transpose(pst[:rows, k, :], src_fn(k), self.identb.t[:, :]), [src_buf, self.identb], [(ps, bt)])
        evac(pst, bt)

    def token_mix(self, l):
        I = self.i
        if "ret" in self.cfg.mix:
            self.retention(l)
            self.finale(l, I["w_ret_out"][l], C_M1)
        if "ssd" in self.cfg.mix:
            self.ssd(l)
            self.finale(l, I["w_ssm_out"][l], C_M2)
        if "gdn" in self.cfg.mix:
            self.gdn(l)
            self.finale(l, I["w_gdn_out"][l], C_M3)

    def build(self):
        c = self.cfg
        self.pT = [self.P.sbuf(f"pT{i}", [128, 2, 128], BF16) for i in range(2)]
        self.alloc_mix()
        for half in range(c.NH):
            self.half = half
            self.has_sample = c.sample and half == c.NH - 1
            self.load_x()
            for l in range(c.layers):
                last = (l == c.layers - 1)
                if c.ffn:
                    self.ffn(l, 0)
                self.layer_norm(l, 0)
                self.token_mix(l)
                self.layer_norm(l, 1)
                if c.ffn:
                    self.ffn(l, 1)
                self.layer_norm(l, 2)
                if c.pegate:
                    self.pe_gate(l)
                self.layer_norm(l, 3, final=last)
        return self.P.finish()


def make_consts(T):
    c = {}
    c["c_ident"] = np.eye(128, dtype=np.float32)
    half = 64
    inv = (np.float32(10000.0) ** (-np.arange(half, dtype=np.float32) / np.float32(half))).astype(np.float32)
    pos = np.concatenate([np.arange(T, dtype=np.float32), np.full(NS, PAST_LEN, np.float32)])
    ang = (pos[:, None] * inv[None, :]).astype(np.float32).astype(np.float64)
    rope = np.zeros((T + NS, 4, 64), np.float32)
    rope[:, 0] = np.cos(ang)
    rope[:, 1] = np.sin(ang)
    rope[:, 2] = np.cos(ang) * 128 ** -0.5
    rope[:, 3] = np.sin(ang) * 128 ** -0.5
    c["c_rope"] = rope
    gam = 1.0 - 2.0 ** (-5.0 - np.arange(4))
    lg = np.log1p(-(2.0 ** (-5.0 - np.arange(4)))).astype(np.float32).astype(np.float64)
    i = np.arange(128)
    dm = i[None, :] - i[:, None]
    mask = np.zeros((4, 128, 128), np.float32)
    row = np.zeros((4, 3, 128), np.float32)
    for h in range(4):
        mask[h] = np.where(dm >= 0, np.exp(lg[h] * np.maximum(dm, 0)), 0.0)
        row[h, 0] = np.exp(lg[h] * (i + 1))
        row[h, 1] = np.exp(lg[h] * (127 - i))
        row[h, 2, 0] = np.exp(lg[h] * 128)
        row[h, 2, 1] = np.exp(lg[h])
    c["c_retmask"] = mask
    c["c_retrow"] = row
    mk = np.zeros((8, 128, 128), np.float32)
    jj, ii = np.meshgrid(np.arange(128), np.arange(128), indexing="ij")
    blk = (jj // 64) == (ii // 64)
    mk[0] = (jj <= ii)
    mk[1] = 1.0
    mk[2] = np.where(jj <= ii, 0.0, -30000.0)
    mk[3] = (jj <= ii) & blk
    mk[4] = np.where((jj <= ii) & blk, 0.0, -30000.0)
    mk[5] = np.where((jj < ii) & blk, 0.0, -30000.0)
    mk[6] = blk
    mk[7] = (jj < 64)
    c["c_masks"] = mk
    return c


_CACHE = {}


def kernel(**inputs):
    NC = 8
    cfg = Cfg(NH=4, NTH=4, sample=True, layers=2)
    mk = MK(cfg)
    mk.build()
    T = cfg.T
    consts = make_consts(T)
    f = lambda a: np.ascontiguousarray(np.asarray(a, dtype=np.float32))
    wnames = ["ln_g", "ln_b", "ffn_wg", "ffn_wu", "ffn_wd", "w_in", "ssm_conv_w", "ssm_conv_b", "ssm_dt_bias",
              "ssm_a_log", "ssm_d", "ssm_norm_w", "gdn_conv_w", "gdn_dt_bias", "gdn_a_log", "gdn_norm_w",
              "w_ret_out", "w_ssm_out", "w_gdn_out", "w_o", "pe_proj", "pe_gate"]
    W = {k: f(inputs[k]) for k in wnames}
    xp, xs = np.asarray(inputs["x_prompt"]), np.asarray(inputs["x_sample"])
    pp, ps_ = np.asarray(inputs["p_prompt"]), np.asarray(inputs["p_sample"])
    in_maps = []
    for c in range(NC):
        sl = slice(NS * c, NS * (c + 1))
        m = dict(W)
        m.update(consts)
        m["xp"] = f(xp[c])
        m["pp"] = f(pp[:, c])
        m["xs"] = f(xs[sl, 0])
        m["ps"] = f(ps_[:, sl, 0])
        m["st_ret"] = f(np.asarray(inputs["state_ret"])[:, sl])
        m["st_ssm"] = f(np.asarray(inputs["state_ssm"])[:, sl])
        m["st_ssm_conv"] = f(np.asarray(inputs["state_ssm_conv"])[:, sl])
        m["st_gdn"] = f(np.asarray(inputs["state_gdn"])[:, sl])
        m["st_gdn_conv"] = f(np.asarray(inputs["state_gdn_conv"])[:, sl])
        in_maps.append({k: v for k, v in m.items() if k in mk.i})
    res = run_bass_kernel_spmd(mk.nc, in_maps, core_ids=list(range(NC)))
    R = res.results
    cat0 = lambda k: np.stack([R[c][k] for c in range(NC)], axis=0)
    y_p = cat0("y_p")
    y_s = np.concatenate([R[c]["y_s"] for c in range(NC)], 0)[:, None, :]
    outs = [y_p, y_s]
    for k in ("ret_p", "ssm_p", "ssm_conv_p", "gdn_p", "gdn_conv_p"):
        outs.append(np.stack([R[c][k] for c in range(NC)], axis=1))
    for k in ("ret_s", "ssm_s", "ssm_conv_s", "gdn_s", "gdn_conv_s"):
        outs.append(np.concatenate([R[c][k] for c in range(NC)], axis=1))
    return tuple(np.ascontiguousarray(o, dtype=np.float32) for o in outs)
```

```python
import numpy as np
from contextlib import ExitStack
import concourse.bass as bass
import concourse.mybir as mybir
from concourse.bass_utils import run_bass_kernel_spmd

F32, BF16 = mybir.dt.float32, mybir.dt.bfloat16
AF = mybir.ActivationFunctionType
ALU = mybir.AluOpType

D = 1024
DEPTH = 2
FFN = 2048
PLE = 256
IN_DIM = 12832
DN_ALPHA = (2 * DEPTH) ** 0.25
LN_EPS = 1e-5
NORM_EPS = 1e-6
PAST_LEN = 16384
NS = 16

C_RQ, C_RK, C_RV, C_RG = 0, 512, 1024, 2048
C_MZ, C_MXBC, C_MDT = 3072, 4096, 5632
C_GQKV, C_GZ, C_GA, C_GB = 5648, 8720, 9744, 9752
C_M1, C_M2, C_M3 = 9760, 10784, 11808


class Sem:
    def __init__(self, name):
        self.name = name
        self.count = 0
        self.h = None


class Reg:
    __slots__ = ("writers", "readers")

    def __init__(self):
        self.writers = {}
        self.readers = {}


class Buf:
    def __init__(self, prog, name, t, nreg=1):
        self.prog, self.name, self.t = prog, name, t
        self.regs = [[Reg()] for _ in range(nreg)]
        self.dsem = None

    def sem(self):
        if self.dsem is None:
            self.dsem = self.prog.named_sem("d_" + getattr(self, "sem_name", self.name))
        return self.dsem

    def __getitem__(self, k):
        return self.t[k]


class Op:
    __slots__ = ("eng", "fn", "deps", "needed", "is_dma", "sem", "val", "waits", "dmawaits")

    def __init__(self, eng, fn, is_dma):
        self.eng, self.fn, self.is_dma = eng, fn, is_dma
        self.deps = []
        self.dmawaits = []
        self.needed = False
        self.sem = None
        self.val = 0


def _regs(spec):
    out = []
    for s in spec:
        if isinstance(s, Buf):
            for g in s.regs:
                out.extend(g)
        else:
            b, idx = s
            if isinstance(idx, int):
                out.extend(b.regs[idx])
            else:
                for i in idx:
                    out.extend(b.regs[i])
    return out


class Arena:
    GRAN = 256

    def __init__(self, prog, name, nbytes):
        self.prog = prog
        self.nbytes = nbytes
        self.base = prog.sbuf(name, [128, nbytes // 2], BF16)
        self.gr = [Reg() for _ in range(nbytes // self.GRAN)]
        self.off = 0
        self.n = 0

    def reset(self, off=0):
        self.off = off

    def alloc(self, name, free_shape, dt, nreg=1):
        esz = 2 if dt == BF16 else 4
        nel = int(np.prod(free_shape))
        nb = nel * esz
        nb_al = (nb + self.GRAN - 1) // self.GRAN * self.GRAN
        assert self.off + nb_al <= self.nbytes, f"arena overflow allocating {name}: {self.off}+{nb_al}>{self.nbytes}"
        o2 = self.off // 2
        v = self.base.t[:, o2:o2 + nb // 2]
        if dt != BF16:
            v = v.bitcast(dt)
        if len(free_shape) > 1:
            names = " ".join(f"d{i}" for i in range(len(free_shape)))
            kw = {f"d{i}": int(free_shape[i]) for i in range(len(free_shape))}
            v = v.rearrange(f"p ({names}) -> p {names}", **kw)
        self.n += 1
        b = Buf(self.prog, f"{name}_{self.n}", v, 1)
        b.sem_name = f"a_{name}_{self.off}"
        g0 = self.off // self.GRAN
        ng = nb_al // self.GRAN
        grs = self.gr[g0:g0 + ng]
        if nreg == 1:
            b.regs = [grs]
        else:
            assert ng % nreg == 0, (name, ng, nreg)
            k = ng // nreg
            b.regs = [grs[i * k:(i + 1) * k] for i in range(nreg)]
        self.off += nb_al
        return b


class Prog:
    ENGS = ("pe", "act", "dve", "pool", "sp")
    ATTR = {"pe": "tensor", "act": "scalar", "dve": "vector", "pool": "gpsimd", "sp": "sync"}

    def __init__(self, nc):
        self.nc = nc
        self.es = ExitStack()
        self.ops = {e: [] for e in self.ENGS}
        self.sems = []
        self.esem = {e: self.new_sem("e_" + e) for e in ("pe", "act", "dve", "pool")}
        self.nbuf = 0
        self.out_sems = set()

    def new_sem(self, name):
        s = Sem(name)
        self.sems.append(s)
        return s

    def named_sem(self, name):
        d = self.__dict__.setdefault("_named", {})
        if name not in d:
            d[name] = self.new_sem(name)
        return d[name]

    def sbuf(self, name, shape, dt, nreg=1):
        nb = int(np.prod(shape[1:])) * (2 if dt == BF16 else 4)
        self.sb_bytes = getattr(self, "sb_bytes", 0) + nb
        self.sb_log = getattr(self, "sb_log", []) + [(name, nb)]
        t = self.es.enter_context(self.nc.sbuf_tensor("s_" + name, list(shape), dt))
        return Buf(self, name, t, nreg)

    def psum(self, name, shape, dt, nreg=1):
        t = self.es.enter_context(self.nc.psum_tensor("p_" + name, list(shape), dt))
        return Buf(self, name, t, nreg)

    def dram(self, name, shape, dt, kind, nreg=1):
        t = self.nc.dram_tensor(name, list(shape), dt, kind=kind)
        return Buf(self, name, t.ap(), nreg)

    def _dep(self, c, p):
        if p is None or p is c:
            return
        if p.is_dma:
            c.dmawaits.append((p.sem, p.sem.count))
            return
        p.needed = True
        c.deps.append(p)

    def op(self, eng, fn, reads=(), writes=(), dma_sem=None):
        is_dma = dma_sem is not None
        o = Op(eng, fn, is_dma)
        st = self.__dict__.setdefault("tagstat", {})
        key = (getattr(self, "tag", "-"), eng)
        st[key] = st.get(key, 0) + 1
        rr, ww = _regs(reads), _regs(writes)
        for r in rr:
            for e, p in r.writers.items():
                if (not is_dma) and (not p.is_dma) and e == eng and eng == "pe":
                    continue
                self._dep(o, p)
        for r in ww:
            for e, p in r.readers.items():
                if (not is_dma) and (not p.is_dma) and e == eng and eng == "pe":
                    continue
                self._dep(o, p)
            for e, p in r.writers.items():
                if (not is_dma) and (not p.is_dma) and e == eng and eng == "pe":
                    continue
                if is_dma and p.is_dma and p.sem is dma_sem:
                    continue
                self._dep(o, p)
        key = ("dma", id(o)) if is_dma else eng
        if is_dma:
            dma_sem.count += 16
            o.sem, o.val = dma_sem, dma_sem.count
        for r in rr:
            r.readers[key] = o
        for r in ww:
            r.writers = {key: o}
            r.readers = {}
        self.ops[eng].append(o)
        return o

    def pe(self, fn, reads, writes):
        return self.op("pe", fn, reads, writes)

    def act(self, fn, reads, writes):
        return self.op("act", fn, reads, writes)

    def dve(self, fn, reads, writes):
        return self.op("dve", fn, reads, writes)

    def pool(self, fn, reads, writes):
        return self.op("pool", fn, reads, writes)

    def dma(self, q, out, in_, reads, writes, sem, is_output=False, **kw):
        if is_output:
            self.out_sems.add(sem)
        return self.op(q, lambda e: e.dma_start(out=out, in_=in_, **kw), reads, writes, dma_sem=sem)

    def finish(self):
        nc = self.nc
        fin = Op("sp", None, False)
        for s in self.sems:
            if s.name.startswith("d_") and s.count > 0:
                fin.dmawaits.append((s, s.count))
        self.ops["sp"].append(fin)
        for e in ("pe", "act", "dve", "pool"):
            n = 0
            for o in self.ops[e]:
                if o.is_dma:
                    continue
                if o.needed:
                    n += 1
                    o.sem, o.val = self.esem[e], n
            self.esem[e].count = n
        for s in self.sems:
            if s.count > 0:
                s.h = self.es.enter_context(nc.semaphore(s.name))
        block = self.es.enter_context(nc.Block())
        stats = {}
        for e in self.ENGS:
            ops = self.ops[e]

            def body(eng, ops=ops, e=e):
                seen = {}
                nw = 0
                for o in ops:
                    ws = {}
                    for p in o.deps:
                        ws[p.sem] = max(ws.get(p.sem, 0), p.val)
                    for s, v in o.dmawaits:
                        ws[s] = max(ws.get(s, 0), v)
                    for s, v in ws.items():
                        if seen.get(s, 0) >= v:
                            continue
                        seen[s] = v
                        eng.wait_ge(s.h, v)
                        nw += 1
                    if o.fn is None:
                        continue
                    ins = o.fn(eng)
                    if o.is_dma:
                        ins.then_inc(o.sem.h, 16)
                    elif o.needed:
                        ins.then_inc(o.sem.h, 1)
                stats[e] = (len(ops), nw)

            getattr(block, self.ATTR[e])(body)
        self.es.close()
        return stats


class Cfg:
    def __init__(self, NH=2, NTH=8, sample=True, layers=2, mix=("ret", "ssd", "gdn"), pegate=True, ffn=True):
        self.NH, self.NTH, self.sample, self.layers = NH, NTH, sample, layers
        self.mix, self.pegate, self.ffn = mix, pegate, ffn
        self.dbg = {}
        self.T = NH * NTH * 128


class MK:
    def __init__(self, cfg):
        self.cfg = cfg
        nc = bass.Bass("TRN2", target_bir_lowering=False)
        self.nc = nc
        self.P = Prog(nc)
        self.declare_io()
        self.alloc()

    def din(self, name, shape):
        return self.nc.dram_tensor(name, list(shape), F32, kind="ExternalInput").ap()

    def dout(self, name, shape):
        return self.nc.dram_tensor(name, list(shape), F32, kind="ExternalOutput").ap()

    def declare_io(self):
        c = self.cfg
        T = c.T
        L = DEPTH
        self.i = {}
        I = self.i
        I["xp"] = self.din("xp", [T, D])
        I["pp"] = self.din("pp", [L, T, PLE])
        if c.sample:
            I["xs"] = self.din("xs", [NS, D])
            I["ps"] = self.din("ps", [L, NS, PLE])
            I["st_ret"] = self.din("st_ret", [L, NS, 4, 128, 256])
            I["st_ssm"] = self.din("st_ssm", [L, NS, 16, 128, 64])
            I["st_ssm_conv"] = self.din("st_ssm_conv", [L, NS, 3, 1536])
            I["st_gdn"] = self.din("st_gdn", [L, NS, 8, 128, 128])
            I["st_gdn_conv"] = self.din("st_gdn_conv", [L, NS, 3, 3072])
        for nm, shp in [("ln_g", [L, 4, D]), ("ln_b", [L, 4, D]), ("ffn_wg", [L, 2, D, FFN]),
                        ("ffn_wu", [L, 2, D, FFN]), ("ffn_wd", [L, 2, FFN, D]), ("w_in", [L, D, IN_DIM]),
                        ("ssm_conv_w", [L, 4, 1536]), ("ssm_conv_b", [L, 1536]), ("ssm_dt_bias", [L, 16]),
                        ("ssm_a_log", [L, 16]), ("ssm_d", [L, 16]), ("ssm_norm_w", [L, 1024]),
                        ("gdn_conv_w", [L, 4, 3072]), ("gdn_dt_bias", [L, 8]), ("gdn_a_log", [L, 8]),
                        ("gdn_norm_w", [L, 128]), ("w_ret_out", [L, D, D]), ("w_ssm_out", [L, D, D]),
                        ("w_gdn_out", [L, D, D]), ("w_o", [L, D, D]), ("pe_proj", [L, PLE, D]),
                        ("pe_gate", [L, D, D])]:
            I[nm] = self.din(nm, shp)
        I["c_ident"] = self.din("c_ident", [128, 128])
        I["c_rope"] = self.din("c_rope", [T + NS, 4, 64])
        I["c_retmask"] = self.din("c_retmask", [4, 128, 128])
        I["c_retrow"] = self.din("c_retrow", [4, 3, 128])
        I["c_masks"] = self.din("c_masks", [8, 128, 128])
        self.o = {}
        O = self.o
        O["y_p"] = self.dout("y_p", [T, D])
        O["ret_p"] = self.dout("ret_p", [L, 4, 128, 256])
        O["ssm_p"] = self.dout("ssm_p", [L, 16, 128, 64])
        O["ssm_conv_p"] = self.dout("ssm_conv_p", [L, 3, 1536])
        O["gdn_p"] = self.dout("gdn_p", [L, 8, 128, 128])
        O["gdn_conv_p"] = self.dout("gdn_conv_p", [L, 3, 3072])
        if c.sample:
            O["y_s"] = self.dout("y_s", [NS, D])
            O["ret_s"] = self.dout("ret_s", [L, NS, 4, 128, 256])
            O["ssm_s"] = self.dout("ssm_s", [L, NS, 16, 128, 64])
            O["ssm_conv_s"] = self.dout("ssm_conv_s", [L, NS, 3, 1536])
            O["gdn_s"] = self.dout("gdn_s", [L, NS, 8, 128, 128])
            O["gdn_conv_s"] = self.dout("gdn_conv_s", [L, NS, 3, 3072])

    def alloc(self):
        c, P = self.cfg, self.P
        self.NTT = c.NTH + (1 if c.sample else 0)
        self.TS = c.NTH * 128 + (NS if c.sample else 0)
        NTT, TS = self.NTT, self.TS
        self.xa = P.sbuf("xa", [128, NTT, D], F32, nreg=NTT * 2)
        self.xT = P.sbuf("xT", [128, 8, TS], BF16, nreg=NTT)
        self.hT = P.sbuf("hT", [128, 16, TS], BF16, nreg=16)
        self.NSLOT = 6
        self.wslots = [P.sbuf(f"w{i}", [128, 8, 512], BF16) for i in range(self.NSLOT)]
        self.wi = 0
        self.lnp = P.sbuf("lnp", [128, 2, D], F32)
        self.identb = P.sbuf("identb", [128, 128], BF16)
        self.identf = P.sbuf("identf", [128, 128], F32)
        self.tmpA = [P.sbuf(f"tmpA{i}", [128, D], F32) for i in range(2)]
        self.tmpB = [P.sbuf(f"tmpB{i}", [128, D], F32) for i in range(2)]
        self.xb = [P.sbuf(f"xb{i}", [128, D], BF16) for i in range(1)]
        self.lnst = [P.sbuf(f"lnst{i}", [128, 2, 6], F32) for i in range(2)]
        self.lnmv = [P.sbuf(f"lnmv{i}", [128, 4], F32) for i in range(2)]
        self.ps = P.psum("ps", [128, 8, 512], F32, nreg=8)
        self.psi = 0
        self.rr = {}
        P.dma("pool", self.identb.t[:], self.i["c_ident"], [], [self.identb], self.identb.sem())
        P.dma("sp", self.identf.t[:], self.i["c_ident"], [], [self.identf], self.identf.sem())

    def rot(self, key, lst):
        i = self.rr.get(key, 0)
        self.rr[key] = i + 1
        return lst[i % len(lst)]

    def bank(self):
        b = 2 + self.psi
        self.psi = (self.psi + 1) % 6
        return b

    def bank2(self):
        if self.psi % 2:
            self.psi = (self.psi + 1) % 6
        b = 2 + self.psi
        self.psi = (self.psi + 2) % 6
        return b

    def wslot(self):
        s = self.wslots[self.wi % self.NSLOT]
        self.wi += 1
        return s

    def wload(self, slot, c0, src2d, K=1024):
        ncols = src2d.shape[1]
        kc = K // 128
        self.P.dma("pool", slot.t[:, 0:kc, c0:c0 + ncols], src2d.rearrange("(k p) c -> p k c", p=128),
                   [], [slot], slot.sem())

    def tiles(self):
        c = self.cfg
        out = [(m, 128, m * 128) for m in range(c.NTH)]
        if self.has_sample:
            out.append((c.NTH, NS, c.NTH * 128))
        return out

    def nblocks(self):
        c = self.cfg
        TP = c.NTH * 128
        out = [(n0, min(512, TP - n0)) for n0 in range(0, TP, 512)]
        if self.has_sample:
            out.append((TP, NS))
        return out

    def xT_regs(self, n0, nsz):
        return (self.xT, list(range(n0 // 128, (n0 + nsz - 1) // 128 + 1)))

    @staticmethod
    def run_pipelined(gens, depth):
        it = iter(gens)
        active = []
        done = False
        while True:
            if not done and len(active) < depth:
                try:
                    active.append(next(it))
                except StopIteration:
                    done = True
            if not active:
                if done:
                    break
                continue
            for g in list(active):
                try:
                    next(g)
                except StopIteration:
                    active.remove(g)

    def mm(self, out, lhsT, rhs, start, stop, reads, writes):
        n = int(np.prod(rhs.shape[1:]))
        cyc = max(64, n) * (4 if rhs.dtype == F32 else 1)
        pc = self.P.__dict__.setdefault("pecost", {})
        t = getattr(self.P, "tag", "-")
        pc[t] = pc.get(t, 0) + cyc
        self.P.pe(lambda e: e.matmul(out, lhsT, rhs, start=start, stop=stop), reads, writes)

    def emit_xT(self, m, rows, col0, src, final_out=None):
        P = self.P
        xa, xT, ps = self.xa, self.xT, self.ps
        xb = self.rot("xb", self.xb)
        P.act(lambda e: e.activation(xa.t[:rows, m, :], src.t[:rows, :], AF.Copy, scale=float(DN_ALPHA)),
              [src], [(xa, [2 * m, 2 * m + 1])])
        P.dve(lambda e: e.tensor_copy(xb.t[:rows, :], src.t[:rows, :]), [src], [xb])
        b = self.bank()
        pst = ps.t[:, b, :].bitcast(BF16).rearrange("p (k n) -> p k n", k=8)
        for k in range(8):
            P.pe(lambda e, k=k: e.transpose(pst[:, k, :rows], xb.t[:rows, k * 128:(k + 1) * 128],
                                            self.identb.t[:rows, :rows]),
                 [xb, self.identb], [(ps, b)])
        P.act(lambda e: e.activation(xT.t[:, :, col0:col0 + rows], pst[:, :, :rows], AF.Copy),
              [(ps, b)], [(xT, m)])

    def layer_norm(self, l, idx, final=False):
        self.P.tag = "ln"
        P = self.P
        I = self.i
        lnp = self.lnp
        P.dma("sp", lnp.t[:, 0, :], I["ln_g"][l, idx, :].partition_broadcast(128), [], [lnp], lnp.sem())
        P.dma("sp", lnp.t[:, 1, :], I["ln_b"][l, idx, :].partition_broadcast(128), [], [lnp], lnp.sem())
        xa = self.xa

        def tile_gen(i, m, rows, col0):
            par = i % 2
            st, mv, tA, tB = self.lnst[par], self.lnmv[par], self.tmpA[par], self.tmpB[par]
            xr = (xa, [2 * m, 2 * m + 1])
            P.dve(lambda e: e.bn_stats(st.t[:rows, 0, :], xa.t[:rows, m, 0:512]), [xr], [st])
            P.dve(lambda e: e.bn_stats(st.t[:rows, 1, :], xa.t[:rows, m, 512:1024]), [xr], [st])
            P.dve(lambda e: e.bn_aggr(mv.t[:rows, 0:2], st.t[:rows, :, :]), [st], [mv])
            yield
            P.act(lambda e: e.activation(mv.t[:rows, 2:3], mv.t[:rows, 1:2], AF.Ln, bias=float(LN_EPS)), [mv], [mv])
            P.act(lambda e: e.activation(mv.t[:rows, 3:4], mv.t[:rows, 2:3], AF.Exp, scale=-0.5), [mv], [mv])
            yield
            P.dve(lambda e: e.tensor_scalar(tA.t[:rows, :], xa.t[:rows, m, :], mv.t[:rows, 0:1], mv.t[:rows, 3:4],
                                            ALU.subtract, ALU.mult), [xr, mv], [tA])
            P.dve(lambda e: e.tensor_tensor(tA.t[:rows, :], tA.t[:rows, :], lnp.t[:rows, 0, :], ALU.mult), [tA, lnp], [tA])
            P.dve(lambda e: e.tensor_tensor(tB.t[:rows, :], tA.t[:rows, :], lnp.t[:rows, 1, :], ALU.add), [tA, lnp], [tB])
            yield
            if final:
                self.store_y(m, rows, tB)
            else:
                self.emit_xT(m, rows, col0, tB)

        self.run_pipelined((tile_gen(i, m, rows, col0) for i, (m, rows, col0) in enumerate(self.tiles())), 2)

    def store_y(self, m, rows, src):
        c = self.cfg
        if rows == 128:
            t0 = (self.half * c.NTH + m) * 128
            dst = self.o["y_p"][t0:t0 + 128, :]
        else:
            dst = self.o["y_s"][:, :]
        self.P.dma("sp", dst, src.t[:rows, :], [src], [], src.sem(), is_output=True)

    def load_x(self):
        c = self.cfg
        for (m, rows, col0) in self.tiles():
            tB = self.rot("tmpB", self.tmpB)
            if rows == 128:
                t0 = (self.half * c.NTH + m) * 128
                src = self.i["xp"][t0:t0 + 128, :]
            else:
                src = self.i["xs"][:, :]
            self.P.dma("sp", tB.t[:rows, :], src, [], [tB], tB.sem())
            self.emit_xT(m, rows, col0, tB)

    def ffn(self, l, idx):
        self.P.tag = "ffn"
        P, I = self.P, self.i
        ps, xT, hT, xa = self.ps, self.xT, self.hT, self.xa
        wg, wu, wd = I["ffn_wg"][l, idx], I["ffn_wu"][l, idx], I["ffn_wd"][l, idx]
        slots = []

        def loadA(hb):
            s = self.wslot()
            self.wload(s, 0, wg[:, hb * 256:(hb + 1) * 256])
            self.wload(s, 256, wu[:, hb * 256:(hb + 1) * 256])
            return s

        nxt = loadA(0)
        for hb in range(8):
            cur = nxt
            if hb + 1 < 8:
                nxt = loadA(hb + 1)
            for jj in range(2):
                j = hb * 2 + jj
                for (n0, nsz) in self.nblocks():
                    bg, bu = self.bank(), self.bank()
                    xr = self.xT_regs(n0, nsz)
                    for k in range(8):
                        self.mm(ps.t[:, bg, :nsz], cur.t[:, k, jj * 128:(jj + 1) * 128], xT.t[:, k, n0:n0 + nsz],
                                k == 0, k == 7, [cur, xr], [(ps, bg)])
                    for k in range(8):
                        self.mm(ps.t[:, bu, :nsz], cur.t[:, k, 256 + jj * 128:256 + (jj + 1) * 128],
                                xT.t[:, k, n0:n0 + nsz], k == 0, k == 7, [cur, xr], [(ps, bu)])
                    tA = self.rot("tmpA", self.tmpA)
                    P.act(lambda e, bg=bg, nsz=nsz, tA=tA: e.activation(tA.t[:, :nsz], ps.t[:, bg, :nsz], AF.Silu),
                          [(ps, bg)], [tA])
                    P.dve(lambda e, bu=bu, nsz=nsz, n0=n0, j=j, tA=tA: e.tensor_tensor(
                        hT.t[:, j, n0:n0 + nsz], ps.t[:, bu, :nsz], tA.t[:, :nsz], ALU.mult),
                        [(ps, bu), tA], [(hT, j)])
        def loadB(o):
            s0, s1 = self.wslot(), self.wslot()
            self.P.dma("pool", s0.t[:, :, :], wd[0:1024, o * 512:(o + 1) * 512].rearrange("(k p) c -> p k c", p=128),
                       [], [s0], s0.sem())
            self.P.dma("pool", s1.t[:, :, :], wd[1024:2048, o * 512:(o + 1) * 512].rearrange("(k p) c -> p k c", p=128),
                       [], [s1], s1.sem())
            return (s0, s1)

        nxt = loadB(0)
        for o in range(2):
            cur = nxt
            if o == 0:
                nxt = loadB(1)
            for (m, rows, col0) in self.tiles():
                b = self.bank()
                for j in range(16):
                    s = cur[j // 8]
                    self.mm(ps.t[:rows, b, :], hT.t[:, j, col0:col0 + rows], s.t[:, j % 8, :], j == 0, j == 15,
                            [(hT, j), s], [(ps, b)])
                P.dve(lambda e, m=m, rows=rows, b=b, o=o: e.scalar_tensor_tensor(
                    xa.t[:rows, m, o * 512:(o + 1) * 512], ps.t[:rows, b, :], 0.5,
                    xa.t[:rows, m, o * 512:(o + 1) * 512], ALU.mult, ALU.add),
                    [(ps, b), (xa, 2 * m + o)], [(xa, 2 * m + o)])

    def pe_gate(self, l):
        self.P.tag = "pegate"
        P, I = self.P, self.i
        c = self.cfg
        ps, xT, xa = self.ps, self.xT, self.xa
        sg0, sg1, sp = self.wslot(), self.wslot(), self.wslot()
        self.wload(sg0, 0, I["pe_gate"][l][:, 0:512])
        self.wload(sg1, 0, I["pe_gate"][l][:, 512:1024])
        for o in range(2):
            P.dma("pool", sp.t[:, 2 * o:2 * o + 2, :],
                  I["pe_proj"][l][:, o * 512:(o + 1) * 512].rearrange("(k p) c -> p k c", p=128), [], [sp], sp.sem())
        sg = (sg0, sg1)

        def tile_body(m, rows, col0):
            pb = self.rot("xb", self.xb)
            if rows == 128:
                t0 = (self.half * c.NTH + m) * 128
                src = I["pp"][l, t0:t0 + 128, :]
            else:
                src = I["ps"][l, :, :]
            P.dma("pool", pb.t[:rows, 0:PLE], src, [], [pb], pb.sem())
            b = self.bank()
            pst = ps.t[:, b, :].bitcast(BF16).rearrange("p (k n) -> p k n", k=8)
            for k in range(2):
                P.pe(lambda e, k=k, rows=rows, pb=pb: e.transpose(pst[:, k, :rows], pb.t[:rows, k * 128:(k + 1) * 128],
                                                                self.identb.t[:rows, :rows]),
                     [pb, self.identb], [(ps, b)])
            pT = self.rot("pT", self.pT)
            P.dve(lambda e, rows=rows, pT=pT: e.tensor_copy(pT.t[:, :, :rows], pst[:, 0:2, :rows]), [(ps, b)], [pT])
            tA = self.rot("tmpA", self.tmpA)
            for o in range(2):
                bg, bp = self.bank(), self.bank()
                for k in range(8):
                    self.mm(ps.t[:rows, bg, :], xT.t[:, k, col0:col0 + rows], sg[o].t[:, k, :], k == 0, k == 7,
                            [(xT, m), sg[o]], [(ps, bg)])
                for k in range(2):
                    self.mm(ps.t[:rows, bp, :], pT.t[:, k, :rows], sp.t[:, 2 * o + k, :], k == 0, k == 1,
                            [pT, sp], [(ps, bp)])
                P.act(lambda e, rows=rows, bg=bg, o=o, tA=tA: e.activation(
                    tA.t[:rows, o * 512:(o + 1) * 512], ps.t[:rows, bg, :], AF.Sigmoid), [(ps, bg)], [tA])
                P.dve(lambda e, rows=rows, bp=bp, o=o, tA=tA: e.tensor_tensor(
                    tA.t[:rows, o * 512:(o + 1) * 512], ps.t[:rows, bp, :], tA.t[:rows, o * 512:(o + 1) * 512], ALU.mult),
                    [(ps, bp), tA], [tA])
                P.dve(lambda e, m=m, rows=rows, o=o, tA=tA: e.tensor_tensor(
                    xa.t[:rows, m, o * 512:(o + 1) * 512], xa.t[:rows, m, o * 512:(o + 1) * 512],
                    tA.t[:rows, o * 512:(o + 1) * 512], ALU.add),
                    [tA, (xa, 2 * m + o)], [(xa, 2 * m + o)])

        for (m, rows, col0) in self.tiles():
            tile_body(m, rows, col0)


    def alloc_mix(self):
        P, I = self.P, self.i
        c = self.cfg
        self.S = P.sbuf("S", [128, 1024], F32, nreg=16)
        self.Sb = P.sbuf("Sb", [128, 1024], BF16, nreg=16)
        self.cst = P.sbuf("cst", [128, 8], F32)
        P.dve(lambda e: e.memset(self.cst.t[:, 0:1], float(LN_EPS)), [], [self.cst])
        P.dve(lambda e: e.memset(self.cst.t[:, 1:2], float(NORM_EPS)), [], [self.cst])
        P.dve(lambda e: e.memset(self.cst.t[:, 2:3], -0.5), [], [self.cst])
        P.dve(lambda e: e.memset(self.cst.t[:, 3:4], 1.0), [], [self.cst])
        P.dve(lambda e: e.memset(self.cst.t[:, 4:5], 1.0 / 512.0), [], [self.cst])
        P.dve(lambda e: e.memset(self.cst.t[:, 5:6], 1.0 / 128.0), [], [self.cst])
        self.eyeb = P.sbuf("eyeb", [128, 16, 16], BF16)
        self.eyep = P.sbuf("eyep", [16, 16], F32)
        self.eyef = P.sbuf("eyef", [128, 16, 16], F32)
        P.dma("sp", self.eyef.t[:], I["c_ident"][0:16, 0:16].unsqueeze(0).broadcast_to([128, 16, 16]), [], [self.eyef], self.eyef.sem())
        P.dma("pool", self.eyeb.t[:], I["c_ident"][0:16, 0:16].unsqueeze(0).broadcast_to([128, 16, 16]),
              [], [self.eyeb], self.eyeb.sem())
        P.dma("sp", self.eyep.t[:], I["c_ident"][0:16, 0:16], [], [self.eyep], self.eyep.sem())
        self.masks = P.sbuf("masks", [128, 8, 128], F32)
        P.dma("sp", self.masks.t[:], I["c_masks"].rearrange("m j i -> j m i"), [], [self.masks], self.masks.sem())
        self.A = Arena(P, "arena", 74 * 1024)
        self.tail_ssm = [P.sbuf(f"tail_ssm{l}", [128, 12, 3], F32, nreg=12) for l in range(DEPTH)]
        self.tail_gdn = [P.sbuf(f"tail_gdn{l}", [128, 24, 3], F32, nreg=24) for l in range(DEPTH)]
        L = DEPTH
        self.d_state = {}
        for nm in ("ret_p", "ssm_p", "gdn_p"):
            self.d_state[nm] = Buf(P, "dst_" + nm, self.o[nm], nreg=L)
        self.state_view = {
            "ret_p": lambda ap: ap.rearrange("h k v -> k h v"),
            "ssm_p": lambda ap: ap.rearrange("h n d -> n h d"),
            "gdn_p": lambda ap: ap.rearrange("h k v -> k h v"),
        }

    def state_load(self, nm, l):
        P, S, Sb = self.P, self.S, self.Sb
        db = self.d_state[nm]
        hd = {"ret_p": 4, "ssm_p": 16, "gdn_p": 8}[nm]
        if self.half == 0:
            P.dve(lambda e: e.memset(S.t[:], 0.0), [], [S])
        else:
            P.dma("sp", S.t[:].rearrange("p (h v) -> p h v", h=hd), self.state_view[nm](db.t[l]), [(db, l)], [S], S.sem())
        P.act(lambda e: e.activation(Sb.t[:], S.t[:], AF.Copy), [S], [Sb])

    def state_store(self, nm, l):
        P, S = self.P, self.S
        db = self.d_state[nm]
        hd = {"ret_p": 4, "ssm_p": 16, "gdn_p": 8}[nm]
        P.dma("sp", self.state_view[nm](db.t[l]), S.t[:].rearrange("p (h v) -> p h v", h=hd), [S], [(db, l)], S.sem(),
              is_output=True)

    def sigmoid_chain(self, tmp_ap, src_ap, reads, tmp_buf, scale_in=-1.0, bias_in=0.0):
        P = self.P
        if isinstance(bias_in, float) and bias_in == 0.0:
            P.act(lambda e: e.activation(tmp_ap, src_ap, AF.Exp, scale=scale_in), reads, [tmp_buf])
        else:
            P.act(lambda e: e.activation(tmp_ap, src_ap, AF.Exp, scale=scale_in, bias=bias_in), reads, [tmp_buf])
        P.act(lambda e: e.activation(tmp_ap, tmp_ap, AF.Ln, bias=1.0), [tmp_buf], [tmp_buf])
        P.act(lambda e: e.activation(tmp_ap, tmp_ap, AF.Exp, scale=-1.0), [tmp_buf], [tmp_buf])

    def rstd_pool(self, nst, rows, eps_col, scale_col=None):
        P = self.P
        eps = float(LN_EPS) if eps_col == 0 else float(NORM_EPS)
        P.act(lambda e: e.activation(nst.t[:rows, 2:3], nst.t[:rows, 1:2], AF.Ln, bias=eps), [nst], [nst])
        P.act(lambda e: e.activation(nst.t[:rows, 3:4], nst.t[:rows, 2:3], AF.Exp, scale=-0.5), [nst], [nst])

    def transposes_to(self, dst_ap_fn, src_fn, n, rows, src_buf, dst_writes, bt=None):
        P, ps = self.P, self.ps
        bt = self.bank() if bt is None else bt
        pst = ps.t[:, bt, :].bitcast(BF16).rearrange("p (k n) -> p k n", k=8)
        for k in range(n):
            P.pe(lambda e, k=k: e.transpose(pst[:, k, :rows], src_fn(k), self.identb.t[:rows, :rows]),
                 [src_buf, self.identb], [(ps, bt)])
        dst_ap_fn(pst, bt)

    def retention(self, l):
        self.P.tag = "ret"
        P, I, c = self.P, self.i, self.cfg
        ps, xT, hT = self.ps, self.xT, self.hT
        w_in = I["w_in"][l]
        S, Sb, A = self.S, self.Sb, self.A
        A.reset()
        retmask = A.alloc("retmask", [4, 128], F32)
        retrow = A.alloc("retrow", [4, 128], F32)
        retcol = A.alloc("retcol", [4], F32)
        P.dma("sp", retmask.t[:], I["c_retmask"].rearrange("h j i -> j h i"), [], [retmask], retmask.sem())
        for h in range(4):
            P.dma("sp", retrow.t[:, h, :], I["c_retrow"][h, 0, :].partition_broadcast(128), [], [retrow], retrow.sem())
        P.dma("sp", retcol.t[:], I["c_retrow"][:, 1, :].rearrange("h j -> j h"), [], [retcol], retcol.sem(),
              allow_slow_non_contiguous=True)
        rope_b = [A.alloc("rope", [4, 64], F32) for _ in range(2)]
        qkr_b = [A.alloc("qkr", [2, 2, 128], BF16) for _ in range(2)]
        rt_b = [A.alloc("rt", [4, 256], F32, nreg=4) for _ in range(2)]
        qkT_b = [A.alloc("qkT", [4, 128], BF16) for _ in range(2)]
        qdT_b = [A.alloc("qdT", [128], BF16) for _ in range(4)]
        kdec_b = [A.alloc("kdec", [128], BF16) for _ in range(4)]
        vbf_b = [A.alloc("vbf", [256], BF16) for _ in range(4)]
        sgt_b = [A.alloc("sgt", [256], F32) for _ in range(4)]
        scm_b = [A.alloc("scm", [128], BF16) for _ in range(4)]
        ogt_b = [A.alloc("ogt", [256], F32) for _ in range(4)]
        ogb_b = [A.alloc("ogb", [256], BF16) for _ in range(2)]
        nst_b = [A.alloc("nst", [8], F32) for _ in range(4)]
        st6_b = [A.alloc("st6", [6], F32) for _ in range(4)]
        if self.has_sample:
            qTm = A.alloc("qTm", [16, 16], BF16)
            ktm = A.alloc("ktm", [16, 128], BF16)
            Ss_b = [A.alloc("Ss", [2, 256], F32) for _ in range(4)]
            Ssb_b = [A.alloc("Ssb", [256], BF16) for _ in range(4)]
        self.state_load("ret_p", l)
        lg = [float(np.float64(np.log1p(-np.float32(2.0) ** np.float32(-5.0 - h)).astype(np.float32))) for h in range(4)]

        def load_pair(pr):
            sA, sB, sC = self.wslot(), self.wslot(), self.wslot()
            h0, h1 = 2 * pr, 2 * pr + 1
            for i, h in enumerate((h0, h1)):
                self.wload(sA, i * 256, w_in[:, C_RQ + h * 128:C_RQ + (h + 1) * 128])
                self.wload(sA, i * 256 + 128, w_in[:, C_RK + h * 128:C_RK + (h + 1) * 128])
            for s_, h in ((sB, h0), (sC, h1)):
                self.wload(s_, 0, w_in[:, C_RV + h * 256:C_RV + (h + 1) * 256])
                self.wload(s_, 256, w_in[:, C_RG + h * 256:C_RG + (h + 1) * 256])
            return (sA, sB, sC)

        def tile_gen(i, pr, W, m, rows, col0):
            par = i % 2
            X, Y, Z = 2 + 3 * par, 3 + 3 * par, 4 + 3 * par
            sA = W[0]
            is_s = rows != 128
            t0 = (self.half * c.NTH * 128 + col0) if not is_s else c.T
            rope, rt, qkr, qkT = rope_b[par], rt_b[par], qkr_b[par], qkT_b[par]
            P.dma("sp", rope.t[:rows], I["c_rope"][t0:t0 + rows], [], [rope], rope.sem())
            for k in range(8):
                self.mm(ps.t[:rows, X, :], xT.t[:, k, col0:col0 + rows], sA.t[:, k, :], k == 0, k == 7,
                        [(xT, m), sA], [(ps, X)])
            qk5 = ps.t[:rows, X, :].rearrange("p (h a b f) -> p h a b f", h=2, a=2, b=2)
            x1, x2 = qk5[:, :, :, 0, :], qk5[:, :, :, 1, :]
            rp = rope.t[:rows].rearrange("p (a b) f -> p a b f", a=2)
            cos = rp[:, :, 0, :].unsqueeze(1).broadcast_to([rows, 2, 2, 64])
            sin = rp[:, :, 1, :].unsqueeze(1).broadcast_to([rows, 2, 2, 64])
            tv = [rt.t[:rows, j, :].rearrange("p (h a f) -> p h a f", h=2, a=2) for j in range(4)]
            P.dve(lambda e: e.tensor_tensor(tv[0], x1, cos, ALU.mult), [(ps, X), rope], [(rt, 0)])
            P.dve(lambda e: e.tensor_tensor(tv[1], x2, sin, ALU.mult), [(ps, X), rope], [(rt, 1)])
            P.dve(lambda e: e.tensor_tensor(tv[2], x1, sin, ALU.mult), [(ps, X), rope], [(rt, 2)])
            P.dve(lambda e: e.tensor_tensor(tv[3], x2, cos, ALU.mult), [(ps, X), rope], [(rt, 3)])
            P.dve(lambda e: e.tensor_tensor(qkr.t[:rows, :, :, 0:64], tv[0], tv[1], ALU.subtract), [(rt, 0), (rt, 1)], [qkr])
            P.dve(lambda e: e.tensor_tensor(qkr.t[:rows, :, :, 64:128], tv[2], tv[3], ALU.add), [(rt, 2), (rt, 3)], [qkr])
            yield

            def evac(pst, bt):
                P.act(lambda e: e.activation(qkT.t[:, :, :rows], pst[:, 0:4, :rows], AF.Copy), [(ps, bt)], [qkT])

            self.transposes_to(evac, lambda k: qkr.t[:rows, k // 2, k % 2, :], 4, rows, qkr, None, bt=Y)
            yield
            for hh in range(2):
                yield from head_gen(par, (X, Y, Z), pr, W, m, rows, col0, hh, qkr, qkT)

        def head_gen(par, banks, pr, W, m, rows, col0, hh, qkr, qkT):
            X, Y, Z = banks
            h = 2 * pr + hh
            sV = W[1 + hh]
            is_s = rows != 128
            gam = float(np.exp(lg[h]))
            bi = 2 * par + hh
            vbf, sgt, qdT, kdec, scm, ogt, nst, st6 = (vbf_b[bi], sgt_b[bi], qdT_b[bi], kdec_b[bi], scm_b[bi], ogt_b[bi],
                                                      nst_b[bi], st6_b[bi])
            ogb = ogb_b[par]
            for k in range(8):
                self.mm(ps.t[:rows, X, :], xT.t[:, k, col0:col0 + rows], sV.t[:, k, :], k == 0, k == 7, [(xT, m), sV], [(ps, X)])
            P.act(lambda e: e.activation(vbf.t[:rows, :], ps.t[:rows, X, 0:256], AF.Copy), [(ps, X)], [vbf])
            self.sigmoid_chain(sgt.t[:rows, :], ps.t[:rows, X, 256:512], [(ps, X)], sgt)
            yield
            P.dve(lambda e: e.tensor_tensor(sgt.t[:rows, :], ps.t[:rows, X, 256:512], sgt.t[:rows, :], ALU.mult), [(ps, X), sgt], [sgt])
            bo = Z if not is_s else 0
            Sr = (S, list(range(4 * h, 4 * h + 4)))
            Sbr = (Sb, list(range(4 * h, 4 * h + 4)))
            if not is_s:
                P.dve(lambda e: e.tensor_tensor(qdT.t[:, :], qkT.t[:, 2 * hh, :], retrow.t[:, h, :], ALU.mult), [qkT, retrow], [qdT])
                P.dve(lambda e: e.tensor_scalar(kdec.t[:, :], qkr.t[:, hh, 1, :], retcol.t[:, h:h + 1], None, ALU.mult), [qkr, retcol], [kdec])
                self.mm(ps.t[:, Y, 0:128], qkT.t[:, 2 * hh + 1, :], qkT.t[:, 2 * hh, :], True, True, [qkT], [(ps, Y)])
                yield
                P.dve(lambda e: e.tensor_tensor(scm.t[:, :], ps.t[:, Y, 0:128], retmask.t[:, h, :], ALU.mult), [(ps, Y), retmask], [scm])
                yield
                self.mm(ps.t[:, bo, 0:256], scm.t[:, :], vbf.t[:, :], True, False, [scm, vbf], [(ps, bo)])
                self.mm(ps.t[:, bo, 0:256], qdT.t[:, :], Sb.t[:, h * 256:(h + 1) * 256], False, True, [qdT, Sbr], [(ps, bo)])
                self.mm(ps.t[:, Y, 0:256], kdec.t[:, :], vbf.t[:, :], True, True, [kdec, vbf], [(ps, Y)])
                yield
                P.dve(lambda e: e.scalar_tensor_tensor(S.t[:, h * 256:(h + 1) * 256], S.t[:, h * 256:(h + 1) * 256],
                                                       float(np.exp(lg[h] * 128)), ps.t[:, Y, 0:256], ALU.mult, ALU.add), [Sr, (ps, Y)], [Sr])
                P.act(lambda e: e.activation(Sb.t[:, h * 256:(h + 1) * 256], S.t[:, h * 256:(h + 1) * 256], AF.Copy), [Sr], [Sbr])
            else:
                P.dve(lambda e: e.tensor_tensor(qTm.t[:], qkT.t[:, 2 * hh, 0:16].unsqueeze(1).broadcast_to([128, 16, 16]),
                                                self.eyeb.t[:], ALU.mult), [qkT, self.eyeb], [qTm])
                P.dve(lambda e: e.tensor_tensor(ktm.t[0:16], qkr.t[0:16, hh, 1, :].unsqueeze(1).broadcast_to([16, 16, 128]),
                                                self.eyep.t[:, :].unsqueeze(2).broadcast_to([16, 16, 128]), ALU.mult), [qkr, self.eyep], [ktm])
                self.run_pipelined((self.ret_sample(l, h, s_, gam, vbf, bo, Ss_b[s_ % 4], Ssb_b[s_ % 4], qTm, ktm, (Y, X, Z, 5 if Z != 5 else 2)[s_ % 4])
                                    for s_ in range(NS)), 4)
            P.dve(lambda e: e.bn_stats(st6.t[:rows, :], ps.t[:rows, bo, 0:256]), [(ps, bo)], [st6])
            P.dve(lambda e: e.bn_aggr(nst.t[:rows, 0:2], st6.t[:rows, :]), [st6], [nst])
            yield
            self.rstd_pool(nst, rows, 0)
            yield
            P.dve(lambda e: e.tensor_scalar(ogt.t[:rows, :], ps.t[:rows, bo, 0:256], nst.t[:rows, 0:1], nst.t[:rows, 3:4],
                                            ALU.subtract, ALU.mult), [(ps, bo), nst], [ogt])
            P.dve(lambda e: e.tensor_tensor(ogb.t[:rows, :], ogt.t[:rows, :], sgt.t[:rows, :], ALU.mult), [ogt, sgt], [ogb])
            yield

            def evac(pst, bt):
                P.act(lambda e: e.activation(hT.t[:, 2 * h:2 * h + 2, col0:col0 + rows], pst[:, 0:2, :rows], AF.Copy),
                      [(ps, bt)], [(hT, [2 * h, 2 * h + 1])])

            self.transposes_to(evac, lambda k: ogb.t[:rows, k * 128:(k + 1) * 128], 2, rows, ogb, None, bt=X)
            yield

        nxt = load_pair(0)
        for pr in range(2):
            W = nxt
            if pr == 0:
                nxt = load_pair(1)
            self.run_pipelined((tile_gen(i, pr, W, m, rows, col0) for i, (m, rows, col0) in enumerate(self.tiles())), 2)
        self.state_store("ret_p", l)

    def ret_sample(self, l, h, s_, gam, vbf, bo, Ss, Ssb, qTm, ktm, bd):
        P, I, ps = self.P, self.i, self.ps
        P.dma("sp", Ss.t[:, 0, :], I["st_ret"][l, s_, h], [], [Ss], Ss.sem())
        self.mm(ps.t[:, bd, 0:256], ktm.t[0:16, s_, :], vbf.t[0:16, :], True, True, [ktm, vbf], [(ps, bd)])
        yield
        P.dve(lambda e: e.scalar_tensor_tensor(Ss.t[:, 1, :], Ss.t[:, 0, :], gam, ps.t[:, bd, 0:256], ALU.mult, ALU.add),
              [Ss, (ps, bd)], [Ss])
        yield
        P.dma("sp", self.o["ret_s"][l, s_, h], Ss.t[:, 1, :], [Ss], [], Ss.sem(), is_output=True)
        P.act(lambda e: e.activation(Ssb.t[:, :], Ss.t[:, 1, :], AF.Copy), [Ss], [Ssb])
        yield
        self.mm(ps.t[0:16, bo, 0:256], qTm.t[:, s_, :], Ssb.t[:, :], s_ == 0, s_ == NS - 1, [qTm, Ssb], [(ps, bo)])

    def finale(self, l, w_out, mcol):
        self.P.tag = "finale"
        P, I = self.P, self.i
        ps, xT, hT, xa = self.ps, self.xT, self.hT, self.xa
        w_in, w_o = I["w_in"][l], I["w_o"][l]
        A = self.A
        A.reset()
        gT_b = [A.alloc("gT", [8, 128], BF16) for _ in range(2)]
        gtok_b = [A.alloc("gtok", [D], BF16) for _ in range(2)]
        Wout = (self.wslot(), self.wslot())
        Wm = (self.wslot(), self.wslot())
        Wo = (self.wslot(), self.wslot())
        for o in range(2):
            self.wload(Wm[o], 0, w_in[:, mcol + o * 512:mcol + (o + 1) * 512])
        for o in range(2):
            self.wload(Wout[o], 0, w_out[:, o * 512:(o + 1) * 512])
        for o in range(2):
            self.wload(Wo[o], 0, w_o[:, o * 512:(o + 1) * 512])

        def tile_gen(i, m, rows, col0):
            par = i % 2
            by, bm = 4 * par, 4 * par + 2
            tA, gtok, gT = self.tmpA[par], gtok_b[par], gT_b[par]
            for o in range(2):
                for k in range(8):
                    self.mm(ps.t[:rows, bm + o, :], xT.t[:, k, col0:col0 + rows], Wm[o].t[:, k, :], k == 0, k == 7,
                            [(xT, m), Wm[o]], [(ps, bm + o)])
            for o in range(2):
                for k in range(8):
                    self.mm(ps.t[:rows, by + o, :], hT.t[:, k, col0:col0 + rows], Wout[o].t[:, k, :], k == 0, k == 7,
                            [(hT, k), Wout[o]], [(ps, by + o)])
            pm = ps.t[:rows, bm:bm + 2, :].rearrange("p b n -> p (b n)")
            py = ps.t[:rows, by:by + 2, :].rearrange("p b n -> p (b n)")
            P.act(lambda e: e.activation(tA.t[:rows, :], pm, AF.Sigmoid), [(ps, bm), (ps, bm + 1)], [tA])
            yield
            P.dve(lambda e: e.tensor_tensor(gtok.t[:rows, :], py, tA.t[:rows, :], ALU.mult), [(ps, by), (ps, by + 1), tA], [gtok])
            yield

            def evac(pst, bt):
                P.act(lambda e: e.activation(gT.t[:, :, :rows], pst[:, :, :rows], AF.Copy), [(ps, bt)], [gT])

            self.transposes_to(evac, lambda k: gtok.t[:rows, k * 128:(k + 1) * 128], 8, rows, gtok, None, bt=bm)
            yield
            for o in range(2):
                bo = by + o
                for k in range(8):
                    self.mm(ps.t[:rows, bo, :], gT.t[:, k, :rows], Wo[o].t[:, k, :], k == 0, k == 7, [gT, Wo[o]], [(ps, bo)])
            yield
            for o in range(2):
                bo = by + o
                P.dve(lambda e, o=o, bo=bo: e.tensor_tensor(xa.t[:rows, m, o * 512:(o + 1) * 512],
                                                            xa.t[:rows, m, o * 512:(o + 1) * 512], ps.t[:rows, bo, :], ALU.add),
                      [(ps, bo), (xa, 2 * m + o)], [(xa, 2 * m + o)])

        self.run_pipelined((tile_gen(i, m, rows, col0) for i, (m, rows, col0) in enumerate(self.tiles())), 2)

    def colvecs(self, dst_ap, tk, r, c, dst_buf):
        P, ps = self.P, self.ps
        b = self.bank()
        for cc in range(c):
            P.pe(lambda e, cc=cc: e.transpose(ps.t[:, b, cc * r:(cc + 1) * r], tk.t[:r, cc * 128:(cc + 1) * 128],
                                              self.identf.t[:r, :r]), [tk, self.identf], [(ps, b)])
        P.act(lambda e: e.activation(dst_ap, ps.t[:, b, 0:c * r], AF.Copy), [(ps, b)], [dst_buf])

    def ssd(self, l):
        self.P.tag = "ssd"
        P, I, c = self.P, self.i, self.cfg
        ps, xT, hT, S, Sb, A = self.ps, self.xT, self.hT, self.S, self.Sb, self.A
        w_in = I["w_in"][l]
        TP = c.NTH * 128
        assert TP <= 512
        hs = self.has_sample
        masks = self.masks
        U, ONES, MB = masks.t[:, 0, :], masks.t[:, 1, :], masks.t[:, 2, :]
        A.reset()
        cwb = A.alloc("cwb", [12, 5], F32)
        negb = A.alloc("negb", [12], F32)
        nwT = A.alloc("nwT", [8], F32)
        dtb = A.alloc("dtb", [16], F32)
        Ab = A.alloc("Ab", [16], F32)
        Db = A.alloc("Db", [16], F32)
        wdt = A.alloc("wdt", [8, 16], BF16)
        off0 = A.off
        tk = A.alloc("tk", [1536], F32)
        tk2 = A.alloc("tk2", [1024], F32)
        P.dma("sp", tk.t[0:4, :], I["ssm_conv_w"][l], [], [tk], tk.sem())
        P.dma("sp", tk.t[4:5, :], I["ssm_conv_b"][l].unsqueeze(0), [], [tk], tk.sem())
        self.colvecs(cwb.t[:].rearrange("p c j -> p (c j)"), tk, 5, 12, cwb)
        P.act(lambda e: e.activation(negb.t[:, :], cwb.t[:, :, 4], AF.Copy, scale=-1.0), [cwb], [negb])
        P.dma("sp", tk2.t[0:1, :], I["ssm_norm_w"][l].unsqueeze(0), [], [tk2], tk2.sem())
        self.colvecs(nwT.t[:, :], tk2, 1, 8, nwT)
        P.dma("sp", dtb.t[:], I["ssm_dt_bias"][l].partition_broadcast(128), [], [dtb], dtb.sem())
        P.dma("sp", Ab.t[:], I["ssm_a_log"][l].partition_broadcast(128), [], [Ab], Ab.sem())
        P.dma("sp", Db.t[:], I["ssm_d"][l].partition_broadcast(128), [], [Db], Db.sem())
        P.act(lambda e: e.activation(Ab.t[:], Ab.t[:], AF.Exp), [Ab], [Ab])
        P.act(lambda e: e.activation(Ab.t[:], Ab.t[:], AF.Copy, scale=-1.0), [Ab], [Ab])
        P.dma("pool", wdt.t[:], w_in[:, C_MDT:C_MDT + 16].rearrange("(k p) c -> p k c", p=128), [], [wdt], wdt.sem())
        A.reset(off0)
        xbc = A.alloc("xbc", [12, TP], BF16, nreg=12)
        NPAR = 2 if hs else 3
        xraw_b = [A.alloc("xraw", [3 + TP], F32) for _ in range(NPAR)]
        acc_b = [A.alloc("acc", [TP], F32) for _ in range(NPAR)]
        sgm_b = acc_b
        f1 = A.alloc("f1", [1024], F32)
        f2 = A.alloc("f2", [1024], F32)
        f3 = A.alloc("f3", [1024], F32)
        xs_sb = A.alloc("xs_sb", [1024], BF16)
        v = A.alloc("v", [1024], BF16)
        vdec = A.alloc("vdec", [1024], BF16)
        Mh_b = [A.alloc("Mh", [8, 128], BF16) for _ in range(2)]
        Bd_b = [A.alloc("Bd", [8, 128], F32) for _ in range(2)]
        Btok = A.alloc("Btok", [2, 128], BF16)
        ogb = A.alloc("ogb", [1024], BF16)
        sm = A.alloc("sm", [8, 16], F32)
        cumT = A.alloc("cumT", [128], F32)
        nst = A.alloc("nst", [8], F32)
        stg = A.alloc("stg", [512], F32)
        if hs:
            xbcS = A.alloc("xbcS", [12, 16], BF16)
            stc_b = [A.alloc("stc", [3, 128], F32) for _ in range(2)]
            xrs_b = [A.alloc("xrs", [4, 16], F32) for _ in range(2)]
            accS_b = [A.alloc("accS", [16], F32) for _ in range(2)]
            sgS_b = [A.alloc("sgS", [16], F32) for _ in range(2)]
            Eall = A.alloc("Eall", [16, 16], F32)
            Bde = A.alloc("Bde", [16, 16], F32)
            Bm_b = [A.alloc("Bm", [256], BF16) for _ in range(2)]
            Cm_b = [A.alloc("Cm", [2, 16], BF16) for _ in range(2)]
        tail = self.tail_ssm[l]
        if self.half == 0:
            P.dve(lambda e: e.memset(tail.t[:], 0.0), [], [tail])
        self.state_load("ssm_p", l)
        Wx = [self.wslot() for _ in range(3)]
        for i in range(3):
            self.wload(Wx[i], 0, w_in[:, C_MXBC + i * 512:C_MXBC + (i + 1) * 512])
        Wz = (self.wslot(), self.wslot())
        for o in range(2):
            self.wload(Wz[o], 0, w_in[:, C_MZ + o * 512:C_MZ + (o + 1) * 512])

        def conv_chunk(cc):
            slot = Wx[cc // 4]
            cs = (cc % 4) * 128
            par = cc % NPAR
            acc, sgm = acc_b[par], sgm_b[par]
            b = 2 + par
            for k in range(8):
                self.mm(ps.t[:, b, :TP], slot.t[:, k, cs:cs + 128], xT.t[:, k, 0:TP], k == 0, k == 7,
                        [slot, (xT, list(range(c.NTH)))], [(ps, b)])
            xraw = xraw_b[par]
            P.act(lambda e: e.activation(xraw.t[:, 0:3], tail.t[:, cc, :], AF.Copy), [(tail, cc)], [xraw])
            P.act(lambda e: e.activation(xraw.t[:, 3:3 + TP], ps.t[:, b, :TP], AF.Copy), [(ps, b)], [xraw])
            P.act(lambda e: e.activation(tail.t[:, cc, :], xraw.t[:, TP:TP + 3], AF.Copy), [xraw], [(tail, cc)])
            yield
            P.dve(lambda e: e.tensor_scalar(acc.t[:, :], xraw.t[:, 0:TP], cwb.t[:, cc, 0:1], None, ALU.mult), [xraw, cwb], [acc])
            for j in range(1, 4):
                P.dve(lambda e, j=j: e.scalar_tensor_tensor(acc.t[:, :], xraw.t[:, j:j + TP], cwb.t[:, cc, j:j + 1], acc.t[:, :],
                                                            ALU.mult, ALU.add), [xraw, cwb, acc], [acc])
            yield
            P.act(lambda e: e.activation(xbc.t[:, cc, :], acc.t[:, :], AF.Silu, bias=cwb.t[:, cc, 4:5]), [acc, cwb], [(xbc, cc)])
            if hs:
                yield
                b2 = 4 + par
                for k in range(8):
                    self.mm(ps.t[:, b2, 0:16], slot.t[:, k, cs:cs + 128], xT.t[:, k, TP:TP + 16], k == 0, k == 7,
                            [slot, (xT, c.NTH)], [(ps, b2)])
                stc = stc_b[par]
                P.dma("sp", stc.t[0:16, :, :], I["st_ssm_conv"][l, :, :, cc * 128:(cc + 1) * 128], [], [stc], stc.sem())
                for j in range(3):
                    P.pe(lambda e, j=j: e.transpose(ps.t[:, b2, 16 + 16 * j:32 + 16 * j], stc.t[0:16, j, :], self.identf.t[0:16, 0:16]),
                         [stc, self.identf], [(ps, b2)])
                xrs = xrs_b[par]
                P.act(lambda e: e.activation(xrs.t[:, 0:3, :], ps.t[:, b2, 16:64].rearrange("p (j s) -> p j s", j=3), AF.Copy),
                      [(ps, b2)], [xrs])
                P.act(lambda e: e.activation(xrs.t[:, 3, :], ps.t[:, b2, 0:16], AF.Copy), [(ps, b2)], [xrs])
                accS, sgS = accS_b[par], sgS_b[par]
                P.dve(lambda e: e.tensor_scalar(accS.t[:, :], xrs.t[:, 0, :], cwb.t[:, cc, 0:1], None, ALU.mult), [xrs, cwb], [accS])
                for j in range(1, 4):
                    P.dve(lambda e, j=j: e.scalar_tensor_tensor(accS.t[:, :], xrs.t[:, j, :], cwb.t[:, cc, j:j + 1], accS.t[:, :],
                                                                ALU.mult, ALU.add), [xrs, cwb, accS], [accS])
                yield
                P.act(lambda e: e.activation(xbcS.t[:, cc, :], accS.t[:, :], AF.Silu, bias=cwb.t[:, cc, 4:5]), [accS, cwb], [xbcS])

        self.run_pipelined((conv_chunk(cc) for cc in range(12)), NPAR)
        def conv_rows(c0, n, dst_fn):
            for i in range(3):
                b = self.bank()
                for k in range(8):
                    self.mm(ps.t[:n, b, :], xT.t[:, k, c0:c0 + n], Wx[i].t[:, k, :], k == 0, k == 7,
                            [Wx[i], (xT, list(range(self.NTT)))], [(ps, b)])
                P.act(lambda e, b=b: e.activation(stg.t[:n, :], ps.t[:n, b, :], AF.Copy), [(ps, b)], [stg])
                P.dma("sp", dst_fn(i), stg.t[:n, :], [stg], [], stg.sem(), is_output=True)

        if self.half == c.NH - 1:
            conv_rows(TP - 3, 3, lambda i: self.o["ssm_conv_p"][l, :, i * 512:(i + 1) * 512])
        if hs:
            conv_rows(TP, 16, lambda i: self.o["ssm_conv_s"][l, :, 2, i * 512:(i + 1) * 512])
            P.dma("sp", self.o["ssm_conv_s"][l, :, 0:2, :], I["st_ssm_conv"][l, :, 1:3, :], [], [], stg.sem(), is_output=True)

        def tile_body(m, rows, col0):
            is_s = rows != 128
            src = xbcS if is_s else xbc
            sc0 = 0 if is_s else col0
            DT, LA, CUM, ETOK, ELAST, DECL, T16 = [sm.t[:rows, i, :] for i in range(7)]
            bdt = 6
            for k in range(8):
                self.mm(ps.t[:rows, bdt, 0:16], xT.t[:, k, col0:col0 + rows], wdt.t[:, k, :], k == 0, k == 7, [(xT, m), wdt], [(ps, bdt)])
            P.dve(lambda e: e.tensor_tensor(T16, ps.t[:rows, bdt, 0:16], dtb.t[:rows, :], ALU.add), [(ps, bdt), dtb], [sm])
            P.act(lambda e: e.activation(T16, T16, AF.Exp), [sm], [sm])
            P.act(lambda e: e.activation(DT, T16, AF.Ln, bias=1.0), [sm], [sm])
            P.dve(lambda e: e.tensor_tensor(LA, DT, Ab.t[:rows, :], ALU.mult), [sm, Ab], [sm])
            def evac_xs(pst, bt):
                P.act(lambda e: e.activation(xs_sb.t[:rows, :], pst[:rows, :, :].rearrange("p k n -> p (k n)"), AF.Copy), [(ps, bt)], [xs_sb])
            self.transposes_T(evac_xs, lambda k: src.t[:, k, sc0:sc0 + rows], 8, rows, src, bt=7)
            if is_s and l == 0 and self.cfg.dbg.get("ssd_dump"):
                dx = self.nc.dram_tensor("dbg_xs", [16, 1024], F32, kind="ExternalOutput").ap()
                dd = self.nc.dram_tensor("dbg_dt", [16, 16], F32, kind="ExternalOutput").ap()
                P.dma("pool", dx, xs_sb.t[0:16, :], [xs_sb], [], xs_sb.sem(), is_output=True)
                P.dma("sp", dd, sm.t[0:16, 0, :], [sm], [], sm.sem(), is_output=True)
            dt_b = DT.unsqueeze(2).broadcast_to([rows, 16, 64])
            P.dve(lambda e: e.tensor_tensor(v.t[:rows, :].rearrange("p (h d) -> p h d", h=16),
                                            xs_sb.t[:rows, :].rearrange("p (h d) -> p h d", h=16), dt_b, ALU.mult), [xs_sb, sm], [v])
            def evac_b(pst, bt):
                P.act(lambda e: e.activation(Btok.t[:rows, :, :], pst[:rows, 0:2, :], AF.Copy), [(ps, bt)], [Btok])
            self.transposes_T(evac_b, lambda k: src.t[:, 8 + k, sc0:sc0 + rows], 2, rows, src, bt=7)
            if not is_s:
                bc = 6
                self.mm(ps.t[:, bc, 32:48], U, LA, True, True, [masks, sm], [(ps, bc)])
                self.mm(ps.t[:, bc, 48:64], ONES, LA, True, True, [masks, sm], [(ps, bc)])
                self.mm(ps.t[0:16, bc, 64:192], LA, U, True, True, [masks, sm], [(ps, bc)])
                P.act(lambda e: e.activation(CUM, ps.t[:, bc, 32:48], AF.Copy), [(ps, bc)], [sm])
                P.act(lambda e: e.activation(ETOK, ps.t[:, bc, 32:48], AF.Exp), [(ps, bc)], [sm])
                P.act(lambda e: e.activation(ELAST, ps.t[:, bc, 48:64], AF.Exp), [(ps, bc)], [sm])
                P.dve(lambda e: e.tensor_tensor(DECL, ps.t[:, bc, 48:64], CUM, ALU.subtract), [(ps, bc), sm], [sm])
                P.act(lambda e: e.activation(DECL, DECL, AF.Exp), [sm], [sm])
                P.act(lambda e: e.activation(cumT.t[0:16, :], ps.t[0:16, bc, 64:192], AF.Copy), [(ps, bc)], [cumT])
                P.dve(lambda e: e.tensor_tensor(vdec.t[:, :].rearrange("p (h d) -> p h d", h=16),
                                                v.t[:, :].rearrange("p (h d) -> p h d", h=16),
                                                DECL.unsqueeze(2).broadcast_to([128, 16, 64]), ALU.mult), [v, sm], [vdec])
                bsc = 6
                for g in range(2):
                    self.mm(ps.t[:, bsc, 256 + g * 128:256 + (g + 1) * 128], xbc.t[:, 8 + g, col0:col0 + 128], xbc.t[:, 10 + g, col0:col0 + 128],
                            True, True, [xbc], [(ps, bsc)])
                po, pcs, pds = 0, 4, 2
                for g in range(2):
                    self.mm(ps.t[:, pcs + g, :], xbc.t[:, 10 + g, col0:col0 + 128], Sb.t[:, g * 512:(g + 1) * 512], True, True,
                            [xbc, (Sb, list(range(8 * g, 8 * g + 8)))], [(ps, pcs + g)])
                P.dve(lambda e: e.tensor_tensor(f2.t[:, :].rearrange("p (h d) -> p h d", h=16),
                                                ps.t[:, pcs:pcs + 2, :].rearrange("p b (h d) -> p (b h) d", h=8),
                                                ETOK.unsqueeze(2).broadcast_to([128, 16, 64]), ALU.mult), [(ps, pcs), (ps, pcs + 1), sm], [f2])
                def grp(g):
                    Bdg, fg, Mh = Bd_b[g], (f1 if g == 0 else f3), Mh_b[g]
                    pa = 2 + 2 * g
                    P.dve(lambda e: e.tensor_tensor(Bdg.t[0:16, :, :], cumT.t[0:16, :].unsqueeze(1).broadcast_to([16, 8, 128]),
                                                    self.eyep.t[:, 8 * g:8 * g + 8].unsqueeze(2).broadcast_to([16, 8, 128]), ALU.mult),
                          [cumT, self.eyep], [Bdg])
                    yield
                    for hf in range(2):
                        self.mm(ps.t[:, pa + hf, :], ONES[0:16, :], Bdg.t[0:16, 4 * hf:4 * hf + 4, :].rearrange("p h i -> p (h i)"),
                                True, False, [masks, Bdg], [(ps, pa + hf)])
                        for hq in range(4):
                            self.mm(ps.t[:, pa + hf, hq * 128:(hq + 1) * 128], self.identf.t[:, :], MB, False, hq == 3,
                                    [masks, self.identf], [(ps, pa + hf)])
                    yield
                    pav = ps.t[:, pa:pa + 2, :].rearrange("p b (h i) -> p (b h) i", h=4)
                    P.dve(lambda e: e.tensor_tensor(fg.t[:, :].rearrange("p (h i) -> p h i", h=8), pav,
                                                    CUM[:, 8 * g:8 * g + 8].unsqueeze(2).broadcast_to([128, 8, 128]),
                                                    ALU.subtract), [(ps, pa), (ps, pa + 1), sm], [fg])
                    yield
                    P.act(lambda e: e.activation(fg.t[:, :], fg.t[:, :], AF.Exp), [fg], [fg])
                    yield
                    P.dve(lambda e: e.tensor_tensor(Mh.t[:, :, :], fg.t[:, :].rearrange("p (h i) -> p h i", h=8),
                                                    ps.t[:, bsc, 256 + g * 128:256 + (g + 1) * 128].unsqueeze(1).broadcast_to([128, 8, 128]),
                                                    ALU.mult), [fg, (ps, bsc)], [Mh])
                    yield
                    for h in range(8):
                        hg = 8 * g + h
                        self.mm(ps.t[:, po + g, h * 64:(h + 1) * 64], Mh.t[:, h, :], v.t[:, hg * 64:(hg + 1) * 64], True, True,
                                [Mh, v], [(ps, po + g)])

                self.run_pipelined([grp(0), grp(1)], 2)
                P.dve(lambda e: e.tensor_tensor(f2.t[:, :], f2.t[:, :], ps.t[:, po:po + 2, :].rearrange("p b n -> p (b n)"), ALU.add),
                      [f2, (ps, po), (ps, po + 1)], [f2])
                for g in range(2):
                    self.mm(ps.t[:, pds + g, :], Btok.t[:, g, :], vdec.t[:, g * 512:(g + 1) * 512], True, True, [Btok, vdec], [(ps, pds + g)])
                P.dve(lambda e: e.tensor_tensor(f1.t[:, :].rearrange("p (h d) -> p h d", h=16), S.t[:, :].rearrange("p (h d) -> p h d", h=16),
                                                ELAST.unsqueeze(2).broadcast_to([128, 16, 64]), ALU.mult), [S, sm], [f1])
                P.dve(lambda e: e.tensor_tensor(S.t[:, :], f1.t[:, :], ps.t[:, pds:pds + 2, :].rearrange("p b n -> p (b n)"), ALU.add),
                      [f1, (ps, pds), (ps, pds + 1)], [S])
                P.act(lambda e: e.activation(Sb.t[:, :], S.t[:, :], AF.Copy), [S], [Sb])
            else:
                ELA = ELAST
                P.act(lambda e: e.activation(ELA, LA, AF.Exp), [sm], [sm])
                P.dve(lambda e: e.tensor_tensor(Bde.t[0:16, :, :], ELA.unsqueeze(1).broadcast_to([16, 16, 16]),
                                                self.eyep.t[:, :].unsqueeze(2).broadcast_to([16, 16, 16]), ALU.mult), [sm, self.eyep], [Bde])
                be = 6
                self.mm(ps.t[:, be, 0:256], ONES[0:16, :], Bde.t[0:16, :, :].rearrange("p s h -> p (s h)"), True, True, [masks, Bde], [(ps, be)])
                P.act(lambda e: e.activation(Eall.t[:, :, :].rearrange("p s h -> p (s h)"), ps.t[:, be, 0:256], AF.Copy), [(ps, be)], [Eall])
                Ss_b = [f2, f3]
                Ssb_b = [vdec, ogb]

                def smp(s_):
                    par = s_ % 2
                    Ss, Ssb, Bm, Cm = Ss_b[par], Ssb_b[par], Bm_b[par], Cm_b[par]
                    pds = 2 + 2 * par
                    P.dma("sp", Ss.t[:, :].rearrange("p (h d) -> p h d", h=16), I["st_ssm"][l, s_].rearrange("h n d -> n h d"), [], [Ss], Ss.sem())
                    P.dve(lambda e: e.tensor_scalar(Bm.t[0:16, :], Btok.t[0:16, :, :].rearrange("p g n -> p (g n)"),
                                                    self.eyep.t[:, s_:s_ + 1], None, ALU.mult), [Btok, self.eyep], [Bm])
                    P.dve(lambda e: e.tensor_tensor(Cm.t[:, :, :], xbcS.t[:, 10:12, :],
                                                    self.eyeb.t[:, s_, :].unsqueeze(1).broadcast_to([128, 2, 16]), ALU.mult),
                          [xbcS, self.eyeb], [Cm])
                    yield
                    for g in range(2):
                        self.mm(ps.t[:, pds + g, :], Bm.t[0:16, g * 128:(g + 1) * 128], v.t[0:16, g * 512:(g + 1) * 512], True, True,
                                [Bm, v], [(ps, pds + g)])
                    yield
                    P.dve(lambda e: e.tensor_tensor(Ss.t[:, :].rearrange("p (h d) -> p h d", h=16), Ss.t[:, :].rearrange("p (h d) -> p h d", h=16),
                                                    Eall.t[:, s_, :].unsqueeze(2).broadcast_to([128, 16, 64]), ALU.mult), [Ss, Eall], [Ss])
                    P.dve(lambda e: e.tensor_tensor(Ss.t[:, :], Ss.t[:, :], ps.t[:, pds:pds + 2, :].rearrange("p b n -> p (b n)"), ALU.add),
                          [Ss, (ps, pds), (ps, pds + 1)], [Ss])
                    yield
                    P.dma("sp", self.o["ssm_s"][l, s_].rearrange("h n d -> n h d"), Ss.t[:, :].rearrange("p (h d) -> p h d", h=16),
                          [Ss], [], Ss.sem(), is_output=True)
                    P.act(lambda e: e.activation(Ssb.t[:, :], Ss.t[:, :], AF.Copy), [Ss], [Ssb])
                    yield
                    for g in range(2):
                        self.mm(ps.t[0:16, g, :], Cm.t[:, g, :], Ssb.t[:, g * 512:(g + 1) * 512], s_ == 0, s_ == NS - 1, [Cm, Ssb], [(ps, g)])

                self.run_pipelined((smp(s_) for s_ in range(NS)), 2)
                P.act(lambda e: e.activation(f2.t[0:16, :], ps.t[0:16, 0:2, :].rearrange("p b n -> p (b n)"), AF.Copy), [(ps, 0), (ps, 1)], [f2])
            P.dve(lambda e: e.tensor_tensor(f3.t[:rows, :].rearrange("p (h d) -> p h d", h=16),
                                            xs_sb.t[:rows, :].rearrange("p (h d) -> p h d", h=16),
                                            Db.t[:rows, :].unsqueeze(2).broadcast_to([rows, 16, 64]), ALU.mult), [xs_sb, Db], [f3])
            P.dve(lambda e: e.tensor_tensor(f2.t[:rows, :], f2.t[:rows, :], f3.t[:rows, :], ALU.add), [f2, f3], [f2])
            pz = 4
            for o in range(2):
                for k in range(8):
                    self.mm(ps.t[:rows, pz + o, :], xT.t[:, k, col0:col0 + rows], Wz[o].t[:, k, :], k == 0, k == 7, [(xT, m), Wz[o]], [(ps, pz + o)])
            pzv = ps.t[:rows, pz:pz + 2, :].rearrange("p b n -> p (b n)")
            self.sigmoid_chain(f3.t[:rows, :], pzv, [(ps, pz), (ps, pz + 1)], f3)
            P.dve(lambda e: e.tensor_tensor(f3.t[:rows, :], pzv, f3.t[:rows, :], ALU.mult), [(ps, pz), (ps, pz + 1), f3], [f3])
            P.dve(lambda e: e.tensor_tensor(f2.t[:rows, :], f2.t[:rows, :], f3.t[:rows, :], ALU.mult), [f2, f3], [f2])
            for g in range(2):
                P.act(lambda e, g=g: e.activation(f3.t[:rows, g * 512:(g + 1) * 512], f2.t[:rows, g * 512:(g + 1) * 512], AF.Square,
                                                  accum_out=nst.t[:rows, 4 + g:5 + g]), [f2], [f3, nst])
            P.act(lambda e: e.activation(nst.t[:rows, 0:2], nst.t[:rows, 4:6], AF.Ln, scale=1.0 / 512.0, bias=float(NORM_EPS)), [nst], [nst])
            P.act(lambda e: e.activation(nst.t[:rows, 2:4], nst.t[:rows, 0:2], AF.Exp, scale=-0.5), [nst], [nst])
            for g in range(2):
                P.dve(lambda e, g=g: e.tensor_scalar(ogb.t[:rows, g * 512:(g + 1) * 512], f2.t[:rows, g * 512:(g + 1) * 512],
                                                     nst.t[:rows, 2 + g:3 + g], None, ALU.mult), [f2, nst], [ogb])

            def evac(pst, bt):
                P.dve(lambda e: e.tensor_tensor(hT.t[:, 0:8, col0:col0 + rows], pst[:, :, :rows],
                                                nwT.t[:, :].unsqueeze(2).broadcast_to([128, 8, rows]), ALU.mult),
                      [(ps, bt), nwT], [(hT, list(range(8)))])

            self.transposes_to(evac, lambda k: ogb.t[:rows, k * 128:(k + 1) * 128], 8, rows, ogb, None, bt=7)

        for (m, rows, col0) in self.tiles():
            tile_body(m, rows, col0)
        self.state_store("ssm_p", l)

    def gdn(self, l):
        self.P.tag = "gdn"
        P, I, c = self.P, self.i, self.cfg
        ps, xT, hT, S, Sb, A = self.ps, self.xT, self.hT, self.S, self.Sb, self.A
        w_in = I["w_in"][l]
        TP = c.NTH * 128
        hs = self.has_sample
        masks = self.masks
        ONES, MBI, MBS = masks.t[:, 1, :], masks.t[:, 4, :], masks.t[:, 5, :]
        U64 = masks.t[:, 3, :]
        A.reset()
        cw = A.alloc("cw", [24, 4], F32)
        dtb = A.alloc("dtb", [8], F32)
        Ab = A.alloc("Ab", [8], F32)
        nwb = A.alloc("nwb", [128], F32)
        wab = A.alloc("wab", [8, 16], BF16)
        off0 = A.off
        tk = A.alloc("tk", [3072], F32)
        P.dma("sp", tk.t[0:4, :], I["gdn_conv_w"][l], [], [tk], tk.sem())
        self.colvecs(cw.t[:].rearrange("p c j -> p (c j)"), tk, 4, 24, cw)
        P.dma("sp", dtb.t[:], I["gdn_dt_bias"][l].partition_broadcast(128), [], [dtb], dtb.sem())
        P.dma("sp", Ab.t[:], I["gdn_a_log"][l].partition_broadcast(128), [], [Ab], Ab.sem())
        P.dma("sp", nwb.t[:], I["gdn_norm_w"][l].partition_broadcast(128), [], [nwb], nwb.sem())
        P.act(lambda e: e.activation(Ab.t[:], Ab.t[:], AF.Exp), [Ab], [Ab])
        P.act(lambda e: e.activation(Ab.t[:], Ab.t[:], AF.Copy, scale=-1.0), [Ab], [Ab])
        P.dma("pool", wab.t[:], w_in[:, C_GA:C_GA + 16].rearrange("(k p) c -> p k c", p=128), [], [wab], wab.sem())
        A.reset(off0)
        qkv = A.alloc("qkv", [24, TP], BF16, nreg=24)
        if hs:
            qkvS = A.alloc("qkvS", [24, 16], F32)
        off1 = A.off
        xraw_b = [A.alloc("xraw", [3 + TP], F32) for _ in range(2)]
        NPAR = 2 if hs else 3
        xraw_b = xraw_b + [A.alloc("xraw", [3 + TP], F32) for _ in range(NPAR - 2)]
        acc_b = [A.alloc("acc", [TP], F32) for _ in range(NPAR)]
        sgm_b = [A.alloc("sgm", [TP], F32) for _ in range(NPAR)]
        sq_b = [A.alloc("sq", [TP], F32) for _ in range(NPAR)]
        stg = A.alloc("stg", [512], F32)
        if hs:
            stc_b = [A.alloc("stc", [3, 128], F32) for _ in range(2)]
            xrs_b = [A.alloc("xrs", [4, 16], F32) for _ in range(2)]
            accS_b = [A.alloc("accS", [16], F32) for _ in range(2)]
            sgS_b = [A.alloc("sgS", [16], F32) for _ in range(2)]
            sqS_b = [A.alloc("sqS", [16], F32) for _ in range(2)]
        tail = self.tail_gdn[l]
        if self.half == 0:
            P.dve(lambda e: e.memset(tail.t[:], 0.0), [], [tail])
        self.state_load("gdn_p", l)
        lnq = float(np.log(128.0 ** -0.5))

        def l2n_g(dst_ap, xin_ap, sq_ap, n, is_q, bufs_r, bufs_w, b):
            P.act(lambda e: e.activation(sq_ap, xin_ap, AF.Square), bufs_r, [bufs_w[0]])
            self.mm(ps.t[:, b, :n], self.masks.t[:, 1, :], sq_ap, True, True, [masks, bufs_w[0]], [(ps, b)])
            yield
            P.act(lambda e: e.activation(sq_ap, ps.t[:, b, :n], AF.Ln, bias=float(NORM_EPS)), [(ps, b)], [bufs_w[0]])
            if is_q:
                P.act(lambda e: e.activation(sq_ap, sq_ap, AF.Exp, scale=-0.5, bias=lnq), [bufs_w[0]], [bufs_w[0]])
            else:
                P.act(lambda e: e.activation(sq_ap, sq_ap, AF.Exp, scale=-0.5), [bufs_w[0]], [bufs_w[0]])
            yield
            P.dve(lambda e: e.tensor_tensor(dst_ap, xin_ap, sq_ap, ALU.mult), list(bufs_r) + [bufs_w[0]], [bufs_w[1]])

        def conv_chunk(cc, slot):
            cs = (cc % 4) * 128
            par = cc % NPAR
            acc, sgm, sq = acc_b[par], sgm_b[par], sq_b[par]
            b = 2 + 2 * par
            bl = 3 + 2 * par
            for k in range(8):
                self.mm(ps.t[:, b, :TP], slot.t[:, k, cs:cs + 128], xT.t[:, k, 0:TP], k == 0, k == 7,
                        [slot, (xT, list(range(c.NTH)))], [(ps, b)])
            xraw = xraw_b[par]
            P.act(lambda e: e.activation(xraw.t[:, 0:3], tail.t[:, cc, :], AF.Copy), [(tail, cc)], [xraw])
            P.act(lambda e: e.activation(xraw.t[:, 3:3 + TP], ps.t[:, b, :TP], AF.Copy), [(ps, b)], [xraw])
            P.act(lambda e: e.activation(tail.t[:, cc, :], xraw.t[:, TP:TP + 3], AF.Copy), [xraw], [(tail, cc)])
            yield
            P.dve(lambda e: e.tensor_scalar(acc.t[:, :], xraw.t[:, 0:TP], cw.t[:, cc, 0:1], None, ALU.mult), [xraw, cw], [acc])
            for j in range(1, 4):
                P.dve(lambda e, j=j: e.scalar_tensor_tensor(acc.t[:, :], xraw.t[:, j:j + TP], cw.t[:, cc, j:j + 1], acc.t[:, :],
                                                            ALU.mult, ALU.add), [xraw, cw, acc], [acc])
            yield
            if cc < 16:
                self.sigmoid_chain(sgm.t[:, :], acc.t[:, :], [acc], sgm)
                yield
                P.dve(lambda e: e.tensor_tensor(acc.t[:, :], acc.t[:, :], sgm.t[:, :], ALU.mult), [acc, sgm], [acc])
                yield from l2n_g(qkv.t[:, cc, :], acc.t[:, :], sq.t[:, :], TP, cc < 8, [acc], [sq, (qkv, cc)], bl)
            else:
                P.act(lambda e: e.activation(qkv.t[:, cc, :], acc.t[:, :], AF.Silu), [acc], [(qkv, cc)])
            if hs:
                yield
                b2 = 6 + par
                for k in range(8):
                    self.mm(ps.t[:, b2, 0:16], slot.t[:, k, cs:cs + 128], xT.t[:, k, TP:TP + 16], k == 0, k == 7,
                            [slot, (xT, c.NTH)], [(ps, b2)])
                stc = stc_b[par]
                P.dma("sp", stc.t[0:16, :, :], I["st_gdn_conv"][l, :, :, cc * 128:(cc + 1) * 128], [], [stc], stc.sem())
                for j in range(3):
                    P.pe(lambda e, j=j: e.transpose(ps.t[:, b2, 16 + 16 * j:32 + 16 * j], stc.t[0:16, j, :], self.identf.t[0:16, 0:16]),
                         [stc, self.identf], [(ps, b2)])
                xrs = xrs_b[par]
                P.act(lambda e: e.activation(xrs.t[:, 0:3, :], ps.t[:, b2, 16:64].rearrange("p (j s) -> p j s", j=3), AF.Copy),
                      [(ps, b2)], [xrs])
                P.act(lambda e: e.activation(xrs.t[:, 3, :], ps.t[:, b2, 0:16], AF.Copy), [(ps, b2)], [xrs])
                accS, sgS, sqS = accS_b[par], sgS_b[par], sqS_b[par]
                P.dve(lambda e: e.tensor_scalar(accS.t[:, :], xrs.t[:, 0, :], cw.t[:, cc, 0:1], None, ALU.mult), [xrs, cw], [accS])
                for j in range(1, 4):
                    P.dve(lambda e, j=j: e.scalar_tensor_tensor(accS.t[:, :], xrs.t[:, j, :], cw.t[:, cc, j:j + 1], accS.t[:, :],
                                                                ALU.mult, ALU.add), [xrs, cw, accS], [accS])
                yield
                if cc < 16:
                    self.sigmoid_chain(sgS.t[:, :], accS.t[:, :], [accS], sgS)
                    yield
                    P.dve(lambda e: e.tensor_tensor(accS.t[:, :], accS.t[:, :], sgS.t[:, :], ALU.mult), [accS, sgS], [accS])
                    yield from l2n_g(qkvS.t[:, cc, :], accS.t[:, :], sqS.t[:, :], 16, cc < 8, [accS], [sqS, qkvS], b2)
                else:
                    P.act(lambda e: e.activation(qkvS.t[:, cc, :], accS.t[:, :], AF.Silu), [accS], [qkvS])

        def conv_rows(slot, i, c0, n, dst):
            b = self.bank()
            for k in range(8):
                self.mm(ps.t[:n, b, :], xT.t[:, k, c0:c0 + n], slot.t[:, k, :], k == 0, k == 7,
                        [slot, (xT, list(range(self.NTT)))], [(ps, b)])
            P.act(lambda e: e.activation(stg.t[:n, :], ps.t[:n, b, :], AF.Copy), [(ps, b)], [stg])
            P.dma("sp", dst, stg.t[:n, :], [stg], [], stg.sem(), is_output=True)

        def load_slot(i):
            sl = self.wslot()
            self.wload(sl, 0, w_in[:, C_GQKV + i * 512:C_GQKV + (i + 1) * 512])
            return sl

        nxt = load_slot(0)
        for i in range(6):
            slot = nxt
            if i + 1 < 6:
                nxt = load_slot(i + 1)
            self.run_pipelined((conv_chunk(cc, slot) for cc in range(4 * i, 4 * i + 4)), NPAR)
            if self.half == c.NH - 1:
                conv_rows(slot, i, TP - 3, 3, self.o["gdn_conv_p"][l, :, i * 512:(i + 1) * 512])
            if hs:
                conv_rows(slot, i, TP, 16, self.o["gdn_conv_s"][l, :, 2, i * 512:(i + 1) * 512])
        if hs:
            P.dma("sp", self.o["gdn_conv_s"][l, :, 0:2, :], I["st_gdn_conv"][l, :, 1:3, :], [], [], stg.sem(), is_output=True)
        if self.cfg.dbg.get("gdn_stop", 9) <= 1:
            return
        Wz = (self.wslot(), self.wslot())
        for o in range(2):
            self.wload(Wz[o], 0, w_in[:, C_GZ + o * 512:C_GZ + (o + 1) * 512])

        A.reset(off1)
        sm = A.alloc("sm", [14, 8], F32)
        osb = A.alloc("osb", [1024], F32, nreg=2)
        f3 = A.alloc("f3", [1024], F32)
        ogb = A.alloc("ogb", [1024], BF16)
        nst = A.alloc("nst", [3, 8], F32)
        off2 = A.off
        sm_b = [sm, A.alloc("sm1", [14, 8], F32)]
        cumT_b = [A.alloc("cumT", [2, 128], F32) for _ in range(2)]
        Bd1 = A.alloc("Bd1", [4, 128], F32)
        o_d = A.off
        dtmp = A.alloc("dtmp", [4, 128], F32)
        A.reset(o_d)
        ktok = A.alloc("ktok", [4, 128], BF16)
        A.reset(o_d + 2048)
        f1 = A.alloc("f1", [512], F32)
        Ya, Yb = A.alloc("Ya", [4, 128], F32), A.alloc("Yb", [4, 128], F32)
        YTa, YTb = A.alloc("YTa", [4, 128], F32), A.alloc("YTb", [4, 128], F32)
        rhs = A.alloc("rhs", [4, 128], F32)
        ub = A.alloc("ub", [4, 128], BF16)
        PT_b = [A.alloc("PT", [4, 128], F32) for _ in range(2)]
        attnT_b = [A.alloc("attnT", [4, 128], BF16) for _ in range(2)]
        qdT_b = [A.alloc("qdT", [4, 128], BF16) for _ in range(2)]
        kdec_b = [A.alloc("kdec", [2, 4, 128], BF16) for _ in range(2)]
        vb_b = [A.alloc("vb", [4, 128], F32) for _ in range(2)]
        if hs:
            A.reset(off2)
            Eall = A.alloc("Eall", [16, 8], F32)
            Bde = A.alloc("Bde", [16, 8], F32)
            kTm_b = [A.alloc("kTm", [8, 16], F32) for _ in range(2)]
            qTm_b = [A.alloc("qTm", [8, 16], F32) for _ in range(2)]
            ktS = A.alloc("ktS", [1024], F32)
            vbS = A.alloc("vbS", [1024], F32)
            um_b = [A.alloc("um", [1024], F32) for _ in range(2)]
            SsB = A.alloc("SsB", [1024], F32)
            oacc = A.alloc("oacc", [1024], F32)

        def gates(m, rows, col0, sm=sm):
            R = lambda i: sm.t[:rows, i, :]
            bab = 6
            for k in range(8):
                self.mm(ps.t[:rows, bab, 0:16], xT.t[:, k, col0:col0 + rows], wab.t[:, k, :], k == 0, k == 7, [(xT, m), wab], [(ps, bab)])
            P.act(lambda e: e.activation(R(7), ps.t[:rows, bab, 8:16], AF.Exp, scale=-1.0), [(ps, bab)], [sm])
            P.act(lambda e: e.activation(R(6), R(7), AF.Ln, bias=1.0), [sm], [sm])
            P.act(lambda e: e.activation(R(6), R(6), AF.Copy, scale=-1.0), [sm], [sm])
            P.act(lambda e: e.activation(R(0), R(6), AF.Exp), [sm], [sm])
            P.dve(lambda e: e.tensor_tensor(R(7), ps.t[:rows, bab, 0:8], dtb.t[:rows, :], ALU.add), [(ps, bab), dtb], [sm])
            P.act(lambda e: e.activation(R(7), R(7), AF.Exp), [sm], [sm])
            P.act(lambda e: e.activation(R(7), R(7), AF.Ln, bias=1.0), [sm], [sm])
            P.dve(lambda e: e.tensor_tensor(R(1), R(7), Ab.t[:rows, :], ALU.mult), [sm, Ab], [sm])

        v4 = lambda ap: ap.rearrange("p (h i) -> p h i", h=4)

        def tile_front(m, col0, sm, cumT):
            R = lambda i: sm.t[:, i, :]
            gates(m, 128, col0, sm)
            bc = 6
            self.mm(ps.t[:, bc, 32:40], U64, R(1), True, True, [masks, sm], [(ps, bc)])
            self.mm(ps.t[:, bc, 40:48], masks.t[:, 6, :], R(1), True, True, [masks, sm], [(ps, bc)])
            self.mm(ps.t[:, bc, 48:56], masks.t[:, 7, :], R(1), True, True, [masks, sm], [(ps, bc)])
            self.mm(ps.t[:, bc, 56:64], ONES, R(1), True, True, [masks, sm], [(ps, bc)])
            P.act(lambda e: e.activation(R(2), ps.t[:, bc, 32:40], AF.Copy), [(ps, bc)], [sm])
            P.act(lambda e: e.activation(R(3), ps.t[:, bc, 32:40], AF.Exp), [(ps, bc)], [sm])
            P.dve(lambda e: e.tensor_tensor(R(5), ps.t[:, bc, 40:48], R(2), ALU.subtract), [(ps, bc), sm], [sm])
            P.act(lambda e: e.activation(R(5), R(5), AF.Exp), [sm], [sm])
            P.dve(lambda e: e.tensor_copy(sm.t[:, 12:14, :], sm.t[:, 5:6, :].broadcast_to([128, 2, 8])), [sm], [sm])
            P.dve(lambda e: e.memset(sm.t[64:128, 12, :], 0.0), [sm], [sm])
            P.dve(lambda e: e.memset(sm.t[0:64, 13, :], 0.0), [sm], [sm])
            P.act(lambda e: e.activation(R(8), ps.t[:, bc, 48:56], AF.Exp), [(ps, bc)], [sm])
            P.act(lambda e: e.activation(R(7), ps.t[:, bc, 48:56], AF.Copy), [(ps, bc)], [sm])
            P.dve(lambda e: e.tensor_tensor(R(9), ps.t[:, bc, 56:64], R(7), ALU.subtract), [(ps, bc), sm], [sm])
            P.act(lambda e: e.activation(R(9), R(9), AF.Exp), [sm], [sm])
            P.dve(lambda e: e.scalar_tensor_tensor(R(4), R(0), -1.0, R(3), ALU.mult, ALU.mult), [sm], [sm])
            P.dve(lambda e: e.tensor_tensor(R(10), R(2), R(6), ALU.add), [sm], [sm])
            self.mm(ps.t[0:8, bc, 64:192], R(2), self.identf.t[:, :], True, True, [sm, self.identf], [(ps, bc)])
            self.mm(ps.t[0:8, bc, 192:320], R(10), self.identf.t[:, :], True, True, [sm, self.identf], [(ps, bc)])
            P.act(lambda e: e.activation(cumT.t[0:8, :, :].rearrange("p a i -> p (a i)"), ps.t[0:8, bc, 64:320], AF.Copy), [(ps, bc)], [cumT])

        def unit_gen(u, m, col0, g, part):
            par = u % 2
            BA, BB, BC = (2, 3, 4) if par == 0 else (0, 1, 5)
            sm, cumT = sm_b[m % 2], cumT_b[m % 2]
            PT, attnT, qdT, kdec, vb = PT_b[par], attnT_b[par], qdT_b[par], kdec_b[par], vb_b[par]
            R = lambda i: sm.t[:, i, :]
            hsl = slice(4 * g, 4 * g + 4)
            if part == "A":
                if g == 0:
                    tile_front(m, col0, sm, cumT)
                    yield
                hsl = slice(4 * g, 4 * g + 4)
                eye_g = self.eyep.t[0:8, 4 * g:4 * g + 4].unsqueeze(2).broadcast_to([8, 4, 128])
                bdf = Bd1.t[0:8, :, :].rearrange("p h i -> p (h i)")
                cum_b = R(2)[:, hsl].unsqueeze(2).broadcast_to([128, 4, 128])
                P.dve(lambda e: e.tensor_tensor(Bd1.t[0:8, :, :], cumT.t[0:8, 0, :].unsqueeze(1).broadcast_to([8, 4, 128]), eye_g, ALU.mult),
                      [cumT, self.eyep], [Bd1])
                self.mm(ps.t[:, BA, :], ONES[0:8, :], bdf, True, True, [masks, Bd1], [(ps, BA)])
                self.mm(ps.t[:, BB, :], ONES[0:8, :], bdf, True, False, [masks, Bd1], [(ps, BB)])
                for hq in range(4):
                    self.mm(ps.t[:, BB, hq * 128:(hq + 1) * 128], self.identf.t[:, :], MBI, False, hq == 3, [masks, self.identf], [(ps, BB)])
                for h in range(4):
                    hh = 4 * g + h
                    self.mm(ps.t[:, BC, h * 128:(h + 1) * 128], qkv.t[:, 8 + hh, col0:col0 + 128], qkv.t[:, hh, col0:col0 + 128], True, True,
                            [(qkv, [hh, 8 + hh])], [(ps, BC)])
                yield
                P.act(lambda e: e.activation(dtmp.t[:, :, :].rearrange("p h i -> p (h i)"), ps.t[:, BA, :], AF.Exp), [(ps, BA)], [dtmp])
                P.dve(lambda e: e.tensor_tensor(f1.t[:, :].rearrange("p (h i) -> p h i", h=4), v4(ps.t[:, BB, :]), cum_b, ALU.subtract), [(ps, BB), sm], [f1])
                yield
                P.dve(lambda e: e.tensor_tensor(qdT.t[:, :, :], qkv.t[:, 4 * g:4 * g + 4, col0:col0 + 128], dtmp.t[:, :, :], ALU.mult),
                      [(qkv, list(range(4 * g, 4 * g + 4))), dtmp], [qdT])
                P.act(lambda e: e.activation(f1.t[:, :], f1.t[:, :], AF.Exp), [f1], [f1])
                yield
                P.dve(lambda e: e.tensor_tensor(attnT.t[:, :, :], v4(ps.t[:, BC, :]), f1.t[:, :].rearrange("p (h i) -> p h i", h=4), ALU.mult),
                      [(ps, BC), f1], [attnT])
                P.dve(lambda e: e.tensor_tensor(Bd1.t[0:8, :, :], cumT.t[0:8, 1, :].unsqueeze(1).broadcast_to([8, 4, 128]), eye_g, ALU.mult),
                      [cumT, self.eyep], [Bd1])
                self.mm(ps.t[:, BB, :], ONES[0:8, :], bdf, True, False, [masks, Bd1], [(ps, BB)])
                for hq in range(4):
                    self.mm(ps.t[:, BB, hq * 128:(hq + 1) * 128], self.identf.t[:, :], MBS, False, hq == 3, [masks, self.identf], [(ps, BB)])
                for h in range(4):
                    hh = 4 * g + h
                    self.mm(ps.t[:, BA, h * 128:(h + 1) * 128], qkv.t[:, 8 + hh, col0:col0 + 128], qkv.t[:, 8 + hh, col0:col0 + 128], True, True,
                            [(qkv, 8 + hh)], [(ps, BA)])
                yield
                P.dve(lambda e: e.tensor_tensor(dtmp.t[:, :, :], v4(ps.t[:, BB, :]), cum_b, ALU.subtract), [(ps, BB), sm], [dtmp])
                yield
                P.act(lambda e: e.activation(dtmp.t[:, :, :], dtmp.t[:, :, :], AF.Exp), [dtmp], [dtmp])
                yield
                P.dve(lambda e: e.scalar_tensor_tensor(YTa.t[:, :, :], v4(ps.t[:, BA, :]), -1.0, dtmp.t[:, :, :], ALU.mult, ALU.mult),
                      [(ps, BA), dtmp], [YTa])
                for h in range(4):
                    P.pe(lambda e, h=h: e.transpose(ps.t[:, BB, h * 128:(h + 1) * 128], YTa.t[:, h, :], self.identf.t[:, :]),
                         [YTa, self.identf], [(ps, BB)])
                yield
                P.act(lambda e: e.activation(Ya.t[:, :, :], v4(ps.t[:, BB, :]), AF.Copy), [(ps, BB)], [Ya])
                P.dve(lambda e: e.tensor_tensor(PT.t[:, :, :], YTa.t[:, :, :], self.identf.t[:, :].unsqueeze(1).broadcast_to([128, 4, 128]), ALU.add),
                      [YTa, self.identf], [PT])
                def ev_k(pst, bt):
                    P.act(lambda e: e.activation(ktok.t[:, :, :], pst[:, 0:4, :], AF.Copy), [(ps, bt)], [ktok])
                self.transposes_T(ev_k, lambda k: qkv.t[:, 8 + 4 * g + k, col0:col0 + 128], 4, 128, (qkv, list(range(8 + 4 * g, 12 + 4 * g))), bt=7)
                yield
                def level(lev, Y, YT, Yn, YTn):
                    for h in range(4):
                        self.mm(ps.t[:, BA, h * 128:(h + 1) * 128], YT.t[:, h, :], Y.t[:, h, :], True, True, [YT, Y], [(ps, BA)])
                    if lev < 5:
                        for h in range(4):
                            self.mm(ps.t[:, BB, h * 128:(h + 1) * 128], Y.t[:, h, :], YT.t[:, h, :], True, True, [YT, Y], [(ps, BB)])
                    yield
                    P.act(lambda e: e.activation(Yn.t[:, :, :], v4(ps.t[:, BA, :]), AF.Copy), [(ps, BA)], [Yn])
                    if lev < 5:
                        P.act(lambda e: e.activation(YTn.t[:, :, :], v4(ps.t[:, BB, :]), AF.Copy), [(ps, BB)], [YTn])
                    yield
                    for h in range(4):
                        self.mm(ps.t[:, BC, h * 128:(h + 1) * 128], Yn.t[:, h, :], PT.t[:, h, :], True, True, [Yn, PT], [(ps, BC)])
                    yield
                    P.dve(lambda e: e.tensor_tensor(PT.t[:, :, :], PT.t[:, :, :], v4(ps.t[:, BC, :]), ALU.add), [PT, (ps, BC)], [PT])
                Y, YT, Yn, YTn = Ya, YTa, Yb, YTb
                for lev in range(1, 6):
                    yield from level(lev, Y, YT, Yn, YTn)
                    if lev == 1:
                        for cq in range(2):
                            P.dve(lambda e, cq=cq: e.tensor_tensor(kdec.t[:, cq, :, :], ktok.t[:, :, :],
                                                                   R(12 + cq)[:, hsl].unsqueeze(2).broadcast_to([128, 4, 128]), ALU.mult),
                                  [ktok, sm], [kdec])
                    if lev == 2:
                        def ev_v(pst, bt):
                            P.dve(lambda e: e.tensor_tensor(vb.t[:, :, :], pst[:, 0:4, :], R(0)[:, hsl].unsqueeze(2).broadcast_to([128, 4, 128]), ALU.mult),
                                  [(ps, bt), sm], [vb])
                        self.transposes_T(ev_v, lambda k: qkv.t[:, 16 + 4 * g + k, col0:col0 + 128], 4, 128,
                                          (qkv, list(range(16 + 4 * g, 20 + 4 * g))), bt=7)
                    Y, YT, Yn, YTn = Yn, YTn, Y, YT
                    yield
                return
            Sg = (S, list(range(8 * g, 8 * g + 8)))
            Sbg = (Sb, list(range(8 * g, 8 * g + 8)))

            def chunk(ch):
                rs = slice(64 * ch, 64 * ch + 64)
                rw = slice(0, 128) if ch == 0 else rs
                nrw = 128 if ch == 0 else 64
                for h in range(4):
                    hh = 4 * g + h
                    self.mm(ps.t[:, BA, h * 128:(h + 1) * 128], qkv.t[:, 8 + hh, col0:col0 + 128], Sb.t[:, hh * 128:(hh + 1) * 128], True, True,
                            [(qkv, 8 + hh), Sbg], [(ps, BA)])
                yield
                nbe_b = R(4)[rw, hsl].unsqueeze(2).broadcast_to([nrw, 4, 128])
                P.dve(lambda e: e.tensor_tensor(rhs.t[rw, :, :], v4(ps.t[rw, BA, :]), nbe_b, ALU.mult), [(ps, BA), sm], [rhs])
                P.dve(lambda e: e.tensor_tensor(rhs.t[rw, :, :], rhs.t[rw, :, :], vb.t[rw, :, :], ALU.add), [rhs, vb], [rhs])
                yield
                for h in range(4):
                    self.mm(ps.t[:, BB, h * 128:(h + 1) * 128], PT.t[:, h, :], rhs.t[:, h, :], True, True, [PT, rhs], [(ps, BB)])
                yield
                P.act(lambda e: e.activation(ub.t[rw, :, :], v4(ps.t[rw, BB, :]), AF.Copy), [(ps, BB)], [ub])
                yield
                for h in range(4):
                    hh = 4 * g + h
                    self.mm(ps.t[:, BC, h * 128:(h + 1) * 128], qdT.t[:, h, :], Sb.t[:, hh * 128:(hh + 1) * 128], True, False, [qdT, Sbg], [(ps, BC)])
                    self.mm(ps.t[:, BC, h * 128:(h + 1) * 128], attnT.t[:, h, :], ub.t[:, h, :], False, True, [attnT, ub], [(ps, BC)])
                for h in range(4):
                    self.mm(ps.t[:, BA, h * 128:(h + 1) * 128], kdec.t[:, ch, h, :], ub.t[:, h, :], True, True, [kdec, ub], [(ps, BA)])
                yield
                P.act(lambda e: e.activation(osb.t[rs, g * 512:(g + 1) * 512], ps.t[rs, BC, :], AF.Copy), [(ps, BC)], [(osb, g)])
                el_b = sm.t[:, 8 + ch, hsl].unsqueeze(2).broadcast_to([128, 4, 128])
                P.dve(lambda e: e.tensor_tensor(v4(S.t[:, g * 512:(g + 1) * 512]), v4(S.t[:, g * 512:(g + 1) * 512]), el_b, ALU.mult), [Sg, sm], [Sg])
                P.dve(lambda e: e.tensor_tensor(S.t[:, g * 512:(g + 1) * 512], S.t[:, g * 512:(g + 1) * 512], ps.t[:, BA, :], ALU.add), [Sg, (ps, BA)], [Sg])
                yield
                P.act(lambda e: e.activation(Sb.t[:, g * 512:(g + 1) * 512], S.t[:, g * 512:(g + 1) * 512], AF.Copy), [Sg], [Sbg])
                yield

            for ch in range(2):
                yield from chunk(ch)
            if g == 1:
                post(m, 128, col0, osb)

        def post(m, rows, col0, o_buf):
            v8 = lambda ap: ap.rearrange("p (h d) -> p h d", h=8)
            P.dve(lambda e: e.tensor_tensor(f3.t[:rows, :], o_buf.t[:rows, :], o_buf.t[:rows, :], ALU.mult), [o_buf], [f3])
            P.dve(lambda e: e.tensor_reduce(nst.t[:rows, 0, :], v8(f3.t[:rows, :]), mybir.AxisListType.X, ALU.add), [f3], [nst])
            P.act(lambda e: e.activation(nst.t[:rows, 1, :], nst.t[:rows, 0, :], AF.Ln, scale=1.0 / 128.0, bias=float(NORM_EPS)), [nst], [nst])
            P.act(lambda e: e.activation(nst.t[:rows, 2, :], nst.t[:rows, 1, :], AF.Exp, scale=-0.5), [nst], [nst])
            P.dve(lambda e: e.tensor_tensor(v8(o_buf.t[:rows, :]), v8(o_buf.t[:rows, :]), nst.t[:rows, 2, :].unsqueeze(2).broadcast_to([rows, 8, 128]),
                                            ALU.mult), [o_buf, nst], [o_buf])
            P.dve(lambda e: e.tensor_tensor(v8(o_buf.t[:rows, :]), v8(o_buf.t[:rows, :]), nwb.t[:rows, :].unsqueeze(1).broadcast_to([rows, 8, 128]),
                                            ALU.mult), [o_buf, nwb], [o_buf])
            pz = 4
            for o in range(2):
                for k in range(8):
                    self.mm(ps.t[:rows, pz + o, :], xT.t[:, k, col0:col0 + rows], Wz[o].t[:, k, :], k == 0, k == 7, [(xT, m), Wz[o]], [(ps, pz + o)])
            pzv = ps.t[:rows, pz:pz + 2, :].rearrange("p b n -> p (b n)")
            self.sigmoid_chain(f3.t[:rows, :], pzv, [(ps, pz), (ps, pz + 1)], f3)
            P.dve(lambda e: e.tensor_tensor(f3.t[:rows, :], pzv, f3.t[:rows, :], ALU.mult), [(ps, pz), (ps, pz + 1), f3], [f3])
            P.dve(lambda e: e.tensor_tensor(ogb.t[:rows, :], o_buf.t[:rows, :], f3.t[:rows, :], ALU.mult), [o_buf, f3], [ogb])

            def evac(pst, bt):
                P.act(lambda e: e.activation(hT.t[:, 0:8, col0:col0 + rows], pst[:, :, :rows], AF.Copy), [(ps, bt)], [(hT, list(range(8)))])

            self.transposes_to(evac, lambda k: ogb.t[:rows, k * 128:(k + 1) * 128], 8, rows, ogb, None, bt=7)

        def sample_body(m, col0):
            R = lambda i: sm.t[0:16, i, :]
            gates(m, 16, col0)
            P.act(lambda e: e.activation(R(3), R(1), AF.Exp), [sm], [sm])
            P.dve(lambda e: e.scalar_tensor_tensor(R(4), R(0), -1.0, R(3), ALU.mult, ALU.mult), [sm], [sm])
            P.dve(lambda e: e.tensor_tensor(Bde.t[0:16, :, :], R(3).unsqueeze(1).broadcast_to([16, 16, 8]),
                                            self.eyep.t[:, :].unsqueeze(2).broadcast_to([16, 16, 8]), ALU.mult), [sm, self.eyep], [Bde])
            self.mm(ps.t[:, 6, 0:128], ONES[0:16, :], Bde.t[0:16, :, :].rearrange("p s h -> p (s h)"), True, True, [masks, Bde], [(ps, 6)])
            P.act(lambda e: e.activation(Eall.t[:, :, :].rearrange("p s h -> p (s h)"), ps.t[:, 6, 0:128], AF.Copy), [(ps, 6)], [Eall])
            for h in range(8):
                P.pe(lambda e, h=h: e.transpose(ps.t[0:16, 2 + h // 4, (h % 4) * 128:(h % 4 + 1) * 128], qkvS.t[:, 8 + h, :], self.identf.t[:, :]),
                     [qkvS, self.identf], [(ps, 2 + h // 4)])
            P.act(lambda e: e.activation(ktS.t[0:16, :], ps.t[0:16, 2:4, :].rearrange("p b n -> p (b n)"), AF.Copy), [(ps, 2), (ps, 3)], [ktS])
            for h in range(8):
                P.pe(lambda e, h=h: e.transpose(ps.t[0:16, 4 + h // 4, (h % 4) * 128:(h % 4 + 1) * 128], qkvS.t[:, 16 + h, :], self.identf.t[:, :]),
                     [qkvS, self.identf], [(ps, 4 + h // 4)])
            P.dve(lambda e: e.tensor_tensor(vbS.t[0:16, :].rearrange("p (h d) -> p h d", h=8),
                                            ps.t[0:16, 4:6, :].rearrange("p b (h d) -> p (b h) d", h=4),
                                            R(0).unsqueeze(2).broadcast_to([16, 8, 128]), ALU.mult), [(ps, 4), (ps, 5), sm], [vbS])
            Ss_b = [osb, SsB]
            P.dve(lambda e: e.memset(oacc.t[0:16, :], 0.0), [], [oacc])
            v8 = lambda ap: ap.rearrange("p (h d) -> p h d", h=8)

            def smp(s_):
                par = s_ % 2
                Ss, kTm, qTm, um = Ss_b[par], kTm_b[par], qTm_b[par], um_b[par]
                pb = 2 + 2 * par
                pbv = ps.t[:, pb:pb + 2, :].rearrange("p b n -> p (b n)")
                P.dma("sp", Ss.t[:, :].rearrange("p (h v) -> p h v", h=8), I["st_gdn"][l, s_].rearrange("h k v -> k h v"), [], [Ss], Ss.sem())
                P.dve(lambda e: e.tensor_tensor(kTm.t[:, :, :], qkvS.t[:, 8:16, :],
                                                self.eyef.t[:, s_, :].unsqueeze(1).broadcast_to([128, 8, 16]), ALU.mult), [qkvS, self.eyef], [kTm])
                P.dve(lambda e: e.tensor_tensor(qTm.t[:, :, :], qkvS.t[:, 0:8, :],
                                                self.eyef.t[:, s_, :].unsqueeze(1).broadcast_to([128, 8, 16]), ALU.mult), [qkvS, self.eyef], [qTm])
                yield
                for h in range(8):
                    self.mm(ps.t[0:16, pb + h // 4, (h % 4) * 128:(h % 4 + 1) * 128], kTm.t[:, h, :], Ss.t[:, h * 128:(h + 1) * 128], True, True,
                            [kTm, Ss], [(ps, pb + h // 4)])
                yield
                P.dve(lambda e: e.tensor_tensor(v8(um.t[0:16, :]), v8(pbv[0:16, :]), R(4).unsqueeze(2).broadcast_to([16, 8, 128]), ALU.mult),
                      [(ps, pb), (ps, pb + 1), sm], [um])
                P.dve(lambda e: e.scalar_tensor_tensor(um.t[0:16, :], vbS.t[0:16, :], self.eyep.t[0:16, s_:s_ + 1], um.t[0:16, :],
                                                       ALU.mult, ALU.add), [vbS, self.eyep, um], [um])
                yield
                for h in range(8):
                    self.mm(ps.t[:, pb + h // 4, (h % 4) * 128:(h % 4 + 1) * 128], ktS.t[0:16, h * 128:(h + 1) * 128], um.t[0:16, h * 128:(h + 1) * 128],
                            True, True, [ktS, um], [(ps, pb + h // 4)])
                yield
                P.dve(lambda e: e.tensor_tensor(v8(Ss.t[:, :]), v8(Ss.t[:, :]), Eall.t[:, s_, :].unsqueeze(2).broadcast_to([128, 8, 128]), ALU.mult),
                      [Ss, Eall], [Ss])
                P.dve(lambda e: e.tensor_tensor(Ss.t[:, :], Ss.t[:, :], pbv, ALU.add), [Ss, (ps, pb), (ps, pb + 1)], [Ss])
                yield
                P.dma("sp", self.o["gdn_s"][l, s_].rearrange("h k v -> k h v"), Ss.t[:, :].rearrange("p (h v) -> p h v", h=8), [Ss], [], Ss.sem(),
                      is_output=True)
                for h in range(8):
                    self.mm(ps.t[0:16, pb + h // 4, (h % 4) * 128:(h % 4 + 1) * 128], qTm.t[:, h, :], Ss.t[:, h * 128:(h + 1) * 128],
                            True, True, [qTm, Ss], [(ps, pb + h // 4)])
                yield
                P.dve(lambda e: e.tensor_tensor(oacc.t[0:16, :], oacc.t[0:16, :], pbv[0:16, :], ALU.add), [oacc, (ps, pb), (ps, pb + 1)], [oacc])

            self.run_pipelined((smp(s_) for s_ in range(NS)), 2)
            post(m, 16, col0, oacc)

        units = [(m, col0, g) for (m, rows, col0) in self.tiles() if rows == 128 for g in range(2)]
        self.run_pipelined([unit_gen(0, *units[0], "A")], 1)
        for u in range(len(units)):
            gens = [unit_gen(u, *units[u], "B")]
            if u + 1 < len(units):
                gens.append(unit_gen(u + 1, *units[u + 1], "A"))
            self.run_pipelined(gens, 2)
        for (m, rows, col0) in self.tiles():
            if rows != 128:
                sample_body(m, col0)
        self.state_store("gdn_p", l)

    def transposes_T(self, evac, src_fn, n, rows, src_buf, bt=None):
        P, ps = self.P, self.ps
        bt = self.bank() if bt is None else bt
        pst = ps.t[:, bt, :].bitcast(BF16).rearrange("p (k n) -> p k n", k=8)
        for k in range(n):
            P.pe(lambda e, k=k: e.transpose(pst[:rows, k, :], src_fn(k), self.identb.t[:, :]), [src_buf, self.identb], [(ps, bt)])
        evac(pst, bt)

    def token_mix(self, l):
        I = self.i
        if "ret" in self.cfg.mix:
            self.retention(l)
            self.finale(l, I["w_ret_out"][l], C_M1)
        if "ssd" in self.cfg.mix:
            self.ssd(l)
            self.finale(l, I["w_ssm_out"][l], C_M2)
        if "gdn" in self.cfg.mix:
            self.gdn(l)
            self.finale(l, I["w_gdn_out"][l], C_M3)

    def build(self):
        c = self.cfg
        self.pT = [self.P.sbuf(f"pT{i}", [128, 2, 128], BF16) for i in range(2)]
        self.alloc_mix()
        for half in range(c.NH):
            self.half = half
            self.has_sample = c.sample and half == c.NH - 1
            self.load_x()
            for l in range(c.layers):
                last = (l == c.layers - 1)
                if c.ffn:
                    self.ffn(l, 0)
                self.layer_norm(l, 0)
                self.token_mix(l)
                self.layer_norm(l, 1)
                if c.ffn:
                    self.ffn(l, 1)
                self.layer_norm(l, 2)
                if c.pegate:
                    self.pe_gate(l)
                self.layer_norm(l, 3, final=last)
        return self.P.finish()


def make_consts(T):
    c = {}
    c["c_ident"] = np.eye(128, dtype=np.float32)
    half = 64
    inv = (np.float32(10000.0) ** (-np.arange(half, dtype=np.float32) / np.float32(half))).astype(np.float32)
    pos = np.concatenate([np.arange(T, dtype=np.float32), np.full(NS, PAST_LEN, np.float32)])
    ang = (pos[:, None] * inv[None, :]).astype(np.float32).astype(np.float64)
    rope = np.zeros((T + NS, 4, 64), np.float32)
    rope[:, 0] = np.cos(ang)
    rope[:, 1] = np.sin(ang)
    rope[:, 2] = np.cos(ang) * 128 ** -0.5
    rope[:, 3] = np.sin(ang) * 128 ** -0.5
    c["c_rope"] = rope
    gam = 1.0 - 2.0 ** (-5.0 - np.arange(4))
    lg = np.log1p(-(2.0 ** (-5.0 - np.arange(4)))).astype(np.float32).astype(np.float64)
    i = np.arange(128)
    dm = i[None, :] - i[:, None]
    mask = np.zeros((4, 128, 128), np.float32)
    row = np.zeros((4, 3, 128), np.float32)
    for h in range(4):
        mask[h] = np.where(dm >= 0, np.exp(lg[h] * np.maximum(dm, 0)), 0.0)
        row[h, 0] = np.exp(lg[h] * (i + 1))
        row[h, 1] = np.exp(lg[h] * (127 - i))
        row[h, 2, 0] = np.exp(lg[h] * 128)
        row[h, 2, 1] = np.exp(lg[h])
    c["c_retmask"] = mask
    c["c_retrow"] = row
    mk = np.zeros((8, 128, 128), np.float32)
    jj, ii = np.meshgrid(np.arange(128), np.arange(128), indexing="ij")
    blk = (jj // 64) == (ii // 64)
    mk[0] = (jj <= ii)
    mk[1] = 1.0
    mk[2] = np.where(jj <= ii, 0.0, -30000.0)
    mk[3] = (jj <= ii) & blk
    mk[4] = np.where((jj <= ii) & blk, 0.0, -30000.0)
    mk[5] = np.where((jj < ii) & blk, 0.0, -30000.0)
    mk[6] = blk
    mk[7] = (jj < 64)
    c["c_masks"] = mk
    return c


_CACHE = {}


def kernel(**inputs):
    NC = 8
    cfg = Cfg(NH=4, NTH=4, sample=True, layers=2)
    mk = MK(cfg)
    mk.build()
    T = cfg.T
    consts = make_consts(T)
    f = lambda a: np.ascontiguousarray(np.asarray(a, dtype=np.float32))
    wnames = ["ln_g", "ln_b", "ffn_wg", "ffn_wu", "ffn_wd", "w_in", "ssm_conv_w", "ssm_conv_b", "ssm_dt_bias",
              "ssm_a_log", "ssm_d", "ssm_norm_w", "gdn_conv_w", "gdn_dt_bias", "gdn_a_log", "gdn_norm_w",
              "w_ret_out", "w_ssm_out", "w_gdn_out", "w_o", "pe_proj", "pe_gate"]
    W = {k: f(inputs[k]) for k in wnames}
    xp, xs = np.asarray(inputs["x_prompt"]), np.asarray(inputs["x_sample"])
    pp, ps_ = np.asarray(inputs["p_prompt"]), np.asarray(inputs["p_sample"])
    in_maps = []
    for c in range(NC):
        sl = slice(NS * c, NS * (c + 1))
        m = dict(W)
        m.update(consts)
        m["xp"] = f(xp[c])
        m["pp"] = f(pp[:, c])
        m["xs"] = f(xs[sl, 0])
        m["ps"] = f(ps_[:, sl, 0])
        m["st_ret"] = f(np.asarray(inputs["state_ret"])[:, sl])
        m["st_ssm"] = f(np.asarray(inputs["state_ssm"])[:, sl])
        m["st_ssm_conv"] = f(np.asarray(inputs["state_ssm_conv"])[:, sl])
        m["st_gdn"] = f(np.asarray(inputs["state_gdn"])[:, sl])
        m["st_gdn_conv"] = f(np.asarray(inputs["state_gdn_conv"])[:, sl])
        in_maps.append({k: v for k, v in m.items() if k in mk.i})
    res = run_bass_kernel_spmd(mk.nc, in_maps, core_ids=list(range(NC)))
    R = res.results
    cat0 = lambda k: np.stack([R[c][k] for c in range(NC)], axis=0)
    y_p = cat0("y_p")
    y_s = np.concatenate([R[c]["y_s"] for c in range(NC)], 0)[:, None, :]
    outs = [y_p, y_s]
    for k in ("ret_p", "ssm_p", "ssm_conv_p", "gdn_p", "gdn_conv_p"):
        outs.append(np.stack([R[c][k] for c in range(NC)], axis=1))
    for k in ("ret_s", "ssm_s", "ssm_conv_s", "gdn_s", "gdn_conv_s"):
        outs.append(np.concatenate([R[c][k] for c in range(NC)], axis=1))
    return tuple(np.ascontiguousarray(o, dtype=np.float32) for o in outs)
```
